# Optimizing a Trainium2 kernel written in Bass

```python
import math
import jax, jax.numpy as jnp
from jax import lax
import numpy as np

D_MODEL = 1024
BATCH = 4
SEQ = 8192
DEPTH = 1

CTX_LEN = 256
GRID_W = 64
EPS = 1e-6
MIX_WIDTH = D_MODEL
GLA_WIDTH = MIX_WIDTH // 2
DIFF_WIDTH = MIX_WIDTH - GLA_WIDTH
GLA_HEADS = 4
GLA_DV = GLA_WIDTH // GLA_HEADS
GLA_DK = GLA_DV // 2
GLA_QK = GLA_HEADS * GLA_DK
GLA_GATE_RANK = 16
GLA_GATE_NORM = 16.0
GLA_CHUNK = 64
DIFF_HEADS = 4
DIFF_DV = DIFF_WIDTH // DIFF_HEADS
DIFF_DH = DIFF_DV // 2
ROPE_BASE = 10000.0
ROPE_AXIS_DIM = DIFF_DH // 2
Q_BLOCK = 128
FFN_HIDDEN = ((8 * D_MODEL // 3 + 255) // 256) * 256
IN_SIZES = (GLA_QK, GLA_QK, GLA_WIDTH, GLA_WIDTH, 2 * GLA_GATE_RANK, DIFF_WIDTH, DIFF_WIDTH, DIFF_WIDTH)
W_IN_COLS = sum(IN_SIZES)

kernel_name = "hymba_gla_diffattn_dit_block"


def rmsnorm(x, g):
    xf = x.astype(jnp.float32)
    y = xf * lax.rsqrt(jnp.mean(xf * xf, axis=-1, keepdims=True) + EPS)
    return (y * g.astype(jnp.float32)).astype(x.dtype)


def modulate(x, g, shift, scale):
    return rmsnorm(x, g) * (1 + scale) + shift


def split_heads(t, n_heads):
    b, n, _ = t.shape
    return t.reshape(b, n, n_heads, -1).transpose(0, 2, 1, 3)


def merge_heads(t):
    b, h, n, d = t.shape
    return t.transpose(0, 2, 1, 3).reshape(b, n, h * d)


def adaln_params(cond, w_mod, b_mod):
    return jnp.split(jax.nn.silu(cond) @ w_mod + b_mod, 6, axis=-1)


def axial_rope_tables(n_tokens):
    rows = n_tokens // GRID_W
    t = jnp.arange(rows * GRID_W)
    row = (t // GRID_W).astype(jnp.float32)
    col = (t % GRID_W).astype(jnp.float32)
    n_freq = ROPE_AXIS_DIM // 2
    inv_freq = ROPE_BASE ** (-jnp.arange(n_freq, dtype=jnp.float32) / n_freq)
    ang = jnp.stack([row[:, None] * inv_freq, col[:, None] * inv_freq], axis=1)
    return jnp.cos(ang), jnp.sin(ang)


def apply_axial_rope(x, cos, sin):
    xr = x.reshape(x.shape[:-1] + (2, 2, ROPE_AXIS_DIM // 2))
    x1, x2 = xr[..., 0, :], xr[..., 1, :]
    cs = cos[:, None].astype(x.dtype)
    sn = sin[:, None].astype(x.dtype)
    out = jnp.stack([x1 * cs - x2 * sn, x2 * cs + x1 * sn], axis=-2)
    return out.reshape(x.shape)


def project(h, w_in):
    points = np.cumsum(IN_SIZES)[:-1].tolist()
    return jnp.split(h @ w_in, points, axis=-1)


def gla_inputs(parts, gate_up, gate_bias):
    gq, gk, gv, gr, gdown = parts[:5]
    q = split_heads(gq, GLA_HEADS) * (GLA_DK ** -0.5)
    k = split_heads(gk, GLA_HEADS)
    v = split_heads(gv, GLA_HEADS)
    down = gdown.astype(jnp.float32).reshape(gdown.shape[:-1] + (2, GLA_GATE_RANK))
    logits = jnp.einsum('bnzr,zrk->bnzk', down, gate_up.astype(jnp.float32)) + gate_bias.astype(jnp.float32)
    log_alpha = jax.nn.log_sigmoid(logits) / GLA_GATE_NORM
    g_f = split_heads(log_alpha[:, :, 0], GLA_HEADS)
    g_b = split_heads(log_alpha[:, :, 1], GLA_HEADS)
    return q, k, v, gr, g_f, g_b


def gla_chunk_scan(q, k, v, g, s0):
    b, h, n, dk = q.shape
    dv = v.shape[-1]
    nc = n // GLA_CHUNK

    def chunks(t):
        return jnp.moveaxis(t.astype(jnp.float32).reshape(b, h, nc, GLA_CHUNK, t.shape[-1]), 2, 0)

    mask = jnp.tril(jnp.ones((GLA_CHUNK, GLA_CHUNK), dtype=bool))

    def step(state, inp):
        qc, kc, vc, gc = inp
        cum = jnp.cumsum(gc, axis=2)
        o_inter = jnp.einsum('bhcd,bhde->bhce', qc * jnp.exp(cum), state)
        rel = cum[:, :, :, None, :] - cum[:, :, None, :, :]
        decay = jnp.exp(jnp.where(mask[:, :, None], rel, -jnp.inf))
        attn = jnp.einsum('bhid,bhjd,bhijd->bhij', qc, kc, decay)
        o = o_inter + jnp.einsum('bhij,bhje->bhie', attn, vc)
        last = cum[:, :, -1:, :]
        state = jnp.exp(last[:, :, 0, :, None]) * state + jnp.einsum('bhcd,bhce->bhde', kc * jnp.exp(last - cum), vc)
        return state, o

    s_final, o = lax.scan(step, s0, (chunks(q), chunks(k), chunks(v), chunks(g)))
    o = jnp.moveaxis(o, 0, 2).reshape(b, h, n, dv)
    return s_final, o.astype(v.dtype)


def gla_bidir(q, k, v, g_f, g_b, s0_f, s0_b):
    flip = lambda t: jnp.flip(t, axis=2)
    s_f, o_f = gla_chunk_scan(q, k, v, g_f, s0_f)
    s_b, o_b = gla_chunk_scan(flip(q), flip(k), flip(v), flip(g_b), s0_b)
    return s_f, s_b, o_f + flip(o_b)


def gla_merge(o, r, norm_g):
    return merge_heads(rmsnorm(o, norm_g)) * jax.nn.silu(r)


def diff_qkv(parts, q_norm_g, k_norm_g):
    dq, dk, dv = parts[5:]
    b, n, _ = dq.shape
    q = rmsnorm(dq.reshape(b, n, DIFF_HEADS, 2, DIFF_DH).transpose(0, 2, 1, 3, 4), q_norm_g)
    k = rmsnorm(dk.reshape(b, n, DIFF_HEADS, 2, DIFF_DH).transpose(0, 2, 1, 3, 4), k_norm_g)
    v = split_heads(dv, DIFF_HEADS)
    return q, k, v


def diff_attend(q, k_all, v_all, lam):
    b, h, n = q.shape[:3]
    nb = n // Q_BLOCK
    q_blocks = jnp.moveaxis(q.reshape(b, h, nb, Q_BLOCK, 2, DIFF_DH), 2, 0)

    def one_block(qb):
        s = jnp.einsum('bhqcd,bhkcd->bhcqk', qb, k_all).astype(jnp.float32) * (DIFF_DH ** -0.5)
        p = jax.nn.softmax(s, axis=-1)
        a = (p[:, :, 0] - lam.astype(jnp.float32) * p[:, :, 1]).astype(v_all.dtype)
        return jnp.einsum('bhqk,bhkd->bhqd', a, v_all)

    o = lax.map(one_block, q_blocks)
    return jnp.moveaxis(o, 0, 2).reshape(b, h, n, DIFF_DV)


def diff_merge(o, norm_g, lam_init):
    return merge_heads(rmsnorm(o, norm_g) * (1.0 - lam_init))


def swiglu(h, w_in, w_out):
    gate, up = jnp.split(h @ w_in, 2, axis=-1)
    return (jax.nn.silu(gate) * up) @ w_out


def setup_inputs(seed: int = 0) -> dict:
    key = jax.random.key(seed)
    ks = jax.random.split(key, 20)
    f32 = jnp.float32
    nrm = lambda k, shape, s: jax.random.normal(k, shape, f32) * s
    return {
        "x": nrm(ks[0], (BATCH, SEQ, D_MODEL), 1.0),
        "c": nrm(ks[1], (BATCH, D_MODEL), 1.0),
        "ctx": nrm(ks[2], (BATCH, CTX_LEN, D_MODEL), 1.0),
        "c_ctx": nrm(ks[3], (D_MODEL,), 1.0),
        "w_mod": nrm(ks[4], (DEPTH, D_MODEL, 6 * D_MODEL), 0.5 * D_MODEL ** -0.5),
        "b_mod": nrm(ks[5], (DEPTH, 6 * D_MODEL), 0.01),
        "norm1_g": 1.0 + nrm(ks[6], (DEPTH, D_MODEL), 0.02),
        "w_in": nrm(ks[7], (DEPTH, D_MODEL, W_IN_COLS), D_MODEL ** -0.5),
        "gla_gate_up": nrm(ks[8], (DEPTH, 2, GLA_GATE_RANK, GLA_QK), GLA_GATE_RANK ** -0.5),
        "gla_gate_bias": nrm(ks[9], (DEPTH, 2, GLA_QK), 0.1),
        "gla_norm_g": 1.0 + nrm(ks[10], (DEPTH, GLA_DV), 0.02),
        "diff_q_norm_g": 1.0 + nrm(ks[11], (DEPTH, DIFF_DH), 0.02),
        "diff_k_norm_g": 1.0 + nrm(ks[12], (DEPTH, DIFF_DH), 0.02),
        "diff_lambda_q": nrm(ks[13], (DEPTH, 2, DIFF_DH), 0.1),
        "diff_lambda_k": nrm(ks[14], (DEPTH, 2, DIFF_DH), 0.1),
        "diff_norm_g": 1.0 + nrm(ks[15], (DEPTH, DIFF_DV), 0.02),
        "w_out": nrm(ks[16], (DEPTH, MIX_WIDTH, D_MODEL), MIX_WIDTH ** -0.5),
        "norm2_g": 1.0 + nrm(ks[17], (DEPTH, D_MODEL), 0.02),
        "w_ffn_in": nrm(ks[18], (DEPTH, D_MODEL, 2 * FFN_HIDDEN), D_MODEL ** -0.5),
        "w_ffn_out": nrm(ks[19], (DEPTH, FFN_HIDDEN, D_MODEL), FFN_HIDDEN ** -0.5),
    }


def reference(x, c, ctx, c_ctx, w_mod, b_mod, norm1_g, w_in, gla_gate_up, gla_gate_bias, gla_norm_g,
              diff_q_norm_g, diff_k_norm_g, diff_lambda_q, diff_lambda_k, diff_norm_g, w_out, norm2_g,
              w_ffn_in, w_ffn_out):
    b = x.shape[0]
    cos, sin = axial_rope_tables(x.shape[1])
    zero_state = jnp.zeros((b, GLA_HEADS, GLA_DK, GLA_DV), jnp.float32)
    for l in range(DEPTH):
        lam_init = 0.8 - 0.6 * math.exp(-0.3 * l)
        lam = (jnp.exp(jnp.sum(diff_lambda_q[l, 0] * diff_lambda_k[l, 0]))
               - jnp.exp(jnp.sum(diff_lambda_q[l, 1] * diff_lambda_k[l, 1])) + lam_init)
        sh1, sc1, gt1, sh2, sc2, gt2 = [m[:, None, :] for m in adaln_params(c, w_mod[l], b_mod[l])]
        csh1, csc1, cgt1, csh2, csc2, cgt2 = adaln_params(c_ctx, w_mod[l], b_mod[l])

        pc = project(modulate(ctx, norm1_g[l], csh1, csc1), w_in[l])
        qg_c, kg_c, vg_c, r_c, gf_c, gb_c = gla_inputs(pc, gla_gate_up[l], gla_gate_bias[l])
        s_f, s_b, og_c = gla_bidir(qg_c, kg_c, vg_c, gf_c, gb_c, zero_state, zero_state)
        qd_c, kd_c, vd_c = diff_qkv(pc, diff_q_norm_g[l], diff_k_norm_g[l])

        px = project(modulate(x, norm1_g[l], sh1, sc1), w_in[l])
        qg, kg, vg, r, gf, gb = gla_inputs(px, gla_gate_up[l], gla_gate_bias[l])
        _, _, og = gla_bidir(qg, kg, vg, gf, gb, s_f, s_b)
        qd, kd, vd = diff_qkv(px, diff_q_norm_g[l], diff_k_norm_g[l])
        qd = apply_axial_rope(qd, cos, sin)
        kd = apply_axial_rope(kd, cos, sin)
        od = diff_attend(qd, jnp.concatenate([kd_c, kd], axis=2), jnp.concatenate([vd_c, vd], axis=2), lam)
        mix = jnp.concatenate([gla_merge(og, r, gla_norm_g[l]), diff_merge(od, diff_norm_g[l], lam_init)], axis=-1) @ w_out[l]
        x = x + gt1 * mix
        x = x + gt2 * swiglu(modulate(x, norm2_g[l], sh2, sc2), w_ffn_in[l], w_ffn_out[l])

        if l < DEPTH - 1:
            od_c = diff_attend(qd_c, kd_c, vd_c, lam)
            mix_c = jnp.concatenate([gla_merge(og_c, r_c, gla_norm_g[l]), diff_merge(od_c, diff_norm_g[l], lam_init)], axis=-1) @ w_out[l]
            ctx = ctx + cgt1 * mix_c
            ctx = ctx + cgt2 * swiglu(modulate(ctx, norm2_g[l], csh2, csc2), w_ffn_in[l], w_ffn_out[l])
    return x
```

```python
import math
from contextlib import ExitStack

import numpy as np
import ml_dtypes

import concourse.bass as bass
import concourse.mybir as mybir
from concourse.bass_utils import run_bass_kernel_spmd

F32 = mybir.dt.float32
BF16 = mybir.dt.bfloat16
AF = mybir.ActivationFunctionType
ALU = mybir.AluOpType
AX = mybir.AxisListType

D = 1024
EPS = 1e-6
NCTX = 2
FFN_H = 2816
W_IN_COLS = 3104
C_GQ, C_GK, C_GV, C_GR, C_GD, C_DQ, C_DK, C_DV = 0, 256, 512, 1024, 1536, 1568, 2080, 2592
LAM_INIT = 0.8 - 0.6 * math.exp(-0.3 * 0)


def C(name, *a, **kw):
    f = lambda e: getattr(e, name)(*a, **kw)
    f.opname, f.a, f.kw = name, a, kw
    return f


def _free_elems(ap):
    n = 1
    for d in list(ap.shape)[1:]:
        n *= int(d)
    return n


def _op_cost(e, fn):
    name = getattr(fn, "opname", None)
    if name is None:
        return 300.0
    out = fn.kw.get("out", fn.a[0] if fn.a else None)
    n = _free_elems(out) if out is not None else 128
    if e == "pe":
        if name == "transpose":
            return 220.0
        lhsT = fn.kw.get("lhsT")
        mult = 3.5 if (lhsT is not None and lhsT.dtype == F32) else 1.0
        return (max(n, 64) / 1.6 + 40.0) * mult
    if e == "act":
        return n / 0.96 + 220.0
    if e == "dve":
        return n / 0.96 * (8.0 if name == "reciprocal" else 1.0) + 80.0
    return n * 1.7 + 150.0


class _Stop(Exception):
    pass


class Sched:
    def __init__(self, nc, stack):
        self.nc = nc
        self.stack = stack
        self.eng = {"pe": nc.tensor, "act": nc.scalar, "dve": nc.vector, "pool": nc.gpsimd, "sp": nc.sync}
        self.sems = {}
        self.cnt = {}
        for e in ("pe", "act", "dve", "pool"):
            self.sems["E" + e] = stack.enter_context(nc.semaphore("s_" + e))
            self.cnt[e] = 0
        self.waited = {e: {} for e in self.eng}
        self.lastw = {}
        self.readers = {}
        self.slots = {}
        self.pending = {e: [] for e in self.eng}
        self.n_inst = 0
        self.excl = set()
        self.lastacc = {}
        self.pe_last = None
        self.rec = None
        self.sim_e, self.sim_w, self.sim_r, self.sim_a = {}, {}, {}, {}

    def record(self, fn):
        assert self.rec is None
        self.rec = []
        fn()
        lst, self.rec = self.rec, None
        return lst

    def _sim_ready(self, e, reads, writes):
        t = 0.0
        for k in reads:
            t = max(t, self.sim_w.get(k, 0.0))
            if k in self.excl:
                t = max(t, self.sim_a.get(k, 0.0))
        for k in writes:
            t = max(t, self.sim_w.get(k, 0.0), self.sim_r.get(k, 0.0))
        return t

    def _sim_commit(self, e, kind, fn_or_bytes, reads, writes):
        ready = self._sim_ready(e, reads, writes)
        start = max(ready + 120.0, self.sim_e.get(e, 0.0))
        if kind == "op":
            fin = start + _op_cost(e, fn_or_bytes)
            self.sim_e[e] = fin
        else:
            self.sim_e[e] = start + 80.0
            fin = start + 2000.0 + fn_or_bytes / 120.0
        for k in writes:
            self.sim_w[k] = fin
            self.sim_r[k] = 0.0
        for k in reads:
            self.sim_r[k] = max(self.sim_r.get(k, 0.0), fin)
        for k in list(reads) + list(writes):
            if k in self.excl:
                self.sim_a[k] = fin

    def play(self, lists):
        cur = [0] * len(lists)
        while True:
            best = None
            for li, lst in enumerate(lists):
                if cur[li] >= len(lst):
                    continue
                kind, args, kw = lst[cur[li]]
                e = args[0]
                reads, writes = (args[2], args[3]) if kind == "op" else (args[3], args[4])
                st = max(self._sim_ready(e, reads, writes) + 120.0, self.sim_e.get(e, 0.0))
                key = (st, cur[li] / len(lst), li)
                if best is None or key < best[0]:
                    best = (key, li)
            if best is None:
                break
            li = best[1]
            kind, args, kw = lists[li][cur[li]]
            cur[li] += 1
            if kind == "op":
                self.op(*args, **kw)
            else:
                self.dma(*args, **kw)

    def _wait(self, e, tok):
        sk, val, _ = tok
        if self.waited[e].get(sk, 0) >= val:
            return
        self.eng[e].wait_ge(self.sems[sk], val)
        self.waited[e][sk] = val

    def _deps(self, e, reads, writes):
        deps = []
        for k in reads:
            t = self.lastw.get(k)
            if t is not None:
                deps.append((t, "raw"))
            if k in self.excl:
                t = self.lastacc.get(k)
                if t is not None and t[2] != e:
                    deps.append((t, "raw"))
        for k in writes:
            t = self.lastw.get(k)
            if t is not None:
                deps.append((t, "waw"))
            for r in self.readers.get(k, ()):
                deps.append((r, "war"))
        for t in self.pending[e]:
            deps.append((t, "raw"))
        self.pending[e] = []
        for t, kind in deps:
            if t[2] == e and e == "pe":
                continue
            self._wait(e, t)

    def _record(self, tok, reads, writes):
        for k in writes:
            self.lastw[k] = tok
            self.readers[k] = []
        for k in reads:
            lst = self.readers.setdefault(k, [])
            lst[:] = [r for r in lst if r[0] != tok[0]]
            lst.append(tok)
        for k in list(reads) + list(writes):
            if k in self.excl:
                self.lastacc[k] = tok

    def op(self, e, fn, reads=(), writes=(), rt=0):
        if self.rec is not None:
            self.rec.append(("op", (e, fn, tuple(reads), tuple(writes)), {"rt": rt}))
            return None
        self._deps(e, reads, writes)
        self._sim_commit(e, "op", fn, reads, writes)
        if e == "pe":
            pl = self.pe_last
            if pl is not None and pl[1] != rt and any(k in pl[2] for k in writes):
                self._wait(e, pl[0])
        inst = fn(self.eng[e])
        self.cnt[e] += 1
        inst.then_inc(self.sems["E" + e], 1)
        tok = ("E" + e, self.cnt[e], e)
        self._record(tok, reads, writes)
        if e == "pe":
            self.pe_last = (tok, rt, set(writes))
        self.n_inst += 1
        return tok

    def dma(self, q, out, in_, reads, writes, slot, **kw):
        if self.rec is not None:
            self.rec.append(("dma", (q, out, in_, tuple(reads), tuple(writes), slot), kw))
            return None
        sk = "D" + slot
        if sk not in self.sems:
            self.sems[sk] = self.stack.enter_context(self.nc.semaphore("d_" + slot))
            self.slots[sk] = 0
        if self.slots[sk] > 0:
            self._wait(q, (sk, self.slots[sk], None))
        self._deps(q, reads, writes)
        nbytes = 128 * _free_elems(out) * (2 if out.dtype == BF16 else 4)
        self._sim_commit(q, "dma", nbytes, reads, writes)
        inst = self.eng[q].dma_start(out=out, in_=in_, **kw)
        self.slots[sk] += 16
        inst.then_inc(self.sems[sk], 16)
        tok = (sk, self.slots[sk], None)
        self._record(tok, reads, writes)
        self.n_inst += 1
        return tok

    def _all_toks(self):
        toks = [("E" + e, c, e) for e, c in self.cnt.items() if c > 0]
        toks += [(sk, v, None) for sk, v in self.slots.items() if v > 0]
        return toks

    def barrier(self):
        toks = self._all_toks()
        for e in self.eng:
            self.pending[e] = [t for t in toks if t[2] != e]

    def finish(self):
        for t in self._all_toks():
            self._wait("sp", t)


def build_program(NOTH, NOWN, dbg=False, stop_after=None):
    assert NOTH % 4 == 0 and NOWN % 4 == 0
    NK = NCTX + NOTH + NOWN
    NTOK = NK * 128
    T0_OTH = NCTX
    T0_OWN = NCTX + NOTH
    NQ = NOWN * 128

    nc = bass.Bass("TRN2", target_bir_lowering=False)

    def din(name, shape, dt=F32):
        return nc.dram_tensor(name, list(shape), dt, kind="ExternalInput").ap()

    def dscr(name, shape, dt):
        return nc.dram_tensor(name, list(shape), dt, kind=("ExternalOutput" if dbg else "Internal")).ap()

    xin = din("xin", [NTOK, D])
    vecs = din("vecs", [80, 128])
    w_mod = din("w_mod", [D, 6 * D])
    b_mod = din("b_mod", [1, 6 * D])
    w_in = din("w_in", [D, W_IN_COLS])
    gu_in = din("gu_in", [49, 256])
    gng_in = din("gng_in", [1, 128])
    dng_in = din("dng_in", [1, 128])
    qkg_in = din("qkg_in", [128, 2])
    lq_in = din("lq", [1, 128])
    lk_in = din("lk", [1, 128])
    w_out = din("w_out", [D, D])
    w_ffi = din("w_ffi", [D, 2 * FFN_H])
    w_ffo = din("w_ffo", [FFN_H, D])
    cos_in = din("cosT", [128, NTOK])
    sin_in = din("sinT", [128, NTOK])
    cmat = din("cmat", [128, 4 * 128 + 2 * 64])
    out_d = nc.dram_tensor("out", [NQ, D], F32, kind="ExternalOutput").ap()

    KT_d = dscr("KT_d", [4, 128, NTOK], BF16)
    VA_d = dscr("VA_d", [128, 4, NK, 129], BF16)
    QT_d = dscr("QT_d", [4, 128, NQ], BF16)
    qeQ_d = dscr("qeQ_d", [NOWN, 128, 2, 128], BF16)
    UQ_d = dscr("UQ_d", [NOWN, 128, 512], F32)
    op_d = dscr("op_d", [NOWN, 128, 512], F32)
    r_d = dscr("r_d", [NOWN, 128, 512], F32)
    mix_d = dscr("mix_d", [NQ, D], BF16)

    dbg_out = {}
    if dbg:
        for nm, shp in (("dbg_mod", [128, 96]), ("dbg_G", [128, 2048]),
                        ("dbg_SP", [128, 256]), ("dbg_SQ", [128, 256])):
            dbg_out[nm] = nc.dram_tensor(nm, shp, F32, kind="ExternalOutput").ap()

    top = ExitStack()
    with top:
        S = Sched(nc, top)
        S.excl.update(["p0T", "p0mod", "p0G0", "p0G1", "ptr0", "ptr1", "b2", "b3", "b4", "b5", "b6", "b7", "Bpo0", "Bpo1",
                       "CST0_0", "CST0_1", "CST1_0", "CST1_1", "CACC0", "CACC1", "CACC2",
                       "Dptb", "Dpmo0", "Dpmo1", "Dptr0", "Dptr1", "Dpgu0", "Dpgu1", "Dpgu2"])

        def sb(stack, name, shape, dt=F32):
            return stack.enter_context(nc.sbuf_tensor(name, list(shape), dt))

        def ps(stack, name, shape, dt=F32):
            return stack.enter_context(nc.psum_tensor(name, list(shape), dt))

        cm = sb(top, "cm", [128, 640])
        ident = cm[:, 0:128]
        triP = cm[:, 128:256]
        triQ = cm[:, 256:384]
        blk64 = cm[:, 384:512]
        maskP = cm[:, 512:576]
        maskQ = cm[:, 576:640]
        cst = sb(top, "cst", [128, 4])
        colv = sb(top, "colv", [128, 80])
        modT = sb(top, "modT", [128, 48, 2])
        A1 = sb(top, "A1", [128, 2, 8])
        B1 = sb(top, "B1", [128, 2, 8])
        A2 = sb(top, "A2", [128, 2, 8])
        B2 = sb(top, "B2", [128, 2, 8])
        G1 = sb(top, "G1", [128, 1024])
        G2 = sb(top, "G2", [128, 1024])
        gng = sb(top, "gng", [128, 128])
        dng = sb(top, "dng", [128, 128])
        qkg = sb(top, "qkg", [128, 2])
        negl = sb(top, "negl", [128, 1])
        gu = sb(top, "gu", [49, 256])
        eQ = sb(top, "eQ", [128, NOWN, 2, 2])
        SQ = sb(top, "SQ", [128, 2, 128])

        S.dma("sp", cm[:], cmat[:, :], [], ["cm"], "c0")
        S.op("dve", C("memset", cst[:, 0:1], EPS), [], ["cst"])
        S.op("dve", C("memset", cst[:, 1:2], 1.0), [], ["cst"])
        S.dma("sp", gu[:], gu_in[:, :], [], ["gu"], "c1")
        S.dma("sp", qkg[:], qkg_in[:, :], [], ["qkg"], "c2")
        S.dma("sp", gng[:], gng_in[0:1, :].to_broadcast([128, 128]), [], ["gng"], "c3")
        S.dma("sp", dng[:], dng_in[0:1, :].to_broadcast([128, 128]), [], ["dng"], "c4")
        S.op("dve", C("tensor_scalar", out=qkg[:, 1:2], in0=qkg[:, 1:2], scalar1=0.125, scalar2=None, op0=ALU.mult),
             ["qkg"], ["qkg"])
        S.op("dve", C("tensor_scalar", out=dng[:], in0=dng[:], scalar1=1.0 - LAM_INIT, scalar2=None, op0=ALU.mult),
             ["dng"], ["dng"])

        try:
            with ExitStack() as ph:
                stage = sb(ph, "stage", [80, 128])
                siluT = sb(ph, "siluT", [128, 8, 2])
                silubc = sb(ph, "silubc", [128, 8, 128])
                ones1 = sb(ph, "ones1", [1, 128])
                bmr = sb(ph, "bmr", [1, 2048])
                lqb = sb(ph, "lqb", [128, 128])
                lkb = sb(ph, "lkb", [128, 128])
                s2 = sb(ph, "s2", [128, 2])
                wm = [sb(ph, f"wm{i}", [128, 8, 512]) for i in range(2)]
                pT = ps(ph, "p0T", [128, 512])
                pmod = ps(ph, "p0mod", [128, 512])
                pG = [ps(ph, f"p0G{i}", [128, 512]) for i in range(2)]

                S.dma("sp", stage[:], vecs[:, :], [], ["stage"], "c0")
                S.dma("sp", bmr[:, 0:1024], b_mod[0:1, 2048:3072], [], ["bmr"], "c1")
                S.dma("sp", bmr[:, 1024:2048], b_mod[0:1, 5120:6144], [], ["bmr"], "c1")
                S.dma("sp", lqb[:], lq_in[0:1, :].to_broadcast([128, 128]), [], ["lqb"], "c2")
                S.dma("sp", lkb[:], lk_in[0:1, :].to_broadcast([128, 128]), [], ["lkb"], "c3")
                S.op("dve", C("memset", ones1[:], 1.0), [], ["ones1"])
                S.op("dve", C("tensor_tensor", out=lqb[:], in0=lqb[:], in1=lkb[:], op=ALU.mult), ["lqb", "lkb"], ["lqb"])
                S.op("dve", C("tensor_reduce", out=s2[:], in_=lqb[:].rearrange("p (a b) -> p a b", b=64), axis=AX.X, op=ALU.add),
                     ["lqb"], ["s2"])
                S.op("act", C("activation", out=s2[:], in_=s2[:], func=AF.Exp), ["s2"], ["s2"])
                S.op("dve", C("tensor_tensor", out=negl[:], in0=s2[:, 1:2], in1=s2[:, 0:1], op=ALU.subtract), ["s2"], ["negl"])
                S.op("dve", C("tensor_scalar", out=negl[:], in0=negl[:], scalar1=-LAM_INIT, scalar2=None, op0=ALU.add),
                     ["negl"], ["negl"])
                S.op("pe", C("transpose", out=pT[:, 0:80], in_=stage[:], identity=ident[0:80, 0:80]), ["stage", "cm"], ["p0T"])
                S.op("dve", C("tensor_copy", out=colv[:], in_=pT[:, 0:80]), ["p0T"], ["colv"])
                S.op("act", C("activation", out=siluT[:].rearrange("p k r -> p r k"),
                                                   in_=colv[:, 64:80].rearrange("p (r k) -> p r k", k=8), func=AF.Silu),
                     ["colv"], ["siluT"])
                for kc in range(8):
                    S.op("dve", C("tensor_scalar", out=silubc[:, kc, :], in0=ident[:, :], scalar1=0.0,
                                                                 scalar2=siluT[:, kc, 0:1], op0=ALU.mult, op1=ALU.add),
                         ["siluT", "cm"], ["silubc"])
                gi = 0
                for cg in range(12):
                    w = wm[cg % 2]
                    wk = f"wm{cg % 2}"
                    S.dma("sp", w[:], w_mod[:, cg * 512:(cg + 1) * 512].rearrange("(k p) c -> p k c", p=128), [], [wk], wk)
                    for jj in range(4):
                        j = cg * 4 + jj
                        for kc in range(8):
                            S.op("pe", C("matmul",
                                pmod[:, 2 * j:2 * j + 2], lhsT=w[:, kc, jj * 128:(jj + 1) * 128], rhs=siluT[:, kc, :],
                                start=(kc == 0), stop=(kc == 7)), [wk, "siluT"], ["p0mod"])
                    if cg in (4, 5, 10, 11):
                        pg = pG[gi % 2]
                        pk_ = f"p0G{gi % 2}"
                        Gt = G1 if cg < 6 else G2
                        gcol = (cg % 2) * 512
                        boff = (0 if cg < 6 else 1024) + gcol
                        for kc in range(8):
                            S.op("pe", C("matmul", pg[:, :], lhsT=silubc[:, kc, :], rhs=w[:, kc, :],
                                                                              start=(kc == 0), stop=False),
                                 [wk, "silubc"], [pk_])
                        S.op("pe", C("matmul", pg[:, :], lhsT=ones1[:, :], rhs=bmr[:, boff:boff + 512],
                                                                        start=False, stop=True), ["ones1", "bmr"], [pk_])
                        S.op("act", C("copy", out=Gt[:, gcol:gcol + 512], in_=pg[:, :]),
                             [pk_], ["G1" if cg < 6 else "G2"])
                        gi += 1
                S.op("dve", C("tensor_tensor", out=modT[:], in0=pmod[:, 0:96].rearrange("p (j r) -> p j r", r=2),
                                                      in1=colv[:, 0:48].unsqueeze(2).to_broadcast([128, 48, 2]), op=ALU.add),
                     ["p0mod", "colv"], ["modT"])
                for r in range(2):
                    S.op("dve", C("scalar_tensor_tensor", out=A1[:, r, :], in0=modT[:, 8:16, r], scalar=1.0,
                                                                      in1=colv[:, 48:56], op0=ALU.add, op1=ALU.mult),
                         ["modT", "colv"], ["A1"])
                    S.op("dve", C("tensor_copy", out=B1[:, r, :], in_=modT[:, 0:8, r]), ["modT"], ["B1"])
                    S.op("dve", C("scalar_tensor_tensor", out=A2[:, r, :], in0=modT[:, 32:40, r], scalar=1.0,
                                                                      in1=colv[:, 56:64], op0=ALU.add, op1=ALU.mult),
                         ["modT", "colv"], ["A2"])
                    S.op("dve", C("tensor_copy", out=B2[:, r, :], in_=modT[:, 24:32, r]), ["modT"], ["B2"])
                if dbg:
                    S.dma("pool", dbg_out["dbg_mod"][:, :], modT[:].rearrange("p j r -> p (j r)"), ["modT"], [], "dbg")
                    S.dma("pool", dbg_out["dbg_G"][:, 0:1024], G1[:], ["G1"], [], "dbg")
                    S.dma("pool", dbg_out["dbg_G"][:, 1024:2048], G2[:], ["G2"], [], "dbg")
                S.barrier()
            if stop_after == "0":
                raise _Stop()

            with ExitStack() as ph:
                Win = sb(ph, "Win", [128, 8, W_IN_COLS], BF16)
                xt = [sb(ph, f"xt{i}", [128, 1024]) for i in range(2)]
                junk = sb(ph, "junk", [128, 1024], BF16)
                st4 = [sb(ph, f"st4_{i}", [128, 4]) for i in range(2)]
                xn = [sb(ph, f"xn{i}", [128, 1024]) for i in range(2)]
                hT = [sb(ph, f"hT{i}", [128, 8, 512], BF16) for i in range(2)]
                cosb = [sb(ph, f"cosb{i}", [128, 512]) for i in range(1)] * 2
                sinb = [sb(ph, f"sinb{i}", [128, 512]) for i in range(1)] * 2
                sq = sb(ph, "sq", [128, 512], BF16)
                blkb = sb(ph, "blkb", [128, 128], BF16)
                lnb = sb(ph, "lnb", [128, 512])
                rsb = sb(ph, "rsb", [128, 512])
                kn = sb(ph, "kn", [128, 512])
                kr = sb(ph, "kr", [128, 512])
                t1 = sb(ph, "t1", [128, 512])
                kst = [sb(ph, f"kst{i}", [128, 4, 512], BF16) for i in range(2)]
                qst = [sb(ph, f"qst{i}", [128, 4, 512], BF16) for i in range(1)] * 2
                vst = [sb(ph, f"vst{i}", [128, 4, 4, 129], BF16) for i in range(2)]
                dA = sb(ph, "dA", [49, 512])
                ex = sb(ph, "ex", [128, 512])
                spb = [sb(ph, f"spb{i}", [128, 2, 256]) for i in range(2)]
                en = sb(ph, "en", [128, 256])
                ke = [[sb(ph, f"ke{z}_{i}", [128, 256], BF16) for i in range(2)] for z in range(2)]
                vbf = [sb(ph, f"vbf{i}", [128, 512], BF16) for i in range(4)]
                ez = [sb(ph, f"ez{z}", [128, 4, 2, 2]) for z in range(2)]
                E1 = [sb(ph, f"E1_{z}", [128, 2, 512]) for z in range(2)]
                E2 = [sb(ph, f"E2_{z}", [128, 2, 512]) for z in range(2)]
                qeT = [sb(ph, f"qeT{z}", [128, 2, 512], BF16) for z in range(2)]
                keT = [sb(ph, f"keT{z}", [128, 2, 512], BF16) for z in range(2)]
                UP = sb(ph, "UPs", [128, 1, 512])
                UQs = [sb(ph, f"UQs{i}", [128, 512]) for i in range(2)]
                UQc = sb(ph, "UQc", [128, 2, 512])
                SP = sb(ph, "SP", [128, 2, 128])
                tmpS = sb(ph, "tmpS", [128, 2, 128])
                Sbf = sb(ph, "Sbf", [128, 8, 2, 128], BF16)
                am = [sb(ph, f"am{z}", [128, 4, 64], BF16) for z in range(2)]
                osb = [sb(ph, f"osb{i}", [128, 512]) for i in range(2)]
                rsbuf = [sb(ph, f"rsbuf{i}", [128, 512]) for i in range(2)]
                ptr = ps(ph, "ptr", [128, 1024])
                b2 = ps(ph, "b2", [128, 512])
                b3 = ps(ph, "b3", [128, 512])
                b4 = ps(ph, "b4", [128, 512])
                b5 = ps(ph, "b5", [128, 512])
                b6 = ps(ph, "b6", [128, 512])
                b7 = ps(ph, "b7", [128, 512])

                for kc in range(8):
                    S.dma("pool", Win[:, kc, :], w_in[kc * 128:(kc + 1) * 128, :], [], [f"Win{kc}"], f"w{kc % 2}")
                WinK = [f"Win{kc}" for kc in range(8)]
                S.op("dve", C("tensor_copy", out=blkb[:], in_=blk64), ["cm"], ["blkb"])
                S.op("dve", C("memset", dA[:], 1.0), [], ["dA"])
                for i in range(2):
                    S.op("dve", C("memset", vst[i][:].rearrange("p a b c -> p (a b c)"), 1.0), [], [f"vst{i}"])
                S.op("dve", C("memset", SP[:].rearrange("p a b -> p (a b)"), 0.0), [], ["SP"])
                S.op("dve", C("memset", SQ[:].rearrange("p a b -> p (a b)"), 0.0), [], ["SQ"])

                groups = [(0, NCTX, "ctx")]
                for g in range(NOTH // 4):
                    groups.append((T0_OTH + 4 * g, 4, "oth"))
                for g in range(NOWN // 4):
                    groups.append((T0_OWN + 4 * g, 4, "own"))

                xc = [0]
                if stop_after == "A0":
                    groups = []
                def stage1(gidx):
                    t0, nt, kind = groups[gidx]
                    T = nt * 128
                    koff = t0 * 128
                    r_mod = 1 if kind == "ctx" else 0
                    own = kind == "own"
                    ctx = kind == "ctx"
                    gb = gidx % 2
                    h_T = hT[gb]
                    hK = f"hT{gb}"
                    for i in range(nt):
                        xb = xc[0] % 2
                        nb = xc[0] % 2
                        xc[0] += 1
                        x_t, xk = xt[xb], f"xt{xb}"
                        s4, sk4 = st4[xb], f"st4_{xb}"
                        x_n, nk = xn[nb], f"xn{nb}"
                        S.dma("sp", x_t[:], xin[(t0 + i) * 128:(t0 + i + 1) * 128, :], [], [xk], xk)
                        S.op("act", C("activation", out=junk[:], in_=x_t[:], func=AF.Square, accum_out=s4[:, 0:1]),
                             [xk], [sk4])
                        S.op("act", C("activation", out=s4[:, 1:2], in_=s4[:, 0:1], func=AF.Ln, scale=1.0 / D, bias=cst[:, 0:1]),
                             [sk4, "cst"], [sk4])
                        S.op("act", C("activation", out=s4[:, 2:3], in_=s4[:, 1:2], func=AF.Exp, scale=-0.5), [sk4], [sk4])
                        S.op("dve", C("tensor_scalar", out=x_n[:], in0=x_t[:], scalar1=s4[:, 2:3], scalar2=None,
                                                                                      op0=ALU.mult), [xk, sk4], [nk])
                        for kc in range(8):
                            S.op("pe", C("transpose", out=ptr[:, kc * 128:(kc + 1) * 128], in_=x_n[:, kc * 128:(kc + 1) * 128],
                                                                           identity=ident), [nk, "cm"], [f"ptr{kc // 4}"])
                        for kc in range(8):
                            dst = h_T[:, kc, i * 128:(i + 1) * 128]
                            src = ptr[:, kc * 128:(kc + 1) * 128]
                            if kc < 4:
                                S.op("dve", C("tensor_scalar",
                                    out=dst, in0=src, scalar1=A1[:, r_mod, kc:kc + 1], scalar2=B1[:, r_mod, kc:kc + 1],
                                    op0=ALU.mult, op1=ALU.add), [f"ptr{kc // 4}", "A1", "B1"], [f"{hK}_{kc}"])
                            else:
                                S.op("act", C("activation",
                                    out=dst, in_=src, func=AF.Identity, scale=A1[:, r_mod, kc:kc + 1], bias=B1[:, r_mod, kc:kc + 1]),
                                    [f"ptr{kc // 4}", "A1", "B1"], [f"{hK}_{kc}"])

                def stageY(gidx):
                    t0, nt, kind = groups[gidx]
                    T = nt * 128
                    koff = t0 * 128
                    r_mod = 1 if kind == "ctx" else 0
                    own = kind == "own"
                    ctx = kind == "ctx"
                    gb = gidx % 2
                    h_T = hT[gb]
                    hK = f"hT{gb}"
                    S.dma("sp", cosb[gb][:, 0:T], cos_in[:, koff:koff + T], [], ["cos0"], "cos0")
                    S.dma("sp", sinb[gb][:, 0:T], sin_in[:, koff:koff + T], [], ["sin0"], "sin0")

                    def qk_proj(col0, gcol, dst, dkey):
                        for h in range(4):
                            for kc in range(8):
                                S.op("pe", C("matmul", b2[:, 0:T], lhsT=Win[:, kc, col0 + h * 128:col0 + (h + 1) * 128],
                                                                          rhs=h_T[:, kc, 0:T], start=(kc == 0), stop=(kc == 7)),
                                     [WinK[kc], f"{hK}_{kc}"], ["b2"])
                            S.op("act", C("activation", out=sq[:, 0:T], in_=b2[:, 0:T], func=AF.Square), ["b2"], ["sq"])
                            S.op("pe", C("matmul", b3[:, 0:T], lhsT=blkb[:], rhs=sq[:, 0:T], start=True, stop=True), ["sq", "blkb"], ["b3"])
                            S.op("act", C("activation", out=lnb[:, 0:T], in_=b3[:, 0:T], func=AF.Ln, bias=cst[:, 0:1]), ["b3", "cst"], ["lnb"])
                            S.op("act", C("activation", out=rsb[:, 0:T], in_=lnb[:, 0:T], func=AF.Exp, scale=-0.5), ["lnb"], ["rsb"])
                            S.op("dve", C("scalar_tensor_tensor", out=kn[:, 0:T], in0=b2[:, 0:T], scalar=qkg[:, gcol:gcol + 1],
                                                                         in1=rsb[:, 0:T], op0=ALU.mult, op1=ALU.mult),
                                 ["b2", "rsb", "qkg"], ["kn"])
                            S.op("dve", C("stream_shuffle", out=kr[:, 0:T], in_=kn[:, 0:T], mask=[(i + 16) % 32 for i in range(32)]),
                                 ["kn"], ["kr"])
                            S.op("pool", C("tensor_tensor", out=t1[:, 0:T], in0=kn[:, 0:T], in1=cosb[gb][:, 0:T], op=ALU.mult),
                                 ["kn", "cos0"], ["t1"])
                            S.op("pool", C("tensor_tensor", out=kr[:, 0:T], in0=kr[:, 0:T], in1=sinb[gb][:, 0:T], op=ALU.mult),
                                 ["kr", "sin0"], ["kr"])
                            S.op("dve", C("tensor_tensor", out=dst[:, h, 0:T], in0=t1[:, 0:T], in1=kr[:, 0:T], op=ALU.add),
                                 ["t1", "kr"], [dkey])

                    qk_proj(C_DK, 0, kst[gb], f"kst{gb}")
                    S.dma("pool", KT_d[:, :, koff:koff + T].rearrange("h p t -> p h t"), kst[gb][:, :, 0:T], [f"kst{gb}"], [], f"ko{gb}")
                    if own:
                        qoff = (t0 - T0_OWN) * 128
                        qk_proj(C_DQ, 1, qst[gb], "qst0")
                        S.dma("pool", QT_d[:, :, qoff:qoff + T].rearrange("h p t -> p h t"), qst[gb][:, :, 0:T], ["qst0"], [], "qo0")
                    for i in range(nt):
                        for kc in range(8):
                            S.op("pe", C("matmul", b3[:, :], lhsT=h_T[:, kc, i * 128:(i + 1) * 128], rhs=Win[:, kc, C_DV:C_DV + 512],
                                                                      start=(kc == 0), stop=(kc == 7)), [WinK[kc], f"{hK}_{kc}"], ["b3"])
                        S.op("act", C("copy", out=vst[gb][:, :, i, 0:128], in_=b3[:, :].rearrange("p (h c) -> p h c", c=128)),
                             ["b3"], [f"vst{gb}"])
                    S.dma("pool", VA_d[:, :, t0:t0 + nt, :], vst[gb][:, :, 0:nt, :], [f"vst{gb}"], [], f"vo{gb}")


                def stageZ(gidx):
                    t0, nt, kind = groups[gidx]
                    T = nt * 128
                    koff = t0 * 128
                    r_mod = 1 if kind == "ctx" else 0
                    own = kind == "own"
                    ctx = kind == "ctx"
                    gb = gidx % 2
                    h_T = hT[gb]
                    hK = f"hT{gb}"
                    zs = (0, 1) if (own or ctx) else (0,)
                    for z in zs:
                        for kc in range(8):
                            S.op("pe", C("matmul", b7[32 * z:32 * z + 16, 0:T], lhsT=Win[:, kc, C_GD + 16 * z:C_GD + 16 * z + 16],
                                                                      rhs=h_T[:, kc, 0:T], start=(kc == 0), stop=(kc == 7)),
                                 [WinK[kc], f"{hK}_{kc}"], ["b7"])
                        S.op("act", C("copy", out=dA[32 * z:32 * z + 16, 0:T], in_=b7[32 * z:32 * z + 16, 0:T]), ["b7"], ["dA"])
                    for i in range(nt):
                        tl = slice(i * 128, (i + 1) * 128)
                        sp_t, spk = spb[i % 2], f"spb{i % 2}"
                        v_b, vk = vbf[i], f"vbf{i}"
                        for kc in range(8):
                            S.op("pe", C("matmul", b4[:, 0:256], lhsT=h_T[:, kc, tl], rhs=Win[:, kc, C_GK:C_GK + 256],
                                                                 start=(kc == 0), stop=(kc == 7)), [WinK[kc], f"{hK}_{kc}"], ["b4"])
                        for kc in range(8):
                            S.op("pe", C("matmul", b5[:, :], lhsT=h_T[:, kc, tl], rhs=Win[:, kc, C_GV:C_GV + 512],
                                                                 start=(kc == 0), stop=(kc == 7)), [WinK[kc], f"{hK}_{kc}"], ["b5"])
                        S.op("act", C("copy", out=v_b[:], in_=b5[:, :]), ["b5"], [vk])
                        for z in zs:
                            S.op("pe", C("matmul", b6[:, 256 * z:256 * z + 256], lhsT=dA[32 * z:32 * z + 17, tl],
                                                               rhs=gu[32 * z:32 * z + 17, :], start=True, stop=True), ["dA", "gu"], ["b6"], rt=32 * z)
                        W_ = 256 * len(zs)
                        S.op("act", C("activation", out=ex[:, 0:W_], in_=b6[:, 0:W_], func=AF.Exp, scale=-1.0), ["b6"], ["ex"])
                        S.op("act", C("activation", out=sp_t[:].rearrange("p z c -> p (z c)")[:, 0:W_], in_=ex[:, 0:W_],
                                                                       func=AF.Ln, bias=cst[:, 1:2]), ["ex", "cst"], [spk])
                        for z in zs:
                            tri = triP if z == 0 else triQ
                            k_e, kek = ke[z][i % 2], f"ke{z}_{i % 2}"
                            S.op("pe", C("matmul", b4[:, 256:512], lhsT=tri, rhs=sp_t[:, z, :], start=True, stop=True),
                                 [spk, "cm"], ["b4"])
                            S.op("act", C("activation", out=en[:], in_=b4[:, 256:512], func=AF.Exp, scale=-1.0), ["b4"], ["en"])
                            S.op("dve", C("tensor_tensor", out=k_e[:], in0=b4[:, 0:256], in1=en[:], op=ALU.mult),
                                 ["b4", "en"], [kek])
                            lc0 = (128 + 63) if z == 0 else 256
                            lastcols = cm[:, lc0:lc0 + 128].rearrange("p (a b) -> p a b", b=64)[:, :, 0]
                            for pr in range(2):
                                S.op("pe", C("matmul",
                                    b6[:, 2 * pr:2 * pr + 2], lhsT=sp_t[:, z, pr * 128:(pr + 1) * 128],
                                    rhs=lastcols, start=True, stop=True), [spk, "cm"], ["b6"])
                            S.op("act", C("activation", out=ez[z][:, i, :, :].rearrange("p a b -> p (a b)"), in_=b6[:, 0:4],
                                                                         func=AF.Exp), ["b6"], [f"ez{z}"])
                            if own:
                                for pr in range(2):
                                    S.op("pe", C("matmul",
                                        b6[:, 128 + pr * 128:256 + pr * 128], lhsT=sp_t[:, z, pr * 128:(pr + 1) * 128], rhs=tri,
                                        start=True, stop=True), [spk, "cm"], ["b6"])
                                S.op("act", C("activation", out=E1[z][:, :, tl], in_=b6[:, 128:384].rearrange("p (a b) -> p a b", b=128),
                                                                        func=AF.Exp), ["b6"], [f"E1_{z}"])
                                S.op("act", C("activation", out=E2[z][:, :, tl], in_=b6[:, 128:384].rearrange("p (a b) -> p a b", b=128),
                                                                        func=AF.Exp, scale=-1.0), ["b6"], [f"E2_{z}"])
                            for c in range(2):
                                for h in range(4):
                                    hp, pr = h % 2, h // 2
                                    S.op("pe", C("matmul",
                                        b7[hp * 64:(hp + 1) * 64, (pr * 2 + c) * 128:(pr * 2 + c + 1) * 128],
                                        lhsT=k_e[c * 64:(c + 1) * 64, h * 64:(h + 1) * 64],
                                        rhs=v_b[c * 64:(c + 1) * 64, h * 128:(h + 1) * 128], start=True, stop=True),
                                        [kek, vk], ["b7"], rt=c * 64)
                            if z == 0:
                                S.op("dve", C("tensor_copy", out=UP[:, 0, :], in_=b7[:, :]), ["b7"], ["UP"])
                            elif ctx:
                                S.op("dve", C("tensor_copy", out=UQc[:, i, :], in_=b7[:, :]), ["b7"], ["UQc"])
                            else:
                                ti = t0 - T0_OWN + i
                                uq, uqk = UQs[i % 2], f"UQs{i % 2}"
                                S.op("dve", C("tensor_copy", out=uq[:], in_=b7[:, :]), ["b7"], [uqk])
                                S.dma("pool", UQ_d[ti, :, :], uq[:], [uqk], [], uqk)
                                S.op("dve", C("tensor_copy", out=eQ[:, ti, :, :], in_=ez[1][:, i, :, :]), ["ez1"], ["eQ"])
                        for c in range(2):
                            if own:
                                S.op("act", C("copy", out=Sbf[:, 2 * i + c, :, :], in_=SP[:]), ["SP"], [f"Sbf{2 * i + c}"])
                            S.op("dve", C("tensor_tensor",
                                out=tmpS[:], in0=SP[:], in1=UP[:, 0, :].rearrange("p (a c d) -> p a c d", c=2, d=128)[:, :, c, :], op=ALU.add),
                                ["SP", "UP"], ["tmpS"])
                            for pr in range(2):
                                S.op("dve", C("tensor_scalar", out=SP[:, pr, :], in0=tmpS[:, pr, :],
                                                                                      scalar1=ez[0][:, i, pr, c:c + 1], scalar2=None, op0=ALU.mult),
                                     ["tmpS", "ez0"], ["SP"])
                    if ctx:
                        for i in reversed(range(nt)):
                            for c in (1, 0):
                                S.op("dve", C("tensor_tensor",
                                    out=tmpS[:], in0=SQ[:], in1=UQc[:, i, :].rearrange("p (a c d) -> p a c d", c=2, d=128)[:, :, c, :], op=ALU.add),
                                    ["SQ", "UQc"], ["tmpS"])
                                for pr in range(2):
                                    S.op("dve", C("tensor_scalar", out=SQ[:, pr, :], in0=tmpS[:, pr, :],
                                                                                          scalar1=ez[1][:, i, pr, c:c + 1], scalar2=None, op0=ALU.mult),
                                         ["tmpS", "ez1"], ["SQ"])
                        if dbg:
                            S.dma("pool", dbg_out["dbg_SP"][:, :], SP[:].rearrange("p a b -> p (a b)"), ["SP"], [], "dbg")
                            S.dma("pool", dbg_out["dbg_SQ"][:, :], SQ[:].rearrange("p a b -> p (a b)"), ["SQ"], [], "dbg")
                    if not own:
                        return

                    for pr in range(2):
                        for (col0, dsts, Es, scale) in ((C_GQ, qeT, E1, 0.125), (C_GK, keT, E2, 1.0)):
                            for kc in range(8):
                                S.op("pe", C("matmul",
                                    b4[:, 0:T], lhsT=Win[:, kc, col0 + pr * 128:col0 + (pr + 1) * 128], rhs=h_T[:, kc, 0:T],
                                    start=(kc == 0), stop=(kc == 7)), [WinK[kc], f"{hK}_{kc}"], ["b4"])
                            for z in range(2):
                                S.op("dve", C("scalar_tensor_tensor",
                                    out=dsts[z][:, pr, 0:T], in0=b4[:, 0:T], scalar=scale, in1=Es[z][:, pr, 0:T],
                                    op0=ALU.mult, op1=ALU.mult), ["b4", f"{'E1' if Es is E1 else 'E2'}_{z}"],
                                    [f"{'qeT' if dsts is qeT else 'keT'}{z}"])
                    for i in range(nt):
                        tl0 = i * 128
                        ti = t0 - T0_OWN + i
                        v_b, vk = vbf[i], f"vbf{i}"
                        for z in range(2):
                            mk = maskP if z == 0 else maskQ
                            for c in range(2):
                                for h in range(4):
                                    hp, pr = h % 2, h // 2
                                    cs = slice(tl0 + c * 64, tl0 + (c + 1) * 64)
                                    S.op("pe", C("matmul",
                                        b7[c * 64:(c + 1) * 64, z * 256 + h * 64:z * 256 + (h + 1) * 64],
                                        lhsT=keT[z][hp * 64:(hp + 1) * 64, pr, cs], rhs=qeT[z][hp * 64:(hp + 1) * 64, pr, cs],
                                        start=True, stop=True), [f"keT{z}", f"qeT{z}"], ["b7"], rt=hp * 64)
                            S.op("dve", C("tensor_tensor",
                                out=am[z][:], in0=b7[:, z * 256:(z + 1) * 256].rearrange("p (h i) -> p h i", i=64),
                                in1=mk.unsqueeze(1).to_broadcast([128, 4, 64]), op=ALU.mult), ["b7", "cm"], [f"am{z}"])
                        for c in range(2):
                            for h in range(4):
                                hp, pr = h % 2, h // 2
                                cs = slice(tl0 + c * 64, tl0 + (c + 1) * 64)
                                o_ap = b6[c * 64:(c + 1) * 64, h * 128:(h + 1) * 128]
                                S.op("pe", C("matmul",
                                    o_ap, lhsT=am[0][c * 64:(c + 1) * 64, h, :], rhs=v_b[c * 64:(c + 1) * 64, h * 128:(h + 1) * 128],
                                    start=True, stop=False), ["am0", vk], ["b6"], rt=c * 64)
                                S.op("pe", C("matmul",
                                    o_ap, lhsT=am[1][c * 64:(c + 1) * 64, h, :], rhs=v_b[c * 64:(c + 1) * 64, h * 128:(h + 1) * 128],
                                    start=False, stop=False), ["am1", vk], ["b6"], rt=c * 64)
                                S.op("pe", C("matmul",
                                    o_ap, lhsT=qeT[0][hp * 64:(hp + 1) * 64, pr, cs], rhs=Sbf[hp * 64:(hp + 1) * 64, 2 * i + c, pr, :],
                                    start=False, stop=True), ["qeT0", f"Sbf{2 * i + c}"], ["b6"], rt=hp * 64)
                        ob, obk = osb[i % 2], f"osb{i % 2}"
                        S.op("act", C("copy", out=ob[:], in_=b6[:, :]), ["b6"], [obk])
                        S.dma("pool", op_d[ti, :, :], ob[:], [obk], [], obk)
                        S.dma("pool", qeQ_d[ti, :, :, :], qeT[1][:, :, tl0:tl0 + 128], ["qeT1"], [], f"qq{i % 2}")
                        for kc in range(8):
                            S.op("pe", C("matmul", b5[:, :], lhsT=h_T[:, kc, tl0:tl0 + 128], rhs=Win[:, kc, C_GR:C_GR + 512],
                                                                          start=(kc == 0), stop=(kc == 7)), [WinK[kc], f"{hK}_{kc}"], ["b5"])
                        rb, rbk = rsbuf[i % 2], f"rsbuf{i % 2}"
                        S.op("act", C("copy", out=rb[:], in_=b5[:, :]), ["b5"], [rbk])
                        S.dma("pool", r_d[ti, :, :], rb[:], [rbk], [], rbk)

                if groups:
                    S.play([S.record(lambda: stage1(0))])
                for gidx in range(len(groups)):
                    lists = []
                    if gidx + 1 < len(groups):
                        lists.append(S.record(lambda: stage1(gidx + 1)))
                    lists.append(S.record(lambda: stageY(gidx)))
                    lists.append(S.record(lambda: stageZ(gidx)))
                    S.play(lists)
                S.barrier()
            if stop_after is not None and stop_after.startswith("A"):
                raise _Stop()

            with ExitStack() as ph:
                qq = [sb(ph, f"Bqq{i}", [128, 2, 128], BF16) for i in range(2)]
                uq = [sb(ph, f"Buq{i}", [128, 512]) for i in range(2)]
                opb = [sb(ph, f"Bop{i}", [128, 512]) for i in range(2)]
                rb = [sb(ph, f"Brb{i}", [128, 512]) for i in range(2)]
                sr = sb(ph, "Bsr", [128, 512])
                ob = sb(ph, "Bo", [128, 512])
                go = [sb(ph, f"Bgo{i}", [128, 512], BF16) for i in range(2)]
                tmpSB = sb(ph, "BtmpS", [128, 2, 128])
                Sbf = [sb(ph, f"BSbf{i}", [128, 2, 128], BF16) for i in range(4)]
                s8B = [sb(ph, f"Bs8_{i}", [128, 12]) for i in range(2)]
                junkB = sb(ph, "Bjunk", [128, 128], BF16)
                po = [ps(ph, "Bpo0", [128, 512])] * 2
                KTh = [sb(ph, f"CK{i}", [128, NTOK], BF16) for i in range(2)]
                VAh = [sb(ph, f"CV{i}", [128, NK, 129], BF16) for i in range(2)]
                QTg = [sb(ph, f"CQ{i}", [128, 4, 512], BF16) for i in range(2)]
                PT = [sb(ph, f"CP{i}", [128, 2, 512], BF16) for i in range(3)]
                accs = sb(ph, "Cacc", [128, 3, 387])
                rd = sb(ph, "Crd", [128, 3, 3])
                tmpo = sb(ph, "Ctmpo", [128, 128])
                oall = sb(ph, "Coall", [128, 4, 4, 128])
                s8 = [sb(ph, f"Cs8_{i}", [128, 12]) for i in range(2)]
                junk = sb(ph, "Cjunk", [128, 128], BF16)
                ao = [sb(ph, f"Cao{i}", [128, 512], BF16) for i in range(2)]
                STp = [ps(ph, f"CST{i}", [128, 2, 512]) for i in range(2)]
                acc = ps(ph, "CACC", [128, 3, 512])

                def threadB():
                    cc = 0
                    for n, ti in enumerate(reversed(range(NOWN))):
                        b = n % 2
                        S.dma("sp", qq[b][:], qeQ_d[ti, :, :, :], [], [f"Bqq{b}"], f"Bqq{b}")
                        S.dma("sp", uq[b][:], UQ_d[ti, :, :], [], [f"Buq{b}"], f"Buq{b}")
                        S.dma("sp", opb[b][:], op_d[ti, :, :], [], [f"Bop{b}"], f"Bop{b}")
                        S.dma("sp", rb[b][:], r_d[ti, :, :], [], [f"Brb{b}"], f"Brb{b}")
                        for c in (1, 0):
                            sbf, sbk = Sbf[cc % 4], f"BSbf{cc % 4}"
                            cc += 1
                            S.op("pool", C("tensor_copy", out=sbf[:], in_=SQ[:]), ["SQ"], [sbk])
                            S.op("dve", C("tensor_tensor",
                                out=tmpSB[:], in0=SQ[:], in1=uq[b][:].rearrange("p (a c d) -> p a c d", c=2, d=128)[:, :, c, :], op=ALU.add),
                                ["SQ", f"Buq{b}"], ["BtmpS"])
                            for pr in range(2):
                                S.op("dve", C("tensor_scalar", out=SQ[:, pr, :], in0=tmpSB[:, pr, :],
                                                                                        scalar1=eQ[:, ti, pr, c:c + 1], scalar2=None, op0=ALU.mult),
                                     ["BtmpS", "eQ"], ["SQ"])
                            for h in range(4):
                                hp, pr = h % 2, h // 2
                                S.op("pe", C("matmul",
                                    po[b][c * 64:(c + 1) * 64, h * 128:(h + 1) * 128], lhsT=qq[b][hp * 64:(hp + 1) * 64, pr, c * 64:(c + 1) * 64],
                                    rhs=sbf[hp * 64:(hp + 1) * 64, pr, :], start=True, stop=True), [f"Bqq{b}", sbk], ["Bpo0"], rt=hp * 64)
                        S.op("dve", C("tensor_tensor", out=ob[:], in0=po[b][:, :], in1=opb[b][:], op=ALU.add),
                             ["Bpo0", f"Bop{b}"], ["Bo"])
                        s_, sk_ = s8B[b], f"Bs8_{b}"
                        for h in range(4):
                            S.op("act", C("activation", out=junkB[:], in_=ob[:, h * 128:(h + 1) * 128], func=AF.Square,
                                                                          accum_out=s_[:, h:h + 1]), ["Bo"], [sk_])
                        S.op("act", C("activation", out=s_[:, 4:8], in_=s_[:, 0:4], func=AF.Ln, scale=1.0 / 128, bias=cst[:, 0:1]),
                             [sk_, "cst"], [sk_])
                        S.op("act", C("activation", out=s_[:, 8:12], in_=s_[:, 4:8], func=AF.Exp, scale=-0.5), [sk_], [sk_])
                        S.op("act", C("activation", out=sr[:], in_=rb[b][:], func=AF.Exp, scale=-1.0), [f"Brb{b}"], ["Bsr"])
                        S.op("dve", C("tensor_scalar", out=sr[:], in0=sr[:], scalar1=1.0, scalar2=None, op0=ALU.add), ["Bsr"], ["Bsr"])
                        S.op("dve", C("reciprocal", out=sr[:], in_=sr[:]), ["Bsr"], ["Bsr"])
                        S.op("pool", C("tensor_tensor", out=sr[:], in0=sr[:], in1=rb[b][:], op=ALU.mult), ["Bsr", f"Brb{b}"], ["Bsr"])
                        for h in range(4):
                            S.op("dve", C("scalar_tensor_tensor",
                                out=ob[:, h * 128:(h + 1) * 128], in0=ob[:, h * 128:(h + 1) * 128], scalar=s_[:, 8 + h:9 + h], in1=gng[:],
                                op0=ALU.mult, op1=ALU.mult), ["Bo", sk_, "gng"], ["Bo"])
                        S.op("dve", C("tensor_tensor", out=go[b][:], in0=ob[:], in1=sr[:], op=ALU.mult), ["Bo", "Bsr"], [f"Bgo{b}"])
                        S.dma("pool", mix_d[ti * 128:(ti + 1) * 128, 0:512], go[b][:], [f"Bgo{b}"], [], f"Bgo{b}")

                def threadC():
                    NQG = NOWN // 4
                    it = 0
                    aoc = 0
                    for qg in range(NQG):
                        qb = qg % 2
                        S.dma("sp", QTg[qb][:], QT_d[:, :, qg * 512:(qg + 1) * 512].rearrange("h p t -> p h t"), [], [f"CQ{qb}"], f"CQ{qb}")
                        for h in range(4):
                            kb = it % 2
                            it += 1
                            S.dma("sp", KTh[kb][:], KT_d[h, :, :], [], [f"CK{kb}"], f"CK{kb}")
                            S.dma("sp", VAh[kb][:], VA_d[:, h, :, :], [], [f"CV{kb}"], f"CV{kb}")

                            def qk(kt):
                                for c in range(2):
                                    S.op("pe", C("matmul",
                                        STp[kt % 2][:, c, :], lhsT=KTh[kb][c * 64:(c + 1) * 64, kt * 128:(kt + 1) * 128],
                                        rhs=QTg[qb][c * 64:(c + 1) * 64, h, :], start=True, stop=True),
                                        [f"CK{kb}", f"CQ{qb}"], [f"CST{c}_{kt % 2}"], rt=c * 64)

                            qk(0)
                            qk(1)
                            for kt in range(NK):
                                S.op("act", C("activation", out=PT[kt % 3][:].rearrange("p c t -> p (c t)"),
                                                                          in_=STp[kt % 2][:].rearrange("p c t -> p (c t)"), func=AF.Exp),
                                     [f"CST0_{kt % 2}", f"CST1_{kt % 2}"], [f"CP{kt % 3}"])
                                if kt + 2 < NK:
                                    qk(kt + 2)
                                for c in range(2):
                                    for qt in range(4):
                                        sl = c * 4 + qt
                                        S.op("pe", C("matmul",
                                            acc[:, sl // 3, (sl % 3) * 129:(sl % 3) * 129 + 129], lhsT=PT[kt % 3][:, c, qt * 128:(qt + 1) * 128],
                                            rhs=VAh[kb][:, kt, :], start=(kt == 0 and sl % 3 == 0), stop=(kt == NK - 1), skip_group_check=True),
                                            [f"CP{kt % 3}", f"CV{kb}"], [f"CACC{sl // 3}"])
                            for bk in range(3):
                                nsl = 3 if bk < 2 else 2
                                S.op("dve", C("tensor_copy", out=accs[:, bk, 0:nsl * 129], in_=acc[:, bk, 0:nsl * 129]),
                                     [f"CACC{bk}"], [f"Cacc{bk}"])
                                S.op("dve", C("reciprocal",
                                    out=rd[:, bk, 0:nsl], in_=accs[:, bk, 0:nsl * 129].rearrange("p (s c) -> p s c", c=129)[:, :, 128]),
                                    [f"Cacc{bk}"], ["Crd"])
                            for sl in range(4, 8):
                                S.op("dve", C("tensor_scalar", out=rd[:, sl // 3, sl % 3:sl % 3 + 1], in0=rd[:, sl // 3, sl % 3:sl % 3 + 1],
                                                                             scalar1=negl[:, 0:1], scalar2=None, op0=ALU.mult), ["Crd", "negl"], ["Crd"])
                            for qt in range(4):
                                s0, s1 = qt, 4 + qt
                                S.op("dve", C("tensor_scalar", out=tmpo[:], in0=accs[:, s0 // 3, (s0 % 3) * 129:(s0 % 3) * 129 + 128],
                                                                             scalar1=rd[:, s0 // 3, s0 % 3:s0 % 3 + 1], scalar2=None, op0=ALU.mult),
                                     [f"Cacc{s0 // 3}", "Crd"], ["Ctmpo"])
                                S.op("dve", C("scalar_tensor_tensor",
                                    out=oall[:, qt, h, :], in0=accs[:, s1 // 3, (s1 % 3) * 129:(s1 % 3) * 129 + 128],
                                    scalar=rd[:, s1 // 3, s1 % 3:s1 % 3 + 1], in1=tmpo[:], op0=ALU.mult, op1=ALU.add),
                                    [f"Cacc{s1 // 3}", "Crd", "Ctmpo"], ["Coall"])
                        for qt in range(4):
                            b = aoc % 2
                            aoc += 1
                            s_, sk_ = s8[b], f"Cs8_{b}"
                            for h in range(4):
                                S.op("act", C("activation", out=junk[:], in_=oall[:, qt, h, :], func=AF.Square,
                                                                                     accum_out=s_[:, h:h + 1]), ["Coall"], [sk_])
                            S.op("act", C("activation", out=s_[:, 4:8], in_=s_[:, 0:4], func=AF.Ln, scale=1.0 / 128, bias=cst[:, 0:1]),
                                 [sk_, "cst"], [sk_])
                            S.op("act", C("activation", out=s_[:, 8:12], in_=s_[:, 4:8], func=AF.Exp, scale=-0.5), [sk_], [sk_])
                            for h in range(4):
                                S.op("dve", C("scalar_tensor_tensor",
                                    out=ao[b][:, h * 128:(h + 1) * 128], in0=oall[:, qt, h, :], scalar=s_[:, 8 + h:9 + h], in1=dng[:],
                                    op0=ALU.mult, op1=ALU.mult), ["Coall", sk_, "dng"], [f"Cao{b}"])
                            row0 = (qg * 4 + qt) * 128
                            S.dma("pool", mix_d[row0:row0 + 128, 512:1024], ao[b][:], [f"Cao{b}"], [], f"Cao{b}")

                S.play([S.record(threadB), S.record(threadC)])
                S.barrier()
            if stop_after in ("B", "C"):
                raise _Stop()

            with ExitStack() as ph:
                GT = 2
                TD = GT * 128
                NG = NOWN // GT
                Wo = sb(ph, "Wo", [128, 8, D], BF16)
                Wfi = sb(ph, "Wfi", [128, 8, 2 * FFN_H], BF16)
                Wfo = sb(ph, "Wfo", [128, 22, D], BF16)
                identb = sb(ph, "identb", [128, 128], BF16)
                mx = sb(ph, "Dmx", [128, 1024], BF16)
                mixT = sb(ph, "DmixT", [128, 8, 128], BF16)
                xr = sb(ph, "Dxr", [128, 1024])
                x1 = sb(ph, "Dx1", [128, GT, 1024])
                xn2 = sb(ph, "Dxn", [128, 1024])
                st4 = [sb(ph, f"Dst4_{i}", [128, 4]) for i in range(2)]
                h2T = [sb(ph, f"Dh2T{i}", [128, 8, TD], BF16) for i in range(2)]
                sg = [sb(ph, f"Dsg{i}", [128, TD]) for i in range(2)]
                actT = sb(ph, "DactT", [128, 22, TD], BF16)
                ptb = ps(ph, "Dptb", [128, 1024], BF16)
                pmoP = ps(ph, "DpmoP", [128, 512])
                ptr = ps(ph, "Dptr", [128, 1024])
                pgu = [ps(ph, f"Dpgu{i}", [128, 512]) for i in range(2)]
                pmoF = ps(ph, "DpmoF", [128, 1024])
                S.excl.update(["Dptb", "DpmoP", "Dptr0", "Dptr1", "Dpgu0", "Dpgu1", "DpmoF0", "DpmoF1"])
                if dbg:
                    print("phase D sbuf remaining", nc.sbuf_bytes_remaining)
                S.op("dve", C("tensor_copy", out=identb[:], in_=ident), ["cm"], ["identb"])
                for kc in range(8):
                    S.dma("pool", Wo[:, kc, :], w_out[kc * 128:(kc + 1) * 128, :], [], [f"Wo{kc}"], f"w{kc % 2}")
                for kc in range(8):
                    S.dma("pool", Wfi[:, kc, :], w_ffi[kc * 128:(kc + 1) * 128, :], [], [f"Wfi{kc}"], f"w{kc % 2}")
                for hc in range(22):
                    S.dma("pool", Wfo[:, hc, :], w_ffo[hc * 128:(hc + 1) * 128, :], [], [f"Wfo{hc}"], f"w{hc % 2}")
                for kc in range(8):
                    S.op("dve" if kc % 2 == 0 else "pool", C("tensor_tensor", out=Wo[:, kc, :], in0=Wo[:, kc, :], in1=G1[:], op=ALU.mult),
                         [f"Wo{kc}", "G1"], [f"Wo{kc}"])
                for hc in range(22):
                    S.op("dve" if hc % 2 == 0 else "pool", C("tensor_tensor", out=Wfo[:, hc, :], in0=Wfo[:, hc, :], in1=G2[:], op=ALU.mult),
                         [f"Wfo{hc}", "G2"], [f"Wfo{hc}"])
                x1t = [[x1[:, 0, :], x1[:, 1, :]], [G1[:], G2[:]]]
                x1k = [["Dx1_0", "Dx1_1"], ["G1", "G2"]]
                tcount = [0]

                def prep(g):
                    hb = g % 2
                    for i in range(GT):
                        ti = g * GT + i
                        b = tcount[0] % 2
                        tcount[0] += 1
                        X1, X1k = x1t[hb][i], x1k[hb][i]
                        S.dma("sp", mx[:], mix_d[ti * 128:(ti + 1) * 128, :], [], ["Dmx"], "Dmx")
                        S.dma("sp", xr[:], xin[(T0_OWN + ti) * 128:(T0_OWN + ti + 1) * 128, :], [], ["Dxr"], "Dxr")
                        for kc in range(8):
                            S.op("pe", C("transpose", out=ptb[:, kc * 128:(kc + 1) * 128], in_=mx[:, kc * 128:(kc + 1) * 128],
                                         identity=identb[:]), ["Dmx", "identb"], ["Dptb"])
                        S.op("act", C("copy", out=mixT[:].rearrange("p k t -> p (k t)"), in_=ptb[:, :]), ["Dptb"], ["DmixT"])
                        for hf in range(2):
                            hs = slice(hf * 512, (hf + 1) * 512)
                            for kc in range(8):
                                S.op("pe", C("matmul", pmoP[:, :], lhsT=mixT[:, kc, :], rhs=Wo[:, kc, hs], start=(kc == 0), stop=(kc == 7)),
                                     ["DmixT", f"Wo{kc}"], ["DpmoP"])
                            S.op("dve", C("tensor_tensor", out=X1[:, hs], in0=pmoP[:, :], in1=xr[:, hs], op=ALU.add),
                                 ["DpmoP", "Dxr"], [X1k])
                        s4, sk4 = st4[b], f"Dst4_{b}"
                        S.op("act", C("activation", out=xn2[:], in_=X1, func=AF.Square, accum_out=s4[:, 0:1]), [X1k], ["Dxn", sk4])
                        S.op("act", C("activation", out=s4[:, 1:2], in_=s4[:, 0:1], func=AF.Ln, scale=1.0 / D, bias=cst[:, 0:1]),
                             [sk4, "cst"], [sk4])
                        S.op("act", C("activation", out=s4[:, 2:3], in_=s4[:, 1:2], func=AF.Exp, scale=-0.5), [sk4], [sk4])
                        S.op("dve", C("tensor_scalar", out=xn2[:], in0=X1, scalar1=s4[:, 2:3], scalar2=None, op0=ALU.mult),
                             [X1k, sk4], ["Dxn"])
                        for kc in range(8):
                            S.op("pe", C("transpose", out=ptr[:, kc * 128:(kc + 1) * 128], in_=xn2[:, kc * 128:(kc + 1) * 128],
                                         identity=ident), ["Dxn", "cm"], [f"Dptr{kc // 4}"])
                        for kc in range(8):
                            dst = h2T[hb][:, kc, i * 128:(i + 1) * 128]
                            src = ptr[:, kc * 128:(kc + 1) * 128]
                            if kc < 4:
                                S.op("dve", C("tensor_scalar", out=dst, in0=src, scalar1=A2[:, 0, kc:kc + 1], scalar2=B2[:, 0, kc:kc + 1],
                                              op0=ALU.mult, op1=ALU.add), [f"Dptr{kc // 4}", "A2", "B2"], [f"Dh2T{hb}_{kc}"])
                            else:
                                S.op("act", C("activation", out=dst, in_=src, func=AF.Identity, scale=A2[:, 0, kc:kc + 1],
                                              bias=B2[:, 0, kc:kc + 1]), [f"Dptr{kc // 4}", "A2", "B2"], [f"Dh2T{hb}_{kc}"])

                def ffn(g):
                    hb = g % 2
                    for hc in range(22):
                        pg_, pgk_ = pgu[hc % 2], f"Dpgu{hc % 2}"
                        for kc in range(8):
                            S.op("pe", C("matmul", pg_[:, 0:TD], lhsT=Wfi[:, kc, hc * 128:(hc + 1) * 128], rhs=h2T[hb][:, kc, :],
                                         start=(kc == 0), stop=(kc == 7)), [f"Wfi{kc}", f"Dh2T{hb}_{kc}"], [pgk_])
                        for kc in range(8):
                            S.op("pe", C("matmul", pg_[:, TD:2 * TD], lhsT=Wfi[:, kc, FFN_H + hc * 128:FFN_H + (hc + 1) * 128],
                                         rhs=h2T[hb][:, kc, :], start=(kc == 0), stop=(kc == 7)), [f"Wfi{kc}", f"Dh2T{hb}_{kc}"], [pgk_])
                        sgb, sgk = sg[hc % 2], f"Dsg{hc % 2}"
                        S.op("act", C("activation", out=sgb[:], in_=pg_[:, 0:TD], func=AF.Silu), [pgk_], [sgk])
                        S.op("dve", C("tensor_tensor", out=actT[:, hc, :], in0=pg_[:, TD:2 * TD], in1=sgb[:], op=ALU.mult),
                             [pgk_, sgk], [f"DactT{hc}"])
                    for i in range(GT):
                        ti = g * GT + i
                        X1, X1k = x1t[hb][i], x1k[hb][i]
                        for hf in range(2):
                            hs = slice(hf * 512, (hf + 1) * 512)
                            for hc in range(22):
                                S.op("pe", C("matmul", pmoF[:, hs], lhsT=actT[:, hc, i * 128:(i + 1) * 128], rhs=Wfo[:, hc, hs],
                                             start=(hc == 0), stop=(hc == 21)), [f"DactT{hc}", f"Wfo{hc}"], [f"DpmoF{hf}"])
                            S.op("dve", C("tensor_tensor", out=X1[:, hs], in0=pmoF[:, hs], in1=X1[:, hs], op=ALU.add),
                                 [f"DpmoF{hf}", X1k], [X1k])
                        S.dma("sp", out_d[ti * 128:(ti + 1) * 128, :], X1, [X1k], [], "Dyo")

                S.play([S.record(lambda: prep(0))])
                for g in range(NG):
                    lists = []
                    if g + 1 < NG:
                        lists.append(S.record(lambda: prep(g + 1)))
                    lists.append(S.record(lambda: ffn(g)))
                    S.play(lists)
                S.barrier()
            if stop_after == "D":
                raise _Stop()
        except _Stop:
            pass
        S.finish()
    return nc


def _const_mats():
    p = np.arange(128)
    same = (p[:, None] // 64) == (p[None, :] // 64)
    triP = np.where(same & (p[:, None] <= p[None, :]), -1.0 / 16, 0.0)
    triQ = np.where(same & (p[:, None] >= p[None, :]), -1.0 / 16, 0.0)
    blk = np.where(same, 1.0 / 64, 0.0)
    i64 = np.arange(64)
    maskP = ((p[:, None] % 64) <= i64[None, :]).astype(np.float64)
    maskQ = ((p[:, None] % 64) >= i64[None, :]).astype(np.float64)
    return np.concatenate([np.eye(128), triP, triQ, blk, maskP, maskQ], axis=1).astype(np.float32)


def _rope_tables(seq, positions):
    n_freq = 16
    inv_freq = (np.float32(10000.0) ** (-np.arange(n_freq, dtype=np.float32) / np.float32(n_freq))).astype(np.float32)
    row = (positions // 64).astype(np.float32)
    col = (positions % 64).astype(np.float32)
    ang = np.stack([row[:, None] * inv_freq[None, :], col[:, None] * inv_freq[None, :]], axis=0).astype(np.float32)
    cos = np.cos(ang).astype(np.float32)
    sin = np.sin(ang).astype(np.float32)
    n = len(positions)
    cosT = np.ones((128, 256 + n), np.float32)
    sinT = np.zeros((128, 256 + n), np.float32)
    for c in range(2):
        for ax in range(2):
            for hf in range(2):
                p0 = c * 64 + ax * 32 + hf * 16
                cosT[p0:p0 + 16, 256:] = cos[ax].T
                sinT[p0:p0 + 16, 256:] = (-sin[ax].T) if hf == 0 else sin[ax].T
    return cosT, sinT


_PROG_CACHE = {}


def _prep_inputs(inputs):
    f = lambda a: np.ascontiguousarray(np.asarray(a, dtype=np.float32))
    x = f(inputs["x"]); c = f(inputs["c"]); ctx = f(inputs["ctx"]); c_ctx = f(inputs["c_ctx"])
    B, SEQ, _ = x.shape
    half = SEQ // 2
    w_mod = f(inputs["w_mod"][0]); b_mod = f(inputs["b_mod"][0]).reshape(1, -1)
    g1 = f(inputs["norm1_g"][0]); g2 = f(inputs["norm2_g"][0])
    w_in = f(inputs["w_in"][0])
    gate_up = f(inputs["gla_gate_up"][0]); gate_bias = f(inputs["gla_gate_bias"][0])
    gng = f(inputs["gla_norm_g"][0]).reshape(1, 128); dng = f(inputs["diff_norm_g"][0]).reshape(1, 128)
    gq = f(inputs["diff_q_norm_g"][0]); gk = f(inputs["diff_k_norm_g"][0])
    lq = f(inputs["diff_lambda_q"][0]).reshape(1, 128); lk = f(inputs["diff_lambda_k"][0]).reshape(1, 128)
    w_out = f(inputs["w_out"][0]); w_ffi = f(inputs["w_ffn_in"][0]); w_ffo = f(inputs["w_ffn_out"][0])
    cmat = _const_mats()
    qkg = np.stack([np.tile(gk, 2), np.tile(gq, 2)], axis=1).astype(np.float32)
    in_maps = []
    for core in range(8):
        b, hf = core // 2, core % 2
        if hf == 1:
            oth = x[b, 0:half]; own = x[b, half:SEQ]; cx = ctx[b]
            pos = np.arange(SEQ)
            zP, zQ = 0, 1
        else:
            oth = x[b, SEQ - 1:half - 1:-1]; own = x[b, half - 1::-1]; cx = ctx[b, ::-1]
            pos = SEQ - 1 - np.arange(SEQ)
            zP, zQ = 1, 0
        xin = np.ascontiguousarray(np.concatenate([cx, oth, own], axis=0))
        cosT, sinT = _rope_tables(SEQ, pos)
        vecs = np.concatenate([b_mod.reshape(48, 128), g1.reshape(8, 128), g2.reshape(8, 128),
                               c[b].reshape(8, 128), c_ctx.reshape(8, 128)], axis=0).astype(np.float32)
        w_in_c = w_in.copy()
        w_in_c[:, C_GD:C_GD + 16] = w_in[:, C_GD + 16 * zP:C_GD + 16 * zP + 16]
        w_in_c[:, C_GD + 16:C_GD + 32] = w_in[:, C_GD + 16 * zQ:C_GD + 16 * zQ + 16]
        gu = np.zeros((49, 256), np.float32)
        gu[0:16] = gate_up[zP]; gu[16] = gate_bias[zP]
        gu[32:48] = gate_up[zQ]; gu[48] = gate_bias[zQ]
        in_maps.append({
            "xin": xin, "vecs": vecs, "w_mod": w_mod, "b_mod": b_mod, "w_in": w_in_c, "gu_in": gu, "gng_in": gng, "dng_in": dng,
            "qkg_in": qkg, "lq": lq, "lk": lk, "w_out": w_out, "w_ffi": w_ffi, "w_ffo": w_ffo,
            "cosT": cosT, "sinT": sinT, "cmat": cmat,
        })
    return in_maps, B, SEQ


def kernel(**inputs):
    in_maps, B, SEQ = _prep_inputs(inputs)
    half = SEQ // 2
    nt = half // 128
    key = (nt,)
    if key not in _PROG_CACHE:
        _PROG_CACHE[key] = build_program(nt, nt)
    nc = _PROG_CACHE[key]
    res = run_bass_kernel_spmd(nc, in_maps, core_ids=list(range(8)))
    out = np.empty((B, SEQ, D), np.float32)
    for core in range(8):
        b, hf = core // 2, core % 2
        o = np.asarray(res.results[core]["out"], dtype=np.float32)
        if hf == 1:
            out[b, half:SEQ] = o
        else:
            out[b, 0:half] = o[::-1]
    return out
```

```python
import math
from contextlib import ExitStack

import numpy as np
import ml_dtypes

import concourse.bass as bass
import concourse.mybir as mybir
from concourse.bass_utils import run_bass_kernel_spmd

F32 = mybir.dt.float32
BF16 = mybir.dt.bfloat16
AF = mybir.ActivationFunctionType
ALU = mybir.AluOpType
AX = mybir.AxisListType

D = 1024
EPS = 1e-6
NCTX = 2
FFN_H = 2816
W_IN_COLS = 3104
C_GQ, C_GK, C_GV, C_GR, C_GD, C_DQ, C_DK, C_DV = 0, 256, 512, 1024, 1536, 1568, 2080, 2592
LAM_INIT = 0.8 - 0.6 * math.exp(-0.3 * 0)


def C(name, *a, **kw):
    f = lambda e: getattr(e, name)(*a, **kw)
    f.opname, f.a, f.kw = name, a, kw
    return f


def _free_elems(ap):
    n = 1
    for d in list(ap.shape)[1:]:
        n *= int(d)
    return n


def _op_cost(e, fn):
    name = getattr(fn, "opname", None)
    if name is None:
        return 300.0
    out = fn.kw.get("out", fn.a[0] if fn.a else None)
    n = _free_elems(out) if out is not None else 128
    if e == "pe":
        if name == "transpose":
            return 220.0
        lhsT = fn.kw.get("lhsT")
        mult = 3.5 if (lhsT is not None and lhsT.dtype == F32) else 1.0
        return (max(n, 64) / 1.6 + 40.0) * mult
    if e == "act":
        return n / 0.96 + 220.0
    if e == "dve":
        return n / 0.96 * (8.0 if name == "reciprocal" else 1.0) + 80.0
    return n * 1.7 + 150.0


class _Stop(Exception):
    pass


class Sched:
    def __init__(self, nc, stack):
        self.nc = nc
        self.stack = stack
        self.eng = {"pe": nc.tensor, "act": nc.scalar, "dve": nc.vector, "pool": nc.gpsimd, "sp": nc.sync}
        self.sems = {}
        self.cnt = {}
        for e in ("pe", "act", "dve", "pool"):
            self.sems["E" + e] = stack.enter_context(nc.semaphore("s_" + e))
            self.cnt[e] = 0
        self.waited = {e: {} for e in self.eng}
        self.lastw = {}
        self.readers = {}
        self.slots = {}
        self.pending = {e: [] for e in self.eng}
        self.n_inst = 0
        self.excl = set()
        self.lastacc = {}
        self.pe_last = None
        self.rec = None
        self.sim_e, self.sim_w, self.sim_r, self.sim_a = {}, {}, {}, {}

    def record(self, fn):
        assert self.rec is None
        self.rec = []
        fn()
        lst, self.rec = self.rec, None
        return lst

    def _sim_ready(self, e, reads, writes):
        t = 0.0
        for k in reads:
            t = max(t, self.sim_w.get(k, 0.0))
            if k in self.excl:
                t = max(t, self.sim_a.get(k, 0.0))
        for k in writes:
            t = max(t, self.sim_w.get(k, 0.0), self.sim_r.get(k, 0.0))
        return t

    def _sim_commit(self, e, kind, fn_or_bytes, reads, writes):
        ready = self._sim_ready(e, reads, writes)
        start = max(ready + 120.0, self.sim_e.get(e, 0.0))
        if kind == "op":
            fin = start + _op_cost(e, fn_or_bytes)
            self.sim_e[e] = fin
        else:
            self.sim_e[e] = start + 80.0
            fin = start + 2000.0 + fn_or_bytes / 120.0
        for k in writes:
            self.sim_w[k] = fin
            self.sim_r[k] = 0.0
        for k in reads:
            self.sim_r[k] = max(self.sim_r.get(k, 0.0), fin)
        for k in list(reads) + list(writes):
            if k in self.excl:
                self.sim_a[k] = fin

    def play(self, lists, greedy=False):
        cur = [0] * len(lists)
        while True:
            best = None
            for li, lst in enumerate(lists):
                if cur[li] >= len(lst):
                    continue
                kind, args, kw = lst[cur[li]]
                if greedy:
                    e = args[0]
                    reads, writes = (args[2], args[3]) if kind == "op" else (args[3], args[4])
                    st = max(self._sim_ready(e, reads, writes) + 120.0, self.sim_e.get(e, 0.0))
                    key = (st, cur[li] / len(lst), li)
                else:
                    key = ((cur[li] + 0.5) / len(lst), li)
                if best is None or key < best[0]:
                    best = (key, li)
            if best is None:
                break
            li = best[1]
            kind, args, kw = lists[li][cur[li]]
            cur[li] += 1
            if kind == "op":
                self.op(*args, **kw)
            else:
                self.dma(*args, **kw)

    def _wait(self, e, tok):
        sk, val, _ = tok
        if self.waited[e].get(sk, 0) >= val:
            return
        self.eng[e].wait_ge(self.sems[sk], val)
        self.waited[e][sk] = val

    def _deps(self, e, reads, writes):
        deps = []
        for k in reads:
            t = self.lastw.get(k)
            if t is not None:
                deps.append((t, "raw"))
            if k in self.excl:
                t = self.lastacc.get(k)
                if t is not None and t[2] != e:
                    deps.append((t, "raw"))
        for k in writes:
            t = self.lastw.get(k)
            if t is not None:
                deps.append((t, "waw"))
            for r in self.readers.get(k, ()):
                deps.append((r, "war"))
        for t in self.pending[e]:
            deps.append((t, "raw"))
        self.pending[e] = []
        for t, kind in deps:
            if t[2] == e and e == "pe":
                continue
            self._wait(e, t)

    def _record(self, tok, reads, writes):
        for k in writes:
            self.lastw[k] = tok
            self.readers[k] = []
        for k in reads:
            lst = self.readers.setdefault(k, [])
            lst[:] = [r for r in lst if r[0] != tok[0]]
            lst.append(tok)
        for k in list(reads) + list(writes):
            if k in self.excl:
                self.lastacc[k] = tok

    def op(self, e, fn, reads=(), writes=(), rt=0):
        if self.rec is not None:
            self.rec.append(("op", (e, fn, tuple(reads), tuple(writes)), {"rt": rt}))
            return None
        self._deps(e, reads, writes)
        self._sim_commit(e, "op", fn, reads, writes)
        if e == "pe":
            pl = self.pe_last
            if pl is not None and pl[1] != rt and any(k in pl[2] for k in writes):
                self._wait(e, pl[0])
        inst = fn(self.eng[e])
        self.cnt[e] += 1
        inst.then_inc(self.sems["E" + e], 1)
        tok = ("E" + e, self.cnt[e], e)
        self._record(tok, reads, writes)
        if e == "pe":
            self.pe_last = (tok, rt, set(writes))
        self.n_inst += 1
        return tok

    def dma(self, q, out, in_, reads, writes, slot, **kw):
        if self.rec is not None:
            self.rec.append(("dma", (q, out, in_, tuple(reads), tuple(writes), slot), kw))
            return None
        sk = "D" + slot
        if sk not in self.sems:
            self.sems[sk] = self.stack.enter_context(self.nc.semaphore("d_" + slot))
            self.slots[sk] = 0
        if self.slots[sk] > 0:
            self._wait(q, (sk, self.slots[sk], None))
        self._deps(q, reads, writes)
        nbytes = 128 * _free_elems(out) * (2 if out.dtype == BF16 else 4)
        self._sim_commit(q, "dma", nbytes, reads, writes)
        inst = self.eng[q].dma_start(out=out, in_=in_, **kw)
        self.slots[sk] += 16
        inst.then_inc(self.sems[sk], 16)
        tok = (sk, self.slots[sk], None)
        self._record(tok, reads, writes)
        self.n_inst += 1
        return tok

    def _all_toks(self):
        toks = [("E" + e, c, e) for e, c in self.cnt.items() if c > 0]
        toks += [(sk, v, None) for sk, v in self.slots.items() if v > 0]
        return toks

    def barrier(self):
        toks = self._all_toks()
        for e in self.eng:
            self.pending[e] = [t for t in toks if t[2] != e]

    def finish(self):
        for t in self._all_toks():
            self._wait("sp", t)


def build_program(NOTH, NOWN, dbg=False, stop_after=None):
    assert NOTH % 4 == 0 and NOWN % 4 == 0
    NK = NCTX + NOTH + NOWN
    NTOK = NK * 128
    T0_OTH = NCTX
    T0_OWN = NCTX + NOTH
    NQ = NOWN * 128

    nc = bass.Bass("TRN2", target_bir_lowering=False)

    def din(name, shape, dt=F32):
        return nc.dram_tensor(name, list(shape), dt, kind="ExternalInput").ap()

    def dscr(name, shape, dt):
        return nc.dram_tensor(name, list(shape), dt, kind=("ExternalOutput" if dbg else "Internal")).ap()

    xin = din("xin", [NTOK, D])
    vecs = din("vecs", [80, 128])
    w_mod = din("w_mod", [D, 6 * D])
    b_mod = din("b_mod", [1, 6 * D])
    w_in = din("w_in", [D, W_IN_COLS])
    gu_in = din("gu_in", [49, 256])
    gng_in = din("gng_in", [1, 128])
    dng_in = din("dng_in", [1, 128])
    qkg_in = din("qkg_in", [128, 2])
    lq_in = din("lq", [1, 128])
    lk_in = din("lk", [1, 128])
    w_out = din("w_out", [D, D])
    w_ffi = din("w_ffi", [D, 2 * FFN_H])
    w_ffo = din("w_ffo", [FFN_H, D])
    cos_in = din("cosT", [128, NTOK])
    sin_in = din("sinT", [128, NTOK])
    cmat = din("cmat", [128, 4 * 128 + 2 * 64])
    out_d = nc.dram_tensor("out", [NQ, D], F32, kind="ExternalOutput").ap()

    KT_d = dscr("KT_d", [4, 128, NTOK], BF16)
    VA_d = dscr("VA_d", [128, 4, NK, 129], BF16)
    QT_d = dscr("QT_d", [4, 128, NQ], BF16)
    qeQ_d = dscr("qeQ_d", [NOWN, 128, 2, 128], BF16)
    UQ_d = dscr("UQ_d", [NOWN, 128, 512], F32)
    op_d = dscr("op_d", [NOWN, 128, 512], F32)
    r_d = dscr("r_d", [NOWN, 128, 512], F32)
    mix_d = dscr("mix_d", [NQ, D], BF16)

    dbg_out = {}
    if dbg:
        for nm, shp in (("dbg_mod", [128, 96]), ("dbg_G", [128, 2048]),
                        ("dbg_SP", [128, 256]), ("dbg_SQ", [128, 256])):
            dbg_out[nm] = nc.dram_tensor(nm, shp, F32, kind="ExternalOutput").ap()

    top = ExitStack()
    with top:
        S = Sched(nc, top)
        S.excl.update(["p0T", "p0mod", "p0G0", "p0G1", "ptr0", "ptr1", "b2", "b3", "b4", "b5", "b6", "b7", "Bpo0", "Bpo1",
                       "CST0_0", "CST0_1", "CST1_0", "CST1_1", "CACC0", "CACC1", "CACC2",
                       "Dptb", "Dpmo0", "Dpmo1", "Dptr0", "Dptr1", "Dpgu0", "Dpgu1", "Dpgu2"])

        def sb(stack, name, shape, dt=F32):
            return stack.enter_context(nc.sbuf_tensor(name, list(shape), dt))

        def ps(stack, name, shape, dt=F32):
            return stack.enter_context(nc.psum_tensor(name, list(shape), dt))

        cm = sb(top, "cm", [128, 640])
        ident = cm[:, 0:128]
        triP = cm[:, 128:256]
        triQ = cm[:, 256:384]
        blk64 = cm[:, 384:512]
        maskP = cm[:, 512:576]
        maskQ = cm[:, 576:640]
        cst = sb(top, "cst", [128, 4])
        colv = sb(top, "colv", [128, 80])
        modT = sb(top, "modT", [128, 48, 2])
        A1 = sb(top, "A1", [128, 2, 8])
        B1 = sb(top, "B1", [128, 2, 8])
        A2 = sb(top, "A2", [128, 2, 8])
        B2 = sb(top, "B2", [128, 2, 8])
        G1 = sb(top, "G1", [128, 1024])
        G2 = sb(top, "G2", [128, 1024])
        gng = sb(top, "gng", [128, 128])
        dng = sb(top, "dng", [128, 128])
        qkg = sb(top, "qkg", [128, 2])
        negl = sb(top, "negl", [128, 1])
        gu = sb(top, "gu", [49, 256])
        eQ = sb(top, "eQ", [128, NOWN, 2, 2])
        SQ = sb(top, "SQ", [128, 2, 128])

        S.dma("sp", cm[:], cmat[:, :], [], ["cm"], "c0")
        S.op("dve", C("memset", cst[:, 0:1], EPS), [], ["cst"])
        S.op("dve", C("memset", cst[:, 1:2], 1.0), [], ["cst"])
        S.dma("sp", gu[:], gu_in[:, :], [], ["gu"], "c1")
        S.dma("sp", qkg[:], qkg_in[:, :], [], ["qkg"], "c2")
        S.dma("sp", gng[:], gng_in[0:1, :].to_broadcast([128, 128]), [], ["gng"], "c3")
        S.dma("sp", dng[:], dng_in[0:1, :].to_broadcast([128, 128]), [], ["dng"], "c4")
        S.op("dve", C("tensor_scalar", out=qkg[:, 1:2], in0=qkg[:, 1:2], scalar1=0.125, scalar2=None, op0=ALU.mult),
             ["qkg"], ["qkg"])
        S.op("dve", C("tensor_scalar", out=dng[:], in0=dng[:], scalar1=1.0 - LAM_INIT, scalar2=None, op0=ALU.mult),
             ["dng"], ["dng"])

        open_stacks = []
        try:
            phA = ExitStack()
            open_stacks.append(phA)
            Win = sb(phA, "Win", [128, 8, W_IN_COLS], BF16)
            for kc in range(8):
                S.dma("pool", Win[:, kc, :], w_in[kc * 128:(kc + 1) * 128, :], [], [f"Win{kc}"], f"w{kc % 2}")
            with ExitStack() as ph:
                stage = sb(ph, "stage", [80, 128])
                siluT = sb(ph, "siluT", [128, 8, 2])
                silubc = sb(ph, "silubc", [128, 8, 128])
                ones1 = sb(ph, "ones1", [1, 128])
                bmr = sb(ph, "bmr", [1, 2048])
                lqb = sb(ph, "lqb", [128, 128])
                lkb = sb(ph, "lkb", [128, 128])
                s2 = sb(ph, "s2", [128, 2])
                wm = [sb(ph, f"wm{i}", [128, 8, 512]) for i in range(2)]
                pT = ps(ph, "p0T", [128, 512])
                pmod = ps(ph, "p0mod", [128, 512])
                pG = [ps(ph, f"p0G{i}", [128, 512]) for i in range(2)]

                S.dma("sp", stage[:], vecs[:, :], [], ["stage"], "c0")
                S.dma("sp", bmr[:, 0:1024], b_mod[0:1, 2048:3072], [], ["bmr"], "c1")
                S.dma("sp", bmr[:, 1024:2048], b_mod[0:1, 5120:6144], [], ["bmr"], "c1")
                S.dma("sp", lqb[:], lq_in[0:1, :].to_broadcast([128, 128]), [], ["lqb"], "c2")
                S.dma("sp", lkb[:], lk_in[0:1, :].to_broadcast([128, 128]), [], ["lkb"], "c3")
                S.op("dve", C("memset", ones1[:], 1.0), [], ["ones1"])
                S.op("dve", C("tensor_tensor", out=lqb[:], in0=lqb[:], in1=lkb[:], op=ALU.mult), ["lqb", "lkb"], ["lqb"])
                S.op("dve", C("tensor_reduce", out=s2[:], in_=lqb[:].rearrange("p (a b) -> p a b", b=64), axis=AX.X, op=ALU.add),
                     ["lqb"], ["s2"])
                S.op("act", C("activation", out=s2[:], in_=s2[:], func=AF.Exp), ["s2"], ["s2"])
                S.op("dve", C("tensor_tensor", out=negl[:], in0=s2[:, 1:2], in1=s2[:, 0:1], op=ALU.subtract), ["s2"], ["negl"])
                S.op("dve", C("tensor_scalar", out=negl[:], in0=negl[:], scalar1=-LAM_INIT, scalar2=None, op0=ALU.add),
                     ["negl"], ["negl"])
                S.op("pe", C("transpose", out=pT[:, 0:80], in_=stage[:], identity=ident[0:80, 0:80]), ["stage", "cm"], ["p0T"])
                S.op("dve", C("tensor_copy", out=colv[:], in_=pT[:, 0:80]), ["p0T"], ["colv"])
                S.op("act", C("activation", out=siluT[:].rearrange("p k r -> p r k"),
                                                   in_=colv[:, 64:80].rearrange("p (r k) -> p r k", k=8), func=AF.Silu),
                     ["colv"], ["siluT"])
                for kc in range(8):
                    S.op("dve", C("tensor_scalar", out=silubc[:, kc, :], in0=ident[:, :], scalar1=0.0,
                                                                 scalar2=siluT[:, kc, 0:1], op0=ALU.mult, op1=ALU.add),
                         ["siluT", "cm"], ["silubc"])
                gi = 0
                for cg in range(12):
                    w = wm[cg % 2]
                    wk = f"wm{cg % 2}"
                    S.dma("sp", w[:], w_mod[:, cg * 512:(cg + 1) * 512].rearrange("(k p) c -> p k c", p=128), [], [wk], wk)
                    for jj in range(4):
                        j = cg * 4 + jj
                        for kc in range(8):
                            S.op("pe", C("matmul",
                                pmod[:, 2 * j:2 * j + 2], lhsT=w[:, kc, jj * 128:(jj + 1) * 128], rhs=siluT[:, kc, :],
                                start=(kc == 0), stop=(kc == 7)), [wk, "siluT"], ["p0mod"])
                    if cg in (4, 5, 10, 11):
                        pg = pG[gi % 2]
                        pk_ = f"p0G{gi % 2}"
                        Gt = G1 if cg < 6 else G2
                        gcol = (cg % 2) * 512
                        boff = (0 if cg < 6 else 1024) + gcol
                        for kc in range(8):
                            S.op("pe", C("matmul", pg[:, :], lhsT=silubc[:, kc, :], rhs=w[:, kc, :],
                                                                              start=(kc == 0), stop=False),
                                 [wk, "silubc"], [pk_])
                        S.op("pe", C("matmul", pg[:, :], lhsT=ones1[:, :], rhs=bmr[:, boff:boff + 512],
                                                                        start=False, stop=True), ["ones1", "bmr"], [pk_])
                        S.op("act", C("copy", out=Gt[:, gcol:gcol + 512], in_=pg[:, :]),
                             [pk_], ["G1" if cg < 6 else "G2"])
                        gi += 1
                S.op("dve", C("tensor_tensor", out=modT[:], in0=pmod[:, 0:96].rearrange("p (j r) -> p j r", r=2),
                                                      in1=colv[:, 0:48].unsqueeze(2).to_broadcast([128, 48, 2]), op=ALU.add),
                     ["p0mod", "colv"], ["modT"])
                for r in range(2):
                    S.op("dve", C("scalar_tensor_tensor", out=A1[:, r, :], in0=modT[:, 8:16, r], scalar=1.0,
                                                                      in1=colv[:, 48:56], op0=ALU.add, op1=ALU.mult),
                         ["modT", "colv"], ["A1"])
                    S.op("dve", C("tensor_copy", out=B1[:, r, :], in_=modT[:, 0:8, r]), ["modT"], ["B1"])
                    S.op("dve", C("scalar_tensor_tensor", out=A2[:, r, :], in0=modT[:, 32:40, r], scalar=1.0,
                                                                      in1=colv[:, 56:64], op0=ALU.add, op1=ALU.mult),
                         ["modT", "colv"], ["A2"])
                    S.op("dve", C("tensor_copy", out=B2[:, r, :], in_=modT[:, 24:32, r]), ["modT"], ["B2"])
                if dbg:
                    S.dma("pool", dbg_out["dbg_mod"][:, :], modT[:].rearrange("p j r -> p (j r)"), ["modT"], [], "dbg")
                    S.dma("pool", dbg_out["dbg_G"][:, 0:1024], G1[:], ["G1"], [], "dbg")
                    S.dma("pool", dbg_out["dbg_G"][:, 1024:2048], G2[:], ["G2"], [], "dbg")
                S.barrier()
            if stop_after == "0":
                raise _Stop()

            with ExitStack() as ph:
                xt = [sb(ph, f"xt{i}", [128, 1024]) for i in range(2)]
                junk = sb(ph, "junk", [128, 1024], BF16)
                st4 = [sb(ph, f"st4_{i}", [128, 4]) for i in range(2)]
                xn = [sb(ph, f"xn{i}", [128, 1024]) for i in range(2)]
                hT = [sb(ph, f"hT{i}", [128, 8, 512], BF16) for i in range(2)]
                cosb = [sb(ph, f"cosb{i}", [128, 512]) for i in range(1)] * 2
                sinb = [sb(ph, f"sinb{i}", [128, 512]) for i in range(1)] * 2
                sq = sb(ph, "sq", [128, 512], BF16)
                blkb = sb(ph, "blkb", [128, 128], BF16)
                lnb = sb(ph, "lnb", [128, 512])
                rsb = sb(ph, "rsb", [128, 512])
                kn = sb(ph, "kn", [128, 512])
                kr = sb(ph, "kr", [128, 512])
                t1 = sb(ph, "t1", [128, 512])
                kst = [sb(ph, f"kst{i}", [128, 4, 512], BF16) for i in range(2)]
                qst = [sb(ph, f"qst{i}", [128, 4, 512], BF16) for i in range(1)] * 2
                vst = [sb(ph, f"vst{i}", [128, 4, 4, 129], BF16) for i in range(2)]
                dA = sb(ph, "dA", [49, 512])
                ex = sb(ph, "ex", [128, 512])
                spb = [sb(ph, f"spb{i}", [128, 2, 256]) for i in range(2)]
                en = sb(ph, "en", [128, 256])
                ke = [[sb(ph, f"ke{z}_{i}", [128, 256], BF16) for i in range(2)] for z in range(2)]
                vbf = [sb(ph, f"vbf{i}", [128, 512], BF16) for i in range(4)]
                ez = [sb(ph, f"ez{z}", [128, 4, 2, 2]) for z in range(2)]
                E1 = [sb(ph, f"E1_{z}", [128, 2, 512]) for z in range(2)]
                E2 = [sb(ph, f"E2_{z}", [128, 2, 512]) for z in range(2)]
                qeT = [sb(ph, f"qeT{z}", [128, 2, 512], BF16) for z in range(2)]
                keT = [sb(ph, f"keT{z}", [128, 2, 512], BF16) for z in range(2)]
                UP = sb(ph, "UPs", [128, 1, 512])
                UQs = [sb(ph, f"UQs{i}", [128, 512]) for i in range(2)]
                UQc = sb(ph, "UQc", [128, 2, 512])
                SP = sb(ph, "SP", [128, 2, 128])
                tmpS = sb(ph, "tmpS", [128, 2, 128])
                Sbf = sb(ph, "Sbf", [128, 8, 2, 128], BF16)
                am = [sb(ph, f"am{z}", [128, 4, 64], BF16) for z in range(2)]
                osb = [sb(ph, f"osb{i}", [128, 512]) for i in range(2)]
                rsbuf = [sb(ph, f"rsbuf{i}", [128, 512]) for i in range(2)]
                ptr = ps(ph, "ptr", [128, 1024])
                b2 = ps(ph, "b2", [128, 512])
                b3 = ps(ph, "b3", [128, 512])
                b4 = ps(ph, "b4", [128, 512])
                b5 = ps(ph, "b5", [128, 512])
                b6 = ps(ph, "b6", [128, 512])
                b7 = ps(ph, "b7", [128, 512])

                WinK = [f"Win{kc}" for kc in range(8)]
                S.op("dve", C("tensor_copy", out=blkb[:], in_=blk64), ["cm"], ["blkb"])
                S.op("dve", C("memset", dA[:], 1.0), [], ["dA"])
                for i in range(2):
                    S.op("dve", C("memset", vst[i][:].rearrange("p a b c -> p (a b c)"), 1.0), [], [f"vst{i}"])
                S.op("dve", C("memset", SP[:].rearrange("p a b -> p (a b)"), 0.0), [], ["SP"])
                S.op("dve", C("memset", SQ[:].rearrange("p a b -> p (a b)"), 0.0), [], ["SQ"])

                groups = [(0, NCTX, "ctx")]
                for g in range(NOTH // 4):
                    groups.append((T0_OTH + 4 * g, 4, "oth"))
                for g in range(NOWN // 4):
                    groups.append((T0_OWN + 4 * g, 4, "own"))

                xc = [0]
                if stop_after == "A0":
                    groups = []
                def stage1(gidx):
                    t0, nt, kind = groups[gidx]
                    T = nt * 128
                    koff = t0 * 128
                    r_mod = 1 if kind == "ctx" else 0
                    own = kind == "own"
                    ctx = kind == "ctx"
                    gb = gidx % 2
                    h_T = hT[gb]
                    hK = f"hT{gb}"
                    for i in range(nt):
                        xb = xc[0] % 2
                        nb = xc[0] % 2
                        xc[0] += 1
                        x_t, xk = xt[xb], f"xt{xb}"
                        s4, sk4 = st4[xb], f"st4_{xb}"
                        x_n, nk = xn[nb], f"xn{nb}"
                        S.dma("sp", x_t[:], xin[(t0 + i) * 128:(t0 + i + 1) * 128, :], [], [xk], xk)
                        S.op("act", C("activation", out=junk[:], in_=x_t[:], func=AF.Square, accum_out=s4[:, 0:1]),
                             [xk], [sk4])
                        S.op("act", C("activation", out=s4[:, 1:2], in_=s4[:, 0:1], func=AF.Ln, scale=1.0 / D, bias=cst[:, 0:1]),
                             [sk4, "cst"], [sk4])
                        S.op("act", C("activation", out=s4[:, 2:3], in_=s4[:, 1:2], func=AF.Exp, scale=-0.5), [sk4], [sk4])
                        S.op("dve", C("tensor_scalar", out=x_n[:], in0=x_t[:], scalar1=s4[:, 2:3], scalar2=None,
                                                                                      op0=ALU.mult), [xk, sk4], [nk])
                        for kc in range(8):
                            S.op("pe", C("transpose", out=ptr[:, kc * 128:(kc + 1) * 128], in_=x_n[:, kc * 128:(kc + 1) * 128],
                                                                           identity=ident), [nk, "cm"], [f"ptr{kc // 4}"])
                        for kc in range(8):
                            dst = h_T[:, kc, i * 128:(i + 1) * 128]
                            src = ptr[:, kc * 128:(kc + 1) * 128]
                            if kc < 4:
                                S.op("dve", C("tensor_scalar",
                                    out=dst, in0=src, scalar1=A1[:, r_mod, kc:kc + 1], scalar2=B1[:, r_mod, kc:kc + 1],
                                    op0=ALU.mult, op1=ALU.add), [f"ptr{kc // 4}", "A1", "B1"], [f"{hK}_{kc}"])
                            else:
                                S.op("act", C("activation",
                                    out=dst, in_=src, func=AF.Identity, scale=A1[:, r_mod, kc:kc + 1], bias=B1[:, r_mod, kc:kc + 1]),
                                    [f"ptr{kc // 4}", "A1", "B1"], [f"{hK}_{kc}"])

                def stageY(gidx):
                    t0, nt, kind = groups[gidx]
                    T = nt * 128
                    koff = t0 * 128
                    r_mod = 1 if kind == "ctx" else 0
                    own = kind == "own"
                    ctx = kind == "ctx"
                    gb = gidx % 2
                    h_T = hT[gb]
                    hK = f"hT{gb}"
                    S.dma("sp", cosb[gb][:, 0:T], cos_in[:, koff:koff + T], [], ["cos0"], "cos0")
                    S.dma("sp", sinb[gb][:, 0:T], sin_in[:, koff:koff + T], [], ["sin0"], "sin0")

                    def qk_proj(col0, gcol, dst, dkey):
                        for h in range(4):
                            for kc in range(8):
                                S.op("pe", C("matmul", b2[:, 0:T], lhsT=Win[:, kc, col0 + h * 128:col0 + (h + 1) * 128],
                                                                          rhs=h_T[:, kc, 0:T], start=(kc == 0), stop=(kc == 7)),
                                     [WinK[kc], f"{hK}_{kc}"], ["b2"])
                            S.op("act", C("activation", out=sq[:, 0:T], in_=b2[:, 0:T], func=AF.Square), ["b2"], ["sq"])
                            S.op("pe", C("matmul", b3[:, 0:T], lhsT=blkb[:], rhs=sq[:, 0:T], start=True, stop=True), ["sq", "blkb"], ["b3"])
                            S.op("act", C("activation", out=lnb[:, 0:T], in_=b3[:, 0:T], func=AF.Ln, bias=cst[:, 0:1]), ["b3", "cst"], ["lnb"])
                            S.op("act", C("activation", out=rsb[:, 0:T], in_=lnb[:, 0:T], func=AF.Exp, scale=-0.5), ["lnb"], ["rsb"])
                            S.op("dve", C("scalar_tensor_tensor", out=kn[:, 0:T], in0=b2[:, 0:T], scalar=qkg[:, gcol:gcol + 1],
                                                                         in1=rsb[:, 0:T], op0=ALU.mult, op1=ALU.mult),
                                 ["b2", "rsb", "qkg"], ["kn"])
                            S.op("dve", C("stream_shuffle", out=kr[:, 0:T], in_=kn[:, 0:T], mask=[(i + 16) % 32 for i in range(32)]),
                                 ["kn"], ["kr"])
                            S.op("pool", C("tensor_tensor", out=t1[:, 0:T], in0=kn[:, 0:T], in1=cosb[gb][:, 0:T], op=ALU.mult),
                                 ["kn", "cos0"], ["t1"])
                            S.op("pool", C("tensor_tensor", out=kr[:, 0:T], in0=kr[:, 0:T], in1=sinb[gb][:, 0:T], op=ALU.mult),
                                 ["kr", "sin0"], ["kr"])
                            S.op("dve", C("tensor_tensor", out=dst[:, h, 0:T], in0=t1[:, 0:T], in1=kr[:, 0:T], op=ALU.add),
                                 ["t1", "kr"], [dkey])

                    qk_proj(C_DK, 0, kst[gb], f"kst{gb}")
                    S.dma("pool", KT_d[:, :, koff:koff + T].rearrange("h p t -> p h t"), kst[gb][:, :, 0:T], [f"kst{gb}"], [], f"ko{gb}")
                    if own:
                        qoff = (t0 - T0_OWN) * 128
                        qk_proj(C_DQ, 1, qst[gb], "qst0")
                        S.dma("pool", QT_d[:, :, qoff:qoff + T].rearrange("h p t -> p h t"), qst[gb][:, :, 0:T], ["qst0"], [], "qo0")
                    for i in range(nt):
                        for kc in range(8):
                            S.op("pe", C("matmul", b3[:, :], lhsT=h_T[:, kc, i * 128:(i + 1) * 128], rhs=Win[:, kc, C_DV:C_DV + 512],
                                                                      start=(kc == 0), stop=(kc == 7)), [WinK[kc], f"{hK}_{kc}"], ["b3"])
                        S.op("act", C("copy", out=vst[gb][:, :, i, 0:128], in_=b3[:, :].rearrange("p (h c) -> p h c", c=128)),
                             ["b3"], [f"vst{gb}"])
                    S.dma("pool", VA_d[:, :, t0:t0 + nt, :], vst[gb][:, :, 0:nt, :], [f"vst{gb}"], [], f"vo{gb}")


                def stageZ(gidx):
                    t0, nt, kind = groups[gidx]
                    T = nt * 128
                    koff = t0 * 128
                    r_mod = 1 if kind == "ctx" else 0
                    own = kind == "own"
                    ctx = kind == "ctx"
                    gb = gidx % 2
                    h_T = hT[gb]
                    hK = f"hT{gb}"
                    zs = (0, 1) if (own or ctx) else (0,)
                    for z in zs:
                        for kc in range(8):
                            S.op("pe", C("matmul", b7[32 * z:32 * z + 16, 0:T], lhsT=Win[:, kc, C_GD + 16 * z:C_GD + 16 * z + 16],
                                                                      rhs=h_T[:, kc, 0:T], start=(kc == 0), stop=(kc == 7)),
                                 [WinK[kc], f"{hK}_{kc}"], ["b7"])
                        S.op("act", C("copy", out=dA[32 * z:32 * z + 16, 0:T], in_=b7[32 * z:32 * z + 16, 0:T]), ["b7"], ["dA"])
                    for i in range(nt):
                        tl = slice(i * 128, (i + 1) * 128)
                        sp_t, spk = spb[i % 2], f"spb{i % 2}"
                        v_b, vk = vbf[i], f"vbf{i}"
                        for kc in range(8):
                            S.op("pe", C("matmul", b4[:, 0:256], lhsT=h_T[:, kc, tl], rhs=Win[:, kc, C_GK:C_GK + 256],
                                                                 start=(kc == 0), stop=(kc == 7)), [WinK[kc], f"{hK}_{kc}"], ["b4"])
                        for kc in range(8):
                            S.op("pe", C("matmul", b5[:, :], lhsT=h_T[:, kc, tl], rhs=Win[:, kc, C_GV:C_GV + 512],
                                                                 start=(kc == 0), stop=(kc == 7)), [WinK[kc], f"{hK}_{kc}"], ["b5"])
                        S.op("act", C("copy", out=v_b[:], in_=b5[:, :]), ["b5"], [vk])
                        for z in zs:
                            S.op("pe", C("matmul", b6[:, 256 * z:256 * z + 256], lhsT=dA[32 * z:32 * z + 17, tl],
                                                               rhs=gu[32 * z:32 * z + 17, :], start=True, stop=True), ["dA", "gu"], ["b6"], rt=32 * z)
                        W_ = 256 * len(zs)
                        S.op("act", C("activation", out=ex[:, 0:W_], in_=b6[:, 0:W_], func=AF.Exp, scale=-1.0), ["b6"], ["ex"])
                        S.op("act", C("activation", out=sp_t[:].rearrange("p z c -> p (z c)")[:, 0:W_], in_=ex[:, 0:W_],
                                                                       func=AF.Ln, bias=cst[:, 1:2]), ["ex", "cst"], [spk])
                        for z in zs:
                            tri = triP if z == 0 else triQ
                            k_e, kek = ke[z][i % 2], f"ke{z}_{i % 2}"
                            S.op("pe", C("matmul", b4[:, 256:512], lhsT=tri, rhs=sp_t[:, z, :], start=True, stop=True),
                                 [spk, "cm"], ["b4"])
                            S.op("act", C("activation", out=en[:], in_=b4[:, 256:512], func=AF.Exp, scale=-1.0), ["b4"], ["en"])
                            S.op("dve", C("tensor_tensor", out=k_e[:], in0=b4[:, 0:256], in1=en[:], op=ALU.mult),
                                 ["b4", "en"], [kek])
                            lc0 = (128 + 63) if z == 0 else 256
                            lastcols = cm[:, lc0:lc0 + 128].rearrange("p (a b) -> p a b", b=64)[:, :, 0]
                            for pr in range(2):
                                S.op("pe", C("matmul",
                                    b6[:, 2 * pr:2 * pr + 2], lhsT=sp_t[:, z, pr * 128:(pr + 1) * 128],
                                    rhs=lastcols, start=True, stop=True), [spk, "cm"], ["b6"])
                            S.op("act", C("activation", out=ez[z][:, i, :, :].rearrange("p a b -> p (a b)"), in_=b6[:, 0:4],
                                                                         func=AF.Exp), ["b6"], [f"ez{z}"])
                            if own:
                                for pr in range(2):
                                    S.op("pe", C("matmul",
                                        b6[:, 128 + pr * 128:256 + pr * 128], lhsT=sp_t[:, z, pr * 128:(pr + 1) * 128], rhs=tri,
                                        start=True, stop=True), [spk, "cm"], ["b6"])
                                S.op("act", C("activation", out=E1[z][:, :, tl], in_=b6[:, 128:384].rearrange("p (a b) -> p a b", b=128),
                                                                        func=AF.Exp), ["b6"], [f"E1_{z}"])
                                S.op("act", C("activation", out=E2[z][:, :, tl], in_=b6[:, 128:384].rearrange("p (a b) -> p a b", b=128),
                                                                        func=AF.Exp, scale=-1.0), ["b6"], [f"E2_{z}"])
                            for c in range(2):
                                for h in range(4):
                                    hp, pr = h % 2, h // 2
                                    S.op("pe", C("matmul",
                                        b7[hp * 64:(hp + 1) * 64, (pr * 2 + c) * 128:(pr * 2 + c + 1) * 128],
                                        lhsT=k_e[c * 64:(c + 1) * 64, h * 64:(h + 1) * 64],
                                        rhs=v_b[c * 64:(c + 1) * 64, h * 128:(h + 1) * 128], start=True, stop=True),
                                        [kek, vk], ["b7"], rt=c * 64)
                            if z == 0:
                                S.op("dve", C("tensor_copy", out=UP[:, 0, :], in_=b7[:, :]), ["b7"], ["UP"])
                            elif ctx:
                                S.op("dve", C("tensor_copy", out=UQc[:, i, :], in_=b7[:, :]), ["b7"], ["UQc"])
                            else:
                                ti = t0 - T0_OWN + i
                                uq, uqk = UQs[i % 2], f"UQs{i % 2}"
                                S.op("dve", C("tensor_copy", out=uq[:], in_=b7[:, :]), ["b7"], [uqk])
                                S.dma("pool", UQ_d[ti, :, :], uq[:], [uqk], [], uqk)
                                S.op("dve", C("tensor_copy", out=eQ[:, ti, :, :], in_=ez[1][:, i, :, :]), ["ez1"], ["eQ"])
                        for c in range(2):
                            if own:
                                S.op("act", C("copy", out=Sbf[:, 2 * i + c, :, :], in_=SP[:]), ["SP"], [f"Sbf{2 * i + c}"])
                            S.op("dve", C("tensor_tensor",
                                out=tmpS[:], in0=SP[:], in1=UP[:, 0, :].rearrange("p (a c d) -> p a c d", c=2, d=128)[:, :, c, :], op=ALU.add),
                                ["SP", "UP"], ["tmpS"])
                            for pr in range(2):
                                S.op("dve", C("tensor_scalar", out=SP[:, pr, :], in0=tmpS[:, pr, :],
                                                                                      scalar1=ez[0][:, i, pr, c:c + 1], scalar2=None, op0=ALU.mult),
                                     ["tmpS", "ez0"], ["SP"])
                    if ctx:
                        for i in reversed(range(nt)):
                            for c in (1, 0):
                                S.op("dve", C("tensor_tensor",
                                    out=tmpS[:], in0=SQ[:], in1=UQc[:, i, :].rearrange("p (a c d) -> p a c d", c=2, d=128)[:, :, c, :], op=ALU.add),
                                    ["SQ", "UQc"], ["tmpS"])
                                for pr in range(2):
                                    S.op("dve", C("tensor_scalar", out=SQ[:, pr, :], in0=tmpS[:, pr, :],
                                                                                          scalar1=ez[1][:, i, pr, c:c + 1], scalar2=None, op0=ALU.mult),
                                         ["tmpS", "ez1"], ["SQ"])
                        if dbg:
                            S.dma("pool", dbg_out["dbg_SP"][:, :], SP[:].rearrange("p a b -> p (a b)"), ["SP"], [], "dbg")
                            S.dma("pool", dbg_out["dbg_SQ"][:, :], SQ[:].rearrange("p a b -> p (a b)"), ["SQ"], [], "dbg")
                    if not own:
                        return

                    for pr in range(2):
                        for (col0, dsts, Es, scale) in ((C_GQ, qeT, E1, 0.125), (C_GK, keT, E2, 1.0)):
                            for kc in range(8):
                                S.op("pe", C("matmul",
                                    b4[:, 0:T], lhsT=Win[:, kc, col0 + pr * 128:col0 + (pr + 1) * 128], rhs=h_T[:, kc, 0:T],
                                    start=(kc == 0), stop=(kc == 7)), [WinK[kc], f"{hK}_{kc}"], ["b4"])
                            for z in range(2):
                                S.op("dve", C("scalar_tensor_tensor",
                                    out=dsts[z][:, pr, 0:T], in0=b4[:, 0:T], scalar=scale, in1=Es[z][:, pr, 0:T],
                                    op0=ALU.mult, op1=ALU.mult), ["b4", f"{'E1' if Es is E1 else 'E2'}_{z}"],
                                    [f"{'qeT' if dsts is qeT else 'keT'}{z}"])
                    for i in range(nt):
                        tl0 = i * 128
                        ti = t0 - T0_OWN + i
                        v_b, vk = vbf[i], f"vbf{i}"
                        for z in range(2):
                            mk = maskP if z == 0 else maskQ
                            for hp in range(2):
                                for pr in range(2):
                                    h = pr * 2 + hp
                                    for c in range(2):
                                        cs = slice(tl0 + c * 64, tl0 + (c + 1) * 64)
                                        S.op("pe", C("matmul",
                                            b7[c * 64:(c + 1) * 64, z * 256 + h * 64:z * 256 + (h + 1) * 64],
                                            lhsT=keT[z][hp * 64:(hp + 1) * 64, pr, cs], rhs=qeT[z][hp * 64:(hp + 1) * 64, pr, cs],
                                            start=True, stop=True), [f"keT{z}", f"qeT{z}"], ["b7"], rt=hp * 64)
                            S.op("dve", C("tensor_tensor",
                                out=am[z][:], in0=b7[:, z * 256:(z + 1) * 256].rearrange("p (h i) -> p h i", i=64),
                                in1=mk.unsqueeze(1).to_broadcast([128, 4, 64]), op=ALU.mult), ["b7", "cm"], [f"am{z}"])
                        for c in range(2):
                            first = True
                            for z in range(2):
                                for h in range(4):
                                    S.op("pe", C("matmul", b6[c * 64:(c + 1) * 64, h * 128:(h + 1) * 128],
                                                 lhsT=am[z][c * 64:(c + 1) * 64, h, :], rhs=v_b[c * 64:(c + 1) * 64, h * 128:(h + 1) * 128],
                                                 start=first, stop=False, skip_group_check=True), [f"am{z}", vk], ["b6"], rt=c * 64)
                                    first = False
                            for hp in range(2):
                                for pr in range(2):
                                    h = pr * 2 + hp
                                    cs = slice(tl0 + c * 64, tl0 + (c + 1) * 64)
                                    S.op("pe", C("matmul", b6[c * 64:(c + 1) * 64, h * 128:(h + 1) * 128],
                                                 lhsT=qeT[0][hp * 64:(hp + 1) * 64, pr, cs], rhs=Sbf[hp * 64:(hp + 1) * 64, 2 * i + c, pr, :],
                                                 start=False, stop=True, skip_group_check=True), ["qeT0", f"Sbf{2 * i + c}"], ["b6"], rt=hp * 64)
                        ob, obk = osb[i % 2], f"osb{i % 2}"
                        S.op("act", C("copy", out=ob[:], in_=b6[:, :]), ["b6"], [obk])
                        S.dma("pool", op_d[ti, :, :], ob[:], [obk], [], obk)
                        S.dma("pool", qeQ_d[ti, :, :, :], qeT[1][:, :, tl0:tl0 + 128], ["qeT1"], [], f"qq{i % 2}")
                        for kc in range(8):
                            S.op("pe", C("matmul", b5[:, :], lhsT=h_T[:, kc, tl0:tl0 + 128], rhs=Win[:, kc, C_GR:C_GR + 512],
                                                                          start=(kc == 0), stop=(kc == 7)), [WinK[kc], f"{hK}_{kc}"], ["b5"])
                        rb, rbk = rsbuf[i % 2], f"rsbuf{i % 2}"
                        S.op("act", C("copy", out=rb[:], in_=b5[:, :]), ["b5"], [rbk])
                        S.dma("pool", r_d[ti, :, :], rb[:], [rbk], [], rbk)

                if groups:
                    S.play([S.record(lambda: stage1(0))])
                for gidx in range(len(groups)):
                    lists = []
                    if gidx + 1 < len(groups):
                        lists.append(S.record(lambda: stage1(gidx + 1)))
                    lists.append(S.record(lambda: stageY(gidx)))
                    lists.append(S.record(lambda: stageZ(gidx)))
                    S.play(lists)
                S.barrier()
            open_stacks.remove(phA)
            phA.close()
            if stop_after is not None and stop_after.startswith("A"):
                raise _Stop()

            phD0 = ExitStack()
            open_stacks.append(phD0)
            Wo = sb(phD0, "Wo", [128, 8, D], BF16)
            Wfo = sb(phD0, "Wfo", [128, 22, D], BF16)

            def threadW():
                for kc in range(8):
                    S.dma("pool", Wo[:, kc, :], w_out[kc * 128:(kc + 1) * 128, :], [], [f"Wo{kc}"], f"w{kc % 2}")
                for hc in range(22):
                    S.dma("pool", Wfo[:, hc, :], w_ffo[hc * 128:(hc + 1) * 128, :], [], [f"Wfo{hc}"], f"w{hc % 2}")
                for kc in range(8):
                    S.op("dve" if kc % 2 == 0 else "pool", C("tensor_tensor", out=Wo[:, kc, :], in0=Wo[:, kc, :], in1=G1[:], op=ALU.mult),
                         [f"Wo{kc}", "G1"], [f"Wo{kc}"])
                for hc in range(22):
                    S.op("dve" if hc % 2 == 0 else "pool", C("tensor_tensor", out=Wfo[:, hc, :], in0=Wfo[:, hc, :], in1=G2[:], op=ALU.mult),
                         [f"Wfo{hc}", "G2"], [f"Wfo{hc}"])

            with ExitStack() as ph:
                qq = [sb(ph, f"Bqq{i}", [128, 2, 128], BF16) for i in range(2)]
                uq = [sb(ph, f"Buq{i}", [128, 512]) for i in range(2)]
                opb = [sb(ph, f"Bop{i}", [128, 512]) for i in range(2)]
                rb = [sb(ph, f"Brb{i}", [128, 512]) for i in range(2)]
                sr = sb(ph, "Bsr", [128, 512])
                ob = sb(ph, "Bo", [128, 512])
                go = [sb(ph, f"Bgo{i}", [128, 512], BF16) for i in range(2)]
                tmpSB = sb(ph, "BtmpS", [128, 2, 128])
                Sbf = [sb(ph, f"BSbf{i}", [128, 2, 128], BF16) for i in range(4)]
                s8B = [sb(ph, f"Bs8_{i}", [128, 12]) for i in range(2)]
                junkB = sb(ph, "Bjunk", [128, 128], BF16)
                po = [ps(ph, "Bpo0", [128, 512])] * 2
                KTh = [sb(ph, f"CK{i}", [128, NTOK], BF16) for i in range(2)]
                VAh = [sb(ph, f"CV{i}", [128, NK, 129], BF16) for i in range(2)]
                QTg = [sb(ph, f"CQ{i}", [128, 4, 512], BF16) for i in range(2)]
                PT = [sb(ph, f"CP{i}", [128, 2, 512], BF16) for i in range(3)]
                accs = sb(ph, "Cacc", [128, 3, 387])
                rd = sb(ph, "Crd", [128, 3, 3])
                tmpo = sb(ph, "Ctmpo", [128, 128])
                oall = sb(ph, "Coall", [128, 4, 4, 128])
                s8 = [sb(ph, f"Cs8_{i}", [128, 12]) for i in range(2)]
                junk = sb(ph, "Cjunk", [128, 128], BF16)
                ao = [sb(ph, f"Cao{i}", [128, 512], BF16) for i in range(2)]
                STp = [ps(ph, f"CST{i}", [128, 2, 512]) for i in range(2)]
                acc = ps(ph, "CACC", [128, 3, 512])

                def threadB():
                    cc = 0
                    for n, ti in enumerate(reversed(range(NOWN))):
                        b = n % 2
                        S.dma("sp", qq[b][:], qeQ_d[ti, :, :, :], [], [f"Bqq{b}"], f"Bqq{b}")
                        S.dma("sp", uq[b][:], UQ_d[ti, :, :], [], [f"Buq{b}"], f"Buq{b}")
                        S.dma("sp", opb[b][:], op_d[ti, :, :], [], [f"Bop{b}"], f"Bop{b}")
                        S.dma("sp", rb[b][:], r_d[ti, :, :], [], [f"Brb{b}"], f"Brb{b}")
                        for c in (1, 0):
                            sbf, sbk = Sbf[cc % 4], f"BSbf{cc % 4}"
                            cc += 1
                            S.op("pool", C("tensor_copy", out=sbf[:], in_=SQ[:]), ["SQ"], [sbk])
                            S.op("dve", C("tensor_tensor",
                                out=tmpSB[:], in0=SQ[:], in1=uq[b][:].rearrange("p (a c d) -> p a c d", c=2, d=128)[:, :, c, :], op=ALU.add),
                                ["SQ", f"Buq{b}"], ["BtmpS"])
                            for pr in range(2):
                                S.op("dve", C("tensor_scalar", out=SQ[:, pr, :], in0=tmpSB[:, pr, :],
                                                                                        scalar1=eQ[:, ti, pr, c:c + 1], scalar2=None, op0=ALU.mult),
                                     ["BtmpS", "eQ"], ["SQ"])
                            for h in (0, 2, 1, 3):
                                hp, pr = h % 2, h // 2
                                S.op("pe", C("matmul",
                                    po[b][c * 64:(c + 1) * 64, h * 128:(h + 1) * 128], lhsT=qq[b][hp * 64:(hp + 1) * 64, pr, c * 64:(c + 1) * 64],
                                    rhs=sbf[hp * 64:(hp + 1) * 64, pr, :], start=True, stop=True), [f"Bqq{b}", sbk], ["Bpo0"], rt=hp * 64)
                        S.op("dve", C("tensor_tensor", out=ob[:], in0=po[b][:, :], in1=opb[b][:], op=ALU.add),
                             ["Bpo0", f"Bop{b}"], ["Bo"])
                        s_, sk_ = s8B[b], f"Bs8_{b}"
                        for h in range(4):
                            S.op("act", C("activation", out=junkB[:], in_=ob[:, h * 128:(h + 1) * 128], func=AF.Square,
                                                                          accum_out=s_[:, h:h + 1]), ["Bo"], [sk_])
                        S.op("act", C("activation", out=s_[:, 4:8], in_=s_[:, 0:4], func=AF.Ln, scale=1.0 / 128, bias=cst[:, 0:1]),
                             [sk_, "cst"], [sk_])
                        S.op("act", C("activation", out=s_[:, 8:12], in_=s_[:, 4:8], func=AF.Exp, scale=-0.5), [sk_], [sk_])
                        S.op("act", C("activation", out=sr[:], in_=rb[b][:], func=AF.Exp, scale=-1.0), [f"Brb{b}"], ["Bsr"])
                        S.op("dve", C("tensor_scalar", out=sr[:], in0=sr[:], scalar1=1.0, scalar2=None, op0=ALU.add), ["Bsr"], ["Bsr"])
                        S.op("dve", C("reciprocal", out=sr[:], in_=sr[:]), ["Bsr"], ["Bsr"])
                        S.op("pool", C("tensor_tensor", out=sr[:], in0=sr[:], in1=rb[b][:], op=ALU.mult), ["Bsr", f"Brb{b}"], ["Bsr"])
                        for h in range(4):
                            S.op("dve", C("scalar_tensor_tensor",
                                out=ob[:, h * 128:(h + 1) * 128], in0=ob[:, h * 128:(h + 1) * 128], scalar=s_[:, 8 + h:9 + h], in1=gng[:],
                                op0=ALU.mult, op1=ALU.mult), ["Bo", sk_, "gng"], ["Bo"])
                        S.op("dve", C("tensor_tensor", out=go[b][:], in0=ob[:], in1=sr[:], op=ALU.mult), ["Bo", "Bsr"], [f"Bgo{b}"])
                        S.dma("pool", mix_d[ti * 128:(ti + 1) * 128, 0:512], go[b][:], [f"Bgo{b}"], [], f"Bgo{b}")

                def threadC():
                    NQG = NOWN // 4
                    it = 0
                    aoc = 0
                    for qg in range(NQG):
                        qb = qg % 2
                        S.dma("sp", QTg[qb][:], QT_d[:, :, qg * 512:(qg + 1) * 512].rearrange("h p t -> p h t"), [], [f"CQ{qb}"], f"CQ{qb}")
                        for h in range(4):
                            kb = it % 2
                            it += 1
                            S.dma("sp", KTh[kb][:], KT_d[h, :, :], [], [f"CK{kb}"], f"CK{kb}")
                            S.dma("sp", VAh[kb][:], VA_d[:, h, :, :], [], [f"CV{kb}"], f"CV{kb}")

                            def qk(kt):
                                for c in range(2):
                                    S.op("pe", C("matmul",
                                        STp[kt % 2][:, c, :], lhsT=KTh[kb][c * 64:(c + 1) * 64, kt * 128:(kt + 1) * 128],
                                        rhs=QTg[qb][c * 64:(c + 1) * 64, h, :], start=True, stop=True),
                                        [f"CK{kb}", f"CQ{qb}"], [f"CST{c}_{kt % 2}"], rt=c * 64)

                            qk(0)
                            qk(1)
                            for kt in range(NK):
                                S.op("act", C("activation", out=PT[kt % 3][:].rearrange("p c t -> p (c t)"),
                                                                          in_=STp[kt % 2][:].rearrange("p c t -> p (c t)"), func=AF.Exp),
                                     [f"CST0_{kt % 2}", f"CST1_{kt % 2}"], [f"CP{kt % 3}"])
                                if kt + 2 < NK:
                                    qk(kt + 2)
                                for c in range(2):
                                    for qt in range(4):
                                        sl = c * 4 + qt
                                        S.op("pe", C("matmul",
                                            acc[:, sl // 3, (sl % 3) * 129:(sl % 3) * 129 + 129], lhsT=PT[kt % 3][:, c, qt * 128:(qt + 1) * 128],
                                            rhs=VAh[kb][:, kt, :], start=(kt == 0 and sl % 3 == 0), stop=(kt == NK - 1), skip_group_check=True),
                                            [f"CP{kt % 3}", f"CV{kb}"], [f"CACC{sl // 3}"])
                            for bk in range(3):
                                nsl = 3 if bk < 2 else 2
                                S.op("dve", C("tensor_copy", out=accs[:, bk, 0:nsl * 129], in_=acc[:, bk, 0:nsl * 129]),
                                     [f"CACC{bk}"], [f"Cacc{bk}"])
                                S.op("dve", C("reciprocal",
                                    out=rd[:, bk, 0:nsl], in_=accs[:, bk, 0:nsl * 129].rearrange("p (s c) -> p s c", c=129)[:, :, 128]),
                                    [f"Cacc{bk}"], ["Crd"])
                            for sl in range(4, 8):
                                S.op("dve", C("tensor_scalar", out=rd[:, sl // 3, sl % 3:sl % 3 + 1], in0=rd[:, sl // 3, sl % 3:sl % 3 + 1],
                                                                             scalar1=negl[:, 0:1], scalar2=None, op0=ALU.mult), ["Crd", "negl"], ["Crd"])
                            for qt in range(4):
                                s0, s1 = qt, 4 + qt
                                S.op("dve", C("tensor_scalar", out=tmpo[:], in0=accs[:, s0 // 3, (s0 % 3) * 129:(s0 % 3) * 129 + 128],
                                                                             scalar1=rd[:, s0 // 3, s0 % 3:s0 % 3 + 1], scalar2=None, op0=ALU.mult),
                                     [f"Cacc{s0 // 3}", "Crd"], ["Ctmpo"])
                                S.op("dve", C("scalar_tensor_tensor",
                                    out=oall[:, qt, h, :], in0=accs[:, s1 // 3, (s1 % 3) * 129:(s1 % 3) * 129 + 128],
                                    scalar=rd[:, s1 // 3, s1 % 3:s1 % 3 + 1], in1=tmpo[:], op0=ALU.mult, op1=ALU.add),
                                    [f"Cacc{s1 // 3}", "Crd", "Ctmpo"], ["Coall"])
                        for qt in range(4):
                            b = aoc % 2
                            aoc += 1
                            s_, sk_ = s8[b], f"Cs8_{b}"
                            for h in range(4):
                                S.op("act", C("activation", out=junk[:], in_=oall[:, qt, h, :], func=AF.Square,
                                                                                     accum_out=s_[:, h:h + 1]), ["Coall"], [sk_])
                            S.op("act", C("activation", out=s_[:, 4:8], in_=s_[:, 0:4], func=AF.Ln, scale=1.0 / 128, bias=cst[:, 0:1]),
                                 [sk_, "cst"], [sk_])
                            S.op("act", C("activation", out=s_[:, 8:12], in_=s_[:, 4:8], func=AF.Exp, scale=-0.5), [sk_], [sk_])
                            for h in range(4):
                                S.op("dve", C("scalar_tensor_tensor",
                                    out=ao[b][:, h * 128:(h + 1) * 128], in0=oall[:, qt, h, :], scalar=s_[:, 8 + h:9 + h], in1=dng[:],
                                    op0=ALU.mult, op1=ALU.mult), ["Coall", sk_, "dng"], [f"Cao{b}"])
                            row0 = (qg * 4 + qt) * 128
                            S.dma("pool", mix_d[row0:row0 + 128, 512:1024], ao[b][:], [f"Cao{b}"], [], f"Cao{b}")

                if dbg:
                    print("phase B+C sbuf remaining", nc.sbuf_bytes_remaining)
                S.play([S.record(threadW), S.record(threadB), S.record(threadC)])
                S.barrier()
            if stop_after in ("B", "C"):
                raise _Stop()

            with ExitStack() as ph:
                GT = 2
                TD = GT * 128
                NG = NOWN // GT
                Wfi = sb(ph, "Wfi", [128, 8, 2 * FFN_H], BF16)
                identb = sb(ph, "identb", [128, 128], BF16)
                mx = sb(ph, "Dmx", [128, 1024], BF16)
                mixT = sb(ph, "DmixT", [128, 8, 128], BF16)
                xr = sb(ph, "Dxr", [128, 1024])
                x1 = sb(ph, "Dx1", [128, GT, 1024])
                xn2 = sb(ph, "Dxn", [128, 1024])
                st4 = [sb(ph, f"Dst4_{i}", [128, 4]) for i in range(2)]
                h2T = [sb(ph, f"Dh2T{i}", [128, 8, TD], BF16) for i in range(2)]
                sg = [sb(ph, f"Dsg{i}", [128, TD]) for i in range(2)]
                actT = sb(ph, "DactT", [128, 22, TD], BF16)
                ptb = ps(ph, "Dptb", [128, 1024], BF16)
                pmoP = ps(ph, "DpmoP", [128, 512])
                ptr = ps(ph, "Dptr", [128, 1024])
                pgu = [ps(ph, f"Dpgu{i}", [128, 512]) for i in range(2)]
                pmoF = ps(ph, "DpmoF", [128, 1024])
                S.excl.update(["Dptb", "DpmoP", "Dptr0", "Dptr1", "Dpgu0", "Dpgu1", "DpmoF0", "DpmoF1"])
                if dbg:
                    print("phase D sbuf remaining", nc.sbuf_bytes_remaining)
                S.op("dve", C("tensor_copy", out=identb[:], in_=ident), ["cm"], ["identb"])
                for j in range(11):
                    for part in range(2):
                        c0 = part * FFN_H + j * 256
                        S.dma("pool", Wfi[:, :, c0:c0 + 256], w_ffi[:, c0:c0 + 256].rearrange("(k p) c -> p k c", p=128), [],
                              [f"Wfi{part}_{j}"], f"w{(2 * j + part) % 2}")
                x1t = [[x1[:, 0, :], x1[:, 1, :]], [G1[:], G2[:]]]
                x1k = [["Dx1_0", "Dx1_1"], ["G1", "G2"]]
                tcount = [0]

                def prep(g):
                    hb = g % 2
                    for i in range(GT):
                        ti = g * GT + i
                        b = tcount[0] % 2
                        tcount[0] += 1
                        X1, X1k = x1t[hb][i], x1k[hb][i]
                        S.dma("sp", mx[:], mix_d[ti * 128:(ti + 1) * 128, :], [], ["Dmx"], "Dmx")
                        S.dma("sp", xr[:], xin[(T0_OWN + ti) * 128:(T0_OWN + ti + 1) * 128, :], [], ["Dxr"], "Dxr")
                        for kc in range(8):
                            S.op("pe", C("transpose", out=ptb[:, kc * 128:(kc + 1) * 128], in_=mx[:, kc * 128:(kc + 1) * 128],
                                         identity=identb[:]), ["Dmx", "identb"], ["Dptb"])
                        S.op("act", C("copy", out=mixT[:].rearrange("p k t -> p (k t)"), in_=ptb[:, :]), ["Dptb"], ["DmixT"])
                        for hf in range(2):
                            hs = slice(hf * 512, (hf + 1) * 512)
                            for kc in range(8):
                                S.op("pe", C("matmul", pmoP[:, :], lhsT=mixT[:, kc, :], rhs=Wo[:, kc, hs], start=(kc == 0), stop=(kc == 7)),
                                     ["DmixT", f"Wo{kc}"], ["DpmoP"])
                            S.op("dve", C("tensor_tensor", out=X1[:, hs], in0=pmoP[:, :], in1=xr[:, hs], op=ALU.add),
                                 ["DpmoP", "Dxr"], [X1k])
                        s4, sk4 = st4[b], f"Dst4_{b}"
                        S.op("act", C("activation", out=xn2[:], in_=X1, func=AF.Square, accum_out=s4[:, 0:1]), [X1k], ["Dxn", sk4])
                        S.op("act", C("activation", out=s4[:, 1:2], in_=s4[:, 0:1], func=AF.Ln, scale=1.0 / D, bias=cst[:, 0:1]),
                             [sk4, "cst"], [sk4])
                        S.op("act", C("activation", out=s4[:, 2:3], in_=s4[:, 1:2], func=AF.Exp, scale=-0.5), [sk4], [sk4])
                        S.op("dve", C("tensor_scalar", out=xn2[:], in0=X1, scalar1=s4[:, 2:3], scalar2=None, op0=ALU.mult),
                             [X1k, sk4], ["Dxn"])
                        for kc in range(8):
                            S.op("pe", C("transpose", out=ptr[:, kc * 128:(kc + 1) * 128], in_=xn2[:, kc * 128:(kc + 1) * 128],
                                         identity=ident), ["Dxn", "cm"], [f"Dptr{kc // 4}"])
                        for kc in range(8):
                            dst = h2T[hb][:, kc, i * 128:(i + 1) * 128]
                            src = ptr[:, kc * 128:(kc + 1) * 128]
                            if kc < 4:
                                S.op("dve", C("tensor_scalar", out=dst, in0=src, scalar1=A2[:, 0, kc:kc + 1], scalar2=B2[:, 0, kc:kc + 1],
                                              op0=ALU.mult, op1=ALU.add), [f"Dptr{kc // 4}", "A2", "B2"], [f"Dh2T{hb}_{kc}"])
                            else:
                                S.op("act", C("activation", out=dst, in_=src, func=AF.Identity, scale=A2[:, 0, kc:kc + 1],
                                              bias=B2[:, 0, kc:kc + 1]), [f"Dptr{kc // 4}", "A2", "B2"], [f"Dh2T{hb}_{kc}"])

                def ffn(g):
                    hb = g % 2
                    for hc in range(22):
                        pg_, pgk_ = pgu[hc % 2], f"Dpgu{hc % 2}"
                        for kc in range(8):
                            S.op("pe", C("matmul", pg_[:, 0:TD], lhsT=Wfi[:, kc, hc * 128:(hc + 1) * 128], rhs=h2T[hb][:, kc, :],
                                         start=(kc == 0), stop=(kc == 7)), [f"Wfi0_{hc // 2}", f"Dh2T{hb}_{kc}"], [pgk_])
                        for kc in range(8):
                            S.op("pe", C("matmul", pg_[:, TD:2 * TD], lhsT=Wfi[:, kc, FFN_H + hc * 128:FFN_H + (hc + 1) * 128],
                                         rhs=h2T[hb][:, kc, :], start=(kc == 0), stop=(kc == 7)), [f"Wfi1_{hc // 2}", f"Dh2T{hb}_{kc}"], [pgk_])
                        sgb, sgk = sg[hc % 2], f"Dsg{hc % 2}"
                        S.op("act", C("activation", out=sgb[:], in_=pg_[:, 0:TD], func=AF.Silu), [pgk_], [sgk])
                        S.op("dve", C("tensor_tensor", out=actT[:, hc, :], in0=pg_[:, TD:2 * TD], in1=sgb[:], op=ALU.mult),
                             [pgk_, sgk], [f"DactT{hc}"])
                    for i in range(GT):
                        ti = g * GT + i
                        X1, X1k = x1t[hb][i], x1k[hb][i]
                        for hf in range(2):
                            hs = slice(hf * 512, (hf + 1) * 512)
                            for hc in range(22):
                                S.op("pe", C("matmul", pmoF[:, hs], lhsT=actT[:, hc, i * 128:(i + 1) * 128], rhs=Wfo[:, hc, hs],
                                             start=(hc == 0), stop=(hc == 21)), [f"DactT{hc}", f"Wfo{hc}"], [f"DpmoF{hf}"])
                            S.op("dve", C("tensor_tensor", out=X1[:, hs], in0=pmoF[:, hs], in1=X1[:, hs], op=ALU.add),
                                 [f"DpmoF{hf}", X1k], [X1k])
                        S.dma("sp", out_d[ti * 128:(ti + 1) * 128, :], X1, [X1k], [], "Dyo")

                S.play([S.record(lambda: prep(0))])
                for g in range(NG):
                    lists = []
                    if g + 1 < NG:
                        lists.append(S.record(lambda: prep(g + 1)))
                    lists.append(S.record(lambda: ffn(g)))
                    S.play(lists)
                S.barrier()
            open_stacks.remove(phD0)
            phD0.close()
            if stop_after == "D":
                raise _Stop()
        except _Stop:
            for st_ in reversed(open_stacks):
                st_.close()
        S.finish()
    return nc


def _const_mats():
    p = np.arange(128)
    same = (p[:, None] // 64) == (p[None, :] // 64)
    triP = np.where(same & (p[:, None] <= p[None, :]), -1.0 / 16, 0.0)
    triQ = np.where(same & (p[:, None] >= p[None, :]), -1.0 / 16, 0.0)
    blk = np.where(same, 1.0 / 64, 0.0)
    i64 = np.arange(64)
    maskP = ((p[:, None] % 64) <= i64[None, :]).astype(np.float64)
    maskQ = ((p[:, None] % 64) >= i64[None, :]).astype(np.float64)
    return np.concatenate([np.eye(128), triP, triQ, blk, maskP, maskQ], axis=1).astype(np.float32)


def _rope_tables(seq, positions):
    n_freq = 16
    inv_freq = (np.float32(10000.0) ** (-np.arange(n_freq, dtype=np.float32) / np.float32(n_freq))).astype(np.float32)
    row = (positions // 64).astype(np.float32)
    col = (positions % 64).astype(np.float32)
    ang = np.stack([row[:, None] * inv_freq[None, :], col[:, None] * inv_freq[None, :]], axis=0).astype(np.float32)
    cos = np.cos(ang).astype(np.float32)
    sin = np.sin(ang).astype(np.float32)
    n = len(positions)
    cosT = np.ones((128, 256 + n), np.float32)
    sinT = np.zeros((128, 256 + n), np.float32)
    for c in range(2):
        for ax in range(2):
            for hf in range(2):
                p0 = c * 64 + ax * 32 + hf * 16
                cosT[p0:p0 + 16, 256:] = cos[ax].T
                sinT[p0:p0 + 16, 256:] = (-sin[ax].T) if hf == 0 else sin[ax].T
    return cosT, sinT


_PROG_CACHE = {}


def _prep_inputs(inputs):
    f = lambda a: np.ascontiguousarray(np.asarray(a, dtype=np.float32))
    x = f(inputs["x"]); c = f(inputs["c"]); ctx = f(inputs["ctx"]); c_ctx = f(inputs["c_ctx"])
    B, SEQ, _ = x.shape
    half = SEQ // 2
    w_mod = f(inputs["w_mod"][0]); b_mod = f(inputs["b_mod"][0]).reshape(1, -1)
    g1 = f(inputs["norm1_g"][0]); g2 = f(inputs["norm2_g"][0])
    w_in = f(inputs["w_in"][0])
    gate_up = f(inputs["gla_gate_up"][0]); gate_bias = f(inputs["gla_gate_bias"][0])
    gng = f(inputs["gla_norm_g"][0]).reshape(1, 128); dng = f(inputs["diff_norm_g"][0]).reshape(1, 128)
    gq = f(inputs["diff_q_norm_g"][0]); gk = f(inputs["diff_k_norm_g"][0])
    lq = f(inputs["diff_lambda_q"][0]).reshape(1, 128); lk = f(inputs["diff_lambda_k"][0]).reshape(1, 128)
    w_out = f(inputs["w_out"][0]); w_ffi = f(inputs["w_ffn_in"][0]); w_ffo = f(inputs["w_ffn_out"][0])
    cmat = _const_mats()
    qkg = np.stack([np.tile(gk, 2), np.tile(gq, 2)], axis=1).astype(np.float32)
    in_maps = []
    for core in range(8):
        b, hf = core // 2, core % 2
        if hf == 1:
            oth = x[b, 0:half]; own = x[b, half:SEQ]; cx = ctx[b]
            pos = np.arange(SEQ)
            zP, zQ = 0, 1
        else:
            oth = x[b, SEQ - 1:half - 1:-1]; own = x[b, half - 1::-1]; cx = ctx[b, ::-1]
            pos = SEQ - 1 - np.arange(SEQ)
            zP, zQ = 1, 0
        xin = np.ascontiguousarray(np.concatenate([cx, oth, own], axis=0))
        cosT, sinT = _rope_tables(SEQ, pos)
        vecs = np.concatenate([b_mod.reshape(48, 128), g1.reshape(8, 128), g2.reshape(8, 128),
                               c[b].reshape(8, 128), c_ctx.reshape(8, 128)], axis=0).astype(np.float32)
        w_in_c = w_in.copy()
        w_in_c[:, C_GD:C_GD + 16] = w_in[:, C_GD + 16 * zP:C_GD + 16 * zP + 16]
        w_in_c[:, C_GD + 16:C_GD + 32] = w_in[:, C_GD + 16 * zQ:C_GD + 16 * zQ + 16]
        gu = np.zeros((49, 256), np.float32)
        gu[0:16] = gate_up[zP]; gu[16] = gate_bias[zP]
        gu[32:48] = gate_up[zQ]; gu[48] = gate_bias[zQ]
        in_maps.append({
            "xin": xin, "vecs": vecs, "w_mod": w_mod, "b_mod": b_mod, "w_in": w_in_c, "gu_in": gu, "gng_in": gng, "dng_in": dng,
            "qkg_in": qkg, "lq": lq, "lk": lk, "w_out": w_out, "w_ffi": w_ffi, "w_ffo": w_ffo,
            "cosT": cosT, "sinT": sinT, "cmat": cmat,
        })
    return in_maps, B, SEQ


def kernel(**inputs):
    in_maps, B, SEQ = _prep_inputs(inputs)
    half = SEQ // 2
    nt = half // 128
    key = (nt,)
    if key not in _PROG_CACHE:
        _PROG_CACHE[key] = build_program(nt, nt)
    nc = _PROG_CACHE[key]
    res = run_bass_kernel_spmd(nc, in_maps, core_ids=list(range(8)))
    out = np.empty((B, SEQ, D), np.float32)
    for core in range(8):
        b, hf = core // 2, core % 2
        o = np.asarray(res.results[core]["out"], dtype=np.float32)
        if hf == 1:
            out[b, half:SEQ] = o
        else:
            out[b, 0:half] = o[::-1]
    return out
```

```python
import math
from contextlib import ExitStack

import numpy as np
import ml_dtypes

import concourse.bass as bass
import concourse.mybir as mybir
from concourse.bass_utils import run_bass_kernel_spmd

F32 = mybir.dt.float32
BF16 = mybir.dt.bfloat16
AF = mybir.ActivationFunctionType
ALU = mybir.AluOpType
AX = mybir.AxisListType

D = 1024
EPS = 1e-6
NCTX = 2
FFN_H = 2816
W_IN_COLS = 3104
C_GQ, C_GK, C_GV, C_GR, C_GD, C_DQ, C_DK, C_DV = 0, 256, 512, 1024, 1536, 1568, 2080, 2592
LAM_INIT = 0.8 - 0.6 * math.exp(-0.3 * 0)


def C(name, *a, **kw):
    f = lambda e: getattr(e, name)(*a, **kw)
    f.opname, f.a, f.kw = name, a, kw
    return f


def _free_elems(ap):
    n = 1
    for d in list(ap.shape)[1:]:
        n *= int(d)
    return n


def _op_cost(e, fn):
    name = getattr(fn, "opname", None)
    if name is None:
        return 300.0
    out = fn.kw.get("out", fn.a[0] if fn.a else None)
    n = _free_elems(out) if out is not None else 128
    if e == "pe":
        if name == "transpose":
            return 220.0
        lhsT = fn.kw.get("lhsT")
        mult = 3.5 if (lhsT is not None and lhsT.dtype == F32) else 1.0
        return (max(n, 64) / 1.6 + 40.0) * mult
    if e == "act":
        return n / 0.96 + 220.0
    if e == "dve":
        return n / 0.96 * (8.0 if name == "reciprocal" else 1.0) + 80.0
    return n * 1.7 + 150.0


class _Stop(Exception):
    pass


class Sched:
    def __init__(self, nc, stack):
        self.nc = nc
        self.stack = stack
        self.eng = {"pe": nc.tensor, "act": nc.scalar, "dve": nc.vector, "pool": nc.gpsimd, "sp": nc.sync}
        self.sems = {}
        self.cnt = {}
        for e in ("pe", "act", "dve", "pool"):
            self.sems["E" + e] = stack.enter_context(nc.semaphore("s_" + e))
            self.cnt[e] = 0
        self.waited = {e: {} for e in self.eng}
        self.lastw = {}
        self.readers = {}
        self.slots = {}
        self.pending = {e: [] for e in self.eng}
        self.n_inst = 0
        self.excl = set()
        self.lastacc = {}
        self.pe_last = None
        self.rec = None
        self.sim_e, self.sim_w, self.sim_r, self.sim_a = {}, {}, {}, {}

    def record(self, fn):
        assert self.rec is None
        self.rec = []
        fn()
        lst, self.rec = self.rec, None
        return lst

    def _sim_ready(self, e, reads, writes):
        t = 0.0
        for k in reads:
            t = max(t, self.sim_w.get(k, 0.0))
            if k in self.excl:
                t = max(t, self.sim_a.get(k, 0.0))
        for k in writes:
            t = max(t, self.sim_w.get(k, 0.0), self.sim_r.get(k, 0.0))
        return t

    def _sim_commit(self, e, kind, fn_or_bytes, reads, writes):
        ready = self._sim_ready(e, reads, writes)
        start = max(ready + 120.0, self.sim_e.get(e, 0.0))
        if kind == "op":
            fin = start + _op_cost(e, fn_or_bytes)
            self.sim_e[e] = fin
        else:
            self.sim_e[e] = start + 80.0
            fin = start + 2000.0 + fn_or_bytes / 120.0
        for k in writes:
            self.sim_w[k] = fin
            self.sim_r[k] = 0.0
        for k in reads:
            self.sim_r[k] = max(self.sim_r.get(k, 0.0), fin)
        for k in list(reads) + list(writes):
            if k in self.excl:
                self.sim_a[k] = fin

    def play(self, lists, greedy=False):
        cur = [0] * len(lists)
        while True:
            best = None
            for li, lst in enumerate(lists):
                if cur[li] >= len(lst):
                    continue
                kind, args, kw = lst[cur[li]]
                if greedy:
                    e = args[0]
                    reads, writes = (args[2], args[3]) if kind == "op" else (args[3], args[4])
                    st = max(self._sim_ready(e, reads, writes) + 120.0, self.sim_e.get(e, 0.0))
                    key = (st, cur[li] / len(lst), li)
                else:
                    key = ((cur[li] + 0.5) / len(lst), li)
                if best is None or key < best[0]:
                    best = (key, li)
            if best is None:
                break
            li = best[1]
            kind, args, kw = lists[li][cur[li]]
            cur[li] += 1
            if kind == "op":
                self.op(*args, **kw)
            else:
                self.dma(*args, **kw)

    def _wait(self, e, tok):
        sk, val, _ = tok
        if self.waited[e].get(sk, 0) >= val:
            return
        self.eng[e].wait_ge(self.sems[sk], val)
        self.waited[e][sk] = val

    def _deps(self, e, reads, writes):
        deps = []
        for k in reads:
            t = self.lastw.get(k)
            if t is not None:
                deps.append((t, "raw"))
            if k in self.excl:
                t = self.lastacc.get(k)
                if t is not None and t[2] != e:
                    deps.append((t, "raw"))
        for k in writes:
            t = self.lastw.get(k)
            if t is not None:
                deps.append((t, "waw"))
            for r in self.readers.get(k, ()):
                deps.append((r, "war"))
        for t in self.pending[e]:
            deps.append((t, "raw"))
        self.pending[e] = []
        need = {}
        for t, kind in deps:
            if t[2] == e and e == "pe":
                continue
            if self.waited[e].get(t[0], 0) >= t[1]:
                continue
            if t[0] not in need or need[t[0]][1] < t[1]:
                need[t[0]] = t
        return list(need.values())

    def _emit_waits(self, e, waits, keep_last=False):
        held = None
        if keep_last and waits:
            held = waits[-1]
            waits = waits[:-1]
        for t in waits:
            self._wait(e, t)
        return held

    def _record(self, tok, reads, writes):
        for k in writes:
            self.lastw[k] = tok
            self.readers[k] = []
        for k in reads:
            lst = self.readers.setdefault(k, [])
            lst[:] = [r for r in lst if r[0] != tok[0]]
            lst.append(tok)
        for k in list(reads) + list(writes):
            if k in self.excl:
                self.lastacc[k] = tok

    def op(self, e, fn, reads=(), writes=(), rt=0):
        if self.rec is not None:
            self.rec.append(("op", (e, fn, tuple(reads), tuple(writes)), {"rt": rt}))
            return None
        waits = self._deps(e, reads, writes)
        self._sim_commit(e, "op", fn, reads, writes)
        attach = (e in ("act", "dve", "pool") and getattr(fn, "opname", None) is not None
                  and "accum_out" not in fn.kw and fn.opname not in ("stream_shuffle",))
        held = self._emit_waits(e, waits, keep_last=attach)
        if e == "pe":
            pl = self.pe_last
            if pl is not None and pl[1] != rt and any(k in pl[2] for k in writes):
                self._wait(e, pl[0])
        inst = fn(self.eng[e])
        if held is not None:
            inst._wait_ge(self.sems[held[0]], held[1])
            self.waited[e][held[0]] = held[1]
        self.cnt[e] += 1
        inst.then_inc(self.sems["E" + e], 1)
        tok = ("E" + e, self.cnt[e], e)
        self._record(tok, reads, writes)
        if e == "pe":
            self.pe_last = (tok, rt, set(writes))
        self.n_inst += 1
        return tok

    def dma(self, q, out, in_, reads, writes, slot, **kw):
        if self.rec is not None:
            self.rec.append(("dma", (q, out, in_, tuple(reads), tuple(writes), slot), kw))
            return None
        sk = "D" + slot
        if sk not in self.sems:
            self.sems[sk] = self.stack.enter_context(self.nc.semaphore("d_" + slot))
            self.slots[sk] = 0
        if self.slots[sk] > 0:
            self._wait(q, (sk, self.slots[sk], None))
        self._emit_waits(q, self._deps(q, reads, writes))
        nbytes = 128 * _free_elems(out) * (2 if out.dtype == BF16 else 4)
        self._sim_commit(q, "dma", nbytes, reads, writes)
        inst = self.eng[q].dma_start(out=out, in_=in_, **kw)
        self.slots[sk] += 16
        inst.then_inc(self.sems[sk], 16)
        tok = (sk, self.slots[sk], None)
        self._record(tok, reads, writes)
        self.n_inst += 1
        return tok

    def _all_toks(self):
        toks = [("E" + e, c, e) for e, c in self.cnt.items() if c > 0]
        toks += [(sk, v, None) for sk, v in self.slots.items() if v > 0]
        return toks

    def barrier(self):
        toks = self._all_toks()
        for e in self.eng:
            self.pending[e] = [t for t in toks if t[2] != e]

    def finish(self):
        for t in self._all_toks():
            self._wait("sp", t)


def build_program(NOTH, NOWN, dbg=False, stop_after=None):
    assert NOTH % 4 == 0 and NOWN % 4 == 0
    NK = NCTX + NOTH + NOWN
    NTOK = NK * 128
    T0_OTH = NCTX
    T0_OWN = NCTX + NOTH
    NQ = NOWN * 128

    nc = bass.Bass("TRN2", target_bir_lowering=False)

    def din(name, shape, dt=F32):
        return nc.dram_tensor(name, list(shape), dt, kind="ExternalInput").ap()

    def dscr(name, shape, dt):
        return nc.dram_tensor(name, list(shape), dt, kind=("ExternalOutput" if dbg else "Internal")).ap()

    xin = din("xin", [NTOK, D])
    vecs = din("vecs", [80, 128])
    w_mod = din("w_mod", [D, 6 * D])
    b_mod = din("b_mod", [1, 6 * D])
    w_in = din("w_in", [D, W_IN_COLS])
    gu_in = din("gu_in", [49, 256])
    gng_in = din("gng_in", [1, 128])
    dng_in = din("dng_in", [1, 128])
    qkg_in = din("qkg_in", [128, 2])
    lq_in = din("lq", [1, 128])
    lk_in = din("lk", [1, 128])
    w_out = din("w_out", [D, D])
    w_ffi = din("w_ffi", [D, 2 * FFN_H])
    w_ffo = din("w_ffo", [FFN_H, D])
    cos_in = din("cosT", [128, NTOK])
    sin_in = din("sinT", [128, NTOK])
    cmat = din("cmat", [128, 4 * 128 + 2 * 64])
    out_d = nc.dram_tensor("out", [NQ, D], F32, kind="ExternalOutput").ap()

    KT_d = dscr("KT_d", [4, 128, NTOK], BF16)
    VA_d = dscr("VA_d", [128, 4, NK, 129], BF16)
    QT_d = dscr("QT_d", [4, 128, NQ], BF16)
    qeQ_d = dscr("qeQ_d", [NOWN, 128, 2, 128], BF16)
    UQ_d = dscr("UQ_d", [NOWN, 128, 512], F32)
    op_d = dscr("op_d", [NOWN, 128, 512], F32)
    r_d = dscr("r_d", [NOWN, 128, 512], F32)
    mix_d = dscr("mix_d", [NQ, D], BF16)

    dbg_out = {}
    if dbg:
        for nm, shp in (("dbg_mod", [128, 96]), ("dbg_G", [128, 2048]),
                        ("dbg_SP", [128, 256]), ("dbg_SQ", [128, 256])):
            dbg_out[nm] = nc.dram_tensor(nm, shp, F32, kind="ExternalOutput").ap()

    top = ExitStack()
    with top:
        S = Sched(nc, top)
        S.excl.update(["p0T", "p0mod", "p0G0", "p0G1", "ptr0", "ptr1", "b2", "b3", "b4", "b5", "b6", "b7", "Bpo0", "Bpo1",
                       "CST0_0", "CST0_1", "CST1_0", "CST1_1", "CACC0", "CACC1", "CACC2",
                       "Dptb", "Dpmo0", "Dpmo1", "Dptr0", "Dptr1", "Dpgu0", "Dpgu1", "Dpgu2"])

        def sb(stack, name, shape, dt=F32):
            return stack.enter_context(nc.sbuf_tensor(name, list(shape), dt))

        def ps(stack, name, shape, dt=F32):
            return stack.enter_context(nc.psum_tensor(name, list(shape), dt))

        cm = sb(top, "cm", [128, 640])
        ident = cm[:, 0:128]
        triP = cm[:, 128:256]
        triQ = cm[:, 256:384]
        blk64 = cm[:, 384:512]
        maskP = cm[:, 512:576]
        maskQ = cm[:, 576:640]
        cst = sb(top, "cst", [128, 4])
        colv = sb(top, "colv", [128, 80])
        modT = sb(top, "modT", [128, 48, 2])
        A1 = sb(top, "A1", [128, 2, 8])
        B1 = sb(top, "B1", [128, 2, 8])
        A2 = sb(top, "A2", [128, 2, 8])
        B2 = sb(top, "B2", [128, 2, 8])
        G1 = sb(top, "G1", [128, 1024])
        G2 = sb(top, "G2", [128, 1024])
        gng = sb(top, "gng", [128, 128])
        dng = sb(top, "dng", [128, 128])
        qkg = sb(top, "qkg", [128, 2])
        negl = sb(top, "negl", [128, 1])
        gu = sb(top, "gu", [49, 256])
        eQ = sb(top, "eQ", [128, NOWN, 2, 2])
        SQ = sb(top, "SQ", [128, 2, 128])

        S.dma("sp", cm[:], cmat[:, :], [], ["cm"], "c0")
        S.op("dve", C("memset", cst[:, 0:1], EPS), [], ["cst"])
        S.op("dve", C("memset", cst[:, 1:2], 1.0), [], ["cst"])
        S.dma("sp", gu[:], gu_in[:, :], [], ["gu"], "c1")
        S.dma("sp", qkg[:], qkg_in[:, :], [], ["qkg"], "c2")
        S.dma("sp", gng[:], gng_in[0:1, :].to_broadcast([128, 128]), [], ["gng"], "c3")
        S.dma("sp", dng[:], dng_in[0:1, :].to_broadcast([128, 128]), [], ["dng"], "c4")
        S.op("dve", C("tensor_scalar", out=qkg[:, 1:2], in0=qkg[:, 1:2], scalar1=0.125, scalar2=None, op0=ALU.mult),
             ["qkg"], ["qkg"])
        S.op("dve", C("tensor_scalar", out=dng[:], in0=dng[:], scalar1=1.0 - LAM_INIT, scalar2=None, op0=ALU.mult),
             ["dng"], ["dng"])

        open_stacks = []
        try:
            phA = ExitStack()
            open_stacks.append(phA)
            Win = sb(phA, "Win", [128, 8, W_IN_COLS], BF16)
            for kc in range(8):
                S.dma("pool", Win[:, kc, :], w_in[kc * 128:(kc + 1) * 128, :], [], [f"Win{kc}"], f"w{kc % 2}")
            with ExitStack() as ph:
                stage = sb(ph, "stage", [80, 128])
                siluT = sb(ph, "siluT", [128, 8, 2])
                silubc = sb(ph, "silubc", [128, 8, 128])
                ones1 = sb(ph, "ones1", [1, 128])
                bmr = sb(ph, "bmr", [1, 2048])
                lqb = sb(ph, "lqb", [128, 128])
                lkb = sb(ph, "lkb", [128, 128])
                s2 = sb(ph, "s2", [128, 2])
                wm = [sb(ph, f"wm{i}", [128, 8, 512]) for i in range(2)]
                pT = ps(ph, "p0T", [128, 512])
                pmod = ps(ph, "p0mod", [128, 512])
                pG = [ps(ph, f"p0G{i}", [128, 512]) for i in range(2)]

                S.dma("sp", stage[:], vecs[:, :], [], ["stage"], "c0")
                S.dma("sp", bmr[:, 0:1024], b_mod[0:1, 2048:3072], [], ["bmr"], "c1")
                S.dma("sp", bmr[:, 1024:2048], b_mod[0:1, 5120:6144], [], ["bmr"], "c1")
                S.dma("sp", lqb[:], lq_in[0:1, :].to_broadcast([128, 128]), [], ["lqb"], "c2")
                S.dma("sp", lkb[:], lk_in[0:1, :].to_broadcast([128, 128]), [], ["lkb"], "c3")
                S.op("dve", C("memset", ones1[:], 1.0), [], ["ones1"])
                S.op("dve", C("tensor_tensor", out=lqb[:], in0=lqb[:], in1=lkb[:], op=ALU.mult), ["lqb", "lkb"], ["lqb"])
                S.op("dve", C("tensor_reduce", out=s2[:], in_=lqb[:].rearrange("p (a b) -> p a b", b=64), axis=AX.X, op=ALU.add),
                     ["lqb"], ["s2"])
                S.op("act", C("activation", out=s2[:], in_=s2[:], func=AF.Exp), ["s2"], ["s2"])
                S.op("dve", C("tensor_tensor", out=negl[:], in0=s2[:, 1:2], in1=s2[:, 0:1], op=ALU.subtract), ["s2"], ["negl"])
                S.op("dve", C("tensor_scalar", out=negl[:], in0=negl[:], scalar1=-LAM_INIT, scalar2=None, op0=ALU.add),
                     ["negl"], ["negl"])
                S.op("pe", C("transpose", out=pT[:, 0:80], in_=stage[:], identity=ident[0:80, 0:80]), ["stage", "cm"], ["p0T"])
                S.op("dve", C("tensor_copy", out=colv[:], in_=pT[:, 0:80]), ["p0T"], ["colv"])
                S.op("act", C("activation", out=siluT[:].rearrange("p k r -> p r k"),
                                                   in_=colv[:, 64:80].rearrange("p (r k) -> p r k", k=8), func=AF.Silu),
                     ["colv"], ["siluT"])
                for kc in range(8):
                    S.op("dve", C("tensor_scalar", out=silubc[:, kc, :], in0=ident[:, :], scalar1=0.0,
                                                                 scalar2=siluT[:, kc, 0:1], op0=ALU.mult, op1=ALU.add),
                         ["siluT", "cm"], ["silubc"])
                gi = 0
                for cg in range(12):
                    w = wm[cg % 2]
                    wk = f"wm{cg % 2}"
                    S.dma("sp", w[:], w_mod[:, cg * 512:(cg + 1) * 512].rearrange("(k p) c -> p k c", p=128), [], [wk], wk)
                    for jj in range(4):
                        j = cg * 4 + jj
                        for kc in range(8):
                            S.op("pe", C("matmul",
                                pmod[:, 2 * j:2 * j + 2], lhsT=w[:, kc, jj * 128:(jj + 1) * 128], rhs=siluT[:, kc, :],
                                start=(kc == 0), stop=(kc == 7)), [wk, "siluT"], ["p0mod"])
                    if cg in (4, 5, 10, 11):
                        pg = pG[gi % 2]
                        pk_ = f"p0G{gi % 2}"
                        Gt = G1 if cg < 6 else G2
                        gcol = (cg % 2) * 512
                        boff = (0 if cg < 6 else 1024) + gcol
                        for kc in range(8):
                            S.op("pe", C("matmul", pg[:, :], lhsT=silubc[:, kc, :], rhs=w[:, kc, :],
                                                                              start=(kc == 0), stop=False),
                                 [wk, "silubc"], [pk_])
                        S.op("pe", C("matmul", pg[:, :], lhsT=ones1[:, :], rhs=bmr[:, boff:boff + 512],
                                                                        start=False, stop=True), ["ones1", "bmr"], [pk_])
                        S.op("act", C("copy", out=Gt[:, gcol:gcol + 512], in_=pg[:, :]),
                             [pk_], ["G1" if cg < 6 else "G2"])
                        gi += 1
                S.op("dve", C("tensor_tensor", out=modT[:], in0=pmod[:, 0:96].rearrange("p (j r) -> p j r", r=2),
                                                      in1=colv[:, 0:48].unsqueeze(2).to_broadcast([128, 48, 2]), op=ALU.add),
                     ["p0mod", "colv"], ["modT"])
                for r in range(2):
                    S.op("dve", C("scalar_tensor_tensor", out=A1[:, r, :], in0=modT[:, 8:16, r], scalar=1.0,
                                                                      in1=colv[:, 48:56], op0=ALU.add, op1=ALU.mult),
                         ["modT", "colv"], ["A1"])
                    S.op("dve", C("tensor_copy", out=B1[:, r, :], in_=modT[:, 0:8, r]), ["modT"], ["B1"])
                    S.op("dve", C("scalar_tensor_tensor", out=A2[:, r, :], in0=modT[:, 32:40, r], scalar=1.0,
                                                                      in1=colv[:, 56:64], op0=ALU.add, op1=ALU.mult),
                         ["modT", "colv"], ["A2"])
                    S.op("dve", C("tensor_copy", out=B2[:, r, :], in_=modT[:, 24:32, r]), ["modT"], ["B2"])
                if dbg:
                    S.dma("pool", dbg_out["dbg_mod"][:, :], modT[:].rearrange("p j r -> p (j r)"), ["modT"], [], "dbg")
                    S.dma("pool", dbg_out["dbg_G"][:, 0:1024], G1[:], ["G1"], [], "dbg")
                    S.dma("pool", dbg_out["dbg_G"][:, 1024:2048], G2[:], ["G2"], [], "dbg")
                S.barrier()
            if stop_after == "0":
                raise _Stop()

            with ExitStack() as ph:
                xt = [sb(ph, f"xt{i}", [128, 1024]) for i in range(2)]
                junk = sb(ph, "junk", [128, 1024], BF16)
                st4 = [sb(ph, f"st4_{i}", [128, 4]) for i in range(2)]
                xn = [sb(ph, f"xn{i}", [128, 1024]) for i in range(2)]
                hT = [sb(ph, f"hT{i}", [128, 8, 512], BF16) for i in range(2)]
                cosb = [sb(ph, f"cosb{i}", [128, 512]) for i in range(1)] * 2
                sinb = [sb(ph, f"sinb{i}", [128, 512]) for i in range(1)] * 2
                sq = sb(ph, "sq", [128, 512], BF16)
                blkb = sb(ph, "blkb", [128, 128], BF16)
                lnb = sb(ph, "lnb", [128, 512])
                rsb = sb(ph, "rsb", [128, 512])
                kn = sb(ph, "kn", [128, 512])
                kr = sb(ph, "kr", [128, 512])
                t1 = sb(ph, "t1", [128, 512])
                kst = [sb(ph, f"kst{i}", [128, 4, 512], BF16) for i in range(2)]
                qst = [sb(ph, f"qst{i}", [128, 4, 512], BF16) for i in range(1)] * 2
                vst = [sb(ph, f"vst{i}", [128, 4, 4, 129], BF16) for i in range(2)]
                dA = sb(ph, "dA", [49, 512])
                ex = sb(ph, "ex", [128, 512])
                spb = [sb(ph, f"spb{i}", [128, 2, 256]) for i in range(2)]
                en = sb(ph, "en", [128, 256])
                ke = [[sb(ph, f"ke{z}_{i}", [128, 256], BF16) for i in range(2)] for z in range(2)]
                vbf = [sb(ph, f"vbf{i}", [128, 512], BF16) for i in range(4)]
                ez = [sb(ph, f"ez{z}", [128, 4, 2, 2]) for z in range(2)]
                E1 = [sb(ph, f"E1_{z}", [128, 2, 512]) for z in range(2)]
                E2 = [sb(ph, f"E2_{z}", [128, 2, 512]) for z in range(2)]
                qeT = [sb(ph, f"qeT{z}", [128, 2, 512], BF16) for z in range(2)]
                keT = [sb(ph, f"keT{z}", [128, 2, 512], BF16) for z in range(2)]
                UP = sb(ph, "UPs", [128, 1, 512])
                UQs = [sb(ph, f"UQs{i}", [128, 512]) for i in range(2)]
                UQc = sb(ph, "UQc", [128, 2, 512])
                SP = sb(ph, "SP", [128, 2, 128])
                tmpS = sb(ph, "tmpS", [128, 2, 128])
                Sbf = sb(ph, "Sbf", [128, 8, 2, 128], BF16)
                am = [sb(ph, f"am{z}", [128, 4, 64], BF16) for z in range(2)]
                osb = [sb(ph, f"osb{i}", [128, 512]) for i in range(2)]
                rsbuf = [sb(ph, f"rsbuf{i}", [128, 512]) for i in range(2)]
                ptr = ps(ph, "ptr", [128, 1024])
                b2 = ps(ph, "b2", [128, 512])
                b3 = ps(ph, "b3", [128, 512])
                b4 = ps(ph, "b4", [128, 512])
                b5 = ps(ph, "b5", [128, 512])
                b6 = ps(ph, "b6", [128, 512])
                b7 = ps(ph, "b7", [128, 512])

                WinK = [f"Win{kc}" for kc in range(8)]
                S.op("dve", C("tensor_copy", out=blkb[:], in_=blk64), ["cm"], ["blkb"])
                S.op("dve", C("memset", dA[:], 1.0), [], ["dA"])
                for i in range(2):
                    S.op("dve", C("memset", vst[i][:].rearrange("p a b c -> p (a b c)"), 1.0), [], [f"vst{i}"])
                S.op("dve", C("memset", SP[:].rearrange("p a b -> p (a b)"), 0.0), [], ["SP"])
                S.op("dve", C("memset", SQ[:].rearrange("p a b -> p (a b)"), 0.0), [], ["SQ"])

                groups = [(0, NCTX, "ctx")]
                for g in range(NOTH // 4):
                    groups.append((T0_OTH + 4 * g, 4, "oth"))
                for g in range(NOWN // 4):
                    groups.append((T0_OWN + 4 * g, 4, "own"))

                xc = [0]
                if stop_after == "A0":
                    groups = []
                def stage1(gidx):
                    t0, nt, kind = groups[gidx]
                    T = nt * 128
                    koff = t0 * 128
                    r_mod = 1 if kind == "ctx" else 0
                    own = kind == "own"
                    ctx = kind == "ctx"
                    gb = gidx % 2
                    h_T = hT[gb]
                    hK = f"hT{gb}"
                    for i in range(nt):
                        xb = xc[0] % 2
                        nb = xc[0] % 2
                        xc[0] += 1
                        x_t, xk = xt[xb], f"xt{xb}"
                        s4, sk4 = st4[xb], f"st4_{xb}"
                        x_n, nk = xn[nb], f"xn{nb}"
                        S.dma("sp", x_t[:], xin[(t0 + i) * 128:(t0 + i + 1) * 128, :], [], [xk], xk)
                        S.op("act", C("activation", out=junk[:], in_=x_t[:], func=AF.Square, accum_out=s4[:, 0:1]),
                             [xk], [sk4])
                        S.op("act", C("activation", out=s4[:, 1:2], in_=s4[:, 0:1], func=AF.Ln, scale=1.0 / D, bias=cst[:, 0:1]),
                             [sk4, "cst"], [sk4])
                        S.op("act", C("activation", out=s4[:, 2:3], in_=s4[:, 1:2], func=AF.Exp, scale=-0.5), [sk4], [sk4])
                        S.op("dve", C("tensor_scalar", out=x_n[:], in0=x_t[:], scalar1=s4[:, 2:3], scalar2=None,
                                                                                      op0=ALU.mult), [xk, sk4], [nk])
                        for kc in range(8):
                            S.op("pe", C("transpose", out=ptr[:, kc * 128:(kc + 1) * 128], in_=x_n[:, kc * 128:(kc + 1) * 128],
                                                                           identity=ident), [nk, "cm"], [f"ptr{kc // 4}"])
                        for kc in range(8):
                            dst = h_T[:, kc, i * 128:(i + 1) * 128]
                            src = ptr[:, kc * 128:(kc + 1) * 128]
                            if kc < 4:
                                S.op("dve", C("tensor_scalar",
                                    out=dst, in0=src, scalar1=A1[:, r_mod, kc:kc + 1], scalar2=B1[:, r_mod, kc:kc + 1],
                                    op0=ALU.mult, op1=ALU.add), [f"ptr{kc // 4}", "A1", "B1"], [f"{hK}_{kc}"])
                            else:
                                S.op("act", C("activation",
                                    out=dst, in_=src, func=AF.Identity, scale=A1[:, r_mod, kc:kc + 1], bias=B1[:, r_mod, kc:kc + 1]),
                                    [f"ptr{kc // 4}", "A1", "B1"], [f"{hK}_{kc}"])

                def stageY(gidx):
                    t0, nt, kind = groups[gidx]
                    T = nt * 128
                    koff = t0 * 128
                    r_mod = 1 if kind == "ctx" else 0
                    own = kind == "own"
                    ctx = kind == "ctx"
                    gb = gidx % 2
                    h_T = hT[gb]
                    hK = f"hT{gb}"
                    S.dma("sp", cosb[gb][:, 0:T], cos_in[:, koff:koff + T], [], ["cos0"], "cos0")
                    S.dma("sp", sinb[gb][:, 0:T], sin_in[:, koff:koff + T], [], ["sin0"], "sin0")

                    def qk_proj(col0, gcol, dst, dkey):
                        for h in range(4):
                            for kc in range(8):
                                S.op("pe", C("matmul", b2[:, 0:T], lhsT=Win[:, kc, col0 + h * 128:col0 + (h + 1) * 128],
                                                                          rhs=h_T[:, kc, 0:T], start=(kc == 0), stop=(kc == 7)),
                                     [WinK[kc], f"{hK}_{kc}"], ["b2"])
                            S.op("act", C("activation", out=sq[:, 0:T], in_=b2[:, 0:T], func=AF.Square), ["b2"], ["sq"])
                            S.op("pe", C("matmul", b3[:, 0:T], lhsT=blkb[:], rhs=sq[:, 0:T], start=True, stop=True), ["sq", "blkb"], ["b3"])
                            S.op("act", C("activation", out=lnb[:, 0:T], in_=b3[:, 0:T], func=AF.Ln, bias=cst[:, 0:1]), ["b3", "cst"], ["lnb"])
                            S.op("act", C("activation", out=rsb[:, 0:T], in_=lnb[:, 0:T], func=AF.Exp, scale=-0.5), ["lnb"], ["rsb"])
                            S.op("dve", C("scalar_tensor_tensor", out=kn[:, 0:T], in0=b2[:, 0:T], scalar=qkg[:, gcol:gcol + 1],
                                                                         in1=rsb[:, 0:T], op0=ALU.mult, op1=ALU.mult),
                                 ["b2", "rsb", "qkg"], ["kn"])
                            S.op("dve", C("stream_shuffle", out=kr[:, 0:T], in_=kn[:, 0:T], mask=[(i + 16) % 32 for i in range(32)]),
                                 ["kn"], ["kr"])
                            S.op("pool", C("tensor_tensor", out=t1[:, 0:T], in0=kn[:, 0:T], in1=cosb[gb][:, 0:T], op=ALU.mult),
                                 ["kn", "cos0"], ["t1"])
                            S.op("pool", C("tensor_tensor", out=kr[:, 0:T], in0=kr[:, 0:T], in1=sinb[gb][:, 0:T], op=ALU.mult),
                                 ["kr", "sin0"], ["kr"])
                            S.op("dve", C("tensor_tensor", out=dst[:, h, 0:T], in0=t1[:, 0:T], in1=kr[:, 0:T], op=ALU.add),
                                 ["t1", "kr"], [dkey])

                    qk_proj(C_DK, 0, kst[gb], f"kst{gb}")
                    S.dma("pool", KT_d[:, :, koff:koff + T].rearrange("h p t -> p h t"), kst[gb][:, :, 0:T], [f"kst{gb}"], [], f"ko{gb}")
                    if own:
                        qoff = (t0 - T0_OWN) * 128
                        qk_proj(C_DQ, 1, qst[gb], "qst0")
                        S.dma("pool", QT_d[:, :, qoff:qoff + T].rearrange("h p t -> p h t"), qst[gb][:, :, 0:T], ["qst0"], [], "qo0")
                    for i in range(nt):
                        for kc in range(8):
                            S.op("pe", C("matmul", b3[:, :], lhsT=h_T[:, kc, i * 128:(i + 1) * 128], rhs=Win[:, kc, C_DV:C_DV + 512],
                                                                      start=(kc == 0), stop=(kc == 7)), [WinK[kc], f"{hK}_{kc}"], ["b3"])
                        S.op("act", C("copy", out=vst[gb][:, :, i, 0:128], in_=b3[:, :].rearrange("p (h c) -> p h c", c=128)),
                             ["b3"], [f"vst{gb}"])
                    S.dma("pool", VA_d[:, :, t0:t0 + nt, :], vst[gb][:, :, 0:nt, :], [f"vst{gb}"], [], f"vo{gb}")


                def stageZ(gidx):
                    t0, nt, kind = groups[gidx]
                    T = nt * 128
                    koff = t0 * 128
                    r_mod = 1 if kind == "ctx" else 0
                    own = kind == "own"
                    ctx = kind == "ctx"
                    gb = gidx % 2
                    h_T = hT[gb]
                    hK = f"hT{gb}"
                    zs = (0, 1) if (own or ctx) else (0,)
                    for z in zs:
                        for kc in range(8):
                            S.op("pe", C("matmul", b7[32 * z:32 * z + 16, 0:T], lhsT=Win[:, kc, C_GD + 16 * z:C_GD + 16 * z + 16],
                                                                      rhs=h_T[:, kc, 0:T], start=(kc == 0), stop=(kc == 7)),
                                 [WinK[kc], f"{hK}_{kc}"], ["b7"])
                        S.op("act", C("copy", out=dA[32 * z:32 * z + 16, 0:T], in_=b7[32 * z:32 * z + 16, 0:T]), ["b7"], ["dA"])
                    for i in range(nt):
                        tl = slice(i * 128, (i + 1) * 128)
                        sp_t, spk = spb[i % 2], f"spb{i % 2}"
                        v_b, vk = vbf[i], f"vbf{i}"
                        for kc in range(8):
                            S.op("pe", C("matmul", b4[:, 0:256], lhsT=h_T[:, kc, tl], rhs=Win[:, kc, C_GK:C_GK + 256],
                                                                 start=(kc == 0), stop=(kc == 7)), [WinK[kc], f"{hK}_{kc}"], ["b4"])
                        for kc in range(8):
                            S.op("pe", C("matmul", b5[:, :], lhsT=h_T[:, kc, tl], rhs=Win[:, kc, C_GV:C_GV + 512],
                                                                 start=(kc == 0), stop=(kc == 7)), [WinK[kc], f"{hK}_{kc}"], ["b5"])
                        S.op("act", C("copy", out=v_b[:], in_=b5[:, :]), ["b5"], [vk])
                        for z in zs:
                            S.op("pe", C("matmul", b6[:, 256 * z:256 * z + 256], lhsT=dA[32 * z:32 * z + 17, tl],
                                                               rhs=gu[32 * z:32 * z + 17, :], start=True, stop=True), ["dA", "gu"], ["b6"], rt=32 * z)
                        W_ = 256 * len(zs)
                        S.op("act", C("activation", out=ex[:, 0:W_], in_=b6[:, 0:W_], func=AF.Exp, scale=-1.0), ["b6"], ["ex"])
                        S.op("act", C("activation", out=sp_t[:].rearrange("p z c -> p (z c)")[:, 0:W_], in_=ex[:, 0:W_],
                                                                       func=AF.Ln, bias=cst[:, 1:2]), ["ex", "cst"], [spk])
                        for z in zs:
                            tri = triP if z == 0 else triQ
                            k_e, kek = ke[z][i % 2], f"ke{z}_{i % 2}"
                            S.op("pe", C("matmul", b4[:, 256:512], lhsT=tri, rhs=sp_t[:, z, :], start=True, stop=True),
                                 [spk, "cm"], ["b4"])
                            S.op("act", C("activation", out=en[:], in_=b4[:, 256:512], func=AF.Exp, scale=-1.0), ["b4"], ["en"])
                            S.op("dve", C("tensor_tensor", out=k_e[:], in0=b4[:, 0:256], in1=en[:], op=ALU.mult),
                                 ["b4", "en"], [kek])
                            lc0 = (128 + 63) if z == 0 else 256
                            lastcols = cm[:, lc0:lc0 + 128].rearrange("p (a b) -> p a b", b=64)[:, :, 0]
                            for pr in range(2):
                                S.op("pe", C("matmul",
                                    b6[:, 2 * pr:2 * pr + 2], lhsT=sp_t[:, z, pr * 128:(pr + 1) * 128],
                                    rhs=lastcols, start=True, stop=True), [spk, "cm"], ["b6"])
                            S.op("act", C("activation", out=ez[z][:, i, :, :].rearrange("p a b -> p (a b)"), in_=b6[:, 0:4],
                                                                         func=AF.Exp), ["b6"], [f"ez{z}"])
                            if own:
                                for pr in range(2):
                                    S.op("pe", C("matmul",
                                        b6[:, 128 + pr * 128:256 + pr * 128], lhsT=sp_t[:, z, pr * 128:(pr + 1) * 128], rhs=tri,
                                        start=True, stop=True), [spk, "cm"], ["b6"])
                                S.op("act", C("activation", out=E1[z][:, :, tl], in_=b6[:, 128:384].rearrange("p (a b) -> p a b", b=128),
                                                                        func=AF.Exp), ["b6"], [f"E1_{z}"])
                                S.op("act", C("activation", out=E2[z][:, :, tl], in_=b6[:, 128:384].rearrange("p (a b) -> p a b", b=128),
                                                                        func=AF.Exp, scale=-1.0), ["b6"], [f"E2_{z}"])
                            for c in range(2):
                                for h in range(4):
                                    hp, pr = h % 2, h // 2
                                    S.op("pe", C("matmul",
                                        b7[hp * 64:(hp + 1) * 64, (pr * 2 + c) * 128:(pr * 2 + c + 1) * 128],
                                        lhsT=k_e[c * 64:(c + 1) * 64, h * 64:(h + 1) * 64],
                                        rhs=v_b[c * 64:(c + 1) * 64, h * 128:(h + 1) * 128], start=True, stop=True),
                                        [kek, vk], ["b7"], rt=c * 64)
                            if z == 0:
                                S.op("dve", C("tensor_copy", out=UP[:, 0, :], in_=b7[:, :]), ["b7"], ["UP"])
                            elif ctx:
                                S.op("dve", C("tensor_copy", out=UQc[:, i, :], in_=b7[:, :]), ["b7"], ["UQc"])
                            else:
                                ti = t0 - T0_OWN + i
                                uq, uqk = UQs[i % 2], f"UQs{i % 2}"
                                S.op("dve", C("tensor_copy", out=uq[:], in_=b7[:, :]), ["b7"], [uqk])
                                S.dma("pool", UQ_d[ti, :, :], uq[:], [uqk], [], uqk)
                                S.op("dve", C("tensor_copy", out=eQ[:, ti, :, :], in_=ez[1][:, i, :, :]), ["ez1"], ["eQ"])
                        for c in range(2):
                            if own:
                                S.op("act", C("copy", out=Sbf[:, 2 * i + c, :, :], in_=SP[:]), ["SP"], [f"Sbf{2 * i + c}"])
                            S.op("dve", C("tensor_tensor",
                                out=tmpS[:], in0=SP[:], in1=UP[:, 0, :].rearrange("p (a c d) -> p a c d", c=2, d=128)[:, :, c, :], op=ALU.add),
                                ["SP", "UP"], ["tmpS"])
                            for pr in range(2):
                                S.op("dve", C("tensor_scalar", out=SP[:, pr, :], in0=tmpS[:, pr, :],
                                                                                      scalar1=ez[0][:, i, pr, c:c + 1], scalar2=None, op0=ALU.mult),
                                     ["tmpS", "ez0"], ["SP"])
                    if ctx:
                        for i in reversed(range(nt)):
                            for c in (1, 0):
                                S.op("dve", C("tensor_tensor",
                                    out=tmpS[:], in0=SQ[:], in1=UQc[:, i, :].rearrange("p (a c d) -> p a c d", c=2, d=128)[:, :, c, :], op=ALU.add),
                                    ["SQ", "UQc"], ["tmpS"])
                                for pr in range(2):
                                    S.op("dve", C("tensor_scalar", out=SQ[:, pr, :], in0=tmpS[:, pr, :],
                                                                                          scalar1=ez[1][:, i, pr, c:c + 1], scalar2=None, op0=ALU.mult),
                                         ["tmpS", "ez1"], ["SQ"])
                        if dbg:
                            S.dma("pool", dbg_out["dbg_SP"][:, :], SP[:].rearrange("p a b -> p (a b)"), ["SP"], [], "dbg")
                            S.dma("pool", dbg_out["dbg_SQ"][:, :], SQ[:].rearrange("p a b -> p (a b)"), ["SQ"], [], "dbg")
                    if not own:
                        return

                    for pr in range(2):
                        for (col0, dsts, Es, scale) in ((C_GQ, qeT, E1, 0.125), (C_GK, keT, E2, 1.0)):
                            for kc in range(8):
                                S.op("pe", C("matmul",
                                    b4[:, 0:T], lhsT=Win[:, kc, col0 + pr * 128:col0 + (pr + 1) * 128], rhs=h_T[:, kc, 0:T],
                                    start=(kc == 0), stop=(kc == 7)), [WinK[kc], f"{hK}_{kc}"], ["b4"])
                            for z in range(2):
                                S.op("dve", C("scalar_tensor_tensor",
                                    out=dsts[z][:, pr, 0:T], in0=b4[:, 0:T], scalar=scale, in1=Es[z][:, pr, 0:T],
                                    op0=ALU.mult, op1=ALU.mult), ["b4", f"{'E1' if Es is E1 else 'E2'}_{z}"],
                                    [f"{'qeT' if dsts is qeT else 'keT'}{z}"])
                    for i in range(nt):
                        tl0 = i * 128
                        ti = t0 - T0_OWN + i
                        v_b, vk = vbf[i], f"vbf{i}"
                        for z in range(2):
                            mk = maskP if z == 0 else maskQ
                            for hp in range(2):
                                for pr in range(2):
                                    h = pr * 2 + hp
                                    for c in range(2):
                                        cs = slice(tl0 + c * 64, tl0 + (c + 1) * 64)
                                        S.op("pe", C("matmul",
                                            b7[c * 64:(c + 1) * 64, z * 256 + h * 64:z * 256 + (h + 1) * 64],
                                            lhsT=keT[z][hp * 64:(hp + 1) * 64, pr, cs], rhs=qeT[z][hp * 64:(hp + 1) * 64, pr, cs],
                                            start=True, stop=True), [f"keT{z}", f"qeT{z}"], ["b7"], rt=hp * 64)
                            S.op("dve", C("tensor_tensor",
                                out=am[z][:], in0=b7[:, z * 256:(z + 1) * 256].rearrange("p (h i) -> p h i", i=64),
                                in1=mk.unsqueeze(1).to_broadcast([128, 4, 64]), op=ALU.mult), ["b7", "cm"], [f"am{z}"])
                        for c in range(2):
                            first = True
                            for z in range(2):
                                for h in range(4):
                                    S.op("pe", C("matmul", b6[c * 64:(c + 1) * 64, h * 128:(h + 1) * 128],
                                                 lhsT=am[z][c * 64:(c + 1) * 64, h, :], rhs=v_b[c * 64:(c + 1) * 64, h * 128:(h + 1) * 128],
                                                 start=first, stop=False, skip_group_check=True), [f"am{z}", vk], ["b6"], rt=c * 64)
                                    first = False
                            for hp in range(2):
                                for pr in range(2):
                                    h = pr * 2 + hp
                                    cs = slice(tl0 + c * 64, tl0 + (c + 1) * 64)
                                    S.op("pe", C("matmul", b6[c * 64:(c + 1) * 64, h * 128:(h + 1) * 128],
                                                 lhsT=qeT[0][hp * 64:(hp + 1) * 64, pr, cs], rhs=Sbf[hp * 64:(hp + 1) * 64, 2 * i + c, pr, :],
                                                 start=False, stop=True, skip_group_check=True), ["qeT0", f"Sbf{2 * i + c}"], ["b6"], rt=hp * 64)
                        ob, obk = osb[i % 2], f"osb{i % 2}"
                        S.op("act", C("copy", out=ob[:], in_=b6[:, :]), ["b6"], [obk])
                        S.dma("pool", op_d[ti, :, :], ob[:], [obk], [], obk)
                        S.dma("pool", qeQ_d[ti, :, :, :], qeT[1][:, :, tl0:tl0 + 128], ["qeT1"], [], f"qq{i % 2}")
                        for kc in range(8):
                            S.op("pe", C("matmul", b5[:, :], lhsT=h_T[:, kc, tl0:tl0 + 128], rhs=Win[:, kc, C_GR:C_GR + 512],
                                                                          start=(kc == 0), stop=(kc == 7)), [WinK[kc], f"{hK}_{kc}"], ["b5"])
                        rb, rbk = rsbuf[i % 2], f"rsbuf{i % 2}"
                        S.op("act", C("copy", out=rb[:], in_=b5[:, :]), ["b5"], [rbk])
                        S.dma("pool", r_d[ti, :, :], rb[:], [rbk], [], rbk)

                if groups:
                    S.play([S.record(lambda: stage1(0))])
                for gidx in range(len(groups)):
                    lists = []
                    if gidx + 1 < len(groups):
                        lists.append(S.record(lambda: stage1(gidx + 1)))
                    lists.append(S.record(lambda: stageY(gidx)))
                    lists.append(S.record(lambda: stageZ(gidx)))
                    S.play(lists)
                S.barrier()
            open_stacks.remove(phA)
            phA.close()
            if stop_after is not None and stop_after.startswith("A"):
                raise _Stop()

            phD0 = ExitStack()
            open_stacks.append(phD0)
            Wo = sb(phD0, "Wo", [128, 8, D], BF16)
            Wfo = sb(phD0, "Wfo", [128, 22, D], BF16)

            def threadW():
                for kc in range(8):
                    S.dma("pool", Wo[:, kc, :], w_out[kc * 128:(kc + 1) * 128, :], [], [f"Wo{kc}"], f"w{kc % 2}")
                for hc in range(22):
                    S.dma("pool", Wfo[:, hc, :], w_ffo[hc * 128:(hc + 1) * 128, :], [], [f"Wfo{hc}"], f"w{hc % 2}")
                for kc in range(8):
                    S.op("dve" if kc % 2 == 0 else "pool", C("tensor_tensor", out=Wo[:, kc, :], in0=Wo[:, kc, :], in1=G1[:], op=ALU.mult),
                         [f"Wo{kc}", "G1"], [f"Wo{kc}"])
                for hc in range(22):
                    S.op("dve" if hc % 2 == 0 else "pool", C("tensor_tensor", out=Wfo[:, hc, :], in0=Wfo[:, hc, :], in1=G2[:], op=ALU.mult),
                         [f"Wfo{hc}", "G2"], [f"Wfo{hc}"])

            with ExitStack() as ph:
                qq = [sb(ph, f"Bqq{i}", [128, 2, 128], BF16) for i in range(2)]
                uq = [sb(ph, f"Buq{i}", [128, 512]) for i in range(2)]
                opb = [sb(ph, f"Bop{i}", [128, 512]) for i in range(2)]
                rb = [sb(ph, f"Brb{i}", [128, 512]) for i in range(2)]
                sr = sb(ph, "Bsr", [128, 512])
                ob = sb(ph, "Bo", [128, 512])
                go = [sb(ph, f"Bgo{i}", [128, 512], BF16) for i in range(2)]
                tmpSB = sb(ph, "BtmpS", [128, 2, 128])
                Sbf = [sb(ph, f"BSbf{i}", [128, 2, 128], BF16) for i in range(4)]
                s8B = [sb(ph, f"Bs8_{i}", [128, 12]) for i in range(2)]
                junkB = sb(ph, "Bjunk", [128, 128], BF16)
                po = [ps(ph, "Bpo0", [128, 512])] * 2
                KTh = [sb(ph, f"CK{i}", [128, NTOK], BF16) for i in range(2)]
                VAh = [sb(ph, f"CV{i}", [128, NK, 129], BF16) for i in range(2)]
                QTg = [sb(ph, f"CQ{i}", [128, 4, 512], BF16) for i in range(2)]
                PT = [sb(ph, f"CP{i}", [128, 2, 512], BF16) for i in range(3)]
                accs = sb(ph, "Cacc", [128, 3, 387])
                rd = sb(ph, "Crd", [128, 3, 3])
                tmpo = sb(ph, "Ctmpo", [128, 128])
                oall = sb(ph, "Coall", [128, 4, 4, 128])
                s8 = [sb(ph, f"Cs8_{i}", [128, 12]) for i in range(2)]
                junk = sb(ph, "Cjunk", [128, 128], BF16)
                ao = [sb(ph, f"Cao{i}", [128, 512], BF16) for i in range(2)]
                STp = [ps(ph, f"CST{i}", [128, 2, 512]) for i in range(2)]
                acc = ps(ph, "CACC", [128, 3, 512])

                def threadB():
                    cc = 0
                    for n, ti in enumerate(reversed(range(NOWN))):
                        b = n % 2
                        S.dma("sp", qq[b][:], qeQ_d[ti, :, :, :], [], [f"Bqq{b}"], f"Bqq{b}")
                        S.dma("sp", uq[b][:], UQ_d[ti, :, :], [], [f"Buq{b}"], f"Buq{b}")
                        S.dma("sp", opb[b][:], op_d[ti, :, :], [], [f"Bop{b}"], f"Bop{b}")
                        S.dma("sp", rb[b][:], r_d[ti, :, :], [], [f"Brb{b}"], f"Brb{b}")
                        for c in (1, 0):
                            sbf, sbk = Sbf[cc % 4], f"BSbf{cc % 4}"
                            cc += 1
                            S.op("pool", C("tensor_copy", out=sbf[:], in_=SQ[:]), ["SQ"], [sbk])
                            S.op("dve", C("tensor_tensor",
                                out=tmpSB[:], in0=SQ[:], in1=uq[b][:].rearrange("p (a c d) -> p a c d", c=2, d=128)[:, :, c, :], op=ALU.add),
                                ["SQ", f"Buq{b}"], ["BtmpS"])
                            for pr in range(2):
                                S.op("dve", C("tensor_scalar", out=SQ[:, pr, :], in0=tmpSB[:, pr, :],
                                                                                        scalar1=eQ[:, ti, pr, c:c + 1], scalar2=None, op0=ALU.mult),
                                     ["BtmpS", "eQ"], ["SQ"])
                            for h in (0, 2, 1, 3):
                                hp, pr = h % 2, h // 2
                                S.op("pe", C("matmul",
                                    po[b][c * 64:(c + 1) * 64, h * 128:(h + 1) * 128], lhsT=qq[b][hp * 64:(hp + 1) * 64, pr, c * 64:(c + 1) * 64],
                                    rhs=sbf[hp * 64:(hp + 1) * 64, pr, :], start=True, stop=True), [f"Bqq{b}", sbk], ["Bpo0"], rt=hp * 64)
                        S.op("dve", C("tensor_tensor", out=ob[:], in0=po[b][:, :], in1=opb[b][:], op=ALU.add),
                             ["Bpo0", f"Bop{b}"], ["Bo"])
                        s_, sk_ = s8B[b], f"Bs8_{b}"
                        for h in range(4):
                            S.op("act", C("activation", out=junkB[:], in_=ob[:, h * 128:(h + 1) * 128], func=AF.Square,
                                                                          accum_out=s_[:, h:h + 1]), ["Bo"], [sk_])
                        S.op("act", C("activation", out=s_[:, 4:8], in_=s_[:, 0:4], func=AF.Ln, scale=1.0 / 128, bias=cst[:, 0:1]),
                             [sk_, "cst"], [sk_])
                        S.op("act", C("activation", out=s_[:, 8:12], in_=s_[:, 4:8], func=AF.Exp, scale=-0.5), [sk_], [sk_])
                        S.op("act", C("activation", out=sr[:], in_=rb[b][:], func=AF.Exp, scale=-1.0), [f"Brb{b}"], ["Bsr"])
                        S.op("dve", C("tensor_scalar", out=sr[:], in0=sr[:], scalar1=1.0, scalar2=None, op0=ALU.add), ["Bsr"], ["Bsr"])
                        S.op("dve", C("reciprocal", out=sr[:], in_=sr[:]), ["Bsr"], ["Bsr"])
                        S.op("pool", C("tensor_tensor", out=sr[:], in0=sr[:], in1=rb[b][:], op=ALU.mult), ["Bsr", f"Brb{b}"], ["Bsr"])
                        for h in range(4):
                            S.op("dve", C("scalar_tensor_tensor",
                                out=ob[:, h * 128:(h + 1) * 128], in0=ob[:, h * 128:(h + 1) * 128], scalar=s_[:, 8 + h:9 + h], in1=gng[:],
                                op0=ALU.mult, op1=ALU.mult), ["Bo", sk_, "gng"], ["Bo"])
                        S.op("dve", C("tensor_tensor", out=go[b][:], in0=ob[:], in1=sr[:], op=ALU.mult), ["Bo", "Bsr"], [f"Bgo{b}"])
                        S.dma("pool", mix_d[ti * 128:(ti + 1) * 128, 0:512], go[b][:], [f"Bgo{b}"], [], f"Bgo{b}")

                def threadC():
                    NQG = NOWN // 4
                    it = 0
                    aoc = 0
                    for qg in range(NQG):
                        qb = qg % 2
                        S.dma("sp", QTg[qb][:], QT_d[:, :, qg * 512:(qg + 1) * 512].rearrange("h p t -> p h t"), [], [f"CQ{qb}"], f"CQ{qb}")
                        for h in range(4):
                            kb = it % 2
                            it += 1
                            S.dma("sp", KTh[kb][:], KT_d[h, :, :], [], [f"CK{kb}"], f"CK{kb}")
                            S.dma("sp", VAh[kb][:], VA_d[:, h, :, :], [], [f"CV{kb}"], f"CV{kb}")

                            def qk(kt):
                                for c in range(2):
                                    S.op("pe", C("matmul",
                                        STp[kt % 2][:, c, :], lhsT=KTh[kb][c * 64:(c + 1) * 64, kt * 128:(kt + 1) * 128],
                                        rhs=QTg[qb][c * 64:(c + 1) * 64, h, :], start=True, stop=True),
                                        [f"CK{kb}", f"CQ{qb}"], [f"CST{c}_{kt % 2}"], rt=c * 64)

                            qk(0)
                            qk(1)
                            for kt in range(NK):
                                S.op("act", C("activation", out=PT[kt % 3][:].rearrange("p c t -> p (c t)"),
                                                                          in_=STp[kt % 2][:].rearrange("p c t -> p (c t)"), func=AF.Exp),
                                     [f"CST0_{kt % 2}", f"CST1_{kt % 2}"], [f"CP{kt % 3}"])
                                if kt + 2 < NK:
                                    qk(kt + 2)
                                for c in range(2):
                                    for qt in range(4):
                                        sl = c * 4 + qt
                                        S.op("pe", C("matmul",
                                            acc[:, sl // 3, (sl % 3) * 129:(sl % 3) * 129 + 129], lhsT=PT[kt % 3][:, c, qt * 128:(qt + 1) * 128],
                                            rhs=VAh[kb][:, kt, :], start=(kt == 0 and sl % 3 == 0), stop=(kt == NK - 1), skip_group_check=True),
                                            [f"CP{kt % 3}", f"CV{kb}"], [f"CACC{sl // 3}"])
                            for bk in range(3):
                                nsl = 3 if bk < 2 else 2
                                S.op("dve", C("tensor_copy", out=accs[:, bk, 0:nsl * 129], in_=acc[:, bk, 0:nsl * 129]),
                                     [f"CACC{bk}"], [f"Cacc{bk}"])
                                S.op("dve", C("reciprocal",
                                    out=rd[:, bk, 0:nsl], in_=accs[:, bk, 0:nsl * 129].rearrange("p (s c) -> p s c", c=129)[:, :, 128]),
                                    [f"Cacc{bk}"], ["Crd"])
                            for sl in range(4, 8):
                                S.op("dve", C("tensor_scalar", out=rd[:, sl // 3, sl % 3:sl % 3 + 1], in0=rd[:, sl // 3, sl % 3:sl % 3 + 1],
                                                                             scalar1=negl[:, 0:1], scalar2=None, op0=ALU.mult), ["Crd", "negl"], ["Crd"])
                            for qt in range(4):
                                s0, s1 = qt, 4 + qt
                                S.op("dve", C("tensor_scalar", out=tmpo[:], in0=accs[:, s0 // 3, (s0 % 3) * 129:(s0 % 3) * 129 + 128],
                                                                             scalar1=rd[:, s0 // 3, s0 % 3:s0 % 3 + 1], scalar2=None, op0=ALU.mult),
                                     [f"Cacc{s0 // 3}", "Crd"], ["Ctmpo"])
                                S.op("dve", C("scalar_tensor_tensor",
                                    out=oall[:, qt, h, :], in0=accs[:, s1 // 3, (s1 % 3) * 129:(s1 % 3) * 129 + 128],
                                    scalar=rd[:, s1 // 3, s1 % 3:s1 % 3 + 1], in1=tmpo[:], op0=ALU.mult, op1=ALU.add),
                                    [f"Cacc{s1 // 3}", "Crd", "Ctmpo"], ["Coall"])
                        for qt in range(4):
                            b = aoc % 2
                            aoc += 1
                            s_, sk_ = s8[b], f"Cs8_{b}"
                            for h in range(4):
                                S.op("act", C("activation", out=junk[:], in_=oall[:, qt, h, :], func=AF.Square,
                                                                                     accum_out=s_[:, h:h + 1]), ["Coall"], [sk_])
                            S.op("act", C("activation", out=s_[:, 4:8], in_=s_[:, 0:4], func=AF.Ln, scale=1.0 / 128, bias=cst[:, 0:1]),
                                 [sk_, "cst"], [sk_])
                            S.op("act", C("activation", out=s_[:, 8:12], in_=s_[:, 4:8], func=AF.Exp, scale=-0.5), [sk_], [sk_])
                            for h in range(4):
                                S.op("dve", C("scalar_tensor_tensor",
                                    out=ao[b][:, h * 128:(h + 1) * 128], in0=oall[:, qt, h, :], scalar=s_[:, 8 + h:9 + h], in1=dng[:],
                                    op0=ALU.mult, op1=ALU.mult), ["Coall", sk_, "dng"], [f"Cao{b}"])
                            row0 = (qg * 4 + qt) * 128
                            S.dma("pool", mix_d[row0:row0 + 128, 512:1024], ao[b][:], [f"Cao{b}"], [], f"Cao{b}")

                if dbg:
                    print("phase B+C sbuf remaining", nc.sbuf_bytes_remaining)
                S.play([S.record(threadW), S.record(threadB), S.record(threadC)])
                S.barrier()
            if stop_after in ("B", "C"):
                raise _Stop()

            with ExitStack() as ph:
                GT = 2
                TD = GT * 128
                NG = NOWN // GT
                Wfi = sb(ph, "Wfi", [128, 8, 2 * FFN_H], BF16)
                identb = sb(ph, "identb", [128, 128], BF16)
                mx = sb(ph, "Dmx", [128, 1024], BF16)
                mixT = sb(ph, "DmixT", [128, 8, 128], BF16)
                xr = sb(ph, "Dxr", [128, 1024])
                x1 = sb(ph, "Dx1", [128, GT, 1024])
                xn2 = sb(ph, "Dxn", [128, 1024])
                st4 = [sb(ph, f"Dst4_{i}", [128, 4]) for i in range(2)]
                h2T = [sb(ph, f"Dh2T{i}", [128, 8, TD], BF16) for i in range(2)]
                sg = [sb(ph, f"Dsg{i}", [128, TD]) for i in range(2)]
                actT = sb(ph, "DactT", [128, 22, TD], BF16)
                ptb = ps(ph, "Dptb", [128, 1024], BF16)
                pmoP = ps(ph, "DpmoP", [128, 512])
                ptr = ps(ph, "Dptr", [128, 1024])
                pgu = [ps(ph, f"Dpgu{i}", [128, 512]) for i in range(2)]
                pmoF = ps(ph, "DpmoF", [128, 1024])
                S.excl.update(["Dptb", "DpmoP", "Dptr0", "Dptr1", "Dpgu0", "Dpgu1", "DpmoF0", "DpmoF1"])
                if dbg:
                    print("phase D sbuf remaining", nc.sbuf_bytes_remaining)
                S.op("dve", C("tensor_copy", out=identb[:], in_=ident), ["cm"], ["identb"])
                for j in range(11):
                    for part in range(2):
                        c0 = part * FFN_H + j * 256
                        S.dma("pool", Wfi[:, :, c0:c0 + 256], w_ffi[:, c0:c0 + 256].rearrange("(k p) c -> p k c", p=128), [],
                              [f"Wfi{part}_{j}"], f"w{(2 * j + part) % 2}")
                x1t = [[x1[:, 0, :], x1[:, 1, :]], [G1[:], G2[:]]]
                x1k = [["Dx1_0", "Dx1_1"], ["G1", "G2"]]
                tcount = [0]

                def prep(g):
                    hb = g % 2
                    for i in range(GT):
                        ti = g * GT + i
                        b = tcount[0] % 2
                        tcount[0] += 1
                        X1, X1k = x1t[hb][i], x1k[hb][i]
                        S.dma("sp", mx[:], mix_d[ti * 128:(ti + 1) * 128, :], [], ["Dmx"], "Dmx")
                        S.dma("sp", xr[:], xin[(T0_OWN + ti) * 128:(T0_OWN + ti + 1) * 128, :], [], ["Dxr"], "Dxr")
                        for kc in range(8):
                            S.op("pe", C("transpose", out=ptb[:, kc * 128:(kc + 1) * 128], in_=mx[:, kc * 128:(kc + 1) * 128],
                                         identity=identb[:]), ["Dmx", "identb"], ["Dptb"])
                        S.op("act", C("copy", out=mixT[:].rearrange("p k t -> p (k t)"), in_=ptb[:, :]), ["Dptb"], ["DmixT"])
                        for hf in range(2):
                            hs = slice(hf * 512, (hf + 1) * 512)
                            for kc in range(8):
                                S.op("pe", C("matmul", pmoP[:, :], lhsT=mixT[:, kc, :], rhs=Wo[:, kc, hs], start=(kc == 0), stop=(kc == 7)),
                                     ["DmixT", f"Wo{kc}"], ["DpmoP"])
                            S.op("dve", C("tensor_tensor", out=X1[:, hs], in0=pmoP[:, :], in1=xr[:, hs], op=ALU.add),
                                 ["DpmoP", "Dxr"], [X1k])
                        s4, sk4 = st4[b], f"Dst4_{b}"
                        S.op("act", C("activation", out=xn2[:], in_=X1, func=AF.Square, accum_out=s4[:, 0:1]), [X1k], ["Dxn", sk4])
                        S.op("act", C("activation", out=s4[:, 1:2], in_=s4[:, 0:1], func=AF.Ln, scale=1.0 / D, bias=cst[:, 0:1]),
                             [sk4, "cst"], [sk4])
                        S.op("act", C("activation", out=s4[:, 2:3], in_=s4[:, 1:2], func=AF.Exp, scale=-0.5), [sk4], [sk4])
                        S.op("dve", C("tensor_scalar", out=xn2[:], in0=X1, scalar1=s4[:, 2:3], scalar2=None, op0=ALU.mult),
                             [X1k, sk4], ["Dxn"])
                        for kc in range(8):
                            S.op("pe", C("transpose", out=ptr[:, kc * 128:(kc + 1) * 128], in_=xn2[:, kc * 128:(kc + 1) * 128],
                                         identity=ident), ["Dxn", "cm"], [f"Dptr{kc // 4}"])
                        for kc in range(8):
                            dst = h2T[hb][:, kc, i * 128:(i + 1) * 128]
                            src = ptr[:, kc * 128:(kc + 1) * 128]
                            if kc < 4:
                                S.op("dve", C("tensor_scalar", out=dst, in0=src, scalar1=A2[:, 0, kc:kc + 1], scalar2=B2[:, 0, kc:kc + 1],
                                              op0=ALU.mult, op1=ALU.add), [f"Dptr{kc // 4}", "A2", "B2"], [f"Dh2T{hb}_{kc}"])
                            else:
                                S.op("act", C("activation", out=dst, in_=src, func=AF.Identity, scale=A2[:, 0, kc:kc + 1],
                                              bias=B2[:, 0, kc:kc + 1]), [f"Dptr{kc // 4}", "A2", "B2"], [f"Dh2T{hb}_{kc}"])

                def ffn(g):
                    hb = g % 2
                    for hc in range(22):
                        pg_, pgk_ = pgu[hc % 2], f"Dpgu{hc % 2}"
                        for kc in range(8):
                            S.op("pe", C("matmul", pg_[:, 0:TD], lhsT=Wfi[:, kc, hc * 128:(hc + 1) * 128], rhs=h2T[hb][:, kc, :],
                                         start=(kc == 0), stop=(kc == 7)), [f"Wfi0_{hc // 2}", f"Dh2T{hb}_{kc}"], [pgk_])
                        for kc in range(8):
                            S.op("pe", C("matmul", pg_[:, TD:2 * TD], lhsT=Wfi[:, kc, FFN_H + hc * 128:FFN_H + (hc + 1) * 128],
                                         rhs=h2T[hb][:, kc, :], start=(kc == 0), stop=(kc == 7)), [f"Wfi1_{hc // 2}", f"Dh2T{hb}_{kc}"], [pgk_])
                        sgb, sgk = sg[hc % 2], f"Dsg{hc % 2}"
                        S.op("act", C("activation", out=sgb[:], in_=pg_[:, 0:TD], func=AF.Silu), [pgk_], [sgk])
                        S.op("dve", C("tensor_tensor", out=actT[:, hc, :], in0=pg_[:, TD:2 * TD], in1=sgb[:], op=ALU.mult),
                             [pgk_, sgk], [f"DactT{hc}"])
                    for i in range(GT):
                        ti = g * GT + i
                        X1, X1k = x1t[hb][i], x1k[hb][i]
                        for hf in range(2):
                            hs = slice(hf * 512, (hf + 1) * 512)
                            for hc in range(22):
                                S.op("pe", C("matmul", pmoF[:, hs], lhsT=actT[:, hc, i * 128:(i + 1) * 128], rhs=Wfo[:, hc, hs],
                                             start=(hc == 0), stop=(hc == 21)), [f"DactT{hc}", f"Wfo{hc}"], [f"DpmoF{hf}"])
                            S.op("dve", C("tensor_tensor", out=X1[:, hs], in0=pmoF[:, hs], in1=X1[:, hs], op=ALU.add),
                                 [f"DpmoF{hf}", X1k], [X1k])
                        S.dma("sp", out_d[ti * 128:(ti + 1) * 128, :], X1, [X1k], [], "Dyo")

                S.play([S.record(lambda: prep(0))])
                for g in range(NG):
                    lists = []
                    if g + 1 < NG:
                        lists.append(S.record(lambda: prep(g + 1)))
                    lists.append(S.record(lambda: ffn(g)))
                    S.play(lists)
                S.barrier()
            open_stacks.remove(phD0)
            phD0.close()
            if stop_after == "D":
                raise _Stop()
        except _Stop:
            for st_ in reversed(open_stacks):
                st_.close()
        S.finish()
    return nc


def _const_mats():
    p = np.arange(128)
    same = (p[:, None] // 64) == (p[None, :] // 64)
    triP = np.where(same & (p[:, None] <= p[None, :]), -1.0 / 16, 0.0)
    triQ = np.where(same & (p[:, None] >= p[None, :]), -1.0 / 16, 0.0)
    blk = np.where(same, 1.0 / 64, 0.0)
    i64 = np.arange(64)
    maskP = ((p[:, None] % 64) <= i64[None, :]).astype(np.float64)
    maskQ = ((p[:, None] % 64) >= i64[None, :]).astype(np.float64)
    return np.concatenate([np.eye(128), triP, triQ, blk, maskP, maskQ], axis=1).astype(np.float32)


def _rope_tables(seq, positions):
    n_freq = 16
    inv_freq = (np.float32(10000.0) ** (-np.arange(n_freq, dtype=np.float32) / np.float32(n_freq))).astype(np.float32)
    row = (positions // 64).astype(np.float32)
    col = (positions % 64).astype(np.float32)
    ang = np.stack([row[:, None] * inv_freq[None, :], col[:, None] * inv_freq[None, :]], axis=0).astype(np.float32)
    cos = np.cos(ang).astype(np.float32)
    sin = np.sin(ang).astype(np.float32)
    n = len(positions)
    cosT = np.ones((128, 256 + n), np.float32)
    sinT = np.zeros((128, 256 + n), np.float32)
    for c in range(2):
        for ax in range(2):
            for hf in range(2):
                p0 = c * 64 + ax * 32 + hf * 16
                cosT[p0:p0 + 16, 256:] = cos[ax].T
                sinT[p0:p0 + 16, 256:] = (-sin[ax].T) if hf == 0 else sin[ax].T
    return cosT, sinT


_PROG_CACHE = {}


def _prep_inputs(inputs):
    f = lambda a: np.ascontiguousarray(np.asarray(a, dtype=np.float32))
    x = f(inputs["x"]); c = f(inputs["c"]); ctx = f(inputs["ctx"]); c_ctx = f(inputs["c_ctx"])
    B, SEQ, _ = x.shape
    half = SEQ // 2
    w_mod = f(inputs["w_mod"][0]); b_mod = f(inputs["b_mod"][0]).reshape(1, -1)
    g1 = f(inputs["norm1_g"][0]); g2 = f(inputs["norm2_g"][0])
    w_in = f(inputs["w_in"][0])
    gate_up = f(inputs["gla_gate_up"][0]); gate_bias = f(inputs["gla_gate_bias"][0])
    gng = f(inputs["gla_norm_g"][0]).reshape(1, 128); dng = f(inputs["diff_norm_g"][0]).reshape(1, 128)
    gq = f(inputs["diff_q_norm_g"][0]); gk = f(inputs["diff_k_norm_g"][0])
    lq = f(inputs["diff_lambda_q"][0]).reshape(1, 128); lk = f(inputs["diff_lambda_k"][0]).reshape(1, 128)
    w_out = f(inputs["w_out"][0]); w_ffi = f(inputs["w_ffn_in"][0]); w_ffo = f(inputs["w_ffn_out"][0])
    cmat = _const_mats()
    qkg = np.stack([np.tile(gk, 2), np.tile(gq, 2)], axis=1).astype(np.float32)
    in_maps = []
    for core in range(8):
        b, hf = core // 2, core % 2
        if hf == 1:
            oth = x[b, 0:half]; own = x[b, half:SEQ]; cx = ctx[b]
            pos = np.arange(SEQ)
            zP, zQ = 0, 1
        else:
            oth = x[b, SEQ - 1:half - 1:-1]; own = x[b, half - 1::-1]; cx = ctx[b, ::-1]
            pos = SEQ - 1 - np.arange(SEQ)
            zP, zQ = 1, 0
        xin = np.ascontiguousarray(np.concatenate([cx, oth, own], axis=0))
        cosT, sinT = _rope_tables(SEQ, pos)
        vecs = np.concatenate([b_mod.reshape(48, 128), g1.reshape(8, 128), g2.reshape(8, 128),
                               c[b].reshape(8, 128), c_ctx.reshape(8, 128)], axis=0).astype(np.float32)
        w_in_c = w_in.copy()
        w_in_c[:, C_GD:C_GD + 16] = w_in[:, C_GD + 16 * zP:C_GD + 16 * zP + 16]
        w_in_c[:, C_GD + 16:C_GD + 32] = w_in[:, C_GD + 16 * zQ:C_GD + 16 * zQ + 16]
        gu = np.zeros((49, 256), np.float32)
        gu[0:16] = gate_up[zP]; gu[16] = gate_bias[zP]
        gu[32:48] = gate_up[zQ]; gu[48] = gate_bias[zQ]
        in_maps.append({
            "xin": xin, "vecs": vecs, "w_mod": w_mod, "b_mod": b_mod, "w_in": w_in_c, "gu_in": gu, "gng_in": gng, "dng_in": dng,
            "qkg_in": qkg, "lq": lq, "lk": lk, "w_out": w_out, "w_ffi": w_ffi, "w_ffo": w_ffo,
            "cosT": cosT, "sinT": sinT, "cmat": cmat,
        })
    return in_maps, B, SEQ


def kernel(**inputs):
    in_maps, B, SEQ = _prep_inputs(inputs)
    half = SEQ // 2
    nt = half // 128
    key = (nt,)
    if key not in _PROG_CACHE:
        _PROG_CACHE[key] = build_program(nt, nt)
    nc = _PROG_CACHE[key]
    res = run_bass_kernel_spmd(nc, in_maps, core_ids=list(range(8)))
    out = np.empty((B, SEQ, D), np.float32)
    for core in range(8):
        b, hf = core // 2, core % 2
        o = np.asarray(res.results[core]["out"], dtype=np.float32)
        if hf == 1:
            out[b, half:SEQ] = o
        else:
            out[b, 0:half] = o[::-1]
    return out
```

```python
import math
from contextlib import ExitStack

import numpy as np
import ml_dtypes

import concourse.bass as bass
import concourse.mybir as mybir
from concourse.bass_utils import run_bass_kernel_spmd

F32 = mybir.dt.float32
BF16 = mybir.dt.bfloat16
AF = mybir.ActivationFunctionType
ALU = mybir.AluOpType
AX = mybir.AxisListType

D = 1024
EPS = 1e-6
NCTX = 2
FFN_H = 2816
W_IN_COLS = 3104
C_GQ, C_GK, C_GV, C_GR, C_GD, C_DQ, C_DK, C_DV = 0, 256, 512, 1024, 1536, 1568, 2080, 2592
LAM_INIT = 0.8 - 0.6 * math.exp(-0.3 * 0)


def C(name, *a, **kw):
    f = lambda e: getattr(e, name)(*a, **kw)
    f.opname, f.a, f.kw = name, a, kw
    return f


def _free_elems(ap):
    n = 1
    for d in list(ap.shape)[1:]:
        n *= int(d)
    return n


def _op_cost(e, fn):
    name = getattr(fn, "opname", None)
    if name is None:
        return 300.0
    out = fn.kw.get("out", fn.a[0] if fn.a else None)
    n = _free_elems(out) if out is not None else 128
    if e == "pe":
        if name == "transpose":
            return 220.0
        lhsT = fn.kw.get("lhsT")
        mult = 3.5 if (lhsT is not None and lhsT.dtype == F32) else 1.0
        return (max(n, 64) / 1.6 + 40.0) * mult
    if e == "act":
        return n / 0.96 + 220.0
    if e == "dve":
        return n / 0.96 * (8.0 if name == "reciprocal" else 1.0) + 80.0
    return n * 1.7 + 150.0


class _Stop(Exception):
    pass


class Sched:
    def __init__(self, nc, stack):
        self.nc = nc
        self.stack = stack
        self.eng = {"pe": nc.tensor, "act": nc.scalar, "dve": nc.vector, "pool": nc.gpsimd, "sp": nc.sync}
        self.sems = {}
        self.cnt = {}
        for e in ("pe", "act", "dve", "pool"):
            self.sems["E" + e] = stack.enter_context(nc.semaphore("s_" + e))
            self.cnt[e] = 0
        self.waited = {e: {} for e in self.eng}
        self.lastw = {}
        self.readers = {}
        self.slots = {}
        self.pending = {e: [] for e in self.eng}
        self.n_inst = 0
        self.excl = set()
        self.lastacc = {}
        self.pe_last = None
        self.rec = None
        self.sim_e, self.sim_w, self.sim_r, self.sim_a = {}, {}, {}, {}

    def record(self, fn):
        assert self.rec is None
        self.rec = []
        fn()
        lst, self.rec = self.rec, None
        return lst

    def _sim_ready(self, e, reads, writes):
        t = 0.0
        for k in reads:
            t = max(t, self.sim_w.get(k, 0.0))
            if k in self.excl:
                t = max(t, self.sim_a.get(k, 0.0))
        for k in writes:
            t = max(t, self.sim_w.get(k, 0.0), self.sim_r.get(k, 0.0))
        return t

    def _sim_commit(self, e, kind, fn_or_bytes, reads, writes):
        ready = self._sim_ready(e, reads, writes)
        start = max(ready + 120.0, self.sim_e.get(e, 0.0))
        if kind == "op":
            fin = start + _op_cost(e, fn_or_bytes)
            self.sim_e[e] = fin
        else:
            self.sim_e[e] = start + 80.0
            fin = start + 2000.0 + fn_or_bytes / 120.0
        for k in writes:
            self.sim_w[k] = fin
            self.sim_r[k] = 0.0
        for k in reads:
            self.sim_r[k] = max(self.sim_r.get(k, 0.0), fin)
        for k in list(reads) + list(writes):
            if k in self.excl:
                self.sim_a[k] = fin

    def play(self, lists, greedy=False):
        cur = [0] * len(lists)
        while True:
            best = None
            for li, lst in enumerate(lists):
                if cur[li] >= len(lst):
                    continue
                kind, args, kw = lst[cur[li]]
                if greedy:
                    e = args[0]
                    reads, writes = (args[2], args[3]) if kind == "op" else (args[3], args[4])
                    st = max(self._sim_ready(e, reads, writes) + 120.0, self.sim_e.get(e, 0.0))
                    key = (st, cur[li] / len(lst), li)
                else:
                    key = ((cur[li] + 0.5) / len(lst), li)
                if best is None or key < best[0]:
                    best = (key, li)
            if best is None:
                break
            li = best[1]
            kind, args, kw = lists[li][cur[li]]
            cur[li] += 1
            if kind == "op":
                self.op(*args, **kw)
            else:
                self.dma(*args, **kw)

    def _wait(self, e, tok):
        sk, val, _ = tok
        if self.waited[e].get(sk, 0) >= val:
            return
        self.eng[e].wait_ge(self.sems[sk], val)
        self.waited[e][sk] = val

    def _deps(self, e, reads, writes):
        deps = []
        for k in reads:
            t = self.lastw.get(k)
            if t is not None:
                deps.append((t, "raw"))
            if k in self.excl:
                t = self.lastacc.get(k)
                if t is not None and t[2] != e:
                    deps.append((t, "raw"))
        for k in writes:
            t = self.lastw.get(k)
            if t is not None:
                deps.append((t, "waw"))
            for r in self.readers.get(k, ()):
                deps.append((r, "war"))
        for t in self.pending[e]:
            deps.append((t, "raw"))
        self.pending[e] = []
        need = {}
        for t, kind in deps:
            if t[2] == e and e == "pe":
                continue
            if self.waited[e].get(t[0], 0) >= t[1]:
                continue
            if t[0] not in need or need[t[0]][1] < t[1]:
                need[t[0]] = t
        return list(need.values())

    def _emit_waits(self, e, waits, keep_last=False):
        held = None
        if keep_last and waits:
            held = waits[-1]
            waits = waits[:-1]
        for t in waits:
            self._wait(e, t)
        return held

    def _record(self, tok, reads, writes):
        for k in writes:
            self.lastw[k] = tok
            self.readers[k] = []
        for k in reads:
            lst = self.readers.setdefault(k, [])
            lst[:] = [r for r in lst if r[0] != tok[0]]
            lst.append(tok)
        for k in list(reads) + list(writes):
            if k in self.excl:
                self.lastacc[k] = tok

    def op(self, e, fn, reads=(), writes=(), rt=0):
        if self.rec is not None:
            self.rec.append(("op", (e, fn, tuple(reads), tuple(writes)), {"rt": rt}))
            return None
        waits = self._deps(e, reads, writes)
        self._sim_commit(e, "op", fn, reads, writes)
        attach = (e in ("act", "dve", "pool") and getattr(fn, "opname", None) is not None
                  and "accum_out" not in fn.kw and fn.opname not in ("stream_shuffle",))
        held = self._emit_waits(e, waits, keep_last=attach)
        if e == "pe":
            pl = self.pe_last
            if pl is not None and pl[1] != rt and any(k in pl[2] for k in writes):
                self._wait(e, pl[0])
        inst = fn(self.eng[e])
        if held is not None:
            inst._wait_ge(self.sems[held[0]], held[1])
            self.waited[e][held[0]] = held[1]
        self.cnt[e] += 1
        inst.then_inc(self.sems["E" + e], 1)
        tok = ("E" + e, self.cnt[e], e)
        self._record(tok, reads, writes)
        if e == "pe":
            self.pe_last = (tok, rt, set(writes))
        self.n_inst += 1
        return tok

    def dma(self, q, out, in_, reads, writes, slot, **kw):
        if self.rec is not None:
            self.rec.append(("dma", (q, out, in_, tuple(reads), tuple(writes), slot), kw))
            return None
        sk = "D" + slot
        if sk not in self.sems:
            self.sems[sk] = self.stack.enter_context(self.nc.semaphore("d_" + slot))
            self.slots[sk] = 0
        if self.slots[sk] > 0:
            self._wait(q, (sk, self.slots[sk], None))
        self._emit_waits(q, self._deps(q, reads, writes))
        nbytes = 128 * _free_elems(out) * (2 if out.dtype == BF16 else 4)
        self._sim_commit(q, "dma", nbytes, reads, writes)
        inst = self.eng[q].dma_start(out=out, in_=in_, **kw)
        self.slots[sk] += 16
        inst.then_inc(self.sems[sk], 16)
        tok = (sk, self.slots[sk], None)
        self._record(tok, reads, writes)
        self.n_inst += 1
        return tok

    def _all_toks(self):
        toks = [("E" + e, c, e) for e, c in self.cnt.items() if c > 0]
        toks += [(sk, v, None) for sk, v in self.slots.items() if v > 0]
        return toks

    def barrier(self):
        toks = self._all_toks()
        for e in self.eng:
            self.pending[e] = [t for t in toks if t[2] != e]

    def finish(self):
        for t in self._all_toks():
            self._wait("sp", t)


def build_program(NOTH, NOWN, dbg=False, stop_after=None):
    assert NOTH % 4 == 0 and NOWN % 4 == 0
    NK = NCTX + NOTH + NOWN
    NTOK = NK * 128
    T0_OTH = NCTX
    T0_OWN = NCTX + NOTH
    NQ = NOWN * 128

    nc = bass.Bass("TRN2", target_bir_lowering=False)

    def din(name, shape, dt=F32):
        return nc.dram_tensor(name, list(shape), dt, kind="ExternalInput").ap()

    def dscr(name, shape, dt):
        return nc.dram_tensor(name, list(shape), dt, kind=("ExternalOutput" if dbg else "Internal")).ap()

    xin = din("xin", [NTOK, D])
    vecs = din("vecs", [80, 128])
    w_mod = din("w_mod", [D, 6 * D])
    b_mod = din("b_mod", [1, 6 * D])
    w_in = din("w_in", [D, W_IN_COLS])
    gu_in = din("gu_in", [49, 256])
    gng_in = din("gng_in", [1, 128])
    dng_in = din("dng_in", [1, 128])
    qkg_in = din("qkg_in", [128, 2])
    lq_in = din("lq", [1, 128])
    lk_in = din("lk", [1, 128])
    w_out = din("w_out", [D, D])
    w_ffi = din("w_ffi", [D, 2 * FFN_H])
    w_ffo = din("w_ffo", [FFN_H, D])
    cos_in = din("cosT", [128, NTOK])
    sin_in = din("sinT", [128, NTOK])
    cmat = din("cmat", [128, 4 * 128 + 2 * 64])
    out_d = nc.dram_tensor("out", [NQ, D], F32, kind="ExternalOutput").ap()

    KT_d = dscr("KT_d", [4, 128, NTOK], BF16)
    VA_d = dscr("VA_d", [128, 4, NK, 129], BF16)
    QT_d = dscr("QT_d", [4, 128, NQ], BF16)
    qeQ_d = dscr("qeQ_d", [NOWN, 128, 2, 128], BF16)
    UQ_d = dscr("UQ_d", [NOWN, 128, 512], F32)
    op_d = dscr("op_d", [NOWN, 128, 512], F32)
    r_d = dscr("r_d", [NOWN, 128, 512], F32)
    mix_d = dscr("mix_d", [NQ, D], BF16)

    dbg_out = {}
    if dbg:
        for nm, shp in (("dbg_mod", [128, 96]), ("dbg_G", [128, 2048]),
                        ("dbg_SP", [128, 256]), ("dbg_SQ", [128, 256])):
            dbg_out[nm] = nc.dram_tensor(nm, shp, F32, kind="ExternalOutput").ap()

    top = ExitStack()
    with top:
        S = Sched(nc, top)
        S.excl.update(["p0T", "p0mod", "p0G0", "p0G1", "ptr0", "ptr1", "b2", "b3", "b4", "b5", "b6", "b7", "Bpo0", "Bpo1",
                       "CST0_0", "CST0_1", "CST1_0", "CST1_1", "CACC0", "CACC1", "CACC2",
                       "Dptb", "Dpmo0", "Dpmo1", "Dptr0", "Dptr1", "Dpgu0", "Dpgu1", "Dpgu2"])

        def sb(stack, name, shape, dt=F32):
            return stack.enter_context(nc.sbuf_tensor(name, list(shape), dt))

        def ps(stack, name, shape, dt=F32):
            return stack.enter_context(nc.psum_tensor(name, list(shape), dt))

        cm = sb(top, "cm", [128, 640])
        ident = cm[:, 0:128]
        triP = cm[:, 128:256]
        triQ = cm[:, 256:384]
        blk64 = cm[:, 384:512]
        maskP = cm[:, 512:576]
        maskQ = cm[:, 576:640]
        cst = sb(top, "cst", [128, 4])
        colv = sb(top, "colv", [128, 80])
        modT = sb(top, "modT", [128, 48, 2])
        A1 = sb(top, "A1", [128, 2, 8])
        B1 = sb(top, "B1", [128, 2, 8])
        A2 = sb(top, "A2", [128, 2, 8])
        B2 = sb(top, "B2", [128, 2, 8])
        G1 = sb(top, "G1", [128, 1024])
        G2 = sb(top, "G2", [128, 1024])
        gng = sb(top, "gng", [128, 128])
        dng = sb(top, "dng", [128, 128])
        qkg = sb(top, "qkg", [128, 2])
        negl = sb(top, "negl", [128, 1])
        gu = sb(top, "gu", [49, 256])
        eQ = sb(top, "eQ", [128, NOWN, 2, 2])
        SQ = sb(top, "SQ", [128, 2, 128])

        S.dma("sp", cm[:], cmat[:, :], [], ["cm"], "c0")
        S.op("dve", C("memset", cst[:, 0:1], EPS), [], ["cst"])
        S.op("dve", C("memset", cst[:, 1:2], 1.0), [], ["cst"])
        S.dma("sp", gu[:], gu_in[:, :], [], ["gu"], "c1")
        S.dma("sp", qkg[:], qkg_in[:, :], [], ["qkg"], "c2")
        S.dma("sp", gng[:], gng_in[0:1, :].to_broadcast([128, 128]), [], ["gng"], "c3")
        S.dma("sp", dng[:], dng_in[0:1, :].to_broadcast([128, 128]), [], ["dng"], "c4")
        S.op("dve", C("tensor_scalar", out=qkg[:, 1:2], in0=qkg[:, 1:2], scalar1=0.125, scalar2=None, op0=ALU.mult),
             ["qkg"], ["qkg"])
        S.op("dve", C("tensor_scalar", out=dng[:], in0=dng[:], scalar1=1.0 - LAM_INIT, scalar2=None, op0=ALU.mult),
             ["dng"], ["dng"])

        open_stacks = []
        try:
            phA = ExitStack()
            open_stacks.append(phA)
            Win = sb(phA, "Win", [128, 8, W_IN_COLS], BF16)
            for kc in range(8):
                S.dma("pool", Win[:, kc, :], w_in[kc * 128:(kc + 1) * 128, :], [], [f"Win{kc}"], f"w{kc % 2}")
            with ExitStack() as ph:
                stage = sb(ph, "stage", [80, 128])
                siluT = sb(ph, "siluT", [128, 8, 2])
                silubc = sb(ph, "silubc", [128, 8, 128])
                ones1 = sb(ph, "ones1", [1, 128])
                bmr = sb(ph, "bmr", [1, 2048])
                lqb = sb(ph, "lqb", [128, 128])
                lkb = sb(ph, "lkb", [128, 128])
                s2 = sb(ph, "s2", [128, 2])
                wm = [sb(ph, f"wm{i}", [128, 8, 512]) for i in range(2)]
                pT = ps(ph, "p0T", [128, 512])
                pmod = ps(ph, "p0mod", [128, 512])
                pG = [ps(ph, f"p0G{i}", [128, 512]) for i in range(2)]

                S.dma("sp", stage[:], vecs[:, :], [], ["stage"], "c0")
                S.dma("sp", bmr[:, 0:1024], b_mod[0:1, 2048:3072], [], ["bmr"], "c1")
                S.dma("sp", bmr[:, 1024:2048], b_mod[0:1, 5120:6144], [], ["bmr"], "c1")
                S.dma("sp", lqb[:], lq_in[0:1, :].to_broadcast([128, 128]), [], ["lqb"], "c2")
                S.dma("sp", lkb[:], lk_in[0:1, :].to_broadcast([128, 128]), [], ["lkb"], "c3")
                S.op("dve", C("memset", ones1[:], 1.0), [], ["ones1"])
                S.op("dve", C("tensor_tensor", out=lqb[:], in0=lqb[:], in1=lkb[:], op=ALU.mult), ["lqb", "lkb"], ["lqb"])
                S.op("dve", C("tensor_reduce", out=s2[:], in_=lqb[:].rearrange("p (a b) -> p a b", b=64), axis=AX.X, op=ALU.add),
                     ["lqb"], ["s2"])
                S.op("act", C("activation", out=s2[:], in_=s2[:], func=AF.Exp), ["s2"], ["s2"])
                S.op("dve", C("tensor_tensor", out=negl[:], in0=s2[:, 1:2], in1=s2[:, 0:1], op=ALU.subtract), ["s2"], ["negl"])
                S.op("dve", C("tensor_scalar", out=negl[:], in0=negl[:], scalar1=-LAM_INIT, scalar2=None, op0=ALU.add),
                     ["negl"], ["negl"])
                S.op("pe", C("transpose", out=pT[:, 0:80], in_=stage[:], identity=ident[0:80, 0:80]), ["stage", "cm"], ["p0T"])
                S.op("dve", C("tensor_copy", out=colv[:], in_=pT[:, 0:80]), ["p0T"], ["colv"])
                S.op("act", C("activation", out=siluT[:].rearrange("p k r -> p r k"),
                                                   in_=colv[:, 64:80].rearrange("p (r k) -> p r k", k=8), func=AF.Silu),
                     ["colv"], ["siluT"])
                for kc in range(8):
                    S.op("dve", C("tensor_scalar", out=silubc[:, kc, :], in0=ident[:, :], scalar1=0.0,
                                                                 scalar2=siluT[:, kc, 0:1], op0=ALU.mult, op1=ALU.add),
                         ["siluT", "cm"], ["silubc"])
                gi = 0
                for cg in range(12):
                    w = wm[cg % 2]
                    wk = f"wm{cg % 2}"
                    S.dma("sp", w[:], w_mod[:, cg * 512:(cg + 1) * 512].rearrange("(k p) c -> p k c", p=128), [], [wk], wk)
                    for jj in range(4):
                        j = cg * 4 + jj
                        for kc in range(8):
                            S.op("pe", C("matmul",
                                pmod[:, 2 * j:2 * j + 2], lhsT=w[:, kc, jj * 128:(jj + 1) * 128], rhs=siluT[:, kc, :],
                                start=(kc == 0), stop=(kc == 7)), [wk, "siluT"], ["p0mod"])
                    if cg in (4, 5, 10, 11):
                        pg = pG[gi % 2]
                        pk_ = f"p0G{gi % 2}"
                        Gt = G1 if cg < 6 else G2
                        gcol = (cg % 2) * 512
                        boff = (0 if cg < 6 else 1024) + gcol
                        for kc in range(8):
                            S.op("pe", C("matmul", pg[:, :], lhsT=silubc[:, kc, :], rhs=w[:, kc, :],
                                                                              start=(kc == 0), stop=False),
                                 [wk, "silubc"], [pk_])
                        S.op("pe", C("matmul", pg[:, :], lhsT=ones1[:, :], rhs=bmr[:, boff:boff + 512],
                                                                        start=False, stop=True), ["ones1", "bmr"], [pk_])
                        S.op("act", C("copy", out=Gt[:, gcol:gcol + 512], in_=pg[:, :]),
                             [pk_], ["G1" if cg < 6 else "G2"])
                        gi += 1
                S.op("dve", C("tensor_tensor", out=modT[:], in0=pmod[:, 0:96].rearrange("p (j r) -> p j r", r=2),
                                                      in1=colv[:, 0:48].unsqueeze(2).to_broadcast([128, 48, 2]), op=ALU.add),
                     ["p0mod", "colv"], ["modT"])
                for r in range(2):
                    S.op("dve", C("scalar_tensor_tensor", out=A1[:, r, :], in0=modT[:, 8:16, r], scalar=1.0,
                                                                      in1=colv[:, 48:56], op0=ALU.add, op1=ALU.mult),
                         ["modT", "colv"], ["A1"])
                    S.op("dve", C("tensor_copy", out=B1[:, r, :], in_=modT[:, 0:8, r]), ["modT"], ["B1"])
                    S.op("dve", C("scalar_tensor_tensor", out=A2[:, r, :], in0=modT[:, 32:40, r], scalar=1.0,
                                                                      in1=colv[:, 56:64], op0=ALU.add, op1=ALU.mult),
                         ["modT", "colv"], ["A2"])
                    S.op("dve", C("tensor_copy", out=B2[:, r, :], in_=modT[:, 24:32, r]), ["modT"], ["B2"])
                if dbg:
                    S.dma("pool", dbg_out["dbg_mod"][:, :], modT[:].rearrange("p j r -> p (j r)"), ["modT"], [], "dbg")
                    S.dma("pool", dbg_out["dbg_G"][:, 0:1024], G1[:], ["G1"], [], "dbg")
                    S.dma("pool", dbg_out["dbg_G"][:, 1024:2048], G2[:], ["G2"], [], "dbg")
                S.barrier()
            if stop_after == "0":
                raise _Stop()

            with ExitStack() as ph:
                xt = [sb(ph, f"xt{i}", [128, 1024]) for i in range(2)]
                junk = sb(ph, "junk", [128, 1024], BF16)
                st4 = [sb(ph, f"st4_{i}", [128, 4]) for i in range(2)]
                xn = [sb(ph, f"xn{i}", [128, 1024]) for i in range(2)]
                hT = [sb(ph, f"hT{i}", [128, 8, 512], BF16) for i in range(2)]
                cosb = [sb(ph, f"cosb{i}", [128, 512]) for i in range(1)] * 2
                sinb = [sb(ph, f"sinb{i}", [128, 512]) for i in range(1)] * 2
                sq = sb(ph, "sq", [128, 512], BF16)
                blkb = sb(ph, "blkb", [128, 128], BF16)
                lnb = sb(ph, "lnb", [128, 512])
                rsb = sb(ph, "rsb", [128, 512])
                kn = sb(ph, "kn", [128, 512])
                kr = sb(ph, "kr", [128, 512])
                t1 = sb(ph, "t1", [128, 512])
                kst = [sb(ph, f"kst{i}", [128, 4, 512], BF16) for i in range(2)]
                qst = [sb(ph, f"qst{i}", [128, 4, 512], BF16) for i in range(1)] * 2
                vst = [sb(ph, f"vst{i}", [128, 4, 4, 129], BF16) for i in range(2)]
                dA = sb(ph, "dA", [49, 512])
                ex = sb(ph, "ex", [128, 512])
                spb = [sb(ph, f"spb{i}", [128, 2, 256]) for i in range(2)]
                en = sb(ph, "en", [128, 256])
                ke = [[sb(ph, f"ke{z}_{i}", [128, 256], BF16) for i in range(2)] for z in range(2)]
                vbf = [sb(ph, f"vbf{i}", [128, 512], BF16) for i in range(4)]
                ez = [sb(ph, f"ez{z}", [128, 4, 2, 2]) for z in range(2)]
                E1 = [sb(ph, f"E1_{z}", [128, 2, 512]) for z in range(2)]
                E2 = [sb(ph, f"E2_{z}", [128, 2, 512]) for z in range(2)]
                qeT = [sb(ph, f"qeT{z}", [128, 2, 512], BF16) for z in range(2)]
                keT = [sb(ph, f"keT{z}", [128, 2, 512], BF16) for z in range(2)]
                UP = sb(ph, "UPs", [128, 1, 512])
                UQs = [sb(ph, f"UQs{i}", [128, 512]) for i in range(2)]
                UQc = sb(ph, "UQc", [128, 2, 512])
                SP = sb(ph, "SP", [128, 2, 128])
                tmpS = sb(ph, "tmpS", [128, 2, 128])
                Sbf = sb(ph, "Sbf", [128, 8, 2, 128], BF16)
                am = [sb(ph, f"am{z}", [128, 4, 64], BF16) for z in range(2)]
                osb = [sb(ph, f"osb{i}", [128, 512]) for i in range(2)]
                rsbuf = [sb(ph, f"rsbuf{i}", [128, 512]) for i in range(2)]
                ptr = ps(ph, "ptr", [128, 1024])
                b2 = ps(ph, "b2", [128, 512])
                b3 = ps(ph, "b3", [128, 512])
                b4 = ps(ph, "b4", [128, 512])
                b5 = ps(ph, "b5", [128, 512])
                b6 = ps(ph, "b6", [128, 512])
                b7 = ps(ph, "b7", [128, 512])

                WinK = [f"Win{kc}" for kc in range(8)]
                S.op("dve", C("tensor_copy", out=blkb[:], in_=blk64), ["cm"], ["blkb"])
                S.op("dve", C("memset", dA[:], 1.0), [], ["dA"])
                for i in range(2):
                    S.op("dve", C("memset", vst[i][:].rearrange("p a b c -> p (a b c)"), 1.0), [], [f"vst{i}"])
                S.op("dve", C("memset", SP[:].rearrange("p a b -> p (a b)"), 0.0), [], ["SP"])
                S.op("dve", C("memset", SQ[:].rearrange("p a b -> p (a b)"), 0.0), [], ["SQ"])

                groups = [(0, NCTX, "ctx")]
                for g in range(NOTH // 4):
                    groups.append((T0_OTH + 4 * g, 4, "oth"))
                for g in range(NOWN // 4):
                    groups.append((T0_OWN + 4 * g, 4, "own"))

                xc = [0]
                if stop_after == "A0":
                    groups = []
                def stage1(gidx):
                    t0, nt, kind = groups[gidx]
                    T = nt * 128
                    koff = t0 * 128
                    r_mod = 1 if kind == "ctx" else 0
                    own = kind == "own"
                    ctx = kind == "ctx"
                    gb = gidx % 2
                    h_T = hT[gb]
                    hK = f"hT{gb}"
                    for i in range(nt):
                        xb = xc[0] % 2
                        nb = xc[0] % 2
                        xc[0] += 1
                        x_t, xk = xt[xb], f"xt{xb}"
                        s4, sk4 = st4[xb], f"st4_{xb}"
                        x_n, nk = xn[nb], f"xn{nb}"
                        S.dma("sp", x_t[:], xin[(t0 + i) * 128:(t0 + i + 1) * 128, :], [], [xk], xk)
                        S.op("act", C("activation", out=junk[:], in_=x_t[:], func=AF.Square, accum_out=s4[:, 0:1]),
                             [xk], [sk4])
                        S.op("act", C("activation", out=s4[:, 1:2], in_=s4[:, 0:1], func=AF.Ln, scale=1.0 / D, bias=cst[:, 0:1]),
                             [sk4, "cst"], [sk4])
                        S.op("act", C("activation", out=s4[:, 2:3], in_=s4[:, 1:2], func=AF.Exp, scale=-0.5), [sk4], [sk4])
                        S.op("dve", C("tensor_scalar", out=x_n[:], in0=x_t[:], scalar1=s4[:, 2:3], scalar2=None,
                                                                                      op0=ALU.mult), [xk, sk4], [nk])
                        for kc in range(8):
                            S.op("pe", C("transpose", out=ptr[:, kc * 128:(kc + 1) * 128], in_=x_n[:, kc * 128:(kc + 1) * 128],
                                                                           identity=ident), [nk, "cm"], [f"ptr{kc // 4}"])
                        for kc in range(8):
                            dst = h_T[:, kc, i * 128:(i + 1) * 128]
                            src = ptr[:, kc * 128:(kc + 1) * 128]
                            if kc < 4:
                                S.op("dve", C("tensor_scalar",
                                    out=dst, in0=src, scalar1=A1[:, r_mod, kc:kc + 1], scalar2=B1[:, r_mod, kc:kc + 1],
                                    op0=ALU.mult, op1=ALU.add), [f"ptr{kc // 4}", "A1", "B1"], [f"{hK}_{kc}"])
                            else:
                                S.op("act", C("activation",
                                    out=dst, in_=src, func=AF.Identity, scale=A1[:, r_mod, kc:kc + 1], bias=B1[:, r_mod, kc:kc + 1]),
                                    [f"ptr{kc // 4}", "A1", "B1"], [f"{hK}_{kc}"])

                def stageY(gidx):
                    t0, nt, kind = groups[gidx]
                    T = nt * 128
                    koff = t0 * 128
                    r_mod = 1 if kind == "ctx" else 0
                    own = kind == "own"
                    ctx = kind == "ctx"
                    gb = gidx % 2
                    h_T = hT[gb]
                    hK = f"hT{gb}"
                    S.dma("sp", cosb[gb][:, 0:T], cos_in[:, koff:koff + T], [], ["cos0"], "cos0")
                    S.dma("sp", sinb[gb][:, 0:T], sin_in[:, koff:koff + T], [], ["sin0"], "sin0")

                    def qk_proj(col0, gcol, dst, dkey):
                        for h in range(4):
                            for kc in range(8):
                                S.op("pe", C("matmul", b2[:, 0:T], lhsT=Win[:, kc, col0 + h * 128:col0 + (h + 1) * 128],
                                                                          rhs=h_T[:, kc, 0:T], start=(kc == 0), stop=(kc == 7)),
                                     [WinK[kc], f"{hK}_{kc}"], ["b2"])
                            S.op("act", C("activation", out=sq[:, 0:T], in_=b2[:, 0:T], func=AF.Square), ["b2"], ["sq"])
                            S.op("pe", C("matmul", b3[:, 0:T], lhsT=blkb[:], rhs=sq[:, 0:T], start=True, stop=True), ["sq", "blkb"], ["b3"])
                            S.op("act", C("activation", out=lnb[:, 0:T], in_=b3[:, 0:T], func=AF.Ln, bias=cst[:, 0:1]), ["b3", "cst"], ["lnb"])
                            S.op("act", C("activation", out=rsb[:, 0:T], in_=lnb[:, 0:T], func=AF.Exp, scale=-0.5), ["lnb"], ["rsb"])
                            S.op("dve", C("scalar_tensor_tensor", out=kn[:, 0:T], in0=b2[:, 0:T], scalar=qkg[:, gcol:gcol + 1],
                                                                         in1=rsb[:, 0:T], op0=ALU.mult, op1=ALU.mult),
                                 ["b2", "rsb", "qkg"], ["kn"])
                            S.op("dve", C("stream_shuffle", out=kr[:, 0:T], in_=kn[:, 0:T], mask=[(i + 16) % 32 for i in range(32)]),
                                 ["kn"], ["kr"])
                            S.op("pool", C("tensor_tensor", out=t1[:, 0:T], in0=kn[:, 0:T], in1=cosb[gb][:, 0:T], op=ALU.mult),
                                 ["kn", "cos0"], ["t1"])
                            S.op("pool", C("tensor_tensor", out=kr[:, 0:T], in0=kr[:, 0:T], in1=sinb[gb][:, 0:T], op=ALU.mult),
                                 ["kr", "sin0"], ["kr"])
                            S.op("dve", C("tensor_tensor", out=dst[:, h, 0:T], in0=t1[:, 0:T], in1=kr[:, 0:T], op=ALU.add),
                                 ["t1", "kr"], [dkey])

                    qk_proj(C_DK, 0, kst[gb], f"kst{gb}")
                    S.dma("pool", KT_d[:, :, koff:koff + T].rearrange("h p t -> p h t"), kst[gb][:, :, 0:T], [f"kst{gb}"], [], f"ko{gb}")
                    if own:
                        qoff = (t0 - T0_OWN) * 128
                        qk_proj(C_DQ, 1, qst[gb], "qst0")
                        S.dma("pool", QT_d[:, :, qoff:qoff + T].rearrange("h p t -> p h t"), qst[gb][:, :, 0:T], ["qst0"], [], "qo0")
                    for i in range(nt):
                        for kc in range(8):
                            S.op("pe", C("matmul", b3[:, :], lhsT=h_T[:, kc, i * 128:(i + 1) * 128], rhs=Win[:, kc, C_DV:C_DV + 512],
                                                                      start=(kc == 0), stop=(kc == 7)), [WinK[kc], f"{hK}_{kc}"], ["b3"])
                        S.op("act", C("copy", out=vst[gb][:, :, i, 0:128], in_=b3[:, :].rearrange("p (h c) -> p h c", c=128)),
                             ["b3"], [f"vst{gb}"])
                    S.dma("pool", VA_d[:, :, t0:t0 + nt, :], vst[gb][:, :, 0:nt, :], [f"vst{gb}"], [], f"vo{gb}")


                def stageZ(gidx):
                    t0, nt, kind = groups[gidx]
                    T = nt * 128
                    koff = t0 * 128
                    r_mod = 1 if kind == "ctx" else 0
                    own = kind == "own"
                    ctx = kind == "ctx"
                    gb = gidx % 2
                    h_T = hT[gb]
                    hK = f"hT{gb}"
                    zs = (0, 1) if (own or ctx) else (0,)
                    for z in zs:
                        for kc in range(8):
                            S.op("pe", C("matmul", b7[32 * z:32 * z + 16, 0:T], lhsT=Win[:, kc, C_GD + 16 * z:C_GD + 16 * z + 16],
                                                                      rhs=h_T[:, kc, 0:T], start=(kc == 0), stop=(kc == 7)),
                                 [WinK[kc], f"{hK}_{kc}"], ["b7"])
                        S.op("act", C("copy", out=dA[32 * z:32 * z + 16, 0:T], in_=b7[32 * z:32 * z + 16, 0:T]), ["b7"], ["dA"])
                    for i in range(nt):
                        tl = slice(i * 128, (i + 1) * 128)
                        sp_t, spk = spb[i % 2], f"spb{i % 2}"
                        v_b, vk = vbf[i], f"vbf{i}"
                        for kc in range(8):
                            S.op("pe", C("matmul", b4[:, 0:256], lhsT=h_T[:, kc, tl], rhs=Win[:, kc, C_GK:C_GK + 256],
                                                                 start=(kc == 0), stop=(kc == 7)), [WinK[kc], f"{hK}_{kc}"], ["b4"])
                        for kc in range(8):
                            S.op("pe", C("matmul", b5[:, :], lhsT=h_T[:, kc, tl], rhs=Win[:, kc, C_GV:C_GV + 512],
                                                                 start=(kc == 0), stop=(kc == 7)), [WinK[kc], f"{hK}_{kc}"], ["b5"])
                        S.op("act", C("copy", out=v_b[:], in_=b5[:, :]), ["b5"], [vk])
                        for z in zs:
                            S.op("pe", C("matmul", b6[:, 256 * z:256 * z + 256], lhsT=dA[32 * z:32 * z + 17, tl],
                                                               rhs=gu[32 * z:32 * z + 17, :], start=True, stop=True), ["dA", "gu"], ["b6"], rt=32 * z)
                        W_ = 256 * len(zs)
                        S.op("act", C("activation", out=ex[:, 0:W_], in_=b6[:, 0:W_], func=AF.Exp, scale=-1.0), ["b6"], ["ex"])
                        S.op("act", C("activation", out=sp_t[:].rearrange("p z c -> p (z c)")[:, 0:W_], in_=ex[:, 0:W_],
                                                                       func=AF.Ln, bias=cst[:, 1:2]), ["ex", "cst"], [spk])
                        for z in zs:
                            tri = triP if z == 0 else triQ
                            k_e, kek = ke[z][i % 2], f"ke{z}_{i % 2}"
                            S.op("pe", C("matmul", b4[:, 256:512], lhsT=tri, rhs=sp_t[:, z, :], start=True, stop=True),
                                 [spk, "cm"], ["b4"])
                            S.op("act", C("activation", out=en[:], in_=b4[:, 256:512], func=AF.Exp, scale=-1.0), ["b4"], ["en"])
                            S.op("dve", C("tensor_tensor", out=k_e[:], in0=b4[:, 0:256], in1=en[:], op=ALU.mult),
                                 ["b4", "en"], [kek])
                            lc0 = (128 + 63) if z == 0 else 256
                            lastcols = cm[:, lc0:lc0 + 128].rearrange("p (a b) -> p a b", b=64)[:, :, 0]
                            for pr in range(2):
                                S.op("pe", C("matmul",
                                    b6[:, 2 * pr:2 * pr + 2], lhsT=sp_t[:, z, pr * 128:(pr + 1) * 128],
                                    rhs=lastcols, start=True, stop=True), [spk, "cm"], ["b6"])
                            S.op("act", C("activation", out=ez[z][:, i, :, :].rearrange("p a b -> p (a b)"), in_=b6[:, 0:4],
                                                                         func=AF.Exp), ["b6"], [f"ez{z}"])
                            if own:
                                for pr in range(2):
                                    S.op("pe", C("matmul",
                                        b6[:, 128 + pr * 128:256 + pr * 128], lhsT=sp_t[:, z, pr * 128:(pr + 1) * 128], rhs=tri,
                                        start=True, stop=True), [spk, "cm"], ["b6"])
                                S.op("act", C("activation", out=E1[z][:, :, tl], in_=b6[:, 128:384].rearrange("p (a b) -> p a b", b=128),
                                                                        func=AF.Exp), ["b6"], [f"E1_{z}"])
                                S.op("act", C("activation", out=E2[z][:, :, tl], in_=b6[:, 128:384].rearrange("p (a b) -> p a b", b=128),
                                                                        func=AF.Exp, scale=-1.0), ["b6"], [f"E2_{z}"])
                            for c in range(2):
                                for h in range(4):
                                    hp, pr = h % 2, h // 2
                                    S.op("pe", C("matmul",
                                        b7[hp * 64:(hp + 1) * 64, (pr * 2 + c) * 128:(pr * 2 + c + 1) * 128],
                                        lhsT=k_e[c * 64:(c + 1) * 64, h * 64:(h + 1) * 64],
                                        rhs=v_b[c * 64:(c + 1) * 64, h * 128:(h + 1) * 128], start=True, stop=True),
                                        [kek, vk], ["b7"], rt=c * 64)
                            if z == 0:
                                S.op("dve", C("tensor_copy", out=UP[:, 0, :], in_=b7[:, :]), ["b7"], ["UP"])
                            elif ctx:
                                S.op("dve", C("tensor_copy", out=UQc[:, i, :], in_=b7[:, :]), ["b7"], ["UQc"])
                            else:
                                ti = t0 - T0_OWN + i
                                uq, uqk = UQs[i % 2], f"UQs{i % 2}"
                                S.op("dve", C("tensor_copy", out=uq[:], in_=b7[:, :]), ["b7"], [uqk])
                                S.dma("pool", UQ_d[ti, :, :], uq[:], [uqk], [], uqk)
                                S.op("dve", C("tensor_copy", out=eQ[:, ti, :, :], in_=ez[1][:, i, :, :]), ["ez1"], ["eQ"])
                        for c in range(2):
                            if own:
                                S.op("act", C("copy", out=Sbf[:, 2 * i + c, :, :], in_=SP[:]), ["SP"], [f"Sbf{2 * i + c}"])
                            S.op("dve", C("tensor_tensor",
                                out=tmpS[:], in0=SP[:], in1=UP[:, 0, :].rearrange("p (a c d) -> p a c d", c=2, d=128)[:, :, c, :], op=ALU.add),
                                ["SP", "UP"], ["tmpS"])
                            for pr in range(2):
                                S.op("dve", C("tensor_scalar", out=SP[:, pr, :], in0=tmpS[:, pr, :],
                                                                                      scalar1=ez[0][:, i, pr, c:c + 1], scalar2=None, op0=ALU.mult),
                                     ["tmpS", "ez0"], ["SP"])
                    if ctx:
                        for i in reversed(range(nt)):
                            for c in (1, 0):
                                S.op("dve", C("tensor_tensor",
                                    out=tmpS[:], in0=SQ[:], in1=UQc[:, i, :].rearrange("p (a c d) -> p a c d", c=2, d=128)[:, :, c, :], op=ALU.add),
                                    ["SQ", "UQc"], ["tmpS"])
                                for pr in range(2):
                                    S.op("dve", C("tensor_scalar", out=SQ[:, pr, :], in0=tmpS[:, pr, :],
                                                                                          scalar1=ez[1][:, i, pr, c:c + 1], scalar2=None, op0=ALU.mult),
                                         ["tmpS", "ez1"], ["SQ"])
                        if dbg:
                            S.dma("pool", dbg_out["dbg_SP"][:, :], SP[:].rearrange("p a b -> p (a b)"), ["SP"], [], "dbg")
                            S.dma("pool", dbg_out["dbg_SQ"][:, :], SQ[:].rearrange("p a b -> p (a b)"), ["SQ"], [], "dbg")
                    if not own:
                        return

                    for pr in range(2):
                        for (col0, dsts, Es, scale) in ((C_GQ, qeT, E1, 0.125), (C_GK, keT, E2, 1.0)):
                            for kc in range(8):
                                S.op("pe", C("matmul",
                                    b4[:, 0:T], lhsT=Win[:, kc, col0 + pr * 128:col0 + (pr + 1) * 128], rhs=h_T[:, kc, 0:T],
                                    start=(kc == 0), stop=(kc == 7)), [WinK[kc], f"{hK}_{kc}"], ["b4"])
                            for z in range(2):
                                S.op("dve", C("scalar_tensor_tensor",
                                    out=dsts[z][:, pr, 0:T], in0=b4[:, 0:T], scalar=scale, in1=Es[z][:, pr, 0:T],
                                    op0=ALU.mult, op1=ALU.mult), ["b4", f"{'E1' if Es is E1 else 'E2'}_{z}"],
                                    [f"{'qeT' if dsts is qeT else 'keT'}{z}"])
                    for i in range(nt):
                        tl0 = i * 128
                        ti = t0 - T0_OWN + i
                        v_b, vk = vbf[i], f"vbf{i}"
                        for z in range(2):
                            mk = maskP if z == 0 else maskQ
                            for hp in range(2):
                                for pr in range(2):
                                    h = pr * 2 + hp
                                    for c in range(2):
                                        cs = slice(tl0 + c * 64, tl0 + (c + 1) * 64)
                                        S.op("pe", C("matmul",
                                            b7[c * 64:(c + 1) * 64, z * 256 + h * 64:z * 256 + (h + 1) * 64],
                                            lhsT=keT[z][hp * 64:(hp + 1) * 64, pr, cs], rhs=qeT[z][hp * 64:(hp + 1) * 64, pr, cs],
                                            start=True, stop=True), [f"keT{z}", f"qeT{z}"], ["b7"], rt=hp * 64)
                            S.op("dve", C("tensor_tensor",
                                out=am[z][:], in0=b7[:, z * 256:(z + 1) * 256].rearrange("p (h i) -> p h i", i=64),
                                in1=mk.unsqueeze(1).to_broadcast([128, 4, 64]), op=ALU.mult), ["b7", "cm"], [f"am{z}"])
                        for c in range(2):
                            first = True
                            for z in range(2):
                                for h in range(4):
                                    S.op("pe", C("matmul", b6[c * 64:(c + 1) * 64, h * 128:(h + 1) * 128],
                                                 lhsT=am[z][c * 64:(c + 1) * 64, h, :], rhs=v_b[c * 64:(c + 1) * 64, h * 128:(h + 1) * 128],
                                                 start=first, stop=False, skip_group_check=True), [f"am{z}", vk], ["b6"], rt=c * 64)
                                    first = False
                            for hp in range(2):
                                for pr in range(2):
                                    h = pr * 2 + hp
                                    cs = slice(tl0 + c * 64, tl0 + (c + 1) * 64)
                                    S.op("pe", C("matmul", b6[c * 64:(c + 1) * 64, h * 128:(h + 1) * 128],
                                                 lhsT=qeT[0][hp * 64:(hp + 1) * 64, pr, cs], rhs=Sbf[hp * 64:(hp + 1) * 64, 2 * i + c, pr, :],
                                                 start=False, stop=True, skip_group_check=True), ["qeT0", f"Sbf{2 * i + c}"], ["b6"], rt=hp * 64)
                        ob, obk = osb[i % 2], f"osb{i % 2}"
                        S.op("act", C("copy", out=ob[:], in_=b6[:, :]), ["b6"], [obk])
                        S.dma("pool", op_d[ti, :, :], ob[:], [obk], [], obk)
                        S.dma("pool", qeQ_d[ti, :, :, :], qeT[1][:, :, tl0:tl0 + 128], ["qeT1"], [], f"qq{i % 2}")
                        for kc in range(8):
                            S.op("pe", C("matmul", b5[:, :], lhsT=h_T[:, kc, tl0:tl0 + 128], rhs=Win[:, kc, C_GR:C_GR + 512],
                                                                          start=(kc == 0), stop=(kc == 7)), [WinK[kc], f"{hK}_{kc}"], ["b5"])
                        rb, rbk = rsbuf[i % 2], f"rsbuf{i % 2}"
                        S.op("act", C("copy", out=rb[:], in_=b5[:, :]), ["b5"], [rbk])
                        S.dma("pool", r_d[ti, :, :], rb[:], [rbk], [], rbk)

                if groups:
                    S.play([S.record(lambda: stage1(0))])
                for gidx in range(len(groups)):
                    lists = []
                    if gidx + 1 < len(groups):
                        lists.append(S.record(lambda: stage1(gidx + 1)))
                    lists.append(S.record(lambda: stageY(gidx)))
                    lists.append(S.record(lambda: stageZ(gidx)))
                    S.play(lists, greedy=True)
                S.barrier()
            open_stacks.remove(phA)
            phA.close()
            if stop_after is not None and stop_after.startswith("A"):
                raise _Stop()

            phD0 = ExitStack()
            open_stacks.append(phD0)
            Wo = sb(phD0, "Wo", [128, 8, D], BF16)
            Wfo = sb(phD0, "Wfo", [128, 22, D], BF16)

            def threadW():
                for kc in range(8):
                    S.dma("pool", Wo[:, kc, :], w_out[kc * 128:(kc + 1) * 128, :], [], [f"Wo{kc}"], f"w{kc % 2}")
                for hc in range(22):
                    S.dma("pool", Wfo[:, hc, :], w_ffo[hc * 128:(hc + 1) * 128, :], [], [f"Wfo{hc}"], f"w{hc % 2}")
                for kc in range(8):
                    S.op("dve" if kc % 2 == 0 else "pool", C("tensor_tensor", out=Wo[:, kc, :], in0=Wo[:, kc, :], in1=G1[:], op=ALU.mult),
                         [f"Wo{kc}", "G1"], [f"Wo{kc}"])
                for hc in range(22):
                    S.op("dve" if hc % 2 == 0 else "pool", C("tensor_tensor", out=Wfo[:, hc, :], in0=Wfo[:, hc, :], in1=G2[:], op=ALU.mult),
                         [f"Wfo{hc}", "G2"], [f"Wfo{hc}"])

            with ExitStack() as ph:
                qq = [sb(ph, f"Bqq{i}", [128, 2, 128], BF16) for i in range(2)]
                uq = [sb(ph, f"Buq{i}", [128, 512]) for i in range(2)]
                opb = [sb(ph, f"Bop{i}", [128, 512]) for i in range(2)]
                rb = [sb(ph, f"Brb{i}", [128, 512]) for i in range(2)]
                sr = sb(ph, "Bsr", [128, 512])
                ob = sb(ph, "Bo", [128, 512])
                go = [sb(ph, f"Bgo{i}", [128, 512], BF16) for i in range(2)]
                tmpSB = sb(ph, "BtmpS", [128, 2, 128])
                Sbf = [sb(ph, f"BSbf{i}", [128, 2, 128], BF16) for i in range(4)]
                s8B = [sb(ph, f"Bs8_{i}", [128, 12]) for i in range(2)]
                junkB = sb(ph, "Bjunk", [128, 128], BF16)
                po = [ps(ph, "Bpo0", [128, 512])] * 2
                KTh = [sb(ph, f"CK{i}", [128, NTOK], BF16) for i in range(2)]
                VAh = [sb(ph, f"CV{i}", [128, NK, 129], BF16) for i in range(2)]
                QTg = [sb(ph, f"CQ{i}", [128, 4, 512], BF16) for i in range(2)]
                PT = [sb(ph, f"CP{i}", [128, 2, 512], BF16) for i in range(3)]
                accs = sb(ph, "Cacc", [128, 3, 387])
                rd = sb(ph, "Crd", [128, 3, 3])
                tmpo = sb(ph, "Ctmpo", [128, 128])
                oall = sb(ph, "Coall", [128, 4, 4, 128])
                s8 = [sb(ph, f"Cs8_{i}", [128, 12]) for i in range(2)]
                junk = sb(ph, "Cjunk", [128, 128], BF16)
                ao = [sb(ph, f"Cao{i}", [128, 512], BF16) for i in range(2)]
                STp = [ps(ph, f"CST{i}", [128, 2, 512]) for i in range(2)]
                acc = ps(ph, "CACC", [128, 3, 512])

                def threadB():
                    cc = 0
                    for n, ti in enumerate(reversed(range(NOWN))):
                        b = n % 2
                        S.dma("sp", qq[b][:], qeQ_d[ti, :, :, :], [], [f"Bqq{b}"], f"Bqq{b}")
                        S.dma("sp", uq[b][:], UQ_d[ti, :, :], [], [f"Buq{b}"], f"Buq{b}")
                        S.dma("sp", opb[b][:], op_d[ti, :, :], [], [f"Bop{b}"], f"Bop{b}")
                        S.dma("sp", rb[b][:], r_d[ti, :, :], [], [f"Brb{b}"], f"Brb{b}")
                        for c in (1, 0):
                            sbf, sbk = Sbf[cc % 4], f"BSbf{cc % 4}"
                            cc += 1
                            S.op("pool", C("tensor_copy", out=sbf[:], in_=SQ[:]), ["SQ"], [sbk])
                            S.op("dve", C("tensor_tensor",
                                out=tmpSB[:], in0=SQ[:], in1=uq[b][:].rearrange("p (a c d) -> p a c d", c=2, d=128)[:, :, c, :], op=ALU.add),
                                ["SQ", f"Buq{b}"], ["BtmpS"])
                            for pr in range(2):
                                S.op("dve", C("tensor_scalar", out=SQ[:, pr, :], in0=tmpSB[:, pr, :],
                                                                                        scalar1=eQ[:, ti, pr, c:c + 1], scalar2=None, op0=ALU.mult),
                                     ["BtmpS", "eQ"], ["SQ"])
                            for h in (0, 2, 1, 3):
                                hp, pr = h % 2, h // 2
                                S.op("pe", C("matmul",
                                    po[b][c * 64:(c + 1) * 64, h * 128:(h + 1) * 128], lhsT=qq[b][hp * 64:(hp + 1) * 64, pr, c * 64:(c + 1) * 64],
                                    rhs=sbf[hp * 64:(hp + 1) * 64, pr, :], start=True, stop=True), [f"Bqq{b}", sbk], ["Bpo0"], rt=hp * 64)
                        S.op("dve", C("tensor_tensor", out=ob[:], in0=po[b][:, :], in1=opb[b][:], op=ALU.add),
                             ["Bpo0", f"Bop{b}"], ["Bo"])
                        s_, sk_ = s8B[b], f"Bs8_{b}"
                        for h in range(4):
                            S.op("act", C("activation", out=junkB[:], in_=ob[:, h * 128:(h + 1) * 128], func=AF.Square,
                                                                          accum_out=s_[:, h:h + 1]), ["Bo"], [sk_])
                        S.op("act", C("activation", out=s_[:, 4:8], in_=s_[:, 0:4], func=AF.Ln, scale=1.0 / 128, bias=cst[:, 0:1]),
                             [sk_, "cst"], [sk_])
                        S.op("act", C("activation", out=s_[:, 8:12], in_=s_[:, 4:8], func=AF.Exp, scale=-0.5), [sk_], [sk_])
                        S.op("act", C("activation", out=sr[:], in_=rb[b][:], func=AF.Exp, scale=-1.0), [f"Brb{b}"], ["Bsr"])
                        S.op("dve", C("tensor_scalar", out=sr[:], in0=sr[:], scalar1=1.0, scalar2=None, op0=ALU.add), ["Bsr"], ["Bsr"])
                        S.op("dve", C("reciprocal", out=sr[:], in_=sr[:]), ["Bsr"], ["Bsr"])
                        S.op("pool", C("tensor_tensor", out=sr[:], in0=sr[:], in1=rb[b][:], op=ALU.mult), ["Bsr", f"Brb{b}"], ["Bsr"])
                        for h in range(4):
                            S.op("dve", C("scalar_tensor_tensor",
                                out=ob[:, h * 128:(h + 1) * 128], in0=ob[:, h * 128:(h + 1) * 128], scalar=s_[:, 8 + h:9 + h], in1=gng[:],
                                op0=ALU.mult, op1=ALU.mult), ["Bo", sk_, "gng"], ["Bo"])
                        S.op("dve", C("tensor_tensor", out=go[b][:], in0=ob[:], in1=sr[:], op=ALU.mult), ["Bo", "Bsr"], [f"Bgo{b}"])
                        S.dma("pool", mix_d[ti * 128:(ti + 1) * 128, 0:512], go[b][:], [f"Bgo{b}"], [], f"Bgo{b}")

                def threadC():
                    NQG = NOWN // 4
                    it = 0
                    aoc = 0
                    for qg in range(NQG):
                        qb = qg % 2
                        S.dma("sp", QTg[qb][:], QT_d[:, :, qg * 512:(qg + 1) * 512].rearrange("h p t -> p h t"), [], [f"CQ{qb}"], f"CQ{qb}")
                        for h in range(4):
                            kb = it % 2
                            it += 1
                            S.dma("sp", KTh[kb][:], KT_d[h, :, :], [], [f"CK{kb}"], f"CK{kb}")
                            S.dma("sp", VAh[kb][:], VA_d[:, h, :, :], [], [f"CV{kb}"], f"CV{kb}")

                            def qk(kt):
                                for c in range(2):
                                    S.op("pe", C("matmul",
                                        STp[kt % 2][:, c, :], lhsT=KTh[kb][c * 64:(c + 1) * 64, kt * 128:(kt + 1) * 128],
                                        rhs=QTg[qb][c * 64:(c + 1) * 64, h, :], start=True, stop=True),
                                        [f"CK{kb}", f"CQ{qb}"], [f"CST{c}_{kt % 2}"], rt=c * 64)

                            qk(0)
                            qk(1)
                            for kt in range(NK):
                                S.op("act", C("activation", out=PT[kt % 3][:].rearrange("p c t -> p (c t)"),
                                                                          in_=STp[kt % 2][:].rearrange("p c t -> p (c t)"), func=AF.Exp),
                                     [f"CST0_{kt % 2}", f"CST1_{kt % 2}"], [f"CP{kt % 3}"])
                                if kt + 2 < NK:
                                    qk(kt + 2)
                                for c in range(2):
                                    for qt in range(4):
                                        sl = c * 4 + qt
                                        S.op("pe", C("matmul",
                                            acc[:, sl // 3, (sl % 3) * 129:(sl % 3) * 129 + 129], lhsT=PT[kt % 3][:, c, qt * 128:(qt + 1) * 128],
                                            rhs=VAh[kb][:, kt, :], start=(kt == 0 and sl % 3 == 0), stop=(kt == NK - 1), skip_group_check=True),
                                            [f"CP{kt % 3}", f"CV{kb}"], [f"CACC{sl // 3}"])
                            for bk in range(3):
                                nsl = 3 if bk < 2 else 2
                                S.op("dve", C("tensor_copy", out=accs[:, bk, 0:nsl * 129], in_=acc[:, bk, 0:nsl * 129]),
                                     [f"CACC{bk}"], [f"Cacc{bk}"])
                                S.op("dve", C("reciprocal",
                                    out=rd[:, bk, 0:nsl], in_=accs[:, bk, 0:nsl * 129].rearrange("p (s c) -> p s c", c=129)[:, :, 128]),
                                    [f"Cacc{bk}"], ["Crd"])
                            for sl in range(4, 8):
                                S.op("dve", C("tensor_scalar", out=rd[:, sl // 3, sl % 3:sl % 3 + 1], in0=rd[:, sl // 3, sl % 3:sl % 3 + 1],
                                                                             scalar1=negl[:, 0:1], scalar2=None, op0=ALU.mult), ["Crd", "negl"], ["Crd"])
                            for qt in range(4):
                                s0, s1 = qt, 4 + qt
                                S.op("dve", C("tensor_scalar", out=tmpo[:], in0=accs[:, s0 // 3, (s0 % 3) * 129:(s0 % 3) * 129 + 128],
                                                                             scalar1=rd[:, s0 // 3, s0 % 3:s0 % 3 + 1], scalar2=None, op0=ALU.mult),
                                     [f"Cacc{s0 // 3}", "Crd"], ["Ctmpo"])
                                S.op("dve", C("scalar_tensor_tensor",
                                    out=oall[:, qt, h, :], in0=accs[:, s1 // 3, (s1 % 3) * 129:(s1 % 3) * 129 + 128],
                                    scalar=rd[:, s1 // 3, s1 % 3:s1 % 3 + 1], in1=tmpo[:], op0=ALU.mult, op1=ALU.add),
                                    [f"Cacc{s1 // 3}", "Crd", "Ctmpo"], ["Coall"])
                        for qt in range(4):
                            b = aoc % 2
                            aoc += 1
                            s_, sk_ = s8[b], f"Cs8_{b}"
                            for h in range(4):
                                S.op("act", C("activation", out=junk[:], in_=oall[:, qt, h, :], func=AF.Square,
                                                                                     accum_out=s_[:, h:h + 1]), ["Coall"], [sk_])
                            S.op("act", C("activation", out=s_[:, 4:8], in_=s_[:, 0:4], func=AF.Ln, scale=1.0 / 128, bias=cst[:, 0:1]),
                                 [sk_, "cst"], [sk_])
                            S.op("act", C("activation", out=s_[:, 8:12], in_=s_[:, 4:8], func=AF.Exp, scale=-0.5), [sk_], [sk_])
                            for h in range(4):
                                S.op("dve", C("scalar_tensor_tensor",
                                    out=ao[b][:, h * 128:(h + 1) * 128], in0=oall[:, qt, h, :], scalar=s_[:, 8 + h:9 + h], in1=dng[:],
                                    op0=ALU.mult, op1=ALU.mult), ["Coall", sk_, "dng"], [f"Cao{b}"])
                            row0 = (qg * 4 + qt) * 128
                            S.dma("pool", mix_d[row0:row0 + 128, 512:1024], ao[b][:], [f"Cao{b}"], [], f"Cao{b}")

                if dbg:
                    print("phase B+C sbuf remaining", nc.sbuf_bytes_remaining)
                S.play([S.record(threadW), S.record(threadB), S.record(threadC)])
                S.barrier()
            if stop_after in ("B", "C"):
                raise _Stop()

            with ExitStack() as ph:
                GT = 2
                TD = GT * 128
                NG = NOWN // GT
                Wfi = sb(ph, "Wfi", [128, 8, 2 * FFN_H], BF16)
                identb = sb(ph, "identb", [128, 128], BF16)
                mx = sb(ph, "Dmx", [128, 1024], BF16)
                mixT = sb(ph, "DmixT", [128, 8, 128], BF16)
                xr = sb(ph, "Dxr", [128, 1024])
                x1 = sb(ph, "Dx1", [128, GT, 1024])
                xn2 = sb(ph, "Dxn", [128, 1024])
                st4 = [sb(ph, f"Dst4_{i}", [128, 4]) for i in range(2)]
                h2T = [sb(ph, f"Dh2T{i}", [128, 8, TD], BF16) for i in range(2)]
                sg = [sb(ph, f"Dsg{i}", [128, TD]) for i in range(2)]
                actT = sb(ph, "DactT", [128, 22, TD], BF16)
                ptb = ps(ph, "Dptb", [128, 1024], BF16)
                pmoP = ps(ph, "DpmoP", [128, 512])
                ptr = ps(ph, "Dptr", [128, 1024])
                pgu = [ps(ph, f"Dpgu{i}", [128, 512]) for i in range(2)]
                pmoF = ps(ph, "DpmoF", [128, 1024])
                S.excl.update(["Dptb", "DpmoP", "Dptr0", "Dptr1", "Dpgu0", "Dpgu1", "DpmoF0", "DpmoF1"])
                if dbg:
                    print("phase D sbuf remaining", nc.sbuf_bytes_remaining)
                S.op("dve", C("tensor_copy", out=identb[:], in_=ident), ["cm"], ["identb"])
                for j in range(11):
                    for part in range(2):
                        c0 = part * FFN_H + j * 256
                        S.dma("pool", Wfi[:, :, c0:c0 + 256], w_ffi[:, c0:c0 + 256].rearrange("(k p) c -> p k c", p=128), [],
                              [f"Wfi{part}_{j}"], f"w{(2 * j + part) % 2}")
                x1t = [[x1[:, 0, :], x1[:, 1, :]], [G1[:], G2[:]]]
                x1k = [["Dx1_0", "Dx1_1"], ["G1", "G2"]]
                tcount = [0]

                def prep(g):
                    hb = g % 2
                    for i in range(GT):
                        ti = g * GT + i
                        b = tcount[0] % 2
                        tcount[0] += 1
                        X1, X1k = x1t[hb][i], x1k[hb][i]
                        S.dma("sp", mx[:], mix_d[ti * 128:(ti + 1) * 128, :], [], ["Dmx"], "Dmx")
                        S.dma("sp", xr[:], xin[(T0_OWN + ti) * 128:(T0_OWN + ti + 1) * 128, :], [], ["Dxr"], "Dxr")
                        for kc in range(8):
                            S.op("pe", C("transpose", out=ptb[:, kc * 128:(kc + 1) * 128], in_=mx[:, kc * 128:(kc + 1) * 128],
                                         identity=identb[:]), ["Dmx", "identb"], ["Dptb"])
                        S.op("act", C("copy", out=mixT[:].rearrange("p k t -> p (k t)"), in_=ptb[:, :]), ["Dptb"], ["DmixT"])
                        for hf in range(2):
                            hs = slice(hf * 512, (hf + 1) * 512)
                            for kc in range(8):
                                S.op("pe", C("matmul", pmoP[:, :], lhsT=mixT[:, kc, :], rhs=Wo[:, kc, hs], start=(kc == 0), stop=(kc == 7)),
                                     ["DmixT", f"Wo{kc}"], ["DpmoP"])
                            S.op("dve", C("tensor_tensor", out=X1[:, hs], in0=pmoP[:, :], in1=xr[:, hs], op=ALU.add),
                                 ["DpmoP", "Dxr"], [X1k])
                        s4, sk4 = st4[b], f"Dst4_{b}"
                        S.op("act", C("activation", out=xn2[:], in_=X1, func=AF.Square, accum_out=s4[:, 0:1]), [X1k], ["Dxn", sk4])
                        S.op("act", C("activation", out=s4[:, 1:2], in_=s4[:, 0:1], func=AF.Ln, scale=1.0 / D, bias=cst[:, 0:1]),
                             [sk4, "cst"], [sk4])
                        S.op("act", C("activation", out=s4[:, 2:3], in_=s4[:, 1:2], func=AF.Exp, scale=-0.5), [sk4], [sk4])
                        S.op("dve", C("tensor_scalar", out=xn2[:], in0=X1, scalar1=s4[:, 2:3], scalar2=None, op0=ALU.mult),
                             [X1k, sk4], ["Dxn"])
                        for kc in range(8):
                            S.op("pe", C("transpose", out=ptr[:, kc * 128:(kc + 1) * 128], in_=xn2[:, kc * 128:(kc + 1) * 128],
                                         identity=ident), ["Dxn", "cm"], [f"Dptr{kc // 4}"])
                        for kc in range(8):
                            dst = h2T[hb][:, kc, i * 128:(i + 1) * 128]
                            src = ptr[:, kc * 128:(kc + 1) * 128]
                            if kc < 4:
                                S.op("dve", C("tensor_scalar", out=dst, in0=src, scalar1=A2[:, 0, kc:kc + 1], scalar2=B2[:, 0, kc:kc + 1],
                                              op0=ALU.mult, op1=ALU.add), [f"Dptr{kc // 4}", "A2", "B2"], [f"Dh2T{hb}_{kc}"])
                            else:
                                S.op("act", C("activation", out=dst, in_=src, func=AF.Identity, scale=A2[:, 0, kc:kc + 1],
                                              bias=B2[:, 0, kc:kc + 1]), [f"Dptr{kc // 4}", "A2", "B2"], [f"Dh2T{hb}_{kc}"])

                def ffn(g):
                    hb = g % 2
                    for hc in range(22):
                        pg_, pgk_ = pgu[hc % 2], f"Dpgu{hc % 2}"
                        for kc in range(8):
                            S.op("pe", C("matmul", pg_[:, 0:TD], lhsT=Wfi[:, kc, hc * 128:(hc + 1) * 128], rhs=h2T[hb][:, kc, :],
                                         start=(kc == 0), stop=(kc == 7)), [f"Wfi0_{hc // 2}", f"Dh2T{hb}_{kc}"], [pgk_])
                        for kc in range(8):
                            S.op("pe", C("matmul", pg_[:, TD:2 * TD], lhsT=Wfi[:, kc, FFN_H + hc * 128:FFN_H + (hc + 1) * 128],
                                         rhs=h2T[hb][:, kc, :], start=(kc == 0), stop=(kc == 7)), [f"Wfi1_{hc // 2}", f"Dh2T{hb}_{kc}"], [pgk_])
                        sgb, sgk = sg[hc % 2], f"Dsg{hc % 2}"
                        S.op("act", C("activation", out=sgb[:], in_=pg_[:, 0:TD], func=AF.Silu), [pgk_], [sgk])
                        S.op("dve", C("tensor_tensor", out=actT[:, hc, :], in0=pg_[:, TD:2 * TD], in1=sgb[:], op=ALU.mult),
                             [pgk_, sgk], [f"DactT{hc}"])
                    for i in range(GT):
                        ti = g * GT + i
                        X1, X1k = x1t[hb][i], x1k[hb][i]
                        for hf in range(2):
                            hs = slice(hf * 512, (hf + 1) * 512)
                            for hc in range(22):
                                S.op("pe", C("matmul", pmoF[:, hs], lhsT=actT[:, hc, i * 128:(i + 1) * 128], rhs=Wfo[:, hc, hs],
                                             start=(hc == 0), stop=(hc == 21)), [f"DactT{hc}", f"Wfo{hc}"], [f"DpmoF{hf}"])
                            S.op("dve", C("tensor_tensor", out=X1[:, hs], in0=pmoF[:, hs], in1=X1[:, hs], op=ALU.add),
                                 [f"DpmoF{hf}", X1k], [X1k])
                        S.dma("sp", out_d[ti * 128:(ti + 1) * 128, :], X1, [X1k], [], "Dyo")

                S.play([S.record(lambda: prep(0))])
                for g in range(NG):
                    lists = []
                    if g + 1 < NG:
                        lists.append(S.record(lambda: prep(g + 1)))
                    lists.append(S.record(lambda: ffn(g)))
                    S.play(lists)
                S.barrier()
            open_stacks.remove(phD0)
            phD0.close()
            if stop_after == "D":
                raise _Stop()
        except _Stop:
            for st_ in reversed(open_stacks):
                st_.close()
        S.finish()
    return nc


def _const_mats():
    p = np.arange(128)
    same = (p[:, None] // 64) == (p[None, :] // 64)
    triP = np.where(same & (p[:, None] <= p[None, :]), -1.0 / 16, 0.0)
    triQ = np.where(same & (p[:, None] >= p[None, :]), -1.0 / 16, 0.0)
    blk = np.where(same, 1.0 / 64, 0.0)
    i64 = np.arange(64)
    maskP = ((p[:, None] % 64) <= i64[None, :]).astype(np.float64)
    maskQ = ((p[:, None] % 64) >= i64[None, :]).astype(np.float64)
    return np.concatenate([np.eye(128), triP, triQ, blk, maskP, maskQ], axis=1).astype(np.float32)


def _rope_tables(seq, positions):
    n_freq = 16
    inv_freq = (np.float32(10000.0) ** (-np.arange(n_freq, dtype=np.float32) / np.float32(n_freq))).astype(np.float32)
    row = (positions // 64).astype(np.float32)
    col = (positions % 64).astype(np.float32)
    ang = np.stack([row[:, None] * inv_freq[None, :], col[:, None] * inv_freq[None, :]], axis=0).astype(np.float32)
    cos = np.cos(ang).astype(np.float32)
    sin = np.sin(ang).astype(np.float32)
    n = len(positions)
    cosT = np.ones((128, 256 + n), np.float32)
    sinT = np.zeros((128, 256 + n), np.float32)
    for c in range(2):
        for ax in range(2):
            for hf in range(2):
                p0 = c * 64 + ax * 32 + hf * 16
                cosT[p0:p0 + 16, 256:] = cos[ax].T
                sinT[p0:p0 + 16, 256:] = (-sin[ax].T) if hf == 0 else sin[ax].T
    return cosT, sinT


_PROG_CACHE = {}


def _prep_inputs(inputs):
    f = lambda a: np.ascontiguousarray(np.asarray(a, dtype=np.float32))
    x = f(inputs["x"]); c = f(inputs["c"]); ctx = f(inputs["ctx"]); c_ctx = f(inputs["c_ctx"])
    B, SEQ, _ = x.shape
    half = SEQ // 2
    w_mod = f(inputs["w_mod"][0]); b_mod = f(inputs["b_mod"][0]).reshape(1, -1)
    g1 = f(inputs["norm1_g"][0]); g2 = f(inputs["norm2_g"][0])
    w_in = f(inputs["w_in"][0])
    gate_up = f(inputs["gla_gate_up"][0]); gate_bias = f(inputs["gla_gate_bias"][0])
    gng = f(inputs["gla_norm_g"][0]).reshape(1, 128); dng = f(inputs["diff_norm_g"][0]).reshape(1, 128)
    gq = f(inputs["diff_q_norm_g"][0]); gk = f(inputs["diff_k_norm_g"][0])
    lq = f(inputs["diff_lambda_q"][0]).reshape(1, 128); lk = f(inputs["diff_lambda_k"][0]).reshape(1, 128)
    w_out = f(inputs["w_out"][0]); w_ffi = f(inputs["w_ffn_in"][0]); w_ffo = f(inputs["w_ffn_out"][0])
    cmat = _const_mats()
    qkg = np.stack([np.tile(gk, 2), np.tile(gq, 2)], axis=1).astype(np.float32)
    in_maps = []
    for core in range(8):
        b, hf = core // 2, core % 2
        if hf == 1:
            oth = x[b, 0:half]; own = x[b, half:SEQ]; cx = ctx[b]
            pos = np.arange(SEQ)
            zP, zQ = 0, 1
        else:
            oth = x[b, SEQ - 1:half - 1:-1]; own = x[b, half - 1::-1]; cx = ctx[b, ::-1]
            pos = SEQ - 1 - np.arange(SEQ)
            zP, zQ = 1, 0
        xin = np.ascontiguousarray(np.concatenate([cx, oth, own], axis=0))
        cosT, sinT = _rope_tables(SEQ, pos)
        vecs = np.concatenate([b_mod.reshape(48, 128), g1.reshape(8, 128), g2.reshape(8, 128),
                               c[b].reshape(8, 128), c_ctx.reshape(8, 128)], axis=0).astype(np.float32)
        w_in_c = w_in.copy()
        w_in_c[:, C_GD:C_GD + 16] = w_in[:, C_GD + 16 * zP:C_GD + 16 * zP + 16]
        w_in_c[:, C_GD + 16:C_GD + 32] = w_in[:, C_GD + 16 * zQ:C_GD + 16 * zQ + 16]
        gu = np.zeros((49, 256), np.float32)
        gu[0:16] = gate_up[zP]; gu[16] = gate_bias[zP]
        gu[32:48] = gate_up[zQ]; gu[48] = gate_bias[zQ]
        in_maps.append({
            "xin": xin, "vecs": vecs, "w_mod": w_mod, "b_mod": b_mod, "w_in": w_in_c, "gu_in": gu, "gng_in": gng, "dng_in": dng,
            "qkg_in": qkg, "lq": lq, "lk": lk, "w_out": w_out, "w_ffi": w_ffi, "w_ffo": w_ffo,
            "cosT": cosT, "sinT": sinT, "cmat": cmat,
        })
    return in_maps, B, SEQ


def kernel(**inputs):
    in_maps, B, SEQ = _prep_inputs(inputs)
    half = SEQ // 2
    nt = half // 128
    key = (nt,)
    if key not in _PROG_CACHE:
        _PROG_CACHE[key] = build_program(nt, nt)
    nc = _PROG_CACHE[key]
    res = run_bass_kernel_spmd(nc, in_maps, core_ids=list(range(8)))
    out = np.empty((B, SEQ, D), np.float32)
    for core in range(8):
        b, hf = core // 2, core % 2
        o = np.asarray(res.results[core]["out"], dtype=np.float32)
        if hf == 1:
            out[b, half:SEQ] = o
        else:
            out[b, 0:half] = o[::-1]
    return out
```

```python
import math
from contextlib import ExitStack

import numpy as np
import ml_dtypes

import concourse.bass as bass
import concourse.mybir as mybir
from concourse.bass_utils import run_bass_kernel_spmd

F32 = mybir.dt.float32
BF16 = mybir.dt.bfloat16
AF = mybir.ActivationFunctionType
ALU = mybir.AluOpType
AX = mybir.AxisListType

D = 1024
EPS = 1e-6
NCTX = 2
FFN_H = 2816
W_IN_COLS = 3104
C_GQ, C_GK, C_GV, C_GR, C_GD, C_DQ, C_DK, C_DV = 0, 256, 512, 1024, 1536, 1568, 2080, 2592
LAM_INIT = 0.8 - 0.6 * math.exp(-0.3 * 0)


def C(name, *a, **kw):
    f = lambda e: getattr(e, name)(*a, **kw)
    f.opname, f.a, f.kw = name, a, kw
    return f


def _free_elems(ap):
    n = 1
    for d in list(ap.shape)[1:]:
        n *= int(d)
    return n


def _op_cost(e, fn):
    name = getattr(fn, "opname", None)
    if name is None:
        return 300.0
    out = fn.kw.get("out", fn.a[0] if fn.a else None)
    n = _free_elems(out) if out is not None else 128
    if e == "pe":
        if name == "transpose":
            return 220.0
        lhsT = fn.kw.get("lhsT")
        mult = 3.5 if (lhsT is not None and lhsT.dtype == F32) else 1.0
        return (max(n, 64) / 1.6 + 40.0) * mult
    if e == "act":
        return n / 0.96 + 220.0
    if e == "dve":
        return n / 0.96 * (8.0 if name == "reciprocal" else 1.0) + 80.0
    return n * 1.7 + 150.0


class _Stop(Exception):
    pass


class Sched:
    def __init__(self, nc, stack):
        self.nc = nc
        self.stack = stack
        self.eng = {"pe": nc.tensor, "act": nc.scalar, "dve": nc.vector, "pool": nc.gpsimd, "sp": nc.sync}
        self.sems = {}
        self.cnt = {}
        for e in ("pe", "act", "dve", "pool"):
            self.sems["E" + e] = stack.enter_context(nc.semaphore("s_" + e))
            self.cnt[e] = 0
        self.waited = {e: {} for e in self.eng}
        self.lastw = {}
        self.readers = {}
        self.slots = {}
        self.pending = {e: [] for e in self.eng}
        self.n_inst = 0
        self.excl = set()
        self.lastacc = {}
        self.pe_last = None
        self.rec = None
        self.sim_e, self.sim_w, self.sim_r, self.sim_a = {}, {}, {}, {}

    def record(self, fn):
        assert self.rec is None
        self.rec = []
        fn()
        lst, self.rec = self.rec, None
        return lst

    def _sim_ready(self, e, reads, writes):
        t = 0.0
        for k in reads:
            t = max(t, self.sim_w.get(k, 0.0))
            if k in self.excl:
                t = max(t, self.sim_a.get(k, 0.0))
        for k in writes:
            t = max(t, self.sim_w.get(k, 0.0), self.sim_r.get(k, 0.0))
        return t

    def _sim_commit(self, e, kind, fn_or_bytes, reads, writes):
        ready = self._sim_ready(e, reads, writes)
        start = max(ready + 120.0, self.sim_e.get(e, 0.0))
        if kind == "op":
            fin = start + _op_cost(e, fn_or_bytes)
            self.sim_e[e] = fin
        else:
            self.sim_e[e] = start + 80.0
            fin = start + 2000.0 + fn_or_bytes / 120.0
        for k in writes:
            self.sim_w[k] = fin
            self.sim_r[k] = 0.0
        for k in reads:
            self.sim_r[k] = max(self.sim_r.get(k, 0.0), fin)
        for k in list(reads) + list(writes):
            if k in self.excl:
                self.sim_a[k] = fin

    def play(self, lists, greedy=False):
        cur = [0] * len(lists)
        while True:
            best = None
            for li, lst in enumerate(lists):
                if cur[li] >= len(lst):
                    continue
                kind, args, kw = lst[cur[li]]
                if greedy:
                    e = args[0]
                    reads, writes = (args[2], args[3]) if kind == "op" else (args[3], args[4])
                    st = max(self._sim_ready(e, reads, writes) + 120.0, self.sim_e.get(e, 0.0))
                    key = (st, cur[li] / len(lst), li)
                else:
                    key = ((cur[li] + 0.5) / len(lst), li)
                if best is None or key < best[0]:
                    best = (key, li)
            if best is None:
                break
            li = best[1]
            kind, args, kw = lists[li][cur[li]]
            cur[li] += 1
            if kind == "op":
                self.op(*args, **kw)
            else:
                self.dma(*args, **kw)

    def _wait(self, e, tok):
        sk, val, _ = tok
        if self.waited[e].get(sk, 0) >= val:
            return
        self.eng[e].wait_ge(self.sems[sk], val)
        self.waited[e][sk] = val

    def _deps(self, e, reads, writes):
        deps = []
        for k in reads:
            t = self.lastw.get(k)
            if t is not None:
                deps.append((t, "raw"))
            if k in self.excl:
                t = self.lastacc.get(k)
                if t is not None and t[2] != e:
                    deps.append((t, "raw"))
        for k in writes:
            t = self.lastw.get(k)
            if t is not None:
                deps.append((t, "waw"))
            for r in self.readers.get(k, ()):
                deps.append((r, "war"))
        for t in self.pending[e]:
            deps.append((t, "raw"))
        self.pending[e] = []
        need = {}
        for t, kind in deps:
            if t[2] == e and e == "pe":
                continue
            if self.waited[e].get(t[0], 0) >= t[1]:
                continue
            if t[0] not in need or need[t[0]][1] < t[1]:
                need[t[0]] = t
        return list(need.values())

    def _emit_waits(self, e, waits, keep_last=False):
        held = None
        if keep_last and waits:
            held = waits[-1]
            waits = waits[:-1]
        for t in waits:
            self._wait(e, t)
        return held

    def _record(self, tok, reads, writes):
        for k in writes:
            self.lastw[k] = tok
            self.readers[k] = []
        for k in reads:
            lst = self.readers.setdefault(k, [])
            lst[:] = [r for r in lst if r[0] != tok[0]]
            lst.append(tok)
        for k in list(reads) + list(writes):
            if k in self.excl:
                self.lastacc[k] = tok

    def op(self, e, fn, reads=(), writes=(), rt=0):
        if self.rec is not None:
            self.rec.append(("op", (e, fn, tuple(reads), tuple(writes)), {"rt": rt}))
            return None
        waits = self._deps(e, reads, writes)
        self._sim_commit(e, "op", fn, reads, writes)
        attach = (e in ("act", "dve", "pool") and getattr(fn, "opname", None) is not None
                  and "accum_out" not in fn.kw and fn.opname not in ("stream_shuffle",))
        held = self._emit_waits(e, waits, keep_last=attach)
        if e == "pe":
            pl = self.pe_last
            if pl is not None and pl[1] != rt and any(k in pl[2] for k in writes):
                self._wait(e, pl[0])
        inst = fn(self.eng[e])
        if held is not None:
            inst._wait_ge(self.sems[held[0]], held[1])
            self.waited[e][held[0]] = held[1]
        self.cnt[e] += 1
        inst.then_inc(self.sems["E" + e], 1)
        tok = ("E" + e, self.cnt[e], e)
        self._record(tok, reads, writes)
        if e == "pe":
            self.pe_last = (tok, rt, set(writes))
        self.n_inst += 1
        return tok

    def dma(self, q, out, in_, reads, writes, slot, **kw):
        if self.rec is not None:
            self.rec.append(("dma", (q, out, in_, tuple(reads), tuple(writes), slot), kw))
            return None
        sk = "D" + slot
        if sk not in self.sems:
            self.sems[sk] = self.stack.enter_context(self.nc.semaphore("d_" + slot))
            self.slots[sk] = 0
        if self.slots[sk] > 0:
            self._wait(q, (sk, self.slots[sk], None))
        self._emit_waits(q, self._deps(q, reads, writes))
        nbytes = 128 * _free_elems(out) * (2 if out.dtype == BF16 else 4)
        self._sim_commit(q, "dma", nbytes, reads, writes)
        inst = self.eng[q].dma_start(out=out, in_=in_, **kw)
        self.slots[sk] += 16
        inst.then_inc(self.sems[sk], 16)
        tok = (sk, self.slots[sk], None)
        self._record(tok, reads, writes)
        self.n_inst += 1
        return tok

    def _all_toks(self):
        toks = [("E" + e, c, e) for e, c in self.cnt.items() if c > 0]
        toks += [(sk, v, None) for sk, v in self.slots.items() if v > 0]
        return toks

    def barrier(self):
        toks = self._all_toks()
        for e in self.eng:
            self.pending[e] = [t for t in toks if t[2] != e]

    def finish(self):
        for t in self._all_toks():
            self._wait("sp", t)


def build_program(NOTH, NOWN, dbg=False, stop_after=None):
    assert NOTH % 4 == 0 and NOWN % 4 == 0
    NK = NCTX + NOTH + NOWN
    NTOK = NK * 128
    T0_OTH = NCTX
    T0_OWN = NCTX + NOTH
    NQ = NOWN * 128

    nc = bass.Bass("TRN2", target_bir_lowering=False)

    def din(name, shape, dt=F32):
        return nc.dram_tensor(name, list(shape), dt, kind="ExternalInput").ap()

    def dscr(name, shape, dt):
        return nc.dram_tensor(name, list(shape), dt, kind=("ExternalOutput" if dbg else "Internal")).ap()

    xin = din("xin", [NTOK, D])
    vecs = din("vecs", [80, 128])
    w_mod = din("w_mod", [D, 6 * D])
    b_mod = din("b_mod", [1, 6 * D])
    w_in = din("w_in", [D, W_IN_COLS])
    gu_in = din("gu_in", [49, 256])
    gng_in = din("gng_in", [1, 128])
    dng_in = din("dng_in", [1, 128])
    qkg_in = din("qkg_in", [128, 2])
    lq_in = din("lq", [1, 128])
    lk_in = din("lk", [1, 128])
    w_out = din("w_out", [D, D])
    w_ffi = din("w_ffi", [D, 2 * FFN_H])
    w_ffo = din("w_ffo", [FFN_H, D])
    cos_in = din("cosT", [128, NTOK])
    sin_in = din("sinT", [128, NTOK])
    cmat = din("cmat", [128, 4 * 128 + 2 * 64])
    out_d = nc.dram_tensor("out", [NQ, D], F32, kind="ExternalOutput").ap()

    KT_d = dscr("KT_d", [4, 128, NTOK], BF16)
    VA_d = dscr("VA_d", [128, 4, NK, 129], BF16)
    QT_d = dscr("QT_d", [4, 128, NQ], BF16)
    qeQ_d = dscr("qeQ_d", [NOWN, 128, 2, 128], BF16)
    UQ_d = dscr("UQ_d", [NOWN, 128, 512], F32)
    op_d = dscr("op_d", [NOWN, 128, 512], F32)
    r_d = dscr("r_d", [NOWN, 128, 512], F32)
    mix_d = dscr("mix_d", [NQ, D], BF16)

    dbg_out = {}
    if dbg:
        for nm, shp in (("dbg_mod", [128, 96]), ("dbg_G", [128, 2048]),
                        ("dbg_SP", [128, 256]), ("dbg_SQ", [128, 256])):
            dbg_out[nm] = nc.dram_tensor(nm, shp, F32, kind="ExternalOutput").ap()

    top = ExitStack()
    with top:
        S = Sched(nc, top)
        S.excl.update(["p0T", "p0mod", "p0G0", "p0G1", "ptr0", "ptr1", "b2", "b3", "b4", "b5", "b6", "b7", "Bpo0", "Bpo1",
                       "CST0_0", "CST0_1", "CST1_0", "CST1_1", "CACC0", "CACC1", "CACC2",
                       "Dptb", "Dpmo0", "Dpmo1", "Dptr0", "Dptr1", "Dpgu0", "Dpgu1", "Dpgu2"])

        def sb(stack, name, shape, dt=F32):
            return stack.enter_context(nc.sbuf_tensor(name, list(shape), dt))

        def ps(stack, name, shape, dt=F32):
            return stack.enter_context(nc.psum_tensor(name, list(shape), dt))

        cm = sb(top, "cm", [128, 640])
        ident = cm[:, 0:128]
        triP = cm[:, 128:256]
        triQ = cm[:, 256:384]
        blk64 = cm[:, 384:512]
        maskP = cm[:, 512:576]
        maskQ = cm[:, 576:640]
        cst = sb(top, "cst", [128, 4])
        colv = sb(top, "colv", [128, 80])
        modT = sb(top, "modT", [128, 48, 2])
        A1 = sb(top, "A1", [128, 2, 8])
        B1 = sb(top, "B1", [128, 2, 8])
        A2 = sb(top, "A2", [128, 2, 8])
        B2 = sb(top, "B2", [128, 2, 8])
        G1 = sb(top, "G1", [128, 1024])
        G2 = sb(top, "G2", [128, 1024])
        gng = sb(top, "gng", [128, 128])
        dng = sb(top, "dng", [128, 128])
        qkg = sb(top, "qkg", [128, 2])
        negl = sb(top, "negl", [128, 1])
        gu = sb(top, "gu", [49, 256])
        eQ = sb(top, "eQ", [128, NOWN, 2, 2])
        SQ = sb(top, "SQ", [128, 2, 128])

        S.dma("sp", cm[:], cmat[:, :], [], ["cm"], "c0")
        S.op("dve", C("memset", cst[:, 0:1], EPS), [], ["cst"])
        S.op("dve", C("memset", cst[:, 1:2], 1.0), [], ["cst"])
        S.dma("sp", gu[:], gu_in[:, :], [], ["gu"], "c1")
        S.dma("sp", qkg[:], qkg_in[:, :], [], ["qkg"], "c2")
        S.dma("sp", gng[:], gng_in[0:1, :].to_broadcast([128, 128]), [], ["gng"], "c3")
        S.dma("sp", dng[:], dng_in[0:1, :].to_broadcast([128, 128]), [], ["dng"], "c4")
        S.op("dve", C("tensor_scalar", out=qkg[:, 1:2], in0=qkg[:, 1:2], scalar1=0.125, scalar2=None, op0=ALU.mult),
             ["qkg"], ["qkg"])
        S.op("dve", C("tensor_scalar", out=dng[:], in0=dng[:], scalar1=1.0 - LAM_INIT, scalar2=None, op0=ALU.mult),
             ["dng"], ["dng"])

        open_stacks = []
        try:
            phA = ExitStack()
            open_stacks.append(phA)
            Win = sb(phA, "Win", [128, 8, W_IN_COLS], BF16)
            for kc in range(8):
                S.dma("pool", Win[:, kc, :], w_in[kc * 128:(kc + 1) * 128, :], [], [f"Win{kc}"], f"w{kc % 2}")
            with ExitStack() as ph:
                stage = sb(ph, "stage", [80, 128])
                siluT = sb(ph, "siluT", [128, 8, 2])
                silubc = sb(ph, "silubc", [128, 8, 128])
                ones1 = sb(ph, "ones1", [1, 128])
                bmr = sb(ph, "bmr", [1, 2048])
                lqb = sb(ph, "lqb", [128, 128])
                lkb = sb(ph, "lkb", [128, 128])
                s2 = sb(ph, "s2", [128, 2])
                wm = [sb(ph, f"wm{i}", [128, 8, 512]) for i in range(2)]
                pT = ps(ph, "p0T", [128, 512])
                pmod = ps(ph, "p0mod", [128, 512])
                pG = [ps(ph, f"p0G{i}", [128, 512]) for i in range(2)]

                S.dma("sp", stage[:], vecs[:, :], [], ["stage"], "c0")
                S.dma("sp", bmr[:, 0:1024], b_mod[0:1, 2048:3072], [], ["bmr"], "c1")
                S.dma("sp", bmr[:, 1024:2048], b_mod[0:1, 5120:6144], [], ["bmr"], "c1")
                S.dma("sp", lqb[:], lq_in[0:1, :].to_broadcast([128, 128]), [], ["lqb"], "c2")
                S.dma("sp", lkb[:], lk_in[0:1, :].to_broadcast([128, 128]), [], ["lkb"], "c3")
                S.op("dve", C("memset", ones1[:], 1.0), [], ["ones1"])
                S.op("dve", C("tensor_tensor", out=lqb[:], in0=lqb[:], in1=lkb[:], op=ALU.mult), ["lqb", "lkb"], ["lqb"])
                S.op("dve", C("tensor_reduce", out=s2[:], in_=lqb[:].rearrange("p (a b) -> p a b", b=64), axis=AX.X, op=ALU.add),
                     ["lqb"], ["s2"])
                S.op("act", C("activation", out=s2[:], in_=s2[:], func=AF.Exp), ["s2"], ["s2"])
                S.op("dve", C("tensor_tensor", out=negl[:], in0=s2[:, 1:2], in1=s2[:, 0:1], op=ALU.subtract), ["s2"], ["negl"])
                S.op("dve", C("tensor_scalar", out=negl[:], in0=negl[:], scalar1=-LAM_INIT, scalar2=None, op0=ALU.add),
                     ["negl"], ["negl"])
                S.op("pe", C("transpose", out=pT[:, 0:80], in_=stage[:], identity=ident[0:80, 0:80]), ["stage", "cm"], ["p0T"])
                S.op("dve", C("tensor_copy", out=colv[:], in_=pT[:, 0:80]), ["p0T"], ["colv"])
                S.op("act", C("activation", out=siluT[:].rearrange("p k r -> p r k"),
                                                   in_=colv[:, 64:80].rearrange("p (r k) -> p r k", k=8), func=AF.Silu),
                     ["colv"], ["siluT"])
                for kc in range(8):
                    S.op("dve", C("tensor_scalar", out=silubc[:, kc, :], in0=ident[:, :], scalar1=0.0,
                                                                 scalar2=siluT[:, kc, 0:1], op0=ALU.mult, op1=ALU.add),
                         ["siluT", "cm"], ["silubc"])
                gi = 0
                for cg in range(12):
                    w = wm[cg % 2]
                    wk = f"wm{cg % 2}"
                    S.dma("sp", w[:], w_mod[:, cg * 512:(cg + 1) * 512].rearrange("(k p) c -> p k c", p=128), [], [wk], wk)
                    for jj in range(4):
                        j = cg * 4 + jj
                        for kc in range(8):
                            S.op("pe", C("matmul",
                                pmod[:, 2 * j:2 * j + 2], lhsT=w[:, kc, jj * 128:(jj + 1) * 128], rhs=siluT[:, kc, :],
                                start=(kc == 0), stop=(kc == 7)), [wk, "siluT"], ["p0mod"])
                    if cg in (4, 5, 10, 11):
                        pg = pG[gi % 2]
                        pk_ = f"p0G{gi % 2}"
                        Gt = G1 if cg < 6 else G2
                        gcol = (cg % 2) * 512
                        boff = (0 if cg < 6 else 1024) + gcol
                        for kc in range(8):
                            S.op("pe", C("matmul", pg[:, :], lhsT=silubc[:, kc, :], rhs=w[:, kc, :],
                                                                              start=(kc == 0), stop=False),
                                 [wk, "silubc"], [pk_])
                        S.op("pe", C("matmul", pg[:, :], lhsT=ones1[:, :], rhs=bmr[:, boff:boff + 512],
                                                                        start=False, stop=True), ["ones1", "bmr"], [pk_])
                        S.op("act", C("copy", out=Gt[:, gcol:gcol + 512], in_=pg[:, :]),
                             [pk_], ["G1" if cg < 6 else "G2"])
                        gi += 1
                S.op("dve", C("tensor_tensor", out=modT[:], in0=pmod[:, 0:96].rearrange("p (j r) -> p j r", r=2),
                                                      in1=colv[:, 0:48].unsqueeze(2).to_broadcast([128, 48, 2]), op=ALU.add),
                     ["p0mod", "colv"], ["modT"])
                for r in range(2):
                    S.op("dve", C("scalar_tensor_tensor", out=A1[:, r, :], in0=modT[:, 8:16, r], scalar=1.0,
                                                                      in1=colv[:, 48:56], op0=ALU.add, op1=ALU.mult),
                         ["modT", "colv"], ["A1"])
                    S.op("dve", C("tensor_copy", out=B1[:, r, :], in_=modT[:, 0:8, r]), ["modT"], ["B1"])
                    S.op("dve", C("scalar_tensor_tensor", out=A2[:, r, :], in0=modT[:, 32:40, r], scalar=1.0,
                                                                      in1=colv[:, 56:64], op0=ALU.add, op1=ALU.mult),
                         ["modT", "colv"], ["A2"])
                    S.op("dve", C("tensor_copy", out=B2[:, r, :], in_=modT[:, 24:32, r]), ["modT"], ["B2"])
                if dbg:
                    S.dma("pool", dbg_out["dbg_mod"][:, :], modT[:].rearrange("p j r -> p (j r)"), ["modT"], [], "dbg")
                    S.dma("pool", dbg_out["dbg_G"][:, 0:1024], G1[:], ["G1"], [], "dbg")
                    S.dma("pool", dbg_out["dbg_G"][:, 1024:2048], G2[:], ["G2"], [], "dbg")
                S.barrier()
            if stop_after == "0":
                raise _Stop()

            with ExitStack() as ph:
                xt = [sb(ph, f"xt{i}", [128, 1024]) for i in range(2)]
                junk = sb(ph, "junk", [128, 1024], BF16)
                st4 = [sb(ph, f"st4_{i}", [128, 4]) for i in range(2)]
                xn = [sb(ph, f"xn{i}", [128, 1024]) for i in range(2)]
                hT = [sb(ph, f"hT{i}", [128, 8, 512], BF16) for i in range(2)]
                cosb = [sb(ph, f"cosb{i}", [128, 512]) for i in range(1)] * 2
                sinb = [sb(ph, f"sinb{i}", [128, 512]) for i in range(1)] * 2
                sq = sb(ph, "sq", [128, 512], BF16)
                blkb = sb(ph, "blkb", [128, 128], BF16)
                lnb = sb(ph, "lnb", [128, 512])
                rsb = sb(ph, "rsb", [128, 512])
                kn = sb(ph, "kn", [128, 512])
                kr = sb(ph, "kr", [128, 512])
                t1 = sb(ph, "t1", [128, 512])
                kst = [sb(ph, f"kst{i}", [128, 4, 512], BF16) for i in range(2)]
                qst = [sb(ph, f"qst{i}", [128, 4, 512], BF16) for i in range(1)] * 2
                vst = [sb(ph, f"vst{i}", [128, 4, 4, 129], BF16) for i in range(2)]
                dA = sb(ph, "dA", [49, 512])
                ex = sb(ph, "ex", [128, 512])
                spb = [sb(ph, f"spb{i}", [128, 2, 256]) for i in range(2)]
                en = sb(ph, "en", [128, 256])
                ke = [[sb(ph, f"ke{z}_{i}", [128, 256], BF16) for i in range(2)] for z in range(2)]
                vbf = [sb(ph, f"vbf{i}", [128, 512], BF16) for i in range(4)]
                ez = [sb(ph, f"ez{z}", [128, 4, 2, 2]) for z in range(2)]
                E1 = [sb(ph, f"E1_{z}", [128, 2, 512]) for z in range(2)]
                E2 = [sb(ph, f"E2_{z}", [128, 2, 512]) for z in range(2)]
                qeT = [sb(ph, f"qeT{z}", [128, 2, 512], BF16) for z in range(2)]
                keT = [sb(ph, f"keT{z}", [128, 2, 512], BF16) for z in range(2)]
                UP = sb(ph, "UPs", [128, 1, 512])
                UQs = [sb(ph, f"UQs{i}", [128, 512]) for i in range(2)]
                UQc = sb(ph, "UQc", [128, 2, 512])
                SP = sb(ph, "SP", [128, 2, 128])
                tmpS = sb(ph, "tmpS", [128, 2, 128])
                Sbf = sb(ph, "Sbf", [128, 8, 2, 128], BF16)
                am = [sb(ph, f"am{z}", [128, 4, 64], BF16) for z in range(2)]
                osb = [sb(ph, f"osb{i}", [128, 512]) for i in range(2)]
                rsbuf = [sb(ph, f"rsbuf{i}", [128, 512]) for i in range(2)]
                ptr = ps(ph, "ptr", [128, 1024])
                b2 = ps(ph, "b2", [128, 512])
                b3 = ps(ph, "b3", [128, 512])
                b4 = ps(ph, "b4", [128, 512])
                b5 = ps(ph, "b5", [128, 512])
                b6 = ps(ph, "b6", [128, 512])
                b7 = ps(ph, "b7", [128, 512])

                WinK = [f"Win{kc}" for kc in range(8)]
                S.op("dve", C("tensor_copy", out=blkb[:], in_=blk64), ["cm"], ["blkb"])
                S.op("dve", C("memset", dA[:], 1.0), [], ["dA"])
                for i in range(2):
                    S.op("dve", C("memset", vst[i][:].rearrange("p a b c -> p (a b c)"), 1.0), [], [f"vst{i}"])
                S.op("dve", C("memset", SP[:].rearrange("p a b -> p (a b)"), 0.0), [], ["SP"])
                S.op("dve", C("memset", SQ[:].rearrange("p a b -> p (a b)"), 0.0), [], ["SQ"])

                groups = [(0, NCTX, "ctx")]
                for g in range(NOTH // 4):
                    groups.append((T0_OTH + 4 * g, 4, "oth"))
                for g in range(NOWN // 4):
                    groups.append((T0_OWN + 4 * g, 4, "own"))

                xc = [0]
                if stop_after == "A0":
                    groups = []
                def stage1(gidx):
                    t0, nt, kind = groups[gidx]
                    T = nt * 128
                    koff = t0 * 128
                    r_mod = 1 if kind == "ctx" else 0
                    own = kind == "own"
                    ctx = kind == "ctx"
                    gb = gidx % 2
                    h_T = hT[gb]
                    hK = f"hT{gb}"
                    for i in range(nt):
                        xb = xc[0] % 2
                        nb = xc[0] % 2
                        xc[0] += 1
                        x_t, xk = xt[xb], f"xt{xb}"
                        s4, sk4 = st4[xb], f"st4_{xb}"
                        x_n, nk = xn[nb], f"xn{nb}"
                        S.dma("sp", x_t[:], xin[(t0 + i) * 128:(t0 + i + 1) * 128, :], [], [xk], xk)
                        S.op("act", C("activation", out=junk[:], in_=x_t[:], func=AF.Square, accum_out=s4[:, 0:1]),
                             [xk], [sk4])
                        S.op("act", C("activation", out=s4[:, 1:2], in_=s4[:, 0:1], func=AF.Ln, scale=1.0 / D, bias=cst[:, 0:1]),
                             [sk4, "cst"], [sk4])
                        S.op("act", C("activation", out=s4[:, 2:3], in_=s4[:, 1:2], func=AF.Exp, scale=-0.5), [sk4], [sk4])
                        S.op("dve", C("tensor_scalar", out=x_n[:], in0=x_t[:], scalar1=s4[:, 2:3], scalar2=None,
                                                                                      op0=ALU.mult), [xk, sk4], [nk])
                        for kc in range(8):
                            S.op("pe", C("transpose", out=ptr[:, kc * 128:(kc + 1) * 128], in_=x_n[:, kc * 128:(kc + 1) * 128],
                                                                           identity=ident), [nk, "cm"], [f"ptr{kc // 4}"])
                        for kc in range(8):
                            dst = h_T[:, kc, i * 128:(i + 1) * 128]
                            src = ptr[:, kc * 128:(kc + 1) * 128]
                            if kc < 4:
                                S.op("dve", C("tensor_scalar",
                                    out=dst, in0=src, scalar1=A1[:, r_mod, kc:kc + 1], scalar2=B1[:, r_mod, kc:kc + 1],
                                    op0=ALU.mult, op1=ALU.add), [f"ptr{kc // 4}", "A1", "B1"], [f"{hK}_{kc}"])
                            else:
                                S.op("act", C("activation",
                                    out=dst, in_=src, func=AF.Identity, scale=A1[:, r_mod, kc:kc + 1], bias=B1[:, r_mod, kc:kc + 1]),
                                    [f"ptr{kc // 4}", "A1", "B1"], [f"{hK}_{kc}"])

                def stageY(gidx):
                    t0, nt, kind = groups[gidx]
                    T = nt * 128
                    koff = t0 * 128
                    r_mod = 1 if kind == "ctx" else 0
                    own = kind == "own"
                    ctx = kind == "ctx"
                    gb = gidx % 2
                    h_T = hT[gb]
                    hK = f"hT{gb}"
                    S.dma("sp", cosb[gb][:, 0:T], cos_in[:, koff:koff + T], [], ["cos0"], "cos0")
                    S.dma("sp", sinb[gb][:, 0:T], sin_in[:, koff:koff + T], [], ["sin0"], "sin0")

                    def qk_proj(col0, gcol, dst, dkey):
                        for h in range(4):
                            for kc in range(8):
                                S.op("pe", C("matmul", b2[:, 0:T], lhsT=Win[:, kc, col0 + h * 128:col0 + (h + 1) * 128],
                                                                          rhs=h_T[:, kc, 0:T], start=(kc == 0), stop=(kc == 7)),
                                     [WinK[kc], f"{hK}_{kc}"], ["b2"])
                            S.op("act", C("activation", out=sq[:, 0:T], in_=b2[:, 0:T], func=AF.Square), ["b2"], ["sq"])
                            S.op("pe", C("matmul", b3[:, 0:T], lhsT=blkb[:], rhs=sq[:, 0:T], start=True, stop=True), ["sq", "blkb"], ["b3"])
                            S.op("act", C("activation", out=lnb[:, 0:T], in_=b3[:, 0:T], func=AF.Ln, bias=cst[:, 0:1]), ["b3", "cst"], ["lnb"])
                            S.op("act", C("activation", out=rsb[:, 0:T], in_=lnb[:, 0:T], func=AF.Exp, scale=-0.5), ["lnb"], ["rsb"])
                            S.op("dve", C("scalar_tensor_tensor", out=kn[:, 0:T], in0=b2[:, 0:T], scalar=qkg[:, gcol:gcol + 1],
                                                                         in1=rsb[:, 0:T], op0=ALU.mult, op1=ALU.mult),
                                 ["b2", "rsb", "qkg"], ["kn"])
                            S.op("dve", C("stream_shuffle", out=kr[:, 0:T], in_=kn[:, 0:T], mask=[(i + 16) % 32 for i in range(32)]),
                                 ["kn"], ["kr"])
                            S.op("pool", C("tensor_tensor", out=t1[:, 0:T], in0=kn[:, 0:T], in1=cosb[gb][:, 0:T], op=ALU.mult),
                                 ["kn", "cos0"], ["t1"])
                            S.op("pool", C("tensor_tensor", out=kr[:, 0:T], in0=kr[:, 0:T], in1=sinb[gb][:, 0:T], op=ALU.mult),
                                 ["kr", "sin0"], ["kr"])
                            S.op("dve", C("tensor_tensor", out=dst[:, h, 0:T], in0=t1[:, 0:T], in1=kr[:, 0:T], op=ALU.add),
                                 ["t1", "kr"], [dkey])

                    qk_proj(C_DK, 0, kst[gb], f"kst{gb}")
                    S.dma("pool", KT_d[:, :, koff:koff + T].rearrange("h p t -> p h t"), kst[gb][:, :, 0:T], [f"kst{gb}"], [], f"ko{gb}")
                    if own:
                        qoff = (t0 - T0_OWN) * 128
                        qk_proj(C_DQ, 1, qst[gb], "qst0")
                        S.dma("pool", QT_d[:, :, qoff:qoff + T].rearrange("h p t -> p h t"), qst[gb][:, :, 0:T], ["qst0"], [], "qo0")
                    for i in range(nt):
                        for kc in range(8):
                            S.op("pe", C("matmul", b3[:, :], lhsT=h_T[:, kc, i * 128:(i + 1) * 128], rhs=Win[:, kc, C_DV:C_DV + 512],
                                                                      start=(kc == 0), stop=(kc == 7)), [WinK[kc], f"{hK}_{kc}"], ["b3"])
                        S.op("act", C("copy", out=vst[gb][:, :, i, 0:128], in_=b3[:, :].rearrange("p (h c) -> p h c", c=128)),
                             ["b3"], [f"vst{gb}"])
                    S.dma("pool", VA_d[:, :, t0:t0 + nt, :], vst[gb][:, :, 0:nt, :], [f"vst{gb}"], [], f"vo{gb}")
                    if own:
                        for i in range(nt):
                            ti = t0 - T0_OWN + i
                            for kc in range(8):
                                S.op("pe", C("matmul", b3[:, :], lhsT=h_T[:, kc, i * 128:(i + 1) * 128], rhs=Win[:, kc, C_GR:C_GR + 512],
                                             start=(kc == 0), stop=(kc == 7)), [WinK[kc], f"{hK}_{kc}"], ["b3"])
                            rb, rbk = rsbuf[i % 2], f"rsbuf{i % 2}"
                            S.op("act", C("copy", out=rb[:], in_=b3[:, :]), ["b3"], [rbk])
                            S.dma("pool", r_d[ti, :, :], rb[:], [rbk], [], rbk)


                def stageZ(gidx):
                    t0, nt, kind = groups[gidx]
                    T = nt * 128
                    koff = t0 * 128
                    r_mod = 1 if kind == "ctx" else 0
                    own = kind == "own"
                    ctx = kind == "ctx"
                    gb = gidx % 2
                    h_T = hT[gb]
                    hK = f"hT{gb}"
                    zs = (0, 1) if (own or ctx) else (0,)
                    for z in zs:
                        for kc in range(8):
                            S.op("pe", C("matmul", b7[32 * z:32 * z + 16, 0:T], lhsT=Win[:, kc, C_GD + 16 * z:C_GD + 16 * z + 16],
                                                                      rhs=h_T[:, kc, 0:T], start=(kc == 0), stop=(kc == 7)),
                                 [WinK[kc], f"{hK}_{kc}"], ["b7"])
                        S.op("act", C("copy", out=dA[32 * z:32 * z + 16, 0:T], in_=b7[32 * z:32 * z + 16, 0:T]), ["b7"], ["dA"])
                    for i in range(nt):
                        tl = slice(i * 128, (i + 1) * 128)
                        sp_t, spk = spb[i % 2], f"spb{i % 2}"
                        v_b, vk = vbf[i], f"vbf{i}"
                        for kc in range(8):
                            S.op("pe", C("matmul", b4[:, 0:256], lhsT=h_T[:, kc, tl], rhs=Win[:, kc, C_GK:C_GK + 256],
                                                                 start=(kc == 0), stop=(kc == 7)), [WinK[kc], f"{hK}_{kc}"], ["b4"])
                        for kc in range(8):
                            S.op("pe", C("matmul", b5[:, :], lhsT=h_T[:, kc, tl], rhs=Win[:, kc, C_GV:C_GV + 512],
                                                                 start=(kc == 0), stop=(kc == 7)), [WinK[kc], f"{hK}_{kc}"], ["b5"])
                        S.op("act", C("copy", out=v_b[:], in_=b5[:, :]), ["b5"], [vk])
                        for z in zs:
                            S.op("pe", C("matmul", b6[:, 256 * z:256 * z + 256], lhsT=dA[32 * z:32 * z + 17, tl],
                                                               rhs=gu[32 * z:32 * z + 17, :], start=True, stop=True), ["dA", "gu"], ["b6"], rt=32 * z)
                        W_ = 256 * len(zs)
                        S.op("act", C("activation", out=ex[:, 0:W_], in_=b6[:, 0:W_], func=AF.Exp, scale=-1.0), ["b6"], ["ex"])
                        S.op("act", C("activation", out=sp_t[:].rearrange("p z c -> p (z c)")[:, 0:W_], in_=ex[:, 0:W_],
                                                                       func=AF.Ln, bias=cst[:, 1:2]), ["ex", "cst"], [spk])
                        for z in zs:
                            tri = triP if z == 0 else triQ
                            k_e, kek = ke[z][i % 2], f"ke{z}_{i % 2}"
                            S.op("pe", C("matmul", b4[:, 256:512], lhsT=tri, rhs=sp_t[:, z, :], start=True, stop=True),
                                 [spk, "cm"], ["b4"])
                            S.op("act", C("activation", out=en[:], in_=b4[:, 256:512], func=AF.Exp, scale=-1.0), ["b4"], ["en"])
                            S.op("dve", C("tensor_tensor", out=k_e[:], in0=b4[:, 0:256], in1=en[:], op=ALU.mult),
                                 ["b4", "en"], [kek])
                            lc0 = (128 + 63) if z == 0 else 256
                            lastcols = cm[:, lc0:lc0 + 128].rearrange("p (a b) -> p a b", b=64)[:, :, 0]
                            for pr in range(2):
                                S.op("pe", C("matmul",
                                    b6[:, 2 * pr:2 * pr + 2], lhsT=sp_t[:, z, pr * 128:(pr + 1) * 128],
                                    rhs=lastcols, start=True, stop=True), [spk, "cm"], ["b6"])
                            S.op("act", C("activation", out=ez[z][:, i, :, :].rearrange("p a b -> p (a b)"), in_=b6[:, 0:4],
                                                                         func=AF.Exp), ["b6"], [f"ez{z}"])
                            if own:
                                for pr in range(2):
                                    S.op("pe", C("matmul",
                                        b6[:, 128 + pr * 128:256 + pr * 128], lhsT=sp_t[:, z, pr * 128:(pr + 1) * 128], rhs=tri,
                                        start=True, stop=True), [spk, "cm"], ["b6"])
                                S.op("act", C("activation", out=E1[z][:, :, tl], in_=b6[:, 128:384].rearrange("p (a b) -> p a b", b=128),
                                                                        func=AF.Exp), ["b6"], [f"E1_{z}"])
                                S.op("act", C("activation", out=E2[z][:, :, tl], in_=b6[:, 128:384].rearrange("p (a b) -> p a b", b=128),
                                                                        func=AF.Exp, scale=-1.0), ["b6"], [f"E2_{z}"])
                            for c in range(2):
                                for h in range(4):
                                    hp, pr = h % 2, h // 2
                                    S.op("pe", C("matmul",
                                        b7[hp * 64:(hp + 1) * 64, (pr * 2 + c) * 128:(pr * 2 + c + 1) * 128],
                                        lhsT=k_e[c * 64:(c + 1) * 64, h * 64:(h + 1) * 64],
                                        rhs=v_b[c * 64:(c + 1) * 64, h * 128:(h + 1) * 128], start=True, stop=True),
                                        [kek, vk], ["b7"], rt=c * 64)
                            if z == 0:
                                S.op("dve", C("tensor_copy", out=UP[:, 0, :], in_=b7[:, :]), ["b7"], ["UP"])
                            elif ctx:
                                S.op("dve", C("tensor_copy", out=UQc[:, i, :], in_=b7[:, :]), ["b7"], ["UQc"])
                            else:
                                ti = t0 - T0_OWN + i
                                uq, uqk = UQs[i % 2], f"UQs{i % 2}"
                                S.op("dve", C("tensor_copy", out=uq[:], in_=b7[:, :]), ["b7"], [uqk])
                                S.dma("pool", UQ_d[ti, :, :], uq[:], [uqk], [], uqk)
                                S.op("dve", C("tensor_copy", out=eQ[:, ti, :, :], in_=ez[1][:, i, :, :]), ["ez1"], ["eQ"])
                        for c in range(2):
                            if own:
                                S.op("act", C("copy", out=Sbf[:, 2 * i + c, :, :], in_=SP[:]), ["SP"], [f"Sbf{2 * i + c}"])
                            S.op("dve", C("tensor_tensor",
                                out=tmpS[:], in0=SP[:], in1=UP[:, 0, :].rearrange("p (a c d) -> p a c d", c=2, d=128)[:, :, c, :], op=ALU.add),
                                ["SP", "UP"], ["tmpS"])
                            for pr in range(2):
                                S.op("dve", C("tensor_scalar", out=SP[:, pr, :], in0=tmpS[:, pr, :],
                                                                                      scalar1=ez[0][:, i, pr, c:c + 1], scalar2=None, op0=ALU.mult),
                                     ["tmpS", "ez0"], ["SP"])
                    if ctx:
                        for i in reversed(range(nt)):
                            for c in (1, 0):
                                S.op("dve", C("tensor_tensor",
                                    out=tmpS[:], in0=SQ[:], in1=UQc[:, i, :].rearrange("p (a c d) -> p a c d", c=2, d=128)[:, :, c, :], op=ALU.add),
                                    ["SQ", "UQc"], ["tmpS"])
                                for pr in range(2):
                                    S.op("dve", C("tensor_scalar", out=SQ[:, pr, :], in0=tmpS[:, pr, :],
                                                                                          scalar1=ez[1][:, i, pr, c:c + 1], scalar2=None, op0=ALU.mult),
                                         ["tmpS", "ez1"], ["SQ"])
                        if dbg:
                            S.dma("pool", dbg_out["dbg_SP"][:, :], SP[:].rearrange("p a b -> p (a b)"), ["SP"], [], "dbg")
                            S.dma("pool", dbg_out["dbg_SQ"][:, :], SQ[:].rearrange("p a b -> p (a b)"), ["SQ"], [], "dbg")
                    if not own:
                        return

                    for pr in range(2):
                        for (col0, dsts, Es, scale) in ((C_GQ, qeT, E1, 0.125), (C_GK, keT, E2, 1.0)):
                            for kc in range(8):
                                S.op("pe", C("matmul",
                                    b4[:, 0:T], lhsT=Win[:, kc, col0 + pr * 128:col0 + (pr + 1) * 128], rhs=h_T[:, kc, 0:T],
                                    start=(kc == 0), stop=(kc == 7)), [WinK[kc], f"{hK}_{kc}"], ["b4"])
                            for z in range(2):
                                S.op("dve", C("scalar_tensor_tensor",
                                    out=dsts[z][:, pr, 0:T], in0=b4[:, 0:T], scalar=scale, in1=Es[z][:, pr, 0:T],
                                    op0=ALU.mult, op1=ALU.mult), ["b4", f"{'E1' if Es is E1 else 'E2'}_{z}"],
                                    [f"{'qeT' if dsts is qeT else 'keT'}{z}"])
                    for i in range(nt):
                        tl0 = i * 128
                        ti = t0 - T0_OWN + i
                        v_b, vk = vbf[i], f"vbf{i}"
                        for z in range(2):
                            mk = maskP if z == 0 else maskQ
                            for hp in range(2):
                                for pr in range(2):
                                    h = pr * 2 + hp
                                    for c in range(2):
                                        cs = slice(tl0 + c * 64, tl0 + (c + 1) * 64)
                                        S.op("pe", C("matmul",
                                            b7[c * 64:(c + 1) * 64, z * 256 + h * 64:z * 256 + (h + 1) * 64],
                                            lhsT=keT[z][hp * 64:(hp + 1) * 64, pr, cs], rhs=qeT[z][hp * 64:(hp + 1) * 64, pr, cs],
                                            start=True, stop=True), [f"keT{z}", f"qeT{z}"], ["b7"], rt=hp * 64)
                            S.op("dve", C("tensor_tensor",
                                out=am[z][:], in0=b7[:, z * 256:(z + 1) * 256].rearrange("p (h i) -> p h i", i=64),
                                in1=mk.unsqueeze(1).to_broadcast([128, 4, 64]), op=ALU.mult), ["b7", "cm"], [f"am{z}"])
                        for c in range(2):
                            first = True
                            for z in range(2):
                                for h in range(4):
                                    S.op("pe", C("matmul", b6[c * 64:(c + 1) * 64, h * 128:(h + 1) * 128],
                                                 lhsT=am[z][c * 64:(c + 1) * 64, h, :], rhs=v_b[c * 64:(c + 1) * 64, h * 128:(h + 1) * 128],
                                                 start=first, stop=False, skip_group_check=True), [f"am{z}", vk], ["b6"], rt=c * 64)
                                    first = False
                            for hp in range(2):
                                for pr in range(2):
                                    h = pr * 2 + hp
                                    cs = slice(tl0 + c * 64, tl0 + (c + 1) * 64)
                                    S.op("pe", C("matmul", b6[c * 64:(c + 1) * 64, h * 128:(h + 1) * 128],
                                                 lhsT=qeT[0][hp * 64:(hp + 1) * 64, pr, cs], rhs=Sbf[hp * 64:(hp + 1) * 64, 2 * i + c, pr, :],
                                                 start=False, stop=True, skip_group_check=True), ["qeT0", f"Sbf{2 * i + c}"], ["b6"], rt=hp * 64)
                        ob, obk = osb[i % 2], f"osb{i % 2}"
                        S.op("act", C("copy", out=ob[:], in_=b6[:, :]), ["b6"], [obk])
                        S.dma("pool", op_d[ti, :, :], ob[:], [obk], [], obk)
                        S.dma("pool", qeQ_d[ti, :, :, :], qeT[1][:, :, tl0:tl0 + 128], ["qeT1"], [], f"qq{i % 2}")

                if groups:
                    S.play([S.record(lambda: stage1(0))])
                for gidx in range(len(groups)):
                    lists = []
                    if gidx + 1 < len(groups):
                        lists.append(S.record(lambda: stage1(gidx + 1)))
                    lists.append(S.record(lambda: stageY(gidx)))
                    lists.append(S.record(lambda: stageZ(gidx)))
                    S.play(lists, greedy=True)
                S.barrier()
            open_stacks.remove(phA)
            phA.close()
            if stop_after is not None and stop_after.startswith("A"):
                raise _Stop()

            phD0 = ExitStack()
            open_stacks.append(phD0)
            Wo = sb(phD0, "Wo", [128, 8, D], BF16)
            Wfo = sb(phD0, "Wfo", [128, 22, D], BF16)

            def threadW():
                for kc in range(8):
                    S.dma("pool", Wo[:, kc, :], w_out[kc * 128:(kc + 1) * 128, :], [], [f"Wo{kc}"], f"w{kc % 2}")
                for hc in range(22):
                    S.dma("pool", Wfo[:, hc, :], w_ffo[hc * 128:(hc + 1) * 128, :], [], [f"Wfo{hc}"], f"w{hc % 2}")
                for kc in range(8):
                    S.op("dve" if kc % 2 == 0 else "pool", C("tensor_tensor", out=Wo[:, kc, :], in0=Wo[:, kc, :], in1=G1[:], op=ALU.mult),
                         [f"Wo{kc}", "G1"], [f"Wo{kc}"])
                for hc in range(22):
                    S.op("dve" if hc % 2 == 0 else "pool", C("tensor_tensor", out=Wfo[:, hc, :], in0=Wfo[:, hc, :], in1=G2[:], op=ALU.mult),
                         [f"Wfo{hc}", "G2"], [f"Wfo{hc}"])

            with ExitStack() as ph:
                qq = [sb(ph, f"Bqq{i}", [128, 2, 128], BF16) for i in range(2)]
                uq = [sb(ph, f"Buq{i}", [128, 512]) for i in range(2)]
                opb = [sb(ph, f"Bop{i}", [128, 512]) for i in range(2)]
                rb = [sb(ph, f"Brb{i}", [128, 512]) for i in range(2)]
                sr = sb(ph, "Bsr", [128, 512])
                ob = sb(ph, "Bo", [128, 512])
                go = [sb(ph, f"Bgo{i}", [128, 512], BF16) for i in range(2)]
                tmpSB = sb(ph, "BtmpS", [128, 2, 128])
                Sbf = [sb(ph, f"BSbf{i}", [128, 2, 128], BF16) for i in range(4)]
                s8B = [sb(ph, f"Bs8_{i}", [128, 12]) for i in range(2)]
                junkB = sb(ph, "Bjunk", [128, 128], BF16)
                po = [ps(ph, "Bpo0", [128, 512])] * 2
                KTh = [sb(ph, f"CK{i}", [128, NTOK], BF16) for i in range(2)]
                VAh = [sb(ph, f"CV{i}", [128, NK, 129], BF16) for i in range(2)]
                QTg = [sb(ph, f"CQ{i}", [128, 4, 512], BF16) for i in range(2)]
                PT = [sb(ph, f"CP{i}", [128, 2, 512], BF16) for i in range(3)]
                accs = sb(ph, "Cacc", [128, 3, 387])
                rd = sb(ph, "Crd", [128, 3, 3])
                tmpo = sb(ph, "Ctmpo", [128, 128])
                oall = sb(ph, "Coall", [128, 4, 4, 128])
                s8 = [sb(ph, f"Cs8_{i}", [128, 12]) for i in range(2)]
                junk = sb(ph, "Cjunk", [128, 128], BF16)
                ao = [sb(ph, f"Cao{i}", [128, 512], BF16) for i in range(2)]
                STp = [ps(ph, f"CST{i}", [128, 2, 512]) for i in range(2)]
                acc = ps(ph, "CACC", [128, 3, 512])

                def threadB():
                    cc = 0
                    for n, ti in enumerate(reversed(range(NOWN))):
                        b = n % 2
                        S.dma("sp", qq[b][:], qeQ_d[ti, :, :, :], [], [f"Bqq{b}"], f"Bqq{b}")
                        S.dma("sp", uq[b][:], UQ_d[ti, :, :], [], [f"Buq{b}"], f"Buq{b}")
                        S.dma("sp", opb[b][:], op_d[ti, :, :], [], [f"Bop{b}"], f"Bop{b}")
                        S.dma("sp", rb[b][:], r_d[ti, :, :], [], [f"Brb{b}"], f"Brb{b}")
                        for c in (1, 0):
                            sbf, sbk = Sbf[cc % 4], f"BSbf{cc % 4}"
                            cc += 1
                            S.op("pool", C("tensor_copy", out=sbf[:], in_=SQ[:]), ["SQ"], [sbk])
                            S.op("dve", C("tensor_tensor",
                                out=tmpSB[:], in0=SQ[:], in1=uq[b][:].rearrange("p (a c d) -> p a c d", c=2, d=128)[:, :, c, :], op=ALU.add),
                                ["SQ", f"Buq{b}"], ["BtmpS"])
                            for pr in range(2):
                                S.op("dve", C("tensor_scalar", out=SQ[:, pr, :], in0=tmpSB[:, pr, :],
                                                                                        scalar1=eQ[:, ti, pr, c:c + 1], scalar2=None, op0=ALU.mult),
                                     ["BtmpS", "eQ"], ["SQ"])
                            for h in (0, 2, 1, 3):
                                hp, pr = h % 2, h // 2
                                S.op("pe", C("matmul",
                                    po[b][c * 64:(c + 1) * 64, h * 128:(h + 1) * 128], lhsT=qq[b][hp * 64:(hp + 1) * 64, pr, c * 64:(c + 1) * 64],
                                    rhs=sbf[hp * 64:(hp + 1) * 64, pr, :], start=True, stop=True), [f"Bqq{b}", sbk], ["Bpo0"], rt=hp * 64)
                        S.op("dve", C("tensor_tensor", out=ob[:], in0=po[b][:, :], in1=opb[b][:], op=ALU.add),
                             ["Bpo0", f"Bop{b}"], ["Bo"])
                        s_, sk_ = s8B[b], f"Bs8_{b}"
                        for h in range(4):
                            S.op("act", C("activation", out=junkB[:], in_=ob[:, h * 128:(h + 1) * 128], func=AF.Square,
                                                                          accum_out=s_[:, h:h + 1]), ["Bo"], [sk_])
                        S.op("act", C("activation", out=s_[:, 4:8], in_=s_[:, 0:4], func=AF.Ln, scale=1.0 / 128, bias=cst[:, 0:1]),
                             [sk_, "cst"], [sk_])
                        S.op("act", C("activation", out=s_[:, 8:12], in_=s_[:, 4:8], func=AF.Exp, scale=-0.5), [sk_], [sk_])
                        S.op("act", C("activation", out=sr[:], in_=rb[b][:], func=AF.Exp, scale=-1.0), [f"Brb{b}"], ["Bsr"])
                        S.op("dve", C("tensor_scalar", out=sr[:], in0=sr[:], scalar1=1.0, scalar2=None, op0=ALU.add), ["Bsr"], ["Bsr"])
                        S.op("dve", C("reciprocal", out=sr[:], in_=sr[:]), ["Bsr"], ["Bsr"])
                        S.op("pool", C("tensor_tensor", out=sr[:], in0=sr[:], in1=rb[b][:], op=ALU.mult), ["Bsr", f"Brb{b}"], ["Bsr"])
                        for h in range(4):
                            S.op("dve", C("scalar_tensor_tensor",
                                out=ob[:, h * 128:(h + 1) * 128], in0=ob[:, h * 128:(h + 1) * 128], scalar=s_[:, 8 + h:9 + h], in1=gng[:],
                                op0=ALU.mult, op1=ALU.mult), ["Bo", sk_, "gng"], ["Bo"])
                        S.op("dve", C("tensor_tensor", out=go[b][:], in0=ob[:], in1=sr[:], op=ALU.mult), ["Bo", "Bsr"], [f"Bgo{b}"])
                        S.dma("pool", mix_d[ti * 128:(ti + 1) * 128, 0:512], go[b][:], [f"Bgo{b}"], [], f"Bgo{b}")

                def threadC():
                    NQG = NOWN // 4
                    it = 0
                    aoc = 0
                    for qg in range(NQG):
                        qb = qg % 2
                        S.dma("sp", QTg[qb][:], QT_d[:, :, qg * 512:(qg + 1) * 512].rearrange("h p t -> p h t"), [], [f"CQ{qb}"], f"CQ{qb}")
                        for h in range(4):
                            kb = it % 2
                            it += 1
                            S.dma("sp", KTh[kb][:], KT_d[h, :, :], [], [f"CK{kb}"], f"CK{kb}")
                            S.dma("sp", VAh[kb][:], VA_d[:, h, :, :], [], [f"CV{kb}"], f"CV{kb}")

                            def qk(kt):
                                for c in range(2):
                                    S.op("pe", C("matmul",
                                        STp[kt % 2][:, c, :], lhsT=KTh[kb][c * 64:(c + 1) * 64, kt * 128:(kt + 1) * 128],
                                        rhs=QTg[qb][c * 64:(c + 1) * 64, h, :], start=True, stop=True),
                                        [f"CK{kb}", f"CQ{qb}"], [f"CST{c}_{kt % 2}"], rt=c * 64)

                            qk(0)
                            qk(1)
                            for kt in range(NK):
                                S.op("act", C("activation", out=PT[kt % 3][:].rearrange("p c t -> p (c t)"),
                                                                          in_=STp[kt % 2][:].rearrange("p c t -> p (c t)"), func=AF.Exp),
                                     [f"CST0_{kt % 2}", f"CST1_{kt % 2}"], [f"CP{kt % 3}"])
                                if kt + 2 < NK:
                                    qk(kt + 2)
                                for c in range(2):
                                    for qt in range(4):
                                        sl = c * 4 + qt
                                        S.op("pe", C("matmul",
                                            acc[:, sl // 3, (sl % 3) * 129:(sl % 3) * 129 + 129], lhsT=PT[kt % 3][:, c, qt * 128:(qt + 1) * 128],
                                            rhs=VAh[kb][:, kt, :], start=(kt == 0 and sl % 3 == 0), stop=(kt == NK - 1), skip_group_check=True),
                                            [f"CP{kt % 3}", f"CV{kb}"], [f"CACC{sl // 3}"])
                            for bk in range(3):
                                nsl = 3 if bk < 2 else 2
                                S.op("dve", C("tensor_copy", out=accs[:, bk, 0:nsl * 129], in_=acc[:, bk, 0:nsl * 129]),
                                     [f"CACC{bk}"], [f"Cacc{bk}"])
                                S.op("dve", C("reciprocal",
                                    out=rd[:, bk, 0:nsl], in_=accs[:, bk, 0:nsl * 129].rearrange("p (s c) -> p s c", c=129)[:, :, 128]),
                                    [f"Cacc{bk}"], ["Crd"])
                            for sl in range(4, 8):
                                S.op("dve", C("tensor_scalar", out=rd[:, sl // 3, sl % 3:sl % 3 + 1], in0=rd[:, sl // 3, sl % 3:sl % 3 + 1],
                                                                             scalar1=negl[:, 0:1], scalar2=None, op0=ALU.mult), ["Crd", "negl"], ["Crd"])
                            for qt in range(4):
                                s0, s1 = qt, 4 + qt
                                S.op("dve", C("tensor_scalar", out=tmpo[:], in0=accs[:, s0 // 3, (s0 % 3) * 129:(s0 % 3) * 129 + 128],
                                                                             scalar1=rd[:, s0 // 3, s0 % 3:s0 % 3 + 1], scalar2=None, op0=ALU.mult),
                                     [f"Cacc{s0 // 3}", "Crd"], ["Ctmpo"])
                                S.op("dve", C("scalar_tensor_tensor",
                                    out=oall[:, qt, h, :], in0=accs[:, s1 // 3, (s1 % 3) * 129:(s1 % 3) * 129 + 128],
                                    scalar=rd[:, s1 // 3, s1 % 3:s1 % 3 + 1], in1=tmpo[:], op0=ALU.mult, op1=ALU.add),
                                    [f"Cacc{s1 // 3}", "Crd", "Ctmpo"], ["Coall"])
                        for qt in range(4):
                            b = aoc % 2
                            aoc += 1
                            s_, sk_ = s8[b], f"Cs8_{b}"
                            for h in range(4):
                                S.op("act", C("activation", out=junk[:], in_=oall[:, qt, h, :], func=AF.Square,
                                                                                     accum_out=s_[:, h:h + 1]), ["Coall"], [sk_])
                            S.op("act", C("activation", out=s_[:, 4:8], in_=s_[:, 0:4], func=AF.Ln, scale=1.0 / 128, bias=cst[:, 0:1]),
                                 [sk_, "cst"], [sk_])
                            S.op("act", C("activation", out=s_[:, 8:12], in_=s_[:, 4:8], func=AF.Exp, scale=-0.5), [sk_], [sk_])
                            for h in range(4):
                                S.op("dve", C("scalar_tensor_tensor",
                                    out=ao[b][:, h * 128:(h + 1) * 128], in0=oall[:, qt, h, :], scalar=s_[:, 8 + h:9 + h], in1=dng[:],
                                    op0=ALU.mult, op1=ALU.mult), ["Coall", sk_, "dng"], [f"Cao{b}"])
                            row0 = (qg * 4 + qt) * 128
                            S.dma("pool", mix_d[row0:row0 + 128, 512:1024], ao[b][:], [f"Cao{b}"], [], f"Cao{b}")

                if dbg:
                    print("phase B+C sbuf remaining", nc.sbuf_bytes_remaining)
                S.play([S.record(threadW), S.record(threadB), S.record(threadC)])
                S.barrier()
            if stop_after in ("B", "C"):
                raise _Stop()

            with ExitStack() as ph:
                GT = 2
                TD = GT * 128
                NG = NOWN // GT
                Wfi = sb(ph, "Wfi", [128, 8, 2 * FFN_H], BF16)
                identb = sb(ph, "identb", [128, 128], BF16)
                mx = sb(ph, "Dmx", [128, 1024], BF16)
                mixT = sb(ph, "DmixT", [128, 8, 128], BF16)
                xr = sb(ph, "Dxr", [128, 1024])
                x1 = sb(ph, "Dx1", [128, GT, 1024])
                xn2 = sb(ph, "Dxn", [128, 1024])
                st4 = [sb(ph, f"Dst4_{i}", [128, 4]) for i in range(2)]
                h2T = [sb(ph, f"Dh2T{i}", [128, 8, TD], BF16) for i in range(2)]
                sg = [sb(ph, f"Dsg{i}", [128, TD]) for i in range(2)]
                actT = sb(ph, "DactT", [128, 22, TD], BF16)
                ptb = ps(ph, "Dptb", [128, 1024], BF16)
                pmoP = ps(ph, "DpmoP", [128, 512])
                ptr = ps(ph, "Dptr", [128, 1024])
                pgu = [ps(ph, f"Dpgu{i}", [128, 512]) for i in range(2)]
                pmoF = ps(ph, "DpmoF", [128, 1024])
                S.excl.update(["Dptb", "DpmoP", "Dptr0", "Dptr1", "Dpgu0", "Dpgu1", "DpmoF0", "DpmoF1"])
                if dbg:
                    print("phase D sbuf remaining", nc.sbuf_bytes_remaining)
                S.op("dve", C("tensor_copy", out=identb[:], in_=ident), ["cm"], ["identb"])
                for j in range(11):
                    for part in range(2):
                        c0 = part * FFN_H + j * 256
                        S.dma("pool", Wfi[:, :, c0:c0 + 256], w_ffi[:, c0:c0 + 256].rearrange("(k p) c -> p k c", p=128), [],
                              [f"Wfi{part}_{j}"], f"w{(2 * j + part) % 2}")
                x1t = [[x1[:, 0, :], x1[:, 1, :]], [G1[:], G2[:]]]
                x1k = [["Dx1_0", "Dx1_1"], ["G1", "G2"]]
                tcount = [0]

                def prep(g):
                    hb = g % 2
                    for i in range(GT):
                        ti = g * GT + i
                        b = tcount[0] % 2
                        tcount[0] += 1
                        X1, X1k = x1t[hb][i], x1k[hb][i]
                        S.dma("sp", mx[:], mix_d[ti * 128:(ti + 1) * 128, :], [], ["Dmx"], "Dmx")
                        S.dma("sp", xr[:], xin[(T0_OWN + ti) * 128:(T0_OWN + ti + 1) * 128, :], [], ["Dxr"], "Dxr")
                        for kc in range(8):
                            S.op("pe", C("transpose", out=ptb[:, kc * 128:(kc + 1) * 128], in_=mx[:, kc * 128:(kc + 1) * 128],
                                         identity=identb[:]), ["Dmx", "identb"], ["Dptb"])
                        S.op("act", C("copy", out=mixT[:].rearrange("p k t -> p (k t)"), in_=ptb[:, :]), ["Dptb"], ["DmixT"])
                        for hf in range(2):
                            hs = slice(hf * 512, (hf + 1) * 512)
                            for kc in range(8):
                                S.op("pe", C("matmul", pmoP[:, :], lhsT=mixT[:, kc, :], rhs=Wo[:, kc, hs], start=(kc == 0), stop=(kc == 7)),
                                     ["DmixT", f"Wo{kc}"], ["DpmoP"])
                            S.op("dve", C("tensor_tensor", out=X1[:, hs], in0=pmoP[:, :], in1=xr[:, hs], op=ALU.add),
                                 ["DpmoP", "Dxr"], [X1k])
                        s4, sk4 = st4[b], f"Dst4_{b}"
                        S.op("act", C("activation", out=xn2[:], in_=X1, func=AF.Square, accum_out=s4[:, 0:1]), [X1k], ["Dxn", sk4])
                        S.op("act", C("activation", out=s4[:, 1:2], in_=s4[:, 0:1], func=AF.Ln, scale=1.0 / D, bias=cst[:, 0:1]),
                             [sk4, "cst"], [sk4])
                        S.op("act", C("activation", out=s4[:, 2:3], in_=s4[:, 1:2], func=AF.Exp, scale=-0.5), [sk4], [sk4])
                        S.op("dve", C("tensor_scalar", out=xn2[:], in0=X1, scalar1=s4[:, 2:3], scalar2=None, op0=ALU.mult),
                             [X1k, sk4], ["Dxn"])
                        for kc in range(8):
                            S.op("pe", C("transpose", out=ptr[:, kc * 128:(kc + 1) * 128], in_=xn2[:, kc * 128:(kc + 1) * 128],
                                         identity=ident), ["Dxn", "cm"], [f"Dptr{kc // 4}"])
                        for kc in range(8):
                            dst = h2T[hb][:, kc, i * 128:(i + 1) * 128]
                            src = ptr[:, kc * 128:(kc + 1) * 128]
                            if kc < 4:
                                S.op("dve", C("tensor_scalar", out=dst, in0=src, scalar1=A2[:, 0, kc:kc + 1], scalar2=B2[:, 0, kc:kc + 1],
                                              op0=ALU.mult, op1=ALU.add), [f"Dptr{kc // 4}", "A2", "B2"], [f"Dh2T{hb}_{kc}"])
                            else:
                                S.op("act", C("activation", out=dst, in_=src, func=AF.Identity, scale=A2[:, 0, kc:kc + 1],
                                              bias=B2[:, 0, kc:kc + 1]), [f"Dptr{kc // 4}", "A2", "B2"], [f"Dh2T{hb}_{kc}"])

                def ffn(g):
                    hb = g % 2
                    for hc in range(22):
                        pg_, pgk_ = pgu[hc % 2], f"Dpgu{hc % 2}"
                        for kc in range(8):
                            S.op("pe", C("matmul", pg_[:, 0:TD], lhsT=Wfi[:, kc, hc * 128:(hc + 1) * 128], rhs=h2T[hb][:, kc, :],
                                         start=(kc == 0), stop=(kc == 7)), [f"Wfi0_{hc // 2}", f"Dh2T{hb}_{kc}"], [pgk_])
                        for kc in range(8):
                            S.op("pe", C("matmul", pg_[:, TD:2 * TD], lhsT=Wfi[:, kc, FFN_H + hc * 128:FFN_H + (hc + 1) * 128],
                                         rhs=h2T[hb][:, kc, :], start=(kc == 0), stop=(kc == 7)), [f"Wfi1_{hc // 2}", f"Dh2T{hb}_{kc}"], [pgk_])
                        sgb, sgk = sg[hc % 2], f"Dsg{hc % 2}"
                        S.op("act", C("activation", out=sgb[:], in_=pg_[:, 0:TD], func=AF.Silu), [pgk_], [sgk])
                        S.op("dve", C("tensor_tensor", out=actT[:, hc, :], in0=pg_[:, TD:2 * TD], in1=sgb[:], op=ALU.mult),
                             [pgk_, sgk], [f"DactT{hc}"])
                    for i in range(GT):
                        ti = g * GT + i
                        X1, X1k = x1t[hb][i], x1k[hb][i]
                        for hf in range(2):
                            hs = slice(hf * 512, (hf + 1) * 512)
                            for hc in range(22):
                                S.op("pe", C("matmul", pmoF[:, hs], lhsT=actT[:, hc, i * 128:(i + 1) * 128], rhs=Wfo[:, hc, hs],
                                             start=(hc == 0), stop=(hc == 21)), [f"DactT{hc}", f"Wfo{hc}"], [f"DpmoF{hf}"])
                            S.op("dve", C("tensor_tensor", out=X1[:, hs], in0=pmoF[:, hs], in1=X1[:, hs], op=ALU.add),
                                 [f"DpmoF{hf}", X1k], [X1k])
                        S.dma("sp", out_d[ti * 128:(ti + 1) * 128, :], X1, [X1k], [], "Dyo")

                S.play([S.record(lambda: prep(0))])
                for g in range(NG):
                    lists = []
                    if g + 1 < NG:
                        lists.append(S.record(lambda: prep(g + 1)))
                    lists.append(S.record(lambda: ffn(g)))
                    S.play(lists)
                S.barrier()
            open_stacks.remove(phD0)
            phD0.close()
            if stop_after == "D":
                raise _Stop()
        except _Stop:
            for st_ in reversed(open_stacks):
                st_.close()
        S.finish()
    return nc


def _const_mats():
    p = np.arange(128)
    same = (p[:, None] // 64) == (p[None, :] // 64)
    triP = np.where(same & (p[:, None] <= p[None, :]), -1.0 / 16, 0.0)
    triQ = np.where(same & (p[:, None] >= p[None, :]), -1.0 / 16, 0.0)
    blk = np.where(same, 1.0 / 64, 0.0)
    i64 = np.arange(64)
    maskP = ((p[:, None] % 64) <= i64[None, :]).astype(np.float64)
    maskQ = ((p[:, None] % 64) >= i64[None, :]).astype(np.float64)
    return np.concatenate([np.eye(128), triP, triQ, blk, maskP, maskQ], axis=1).astype(np.float32)


def _rope_tables(seq, positions):
    n_freq = 16
    inv_freq = (np.float32(10000.0) ** (-np.arange(n_freq, dtype=np.float32) / np.float32(n_freq))).astype(np.float32)
    row = (positions // 64).astype(np.float32)
    col = (positions % 64).astype(np.float32)
    ang = np.stack([row[:, None] * inv_freq[None, :], col[:, None] * inv_freq[None, :]], axis=0).astype(np.float32)
    cos = np.cos(ang).astype(np.float32)
    sin = np.sin(ang).astype(np.float32)
    n = len(positions)
    cosT = np.ones((128, 256 + n), np.float32)
    sinT = np.zeros((128, 256 + n), np.float32)
    for c in range(2):
        for ax in range(2):
            for hf in range(2):
                p0 = c * 64 + ax * 32 + hf * 16
                cosT[p0:p0 + 16, 256:] = cos[ax].T
                sinT[p0:p0 + 16, 256:] = (-sin[ax].T) if hf == 0 else sin[ax].T
    return cosT, sinT


_PROG_CACHE = {}


def _prep_inputs(inputs):
    f = lambda a: np.ascontiguousarray(np.asarray(a, dtype=np.float32))
    x = f(inputs["x"]); c = f(inputs["c"]); ctx = f(inputs["ctx"]); c_ctx = f(inputs["c_ctx"])
    B, SEQ, _ = x.shape
    half = SEQ // 2
    w_mod = f(inputs["w_mod"][0]); b_mod = f(inputs["b_mod"][0]).reshape(1, -1)
    g1 = f(inputs["norm1_g"][0]); g2 = f(inputs["norm2_g"][0])
    w_in = f(inputs["w_in"][0])
    gate_up = f(inputs["gla_gate_up"][0]); gate_bias = f(inputs["gla_gate_bias"][0])
    gng = f(inputs["gla_norm_g"][0]).reshape(1, 128); dng = f(inputs["diff_norm_g"][0]).reshape(1, 128)
    gq = f(inputs["diff_q_norm_g"][0]); gk = f(inputs["diff_k_norm_g"][0])
    lq = f(inputs["diff_lambda_q"][0]).reshape(1, 128); lk = f(inputs["diff_lambda_k"][0]).reshape(1, 128)
    w_out = f(inputs["w_out"][0]); w_ffi = f(inputs["w_ffn_in"][0]); w_ffo = f(inputs["w_ffn_out"][0])
    cmat = _const_mats()
    qkg = np.stack([np.tile(gk, 2), np.tile(gq, 2)], axis=1).astype(np.float32)
    in_maps = []
    for core in range(8):
        b, hf = core // 2, core % 2
        if hf == 1:
            oth = x[b, 0:half]; own = x[b, half:SEQ]; cx = ctx[b]
            pos = np.arange(SEQ)
            zP, zQ = 0, 1
        else:
            oth = x[b, SEQ - 1:half - 1:-1]; own = x[b, half - 1::-1]; cx = ctx[b, ::-1]
            pos = SEQ - 1 - np.arange(SEQ)
            zP, zQ = 1, 0
        xin = np.ascontiguousarray(np.concatenate([cx, oth, own], axis=0))
        cosT, sinT = _rope_tables(SEQ, pos)
        vecs = np.concatenate([b_mod.reshape(48, 128), g1.reshape(8, 128), g2.reshape(8, 128),
                               c[b].reshape(8, 128), c_ctx.reshape(8, 128)], axis=0).astype(np.float32)
        w_in_c = w_in.copy()
        w_in_c[:, C_GD:C_GD + 16] = w_in[:, C_GD + 16 * zP:C_GD + 16 * zP + 16]
        w_in_c[:, C_GD + 16:C_GD + 32] = w_in[:, C_GD + 16 * zQ:C_GD + 16 * zQ + 16]
        gu = np.zeros((49, 256), np.float32)
        gu[0:16] = gate_up[zP]; gu[16] = gate_bias[zP]
        gu[32:48] = gate_up[zQ]; gu[48] = gate_bias[zQ]
        in_maps.append({
            "xin": xin, "vecs": vecs, "w_mod": w_mod, "b_mod": b_mod, "w_in": w_in_c, "gu_in": gu, "gng_in": gng, "dng_in": dng,
            "qkg_in": qkg, "lq": lq, "lk": lk, "w_out": w_out, "w_ffi": w_ffi, "w_ffo": w_ffo,
            "cosT": cosT, "sinT": sinT, "cmat": cmat,
        })
    return in_maps, B, SEQ


def kernel(**inputs):
    in_maps, B, SEQ = _prep_inputs(inputs)
    half = SEQ // 2
    nt = half // 128
    key = (nt,)
    if key not in _PROG_CACHE:
        _PROG_CACHE[key] = build_program(nt, nt)
    nc = _PROG_CACHE[key]
    res = run_bass_kernel_spmd(nc, in_maps, core_ids=list(range(8)))
    out = np.empty((B, SEQ, D), np.float32)
    for core in range(8):
        b, hf = core // 2, core % 2
        o = np.asarray(res.results[core]["out"], dtype=np.float32)
        if hf == 1:
            out[b, half:SEQ] = o
        else:
            out[b, 0:half] = o[::-1]
    return out
```

```python
import math
from contextlib import ExitStack

import numpy as np
import ml_dtypes

import concourse.bass as bass
import concourse.mybir as mybir
from concourse.bass_utils import run_bass_kernel_spmd

F32 = mybir.dt.float32
BF16 = mybir.dt.bfloat16
AF = mybir.ActivationFunctionType
ALU = mybir.AluOpType
AX = mybir.AxisListType

D = 1024
EPS = 1e-6
NCTX = 2
FFN_H = 2816
W_IN_COLS = 3104
C_GQ, C_GK, C_GV, C_GR, C_GD, C_DQ, C_DK, C_DV = 0, 256, 512, 1024, 1536, 1568, 2080, 2592
LAM_INIT = 0.8 - 0.6 * math.exp(-0.3 * 0)


def C(name, *a, **kw):
    f = lambda e: getattr(e, name)(*a, **kw)
    f.opname, f.a, f.kw = name, a, kw
    return f


def _free_elems(ap):
    n = 1
    for d in list(ap.shape)[1:]:
        n *= int(d)
    return n


def _op_cost(e, fn):
    name = getattr(fn, "opname", None)
    if name is None:
        return 300.0
    out = fn.kw.get("out", fn.a[0] if fn.a else None)
    n = _free_elems(out) if out is not None else 128
    if e == "pe":
        if name == "transpose":
            return 220.0
        lhsT = fn.kw.get("lhsT")
        mult = 3.5 if (lhsT is not None and lhsT.dtype == F32) else 1.0
        return (max(n, 64) / 1.6 + 40.0) * mult
    if e == "act":
        return n / 0.96 + 220.0
    if e == "dve":
        return n / 0.96 * (8.0 if name == "reciprocal" else 1.0) + 80.0
    return n * 1.7 + 150.0


class _Stop(Exception):
    pass


class Sched:
    def __init__(self, nc, stack):
        self.nc = nc
        self.stack = stack
        self.eng = {"pe": nc.tensor, "act": nc.scalar, "dve": nc.vector, "pool": nc.gpsimd, "sp": nc.sync}
        self.sems = {}
        self.cnt = {}
        for e in ("pe", "act", "dve", "pool"):
            self.sems["E" + e] = stack.enter_context(nc.semaphore("s_" + e))
            self.cnt[e] = 0
        self.waited = {e: {} for e in self.eng}
        self.lastw = {}
        self.readers = {}
        self.slots = {}
        self.pending = {e: [] for e in self.eng}
        self.n_inst = 0
        self.excl = set()
        self.lastacc = {}
        self.pe_last = None
        self.rec = None
        self.sim_e, self.sim_w, self.sim_r, self.sim_a = {}, {}, {}, {}

    def record(self, fn):
        assert self.rec is None
        self.rec = []
        fn()
        lst, self.rec = self.rec, None
        return lst

    def _sim_ready(self, e, reads, writes):
        t = 0.0
        for k in reads:
            t = max(t, self.sim_w.get(k, 0.0))
            if k in self.excl:
                t = max(t, self.sim_a.get(k, 0.0))
        for k in writes:
            t = max(t, self.sim_w.get(k, 0.0), self.sim_r.get(k, 0.0))
        return t

    def _sim_commit(self, e, kind, fn_or_bytes, reads, writes):
        ready = self._sim_ready(e, reads, writes)
        start = max(ready + 120.0, self.sim_e.get(e, 0.0))
        if kind == "op":
            fin = start + _op_cost(e, fn_or_bytes)
            self.sim_e[e] = fin
        else:
            self.sim_e[e] = start + 80.0
            fin = start + 2000.0 + fn_or_bytes / 120.0
        for k in writes:
            self.sim_w[k] = fin
            self.sim_r[k] = 0.0
        for k in reads:
            self.sim_r[k] = max(self.sim_r.get(k, 0.0), fin)
        for k in list(reads) + list(writes):
            if k in self.excl:
                self.sim_a[k] = fin

    def play(self, lists, greedy=False):
        cur = [0] * len(lists)
        while True:
            best = None
            for li, lst in enumerate(lists):
                if cur[li] >= len(lst):
                    continue
                kind, args, kw = lst[cur[li]]
                if greedy:
                    e = args[0]
                    reads, writes = (args[2], args[3]) if kind == "op" else (args[3], args[4])
                    st = max(self._sim_ready(e, reads, writes) + 120.0, self.sim_e.get(e, 0.0))
                    key = (st, cur[li] / len(lst), li)
                else:
                    key = ((cur[li] + 0.5) / len(lst), li)
                if best is None or key < best[0]:
                    best = (key, li)
            if best is None:
                break
            li = best[1]
            kind, args, kw = lists[li][cur[li]]
            cur[li] += 1
            if kind == "op":
                self.op(*args, **kw)
            else:
                self.dma(*args, **kw)

    def _wait(self, e, tok):
        sk, val, _ = tok
        if self.waited[e].get(sk, 0) >= val:
            return
        self.eng[e].wait_ge(self.sems[sk], val)
        self.waited[e][sk] = val

    def _deps(self, e, reads, writes):
        deps = []
        for k in reads:
            t = self.lastw.get(k)
            if t is not None:
                deps.append((t, "raw"))
            if k in self.excl:
                t = self.lastacc.get(k)
                if t is not None and t[2] != e:
                    deps.append((t, "raw"))
        for k in writes:
            t = self.lastw.get(k)
            if t is not None:
                deps.append((t, "waw"))
            for r in self.readers.get(k, ()):
                deps.append((r, "war"))
        for t in self.pending[e]:
            deps.append((t, "raw"))
        self.pending[e] = []
        need = {}
        for t, kind in deps:
            if t[2] == e and e == "pe":
                continue
            if self.waited[e].get(t[0], 0) >= t[1]:
                continue
            if t[0] not in need or need[t[0]][1] < t[1]:
                need[t[0]] = t
        return list(need.values())

    def _emit_waits(self, e, waits, keep_last=False):
        held = None
        if keep_last and waits:
            held = waits[-1]
            waits = waits[:-1]
        for t in waits:
            self._wait(e, t)
        return held

    def _record(self, tok, reads, writes):
        for k in writes:
            self.lastw[k] = tok
            self.readers[k] = []
        for k in reads:
            lst = self.readers.setdefault(k, [])
            lst[:] = [r for r in lst if r[0] != tok[0]]
            lst.append(tok)
        for k in list(reads) + list(writes):
            if k in self.excl:
                self.lastacc[k] = tok

    def op(self, e, fn, reads=(), writes=(), rt=0):
        if self.rec is not None:
            self.rec.append(("op", (e, fn, tuple(reads), tuple(writes)), {"rt": rt}))
            return None
        waits = self._deps(e, reads, writes)
        self._sim_commit(e, "op", fn, reads, writes)
        attach = (e in ("act", "dve", "pool") and getattr(fn, "opname", None) is not None
                  and "accum_out" not in fn.kw and fn.opname not in ("stream_shuffle",))
        held = self._emit_waits(e, waits, keep_last=attach)
        if e == "pe":
            pl = self.pe_last
            if pl is not None and pl[1] != rt and any(k in pl[2] for k in writes):
                self._wait(e, pl[0])
        inst = fn(self.eng[e])
        if held is not None:
            inst._wait_ge(self.sems[held[0]], held[1])
            self.waited[e][held[0]] = held[1]
        self.cnt[e] += 1
        inst.then_inc(self.sems["E" + e], 1)
        tok = ("E" + e, self.cnt[e], e)
        self._record(tok, reads, writes)
        if e == "pe":
            self.pe_last = (tok, rt, set(writes))
        self.n_inst += 1
        return tok

    def dma(self, q, out, in_, reads, writes, slot, **kw):
        if self.rec is not None:
            self.rec.append(("dma", (q, out, in_, tuple(reads), tuple(writes), slot), kw))
            return None
        sk = "D" + slot
        if sk not in self.sems:
            self.sems[sk] = self.stack.enter_context(self.nc.semaphore("d_" + slot))
            self.slots[sk] = 0
        if self.slots[sk] > 0:
            self._wait(q, (sk, self.slots[sk], None))
        self._emit_waits(q, self._deps(q, reads, writes))
        nbytes = 128 * _free_elems(out) * (2 if out.dtype == BF16 else 4)
        self._sim_commit(q, "dma", nbytes, reads, writes)
        inst = self.eng[q].dma_start(out=out, in_=in_, **kw)
        self.slots[sk] += 16
        inst.then_inc(self.sems[sk], 16)
        tok = (sk, self.slots[sk], None)
        self._record(tok, reads, writes)
        self.n_inst += 1
        return tok

    def _all_toks(self):
        toks = [("E" + e, c, e) for e, c in self.cnt.items() if c > 0]
        toks += [(sk, v, None) for sk, v in self.slots.items() if v > 0]
        return toks

    def barrier(self):
        toks = self._all_toks()
        for e in self.eng:
            self.pending[e] = [t for t in toks if t[2] != e]

    def finish(self):
        for t in self._all_toks():
            self._wait("sp", t)


def build_program(NOTH, NOWN, dbg=False, stop_after=None):
    assert NOTH % 4 == 0 and NOWN % 4 == 0
    NK = NCTX + NOTH + NOWN
    NTOK = NK * 128
    T0_OTH = NCTX
    T0_OWN = NCTX + NOTH
    NQ = NOWN * 128

    nc = bass.Bass("TRN2", target_bir_lowering=False)

    def din(name, shape, dt=F32):
        return nc.dram_tensor(name, list(shape), dt, kind="ExternalInput").ap()

    def dscr(name, shape, dt):
        return nc.dram_tensor(name, list(shape), dt, kind=("ExternalOutput" if dbg else "Internal")).ap()

    xin = din("xin", [NTOK, D])
    vecs = din("vecs", [80, 128])
    w_mod = din("w_mod", [D, 6 * D])
    b_mod = din("b_mod", [1, 6 * D])
    w_in = din("w_in", [D, W_IN_COLS])
    gu_in = din("gu_in", [49, 256])
    gng_in = din("gng_in", [1, 128])
    dng_in = din("dng_in", [1, 128])
    qkg_in = din("qkg_in", [128, 2])
    lq_in = din("lq", [1, 128])
    lk_in = din("lk", [1, 128])
    w_out = din("w_out", [D, D])
    w_ffi = din("w_ffi", [D, 2 * FFN_H])
    w_ffo = din("w_ffo", [FFN_H, D])
    cos_in = din("cosT", [128, NTOK])
    sin_in = din("sinT", [128, NTOK])
    cmat = din("cmat", [128, 4 * 128 + 2 * 64])
    out_d = nc.dram_tensor("out", [NQ, D], F32, kind="ExternalOutput").ap()

    KT_d = dscr("KT_d", [4, 128, NTOK], BF16)
    VA_d = dscr("VA_d", [128, 4, NK, 129], BF16)
    QT_d = dscr("QT_d", [4, 128, NQ], BF16)
    qeQ_d = dscr("qeQ_d", [NOWN, 128, 2, 128], BF16)
    UQ_d = dscr("UQ_d", [NOWN, 128, 512], F32)
    op_d = dscr("op_d", [NOWN, 128, 512], F32)
    r_d = dscr("r_d", [NOWN, 128, 512], F32)
    mix_d = dscr("mix_d", [NQ, D], BF16)

    dbg_out = {}
    if dbg:
        for nm, shp in (("dbg_mod", [128, 96]), ("dbg_G", [128, 2048]),
                        ("dbg_SP", [128, 256]), ("dbg_SQ", [128, 256])):
            dbg_out[nm] = nc.dram_tensor(nm, shp, F32, kind="ExternalOutput").ap()

    top = ExitStack()
    with top:
        S = Sched(nc, top)
        S.excl.update(["p0T", "p0mod", "p0G0", "p0G1", "ptr0", "ptr1", "b2", "b3", "b4", "b5", "b6", "b7", "Bpo0", "Bpo1",
                       "CST0_0", "CST0_1", "CST1_0", "CST1_1", "CACC0", "CACC1", "CACC2",
                       "Dptb", "Dpmo0", "Dpmo1", "Dptr0", "Dptr1", "Dpgu0", "Dpgu1", "Dpgu2"])

        def sb(stack, name, shape, dt=F32):
            return stack.enter_context(nc.sbuf_tensor(name, list(shape), dt))

        def ps(stack, name, shape, dt=F32):
            return stack.enter_context(nc.psum_tensor(name, list(shape), dt))

        cm = sb(top, "cm", [128, 640])
        ident = cm[:, 0:128]
        triP = cm[:, 128:256]
        triQ = cm[:, 256:384]
        blk64 = cm[:, 384:512]
        maskP = cm[:, 512:576]
        maskQ = cm[:, 576:640]
        cst = sb(top, "cst", [128, 4])
        colv = sb(top, "colv", [128, 80])
        modT = sb(top, "modT", [128, 48, 2])
        A1 = sb(top, "A1", [128, 2, 8])
        B1 = sb(top, "B1", [128, 2, 8])
        A2 = sb(top, "A2", [128, 2, 8])
        B2 = sb(top, "B2", [128, 2, 8])
        G1 = sb(top, "G1", [128, 1024])
        G2 = sb(top, "G2", [128, 1024])
        gng = sb(top, "gng", [128, 128])
        dng = sb(top, "dng", [128, 128])
        qkg = sb(top, "qkg", [128, 2])
        negl = sb(top, "negl", [128, 1])
        gu = sb(top, "gu", [49, 256])
        eQ = sb(top, "eQ", [128, NOWN, 2, 2])
        SQ = sb(top, "SQ", [128, 2, 128])

        S.dma("sp", cm[:], cmat[:, :], [], ["cm"], "c0")
        S.op("dve", C("memset", cst[:, 0:1], EPS), [], ["cst"])
        S.op("dve", C("memset", cst[:, 1:2], 1.0), [], ["cst"])
        S.dma("sp", gu[:], gu_in[:, :], [], ["gu"], "c1")
        S.dma("sp", qkg[:], qkg_in[:, :], [], ["qkg"], "c2")
        S.dma("sp", gng[:], gng_in[0:1, :].to_broadcast([128, 128]), [], ["gng"], "c3")
        S.dma("sp", dng[:], dng_in[0:1, :].to_broadcast([128, 128]), [], ["dng"], "c4")
        S.op("dve", C("tensor_scalar", out=qkg[:, 1:2], in0=qkg[:, 1:2], scalar1=0.125, scalar2=None, op0=ALU.mult),
             ["qkg"], ["qkg"])
        S.op("dve", C("tensor_scalar", out=dng[:], in0=dng[:], scalar1=1.0 - LAM_INIT, scalar2=None, op0=ALU.mult),
             ["dng"], ["dng"])

        open_stacks = []
        try:
            phA = ExitStack()
            open_stacks.append(phA)
            Win = sb(phA, "Win", [128, 8, W_IN_COLS], BF16)
            for kc in range(8):
                S.dma("pool", Win[:, kc, :], w_in[kc * 128:(kc + 1) * 128, :], [], [f"Win{kc}"], f"w{kc % 2}")
            with ExitStack() as ph:
                stage = sb(ph, "stage", [80, 128])
                siluT = sb(ph, "siluT", [128, 8, 2])
                silubc = sb(ph, "silubc", [128, 8, 128])
                ones1 = sb(ph, "ones1", [1, 128])
                bmr = sb(ph, "bmr", [1, 2048])
                lqb = sb(ph, "lqb", [128, 128])
                lkb = sb(ph, "lkb", [128, 128])
                s2 = sb(ph, "s2", [128, 2])
                wm = [sb(ph, f"wm{i}", [128, 8, 512]) for i in range(2)]
                pT = ps(ph, "p0T", [128, 512])
                pmod = ps(ph, "p0mod", [128, 512])
                pG = [ps(ph, f"p0G{i}", [128, 512]) for i in range(2)]

                S.dma("sp", stage[:], vecs[:, :], [], ["stage"], "c0")
                S.dma("sp", bmr[:, 0:1024], b_mod[0:1, 2048:3072], [], ["bmr"], "c1")
                S.dma("sp", bmr[:, 1024:2048], b_mod[0:1, 5120:6144], [], ["bmr"], "c1")
                S.dma("sp", lqb[:], lq_in[0:1, :].to_broadcast([128, 128]), [], ["lqb"], "c2")
                S.dma("sp", lkb[:], lk_in[0:1, :].to_broadcast([128, 128]), [], ["lkb"], "c3")
                S.op("dve", C("memset", ones1[:], 1.0), [], ["ones1"])
                S.op("dve", C("tensor_tensor", out=lqb[:], in0=lqb[:], in1=lkb[:], op=ALU.mult), ["lqb", "lkb"], ["lqb"])
                S.op("dve", C("tensor_reduce", out=s2[:], in_=lqb[:].rearrange("p (a b) -> p a b", b=64), axis=AX.X, op=ALU.add),
                     ["lqb"], ["s2"])
                S.op("act", C("activation", out=s2[:], in_=s2[:], func=AF.Exp), ["s2"], ["s2"])
                S.op("dve", C("tensor_tensor", out=negl[:], in0=s2[:, 1:2], in1=s2[:, 0:1], op=ALU.subtract), ["s2"], ["negl"])
                S.op("dve", C("tensor_scalar", out=negl[:], in0=negl[:], scalar1=-LAM_INIT, scalar2=None, op0=ALU.add),
                     ["negl"], ["negl"])
                S.op("pe", C("transpose", out=pT[:, 0:80], in_=stage[:], identity=ident[0:80, 0:80]), ["stage", "cm"], ["p0T"])
                S.op("dve", C("tensor_copy", out=colv[:], in_=pT[:, 0:80]), ["p0T"], ["colv"])
                S.op("act", C("activation", out=siluT[:].rearrange("p k r -> p r k"),
                                                   in_=colv[:, 64:80].rearrange("p (r k) -> p r k", k=8), func=AF.Silu),
                     ["colv"], ["siluT"])
                for kc in range(8):
                    S.op("dve", C("tensor_scalar", out=silubc[:, kc, :], in0=ident[:, :], scalar1=0.0,
                                                                 scalar2=siluT[:, kc, 0:1], op0=ALU.mult, op1=ALU.add),
                         ["siluT", "cm"], ["silubc"])
                gi = 0
                for cg in range(12):
                    w = wm[cg % 2]
                    wk = f"wm{cg % 2}"
                    S.dma("sp", w[:], w_mod[:, cg * 512:(cg + 1) * 512].rearrange("(k p) c -> p k c", p=128), [], [wk], wk)
                    for jj in range(4):
                        j = cg * 4 + jj
                        for kc in range(8):
                            S.op("pe", C("matmul",
                                pmod[:, 2 * j:2 * j + 2], lhsT=w[:, kc, jj * 128:(jj + 1) * 128], rhs=siluT[:, kc, :],
                                start=(kc == 0), stop=(kc == 7)), [wk, "siluT"], ["p0mod"])
                    if cg in (4, 5, 10, 11):
                        pg = pG[gi % 2]
                        pk_ = f"p0G{gi % 2}"
                        Gt = G1 if cg < 6 else G2
                        gcol = (cg % 2) * 512
                        boff = (0 if cg < 6 else 1024) + gcol
                        for kc in range(8):
                            S.op("pe", C("matmul", pg[:, :], lhsT=silubc[:, kc, :], rhs=w[:, kc, :],
                                                                              start=(kc == 0), stop=False),
                                 [wk, "silubc"], [pk_])
                        S.op("pe", C("matmul", pg[:, :], lhsT=ones1[:, :], rhs=bmr[:, boff:boff + 512],
                                                                        start=False, stop=True), ["ones1", "bmr"], [pk_])
                        S.op("act", C("copy", out=Gt[:, gcol:gcol + 512], in_=pg[:, :]),
                             [pk_], ["G1" if cg < 6 else "G2"])
                        gi += 1
                S.op("dve", C("tensor_tensor", out=modT[:], in0=pmod[:, 0:96].rearrange("p (j r) -> p j r", r=2),
                                                      in1=colv[:, 0:48].unsqueeze(2).to_broadcast([128, 48, 2]), op=ALU.add),
                     ["p0mod", "colv"], ["modT"])
                for r in range(2):
                    S.op("dve", C("scalar_tensor_tensor", out=A1[:, r, :], in0=modT[:, 8:16, r], scalar=1.0,
                                                                      in1=colv[:, 48:56], op0=ALU.add, op1=ALU.mult),
                         ["modT", "colv"], ["A1"])
                    S.op("dve", C("tensor_copy", out=B1[:, r, :], in_=modT[:, 0:8, r]), ["modT"], ["B1"])
                    S.op("dve", C("scalar_tensor_tensor", out=A2[:, r, :], in0=modT[:, 32:40, r], scalar=1.0,
                                                                      in1=colv[:, 56:64], op0=ALU.add, op1=ALU.mult),
                         ["modT", "colv"], ["A2"])
                    S.op("dve", C("tensor_copy", out=B2[:, r, :], in_=modT[:, 24:32, r]), ["modT"], ["B2"])
                if dbg:
                    S.dma("pool", dbg_out["dbg_mod"][:, :], modT[:].rearrange("p j r -> p (j r)"), ["modT"], [], "dbg")
                    S.dma("pool", dbg_out["dbg_G"][:, 0:1024], G1[:], ["G1"], [], "dbg")
                    S.dma("pool", dbg_out["dbg_G"][:, 1024:2048], G2[:], ["G2"], [], "dbg")
                S.barrier()
            if stop_after == "0":
                raise _Stop()

            with ExitStack() as ph:
                xt = [sb(ph, f"xt{i}", [128, 1024]) for i in range(2)]
                junk = sb(ph, "junk", [128, 1024], BF16)
                st4 = [sb(ph, f"st4_{i}", [128, 4]) for i in range(2)]
                xn = [sb(ph, f"xn{i}", [128, 1024]) for i in range(2)]
                hT = [sb(ph, f"hT{i}", [128, 8, 512], BF16) for i in range(2)]
                cosb = [sb(ph, f"cosb{i}", [128, 512]) for i in range(1)] * 2
                sinb = [sb(ph, f"sinb{i}", [128, 512]) for i in range(1)] * 2
                sq = sb(ph, "sq", [128, 512], BF16)
                blkb = sb(ph, "blkb", [128, 128], BF16)
                lnb = sb(ph, "lnb", [128, 512])
                rsb = sb(ph, "rsb", [128, 512])
                kn = sb(ph, "kn", [128, 512])
                kr = sb(ph, "kr", [128, 512])
                t1 = sb(ph, "t1", [128, 512])
                kst = [sb(ph, f"kst{i}", [128, 4, 512], BF16) for i in range(2)]
                qst = [sb(ph, f"qst{i}", [128, 4, 512], BF16) for i in range(1)] * 2
                vst = [sb(ph, f"vst{i}", [128, 4, 4, 129], BF16) for i in range(2)]
                dA = sb(ph, "dA", [49, 512])
                ex = sb(ph, "ex", [128, 512])
                spb = [sb(ph, f"spb{i}", [128, 2, 256]) for i in range(2)]
                en = sb(ph, "en", [128, 256])
                ke = [[sb(ph, f"ke{z}_{i}", [128, 256], BF16) for i in range(2)] for z in range(2)]
                vbf = [sb(ph, f"vbf{i}", [128, 512], BF16) for i in range(4)]
                ez = [sb(ph, f"ez{z}", [128, 4, 2, 2]) for z in range(2)]
                E1 = [sb(ph, f"E1_{z}", [128, 2, 512]) for z in range(2)]
                E2 = [sb(ph, f"E2_{z}", [128, 2, 512]) for z in range(2)]
                qeT = [sb(ph, f"qeT{z}", [128, 2, 512], BF16) for z in range(2)]
                keT = [sb(ph, f"keT{z}", [128, 2, 512], BF16) for z in range(2)]
                UP = sb(ph, "UPs", [128, 1, 512])
                UQs = [sb(ph, f"UQs{i}", [128, 512]) for i in range(2)]
                UQc = sb(ph, "UQc", [128, 2, 512])
                SP = sb(ph, "SP", [128, 2, 128])
                tmpS = sb(ph, "tmpS", [128, 2, 128])
                Sbf = sb(ph, "Sbf", [128, 8, 2, 128], BF16)
                am = [sb(ph, f"am{z}", [128, 4, 64], BF16) for z in range(2)]
                osb = [sb(ph, f"osb{i}", [128, 512]) for i in range(2)]
                rsbuf = [sb(ph, f"rsbuf{i}", [128, 512]) for i in range(2)]
                ptr = ps(ph, "ptr", [128, 1024])
                b2 = ps(ph, "b2", [128, 512])
                b3 = ps(ph, "b3", [128, 512])
                b4 = ps(ph, "b4", [128, 512])
                b5 = ps(ph, "b5", [128, 512])
                b6 = ps(ph, "b6", [128, 512])
                b7 = ps(ph, "b7", [128, 512])

                WinK = [f"Win{kc}" for kc in range(8)]
                S.op("dve", C("tensor_copy", out=blkb[:], in_=blk64), ["cm"], ["blkb"])
                S.op("dve", C("memset", dA[:], 1.0), [], ["dA"])
                for i in range(2):
                    S.op("dve", C("memset", vst[i][:].rearrange("p a b c -> p (a b c)"), 1.0), [], [f"vst{i}"])
                S.op("dve", C("memset", SP[:].rearrange("p a b -> p (a b)"), 0.0), [], ["SP"])
                S.op("dve", C("memset", SQ[:].rearrange("p a b -> p (a b)"), 0.0), [], ["SQ"])

                groups = [(0, NCTX, "ctx")]
                for g in range(NOTH // 4):
                    groups.append((T0_OTH + 4 * g, 4, "oth"))
                for g in range(NOWN // 4):
                    groups.append((T0_OWN + 4 * g, 4, "own"))

                xc = [0]
                if stop_after == "A0":
                    groups = []
                def stage1(gidx):
                    t0, nt, kind = groups[gidx]
                    T = nt * 128
                    koff = t0 * 128
                    r_mod = 1 if kind == "ctx" else 0
                    own = kind == "own"
                    ctx = kind == "ctx"
                    gb = gidx % 2
                    h_T = hT[gb]
                    hK = f"hT{gb}"
                    for i in range(nt):
                        xb = xc[0] % 2
                        nb = xc[0] % 2
                        xc[0] += 1
                        x_t, xk = xt[xb], f"xt{xb}"
                        s4, sk4 = st4[xb], f"st4_{xb}"
                        x_n, nk = xn[nb], f"xn{nb}"
                        S.dma("sp", x_t[:], xin[(t0 + i) * 128:(t0 + i + 1) * 128, :], [], [xk], xk)
                        S.op("act", C("activation", out=junk[:], in_=x_t[:], func=AF.Square, accum_out=s4[:, 0:1]),
                             [xk], [sk4])
                        S.op("act", C("activation", out=s4[:, 1:2], in_=s4[:, 0:1], func=AF.Ln, scale=1.0 / D, bias=cst[:, 0:1]),
                             [sk4, "cst"], [sk4])
                        S.op("act", C("activation", out=s4[:, 2:3], in_=s4[:, 1:2], func=AF.Exp, scale=-0.5), [sk4], [sk4])
                        S.op("dve", C("tensor_scalar", out=x_n[:], in0=x_t[:], scalar1=s4[:, 2:3], scalar2=None,
                                                                                      op0=ALU.mult), [xk, sk4], [nk])
                        for kc in range(8):
                            S.op("pe", C("transpose", out=ptr[:, kc * 128:(kc + 1) * 128], in_=x_n[:, kc * 128:(kc + 1) * 128],
                                                                           identity=ident), [nk, "cm"], [f"ptr{kc // 4}"])
                        for kc in range(8):
                            dst = h_T[:, kc, i * 128:(i + 1) * 128]
                            src = ptr[:, kc * 128:(kc + 1) * 128]
                            if kc < 4:
                                S.op("dve", C("tensor_scalar",
                                    out=dst, in0=src, scalar1=A1[:, r_mod, kc:kc + 1], scalar2=B1[:, r_mod, kc:kc + 1],
                                    op0=ALU.mult, op1=ALU.add), [f"ptr{kc // 4}", "A1", "B1"], [f"{hK}_{kc}"])
                            else:
                                S.op("act", C("activation",
                                    out=dst, in_=src, func=AF.Identity, scale=A1[:, r_mod, kc:kc + 1], bias=B1[:, r_mod, kc:kc + 1]),
                                    [f"ptr{kc // 4}", "A1", "B1"], [f"{hK}_{kc}"])

                def stageY(gidx):
                    t0, nt, kind = groups[gidx]
                    T = nt * 128
                    koff = t0 * 128
                    r_mod = 1 if kind == "ctx" else 0
                    own = kind == "own"
                    ctx = kind == "ctx"
                    gb = gidx % 2
                    h_T = hT[gb]
                    hK = f"hT{gb}"
                    S.dma("sp", cosb[gb][:, 0:T], cos_in[:, koff:koff + T], [], ["cos0"], "cos0")
                    S.dma("sp", sinb[gb][:, 0:T], sin_in[:, koff:koff + T], [], ["sin0"], "sin0")

                    def qk_proj(col0, gcol, dst, dkey):
                        for h in range(4):
                            for kc in range(8):
                                S.op("pe", C("matmul", b2[:, 0:T], lhsT=Win[:, kc, col0 + h * 128:col0 + (h + 1) * 128],
                                                                          rhs=h_T[:, kc, 0:T], start=(kc == 0), stop=(kc == 7)),
                                     [WinK[kc], f"{hK}_{kc}"], ["b2"])
                            S.op("act", C("activation", out=sq[:, 0:T], in_=b2[:, 0:T], func=AF.Square), ["b2"], ["sq"])
                            S.op("pe", C("matmul", b3[:, 0:T], lhsT=blkb[:], rhs=sq[:, 0:T], start=True, stop=True), ["sq", "blkb"], ["b3"])
                            S.op("act", C("activation", out=lnb[:, 0:T], in_=b3[:, 0:T], func=AF.Ln, bias=cst[:, 0:1]), ["b3", "cst"], ["lnb"])
                            S.op("act", C("activation", out=rsb[:, 0:T], in_=lnb[:, 0:T], func=AF.Exp, scale=-0.5), ["lnb"], ["rsb"])
                            S.op("dve", C("scalar_tensor_tensor", out=kn[:, 0:T], in0=b2[:, 0:T], scalar=qkg[:, gcol:gcol + 1],
                                                                         in1=rsb[:, 0:T], op0=ALU.mult, op1=ALU.mult),
                                 ["b2", "rsb", "qkg"], ["kn"])
                            S.op("dve", C("stream_shuffle", out=kr[:, 0:T], in_=kn[:, 0:T], mask=[(i + 16) % 32 for i in range(32)]),
                                 ["kn"], ["kr"])
                            S.op("pool", C("tensor_tensor", out=t1[:, 0:T], in0=kn[:, 0:T], in1=cosb[gb][:, 0:T], op=ALU.mult),
                                 ["kn", "cos0"], ["t1"])
                            S.op("pool", C("tensor_tensor", out=kr[:, 0:T], in0=kr[:, 0:T], in1=sinb[gb][:, 0:T], op=ALU.mult),
                                 ["kr", "sin0"], ["kr"])
                            S.op("dve", C("tensor_tensor", out=dst[:, h, 0:T], in0=t1[:, 0:T], in1=kr[:, 0:T], op=ALU.add),
                                 ["t1", "kr"], [dkey])

                    qk_proj(C_DK, 0, kst[gb], f"kst{gb}")
                    S.dma("pool", KT_d[:, :, koff:koff + T].rearrange("h p t -> p h t"), kst[gb][:, :, 0:T], [f"kst{gb}"], [], f"ko{gb}")
                    if own:
                        qoff = (t0 - T0_OWN) * 128
                        qk_proj(C_DQ, 1, qst[gb], "qst0")
                        S.dma("pool", QT_d[:, :, qoff:qoff + T].rearrange("h p t -> p h t"), qst[gb][:, :, 0:T], ["qst0"], [], "qo0")
                    for i in range(nt):
                        for kc in range(8):
                            S.op("pe", C("matmul", b3[:, :], lhsT=h_T[:, kc, i * 128:(i + 1) * 128], rhs=Win[:, kc, C_DV:C_DV + 512],
                                                                      start=(kc == 0), stop=(kc == 7)), [WinK[kc], f"{hK}_{kc}"], ["b3"])
                        S.op("act", C("copy", out=vst[gb][:, :, i, 0:128], in_=b3[:, :].rearrange("p (h c) -> p h c", c=128)),
                             ["b3"], [f"vst{gb}"])
                    S.dma("pool", VA_d[:, :, t0:t0 + nt, :], vst[gb][:, :, 0:nt, :], [f"vst{gb}"], [], f"vo{gb}")
                    if own:
                        for i in range(nt):
                            ti = t0 - T0_OWN + i
                            for kc in range(8):
                                S.op("pe", C("matmul", b3[:, :], lhsT=h_T[:, kc, i * 128:(i + 1) * 128], rhs=Win[:, kc, C_GR:C_GR + 512],
                                             start=(kc == 0), stop=(kc == 7)), [WinK[kc], f"{hK}_{kc}"], ["b3"])
                            rb, rbk = rsbuf[i % 2], f"rsbuf{i % 2}"
                            S.op("act", C("copy", out=rb[:], in_=b3[:, :]), ["b3"], [rbk])
                            S.dma("pool", r_d[ti, :, :], rb[:], [rbk], [], rbk)


                def stageZ(gidx):
                    t0, nt, kind = groups[gidx]
                    T = nt * 128
                    koff = t0 * 128
                    r_mod = 1 if kind == "ctx" else 0
                    own = kind == "own"
                    ctx = kind == "ctx"
                    gb = gidx % 2
                    h_T = hT[gb]
                    hK = f"hT{gb}"
                    zs = (0, 1) if (own or ctx) else (0,)
                    for z in zs:
                        for kc in range(8):
                            S.op("pe", C("matmul", b7[32 * z:32 * z + 16, 0:T], lhsT=Win[:, kc, C_GD + 16 * z:C_GD + 16 * z + 16],
                                                                      rhs=h_T[:, kc, 0:T], start=(kc == 0), stop=(kc == 7)),
                                 [WinK[kc], f"{hK}_{kc}"], ["b7"])
                        S.op("act", C("copy", out=dA[32 * z:32 * z + 16, 0:T], in_=b7[32 * z:32 * z + 16, 0:T]), ["b7"], ["dA"])
                    for i in range(nt):
                        tl = slice(i * 128, (i + 1) * 128)
                        sp_t, spk = spb[i % 2], f"spb{i % 2}"
                        v_b, vk = vbf[i], f"vbf{i}"
                        for kc in range(8):
                            S.op("pe", C("matmul", b4[:, 0:256], lhsT=h_T[:, kc, tl], rhs=Win[:, kc, C_GK:C_GK + 256],
                                                                 start=(kc == 0), stop=(kc == 7)), [WinK[kc], f"{hK}_{kc}"], ["b4"])
                        for kc in range(8):
                            S.op("pe", C("matmul", b5[:, :], lhsT=h_T[:, kc, tl], rhs=Win[:, kc, C_GV:C_GV + 512],
                                                                 start=(kc == 0), stop=(kc == 7)), [WinK[kc], f"{hK}_{kc}"], ["b5"])
                        S.op("act", C("copy", out=v_b[:], in_=b5[:, :]), ["b5"], [vk])
                        for z in zs:
                            S.op("pe", C("matmul", b6[:, 256 * z:256 * z + 256], lhsT=dA[32 * z:32 * z + 17, tl],
                                                               rhs=gu[32 * z:32 * z + 17, :], start=True, stop=True), ["dA", "gu"], ["b6"], rt=32 * z)
                        W_ = 256 * len(zs)
                        S.op("act", C("activation", out=ex[:, 0:W_], in_=b6[:, 0:W_], func=AF.Exp, scale=-1.0), ["b6"], ["ex"])
                        S.op("act", C("activation", out=sp_t[:].rearrange("p z c -> p (z c)")[:, 0:W_], in_=ex[:, 0:W_],
                                                                       func=AF.Ln, bias=cst[:, 1:2]), ["ex", "cst"], [spk])
                        for z in zs:
                            tri = triP if z == 0 else triQ
                            k_e, kek = ke[z][i % 2], f"ke{z}_{i % 2}"
                            S.op("pe", C("matmul", b4[:, 256:512], lhsT=tri, rhs=sp_t[:, z, :], start=True, stop=True),
                                 [spk, "cm"], ["b4"])
                            S.op("act", C("activation", out=en[:], in_=b4[:, 256:512], func=AF.Exp, scale=-1.0), ["b4"], ["en"])
                            S.op("dve", C("tensor_tensor", out=k_e[:], in0=b4[:, 0:256], in1=en[:], op=ALU.mult),
                                 ["b4", "en"], [kek])
                            lc0 = (128 + 63) if z == 0 else 256
                            lastcols = cm[:, lc0:lc0 + 128].rearrange("p (a b) -> p a b", b=64)[:, :, 0]
                            for pr in range(2):
                                S.op("pe", C("matmul",
                                    b6[:, 2 * pr:2 * pr + 2], lhsT=sp_t[:, z, pr * 128:(pr + 1) * 128],
                                    rhs=lastcols, start=True, stop=True), [spk, "cm"], ["b6"])
                            S.op("act", C("activation", out=ez[z][:, i, :, :].rearrange("p a b -> p (a b)"), in_=b6[:, 0:4],
                                                                         func=AF.Exp), ["b6"], [f"ez{z}"])
                            if own:
                                for pr in range(2):
                                    S.op("pe", C("matmul",
                                        b6[:, 128 + pr * 128:256 + pr * 128], lhsT=sp_t[:, z, pr * 128:(pr + 1) * 128], rhs=tri,
                                        start=True, stop=True), [spk, "cm"], ["b6"])
                                S.op("act", C("activation", out=E1[z][:, :, tl], in_=b6[:, 128:384].rearrange("p (a b) -> p a b", b=128),
                                                                        func=AF.Exp), ["b6"], [f"E1_{z}"])
                                S.op("act", C("activation", out=E2[z][:, :, tl], in_=b6[:, 128:384].rearrange("p (a b) -> p a b", b=128),
                                                                        func=AF.Exp, scale=-1.0), ["b6"], [f"E2_{z}"])
                            for c in range(2):
                                for h in range(4):
                                    hp, pr = h % 2, h // 2
                                    S.op("pe", C("matmul",
                                        b7[hp * 64:(hp + 1) * 64, (pr * 2 + c) * 128:(pr * 2 + c + 1) * 128],
                                        lhsT=k_e[c * 64:(c + 1) * 64, h * 64:(h + 1) * 64],
                                        rhs=v_b[c * 64:(c + 1) * 64, h * 128:(h + 1) * 128], start=True, stop=True),
                                        [kek, vk], ["b7"], rt=c * 64)
                            if z == 0:
                                S.op("dve", C("tensor_copy", out=UP[:, 0, :], in_=b7[:, :]), ["b7"], ["UP"])
                            elif ctx:
                                S.op("dve", C("tensor_copy", out=UQc[:, i, :], in_=b7[:, :]), ["b7"], ["UQc"])
                            else:
                                ti = t0 - T0_OWN + i
                                uq, uqk = UQs[i % 2], f"UQs{i % 2}"
                                S.op("dve", C("tensor_copy", out=uq[:], in_=b7[:, :]), ["b7"], [uqk])
                                S.dma("pool", UQ_d[ti, :, :], uq[:], [uqk], [], uqk)
                                S.op("dve", C("tensor_copy", out=eQ[:, ti, :, :], in_=ez[1][:, i, :, :]), ["ez1"], ["eQ"])
                        for c in range(2):
                            if own:
                                S.op("act", C("copy", out=Sbf[:, 2 * i + c, :, :], in_=SP[:]), ["SP"], [f"Sbf{2 * i + c}"])
                            S.op("dve", C("tensor_tensor",
                                out=tmpS[:], in0=SP[:], in1=UP[:, 0, :].rearrange("p (a c d) -> p a c d", c=2, d=128)[:, :, c, :], op=ALU.add),
                                ["SP", "UP"], ["tmpS"])
                            for pr in range(2):
                                S.op("dve", C("tensor_scalar", out=SP[:, pr, :], in0=tmpS[:, pr, :],
                                                                                      scalar1=ez[0][:, i, pr, c:c + 1], scalar2=None, op0=ALU.mult),
                                     ["tmpS", "ez0"], ["SP"])
                    if ctx:
                        for i in reversed(range(nt)):
                            for c in (1, 0):
                                S.op("dve", C("tensor_tensor",
                                    out=tmpS[:], in0=SQ[:], in1=UQc[:, i, :].rearrange("p (a c d) -> p a c d", c=2, d=128)[:, :, c, :], op=ALU.add),
                                    ["SQ", "UQc"], ["tmpS"])
                                for pr in range(2):
                                    S.op("dve", C("tensor_scalar", out=SQ[:, pr, :], in0=tmpS[:, pr, :],
                                                                                          scalar1=ez[1][:, i, pr, c:c + 1], scalar2=None, op0=ALU.mult),
                                         ["tmpS", "ez1"], ["SQ"])
                        if dbg:
                            S.dma("pool", dbg_out["dbg_SP"][:, :], SP[:].rearrange("p a b -> p (a b)"), ["SP"], [], "dbg")
                            S.dma("pool", dbg_out["dbg_SQ"][:, :], SQ[:].rearrange("p a b -> p (a b)"), ["SQ"], [], "dbg")
                    if not own:
                        return

                    for pr in range(2):
                        for (col0, dsts, Es, scale) in ((C_GQ, qeT, E1, 0.125), (C_GK, keT, E2, 1.0)):
                            for kc in range(8):
                                S.op("pe", C("matmul",
                                    b4[:, 0:T], lhsT=Win[:, kc, col0 + pr * 128:col0 + (pr + 1) * 128], rhs=h_T[:, kc, 0:T],
                                    start=(kc == 0), stop=(kc == 7)), [WinK[kc], f"{hK}_{kc}"], ["b4"])
                            for z in range(2):
                                S.op("dve", C("scalar_tensor_tensor",
                                    out=dsts[z][:, pr, 0:T], in0=b4[:, 0:T], scalar=scale, in1=Es[z][:, pr, 0:T],
                                    op0=ALU.mult, op1=ALU.mult), ["b4", f"{'E1' if Es is E1 else 'E2'}_{z}"],
                                    [f"{'qeT' if dsts is qeT else 'keT'}{z}"])
                    for i in range(nt):
                        tl0 = i * 128
                        ti = t0 - T0_OWN + i
                        v_b, vk = vbf[i], f"vbf{i}"
                        for z in range(2):
                            mk = maskP if z == 0 else maskQ
                            for hp in range(2):
                                for pr in range(2):
                                    h = pr * 2 + hp
                                    for c in range(2):
                                        cs = slice(tl0 + c * 64, tl0 + (c + 1) * 64)
                                        S.op("pe", C("matmul",
                                            b7[c * 64:(c + 1) * 64, z * 256 + h * 64:z * 256 + (h + 1) * 64],
                                            lhsT=keT[z][hp * 64:(hp + 1) * 64, pr, cs], rhs=qeT[z][hp * 64:(hp + 1) * 64, pr, cs],
                                            start=True, stop=True), [f"keT{z}", f"qeT{z}"], ["b7"], rt=hp * 64)
                            S.op("dve", C("tensor_tensor",
                                out=am[z][:], in0=b7[:, z * 256:(z + 1) * 256].rearrange("p (h i) -> p h i", i=64),
                                in1=mk.unsqueeze(1).to_broadcast([128, 4, 64]), op=ALU.mult), ["b7", "cm"], [f"am{z}"])
                        for c in range(2):
                            first = True
                            for z in range(2):
                                for h in range(4):
                                    S.op("pe", C("matmul", b6[c * 64:(c + 1) * 64, h * 128:(h + 1) * 128],
                                                 lhsT=am[z][c * 64:(c + 1) * 64, h, :], rhs=v_b[c * 64:(c + 1) * 64, h * 128:(h + 1) * 128],
                                                 start=first, stop=False, skip_group_check=True), [f"am{z}", vk], ["b6"], rt=c * 64)
                                    first = False
                            for hp in range(2):
                                for pr in range(2):
                                    h = pr * 2 + hp
                                    cs = slice(tl0 + c * 64, tl0 + (c + 1) * 64)
                                    S.op("pe", C("matmul", b6[c * 64:(c + 1) * 64, h * 128:(h + 1) * 128],
                                                 lhsT=qeT[0][hp * 64:(hp + 1) * 64, pr, cs], rhs=Sbf[hp * 64:(hp + 1) * 64, 2 * i + c, pr, :],
                                                 start=False, stop=True, skip_group_check=True), ["qeT0", f"Sbf{2 * i + c}"], ["b6"], rt=hp * 64)
                        ob, obk = osb[i % 2], f"osb{i % 2}"
                        S.op("act", C("copy", out=ob[:], in_=b6[:, :]), ["b6"], [obk])
                        S.dma("pool", op_d[ti, :, :], ob[:], [obk], [], obk)
                        S.dma("pool", qeQ_d[ti, :, :, :], qeT[1][:, :, tl0:tl0 + 128], ["qeT1"], [], f"qq{i % 2}")

                if groups:
                    S.play([S.record(lambda: stage1(0))])
                for gidx in range(len(groups)):
                    lists = []
                    if gidx + 1 < len(groups):
                        lists.append(S.record(lambda: stage1(gidx + 1)))
                    lists.append(S.record(lambda: stageY(gidx)))
                    lists.append(S.record(lambda: stageZ(gidx)))
                    S.play(lists, greedy=True)
                S.barrier()
            open_stacks.remove(phA)
            phA.close()
            if stop_after is not None and stop_after.startswith("A"):
                raise _Stop()

            phD0 = ExitStack()
            open_stacks.append(phD0)
            Wo = sb(phD0, "Wo", [128, 8, D], BF16)
            Wfo = sb(phD0, "Wfo", [128, 22, D], BF16)

            def threadW():
                for kc in range(8):
                    S.dma("pool", Wo[:, kc, :], w_out[kc * 128:(kc + 1) * 128, :], [], [f"Wo{kc}"], f"w{kc % 2}")
                for hc in range(22):
                    S.dma("pool", Wfo[:, hc, :], w_ffo[hc * 128:(hc + 1) * 128, :], [], [f"Wfo{hc}"], f"w{hc % 2}")
                for kc in range(8):
                    S.op("dve" if kc % 2 == 0 else "pool", C("tensor_tensor", out=Wo[:, kc, :], in0=Wo[:, kc, :], in1=G1[:], op=ALU.mult),
                         [f"Wo{kc}", "G1"], [f"Wo{kc}"])
                for hc in range(22):
                    S.op("dve" if hc % 2 == 0 else "pool", C("tensor_tensor", out=Wfo[:, hc, :], in0=Wfo[:, hc, :], in1=G2[:], op=ALU.mult),
                         [f"Wfo{hc}", "G2"], [f"Wfo{hc}"])

            with ExitStack() as ph:
                qq = [sb(ph, f"Bqq{i}", [128, 2, 128], BF16) for i in range(2)]
                uq = [sb(ph, f"Buq{i}", [128, 512]) for i in range(2)]
                opb = [sb(ph, f"Bop{i}", [128, 512]) for i in range(2)]
                rb = [sb(ph, f"Brb{i}", [128, 512]) for i in range(2)]
                sr = sb(ph, "Bsr", [128, 512])
                ob = sb(ph, "Bo", [128, 512])
                go = [sb(ph, f"Bgo{i}", [128, 512], BF16) for i in range(2)]
                tmpSB = sb(ph, "BtmpS", [128, 2, 128])
                Sbf = [sb(ph, f"BSbf{i}", [128, 2, 128], BF16) for i in range(4)]
                s8B = [sb(ph, f"Bs8_{i}", [128, 12]) for i in range(2)]
                sqB = sb(ph, "BsqB", [128, 512])
                po = [ps(ph, "Bpo0", [128, 512])] * 2
                KTh = [sb(ph, f"CK{i}", [128, NTOK], BF16) for i in range(2)]
                VAh = [sb(ph, f"CV{i}", [128, NK, 129], BF16) for i in range(2)]
                QTg = [sb(ph, f"CQ{i}", [128, 4, 512], BF16) for i in range(2)]
                PT = [sb(ph, f"CP{i}", [128, 2, 512], BF16) for i in range(3)]
                accs = sb(ph, "Cacc", [128, 3, 387])
                rd = sb(ph, "Crd", [128, 3, 3])
                tmpo = sb(ph, "Ctmpo", [128, 128])
                oall = sb(ph, "Coall", [128, 4, 4, 128])
                s8 = [sb(ph, f"Cs8_{i}", [128, 12]) for i in range(2)]
                sqC = sb(ph, "CsqC", [128, 4, 128])
                ao = [sb(ph, f"Cao{i}", [128, 512], BF16) for i in range(2)]
                STp = [ps(ph, f"CST{i}", [128, 2, 512]) for i in range(2)]
                acc = ps(ph, "CACC", [128, 3, 512])

                def threadB():
                    cc = 0
                    for n, ti in enumerate(reversed(range(NOWN))):
                        b = n % 2
                        S.dma("sp", qq[b][:], qeQ_d[ti, :, :, :], [], [f"Bqq{b}"], f"Bqq{b}")
                        S.dma("sp", uq[b][:], UQ_d[ti, :, :], [], [f"Buq{b}"], f"Buq{b}")
                        S.dma("sp", opb[b][:], op_d[ti, :, :], [], [f"Bop{b}"], f"Bop{b}")
                        S.dma("sp", rb[b][:], r_d[ti, :, :], [], [f"Brb{b}"], f"Brb{b}")
                        for c in (1, 0):
                            sbf, sbk = Sbf[cc % 4], f"BSbf{cc % 4}"
                            cc += 1
                            S.op("pool", C("tensor_copy", out=sbf[:], in_=SQ[:]), ["SQ"], [sbk])
                            S.op("dve", C("tensor_tensor",
                                out=tmpSB[:], in0=SQ[:], in1=uq[b][:].rearrange("p (a c d) -> p a c d", c=2, d=128)[:, :, c, :], op=ALU.add),
                                ["SQ", f"Buq{b}"], ["BtmpS"])
                            for pr in range(2):
                                S.op("dve", C("tensor_scalar", out=SQ[:, pr, :], in0=tmpSB[:, pr, :],
                                                                                        scalar1=eQ[:, ti, pr, c:c + 1], scalar2=None, op0=ALU.mult),
                                     ["BtmpS", "eQ"], ["SQ"])
                            for h in (0, 2, 1, 3):
                                hp, pr = h % 2, h // 2
                                S.op("pe", C("matmul",
                                    po[b][c * 64:(c + 1) * 64, h * 128:(h + 1) * 128], lhsT=qq[b][hp * 64:(hp + 1) * 64, pr, c * 64:(c + 1) * 64],
                                    rhs=sbf[hp * 64:(hp + 1) * 64, pr, :], start=True, stop=True), [f"Bqq{b}", sbk], ["Bpo0"], rt=hp * 64)
                        S.op("dve", C("tensor_tensor", out=ob[:], in0=po[b][:, :], in1=opb[b][:], op=ALU.add),
                             ["Bpo0", f"Bop{b}"], ["Bo"])
                        s_, sk_ = s8B[b], f"Bs8_{b}"
                        S.op("dve", C("tensor_tensor", out=sqB[:], in0=ob[:], in1=ob[:], op=ALU.mult), ["Bo"], ["BsqB"])
                        S.op("dve", C("tensor_reduce", out=s_[:, 0:4], in_=sqB[:].rearrange("p (h c) -> p h c", c=128), axis=AX.X, op=ALU.add),
                             ["BsqB"], [sk_])
                        S.op("act", C("activation", out=s_[:, 4:8], in_=s_[:, 0:4], func=AF.Ln, scale=1.0 / 128, bias=cst[:, 0:1]),
                             [sk_, "cst"], [sk_])
                        S.op("act", C("activation", out=s_[:, 8:12], in_=s_[:, 4:8], func=AF.Exp, scale=-0.5), [sk_], [sk_])
                        S.op("act", C("activation", out=sr[:], in_=rb[b][:], func=AF.Exp, scale=-1.0), [f"Brb{b}"], ["Bsr"])
                        S.op("dve", C("tensor_scalar", out=sr[:], in0=sr[:], scalar1=1.0, scalar2=None, op0=ALU.add), ["Bsr"], ["Bsr"])
                        S.op("dve", C("reciprocal", out=sr[:], in_=sr[:]), ["Bsr"], ["Bsr"])
                        S.op("pool", C("tensor_tensor", out=sr[:], in0=sr[:], in1=rb[b][:], op=ALU.mult), ["Bsr", f"Brb{b}"], ["Bsr"])
                        for h in range(4):
                            S.op("dve", C("scalar_tensor_tensor",
                                out=ob[:, h * 128:(h + 1) * 128], in0=ob[:, h * 128:(h + 1) * 128], scalar=s_[:, 8 + h:9 + h], in1=gng[:],
                                op0=ALU.mult, op1=ALU.mult), ["Bo", sk_, "gng"], ["Bo"])
                        S.op("dve", C("tensor_tensor", out=go[b][:], in0=ob[:], in1=sr[:], op=ALU.mult), ["Bo", "Bsr"], [f"Bgo{b}"])
                        S.dma("pool", mix_d[ti * 128:(ti + 1) * 128, 0:512], go[b][:], [f"Bgo{b}"], [], f"Bgo{b}")

                def threadC():
                    NQG = NOWN // 4
                    it = 0
                    aoc = 0
                    for qg in range(NQG):
                        qb = qg % 2
                        S.dma("sp", QTg[qb][:], QT_d[:, :, qg * 512:(qg + 1) * 512].rearrange("h p t -> p h t"), [], [f"CQ{qb}"], f"CQ{qb}")
                        for h in range(4):
                            kb = it % 2
                            it += 1
                            S.dma("sp", KTh[kb][:], KT_d[h, :, :], [], [f"CK{kb}"], f"CK{kb}")
                            S.dma("sp", VAh[kb][:], VA_d[:, h, :, :], [], [f"CV{kb}"], f"CV{kb}")

                            def qk(kt):
                                for c in range(2):
                                    S.op("pe", C("matmul",
                                        STp[kt % 2][:, c, :], lhsT=KTh[kb][c * 64:(c + 1) * 64, kt * 128:(kt + 1) * 128],
                                        rhs=QTg[qb][c * 64:(c + 1) * 64, h, :], start=True, stop=True),
                                        [f"CK{kb}", f"CQ{qb}"], [f"CST{c}_{kt % 2}"], rt=c * 64)

                            qk(0)
                            qk(1)
                            for kt in range(NK):
                                S.op("act", C("activation", out=PT[kt % 3][:].rearrange("p c t -> p (c t)"),
                                                                          in_=STp[kt % 2][:].rearrange("p c t -> p (c t)"), func=AF.Exp),
                                     [f"CST0_{kt % 2}", f"CST1_{kt % 2}"], [f"CP{kt % 3}"])
                                if kt + 2 < NK:
                                    qk(kt + 2)
                                for c in range(2):
                                    for qt in range(4):
                                        sl = c * 4 + qt
                                        S.op("pe", C("matmul",
                                            acc[:, sl // 3, (sl % 3) * 129:(sl % 3) * 129 + 129], lhsT=PT[kt % 3][:, c, qt * 128:(qt + 1) * 128],
                                            rhs=VAh[kb][:, kt, :], start=(kt == 0 and sl % 3 == 0), stop=(kt == NK - 1), skip_group_check=True),
                                            [f"CP{kt % 3}", f"CV{kb}"], [f"CACC{sl // 3}"])
                            for bk in range(3):
                                nsl = 3 if bk < 2 else 2
                                S.op("dve", C("tensor_copy", out=accs[:, bk, 0:nsl * 129], in_=acc[:, bk, 0:nsl * 129]),
                                     [f"CACC{bk}"], [f"Cacc{bk}"])
                                S.op("dve", C("reciprocal",
                                    out=rd[:, bk, 0:nsl], in_=accs[:, bk, 0:nsl * 129].rearrange("p (s c) -> p s c", c=129)[:, :, 128]),
                                    [f"Cacc{bk}"], ["Crd"])
                            for sl in range(4, 8):
                                S.op("dve", C("tensor_scalar", out=rd[:, sl // 3, sl % 3:sl % 3 + 1], in0=rd[:, sl // 3, sl % 3:sl % 3 + 1],
                                                                             scalar1=negl[:, 0:1], scalar2=None, op0=ALU.mult), ["Crd", "negl"], ["Crd"])
                            for qt in range(4):
                                s0, s1 = qt, 4 + qt
                                S.op("dve", C("tensor_scalar", out=tmpo[:], in0=accs[:, s0 // 3, (s0 % 3) * 129:(s0 % 3) * 129 + 128],
                                                                             scalar1=rd[:, s0 // 3, s0 % 3:s0 % 3 + 1], scalar2=None, op0=ALU.mult),
                                     [f"Cacc{s0 // 3}", "Crd"], ["Ctmpo"])
                                S.op("dve", C("scalar_tensor_tensor",
                                    out=oall[:, qt, h, :], in0=accs[:, s1 // 3, (s1 % 3) * 129:(s1 % 3) * 129 + 128],
                                    scalar=rd[:, s1 // 3, s1 % 3:s1 % 3 + 1], in1=tmpo[:], op0=ALU.mult, op1=ALU.add),
                                    [f"Cacc{s1 // 3}", "Crd", "Ctmpo"], ["Coall"])
                        for qt in range(4):
                            b = aoc % 2
                            aoc += 1
                            s_, sk_ = s8[b], f"Cs8_{b}"
                            S.op("dve", C("tensor_tensor", out=sqC[:], in0=oall[:, qt, :, :], in1=oall[:, qt, :, :], op=ALU.mult),
                                 ["Coall"], ["CsqC"])
                            S.op("dve", C("tensor_reduce", out=s_[:, 0:4], in_=sqC[:], axis=AX.X, op=ALU.add), ["CsqC"], [sk_])
                            S.op("act", C("activation", out=s_[:, 4:8], in_=s_[:, 0:4], func=AF.Ln, scale=1.0 / 128, bias=cst[:, 0:1]),
                                 [sk_, "cst"], [sk_])
                            S.op("act", C("activation", out=s_[:, 8:12], in_=s_[:, 4:8], func=AF.Exp, scale=-0.5), [sk_], [sk_])
                            for h in range(4):
                                S.op("dve", C("scalar_tensor_tensor",
                                    out=ao[b][:, h * 128:(h + 1) * 128], in0=oall[:, qt, h, :], scalar=s_[:, 8 + h:9 + h], in1=dng[:],
                                    op0=ALU.mult, op1=ALU.mult), ["Coall", sk_, "dng"], [f"Cao{b}"])
                            row0 = (qg * 4 + qt) * 128
                            S.dma("pool", mix_d[row0:row0 + 128, 512:1024], ao[b][:], [f"Cao{b}"], [], f"Cao{b}")

                if dbg:
                    print("phase B+C sbuf remaining", nc.sbuf_bytes_remaining)
                S.play([S.record(threadW), S.record(threadB), S.record(threadC)])
                S.barrier()
            if stop_after in ("B", "C"):
                raise _Stop()

            with ExitStack() as ph:
                GT = 2
                TD = GT * 128
                NG = NOWN // GT
                Wfi = sb(ph, "Wfi", [128, 8, 2 * FFN_H], BF16)
                identb = sb(ph, "identb", [128, 128], BF16)
                mx = sb(ph, "Dmx", [128, 1024], BF16)
                mixT = sb(ph, "DmixT", [128, 8, 128], BF16)
                xr = sb(ph, "Dxr", [128, 1024])
                x1 = sb(ph, "Dx1", [128, GT, 1024])
                xn2 = sb(ph, "Dxn", [128, 1024])
                st4 = [sb(ph, f"Dst4_{i}", [128, 4]) for i in range(2)]
                h2T = [sb(ph, f"Dh2T{i}", [128, 8, TD], BF16) for i in range(2)]
                sg = [sb(ph, f"Dsg{i}", [128, TD]) for i in range(2)]
                actT = sb(ph, "DactT", [128, 22, TD], BF16)
                ptb = ps(ph, "Dptb", [128, 1024], BF16)
                pmoP = ps(ph, "DpmoP", [128, 512])
                ptr = ps(ph, "Dptr", [128, 1024])
                pgu = [ps(ph, f"Dpgu{i}", [128, 512]) for i in range(2)]
                pmoF = ps(ph, "DpmoF", [128, 1024])
                S.excl.update(["Dptb", "DpmoP", "Dptr0", "Dptr1", "Dpgu0", "Dpgu1", "DpmoF0", "DpmoF1"])
                if dbg:
                    print("phase D sbuf remaining", nc.sbuf_bytes_remaining)
                S.op("dve", C("tensor_copy", out=identb[:], in_=ident), ["cm"], ["identb"])
                for j in range(11):
                    for part in range(2):
                        c0 = part * FFN_H + j * 256
                        S.dma("pool", Wfi[:, :, c0:c0 + 256], w_ffi[:, c0:c0 + 256].rearrange("(k p) c -> p k c", p=128), [],
                              [f"Wfi{part}_{j}"], f"w{(2 * j + part) % 2}")
                x1t = [[x1[:, 0, :], x1[:, 1, :]], [G1[:], G2[:]]]
                x1k = [["Dx1_0", "Dx1_1"], ["G1", "G2"]]
                tcount = [0]

                def prep(g):
                    hb = g % 2
                    for i in range(GT):
                        ti = g * GT + i
                        b = tcount[0] % 2
                        tcount[0] += 1
                        X1, X1k = x1t[hb][i], x1k[hb][i]
                        S.dma("sp", mx[:], mix_d[ti * 128:(ti + 1) * 128, :], [], ["Dmx"], "Dmx")
                        S.dma("sp", xr[:], xin[(T0_OWN + ti) * 128:(T0_OWN + ti + 1) * 128, :], [], ["Dxr"], "Dxr")
                        for kc in range(8):
                            S.op("pe", C("transpose", out=ptb[:, kc * 128:(kc + 1) * 128], in_=mx[:, kc * 128:(kc + 1) * 128],
                                         identity=identb[:]), ["Dmx", "identb"], ["Dptb"])
                        S.op("act", C("copy", out=mixT[:].rearrange("p k t -> p (k t)"), in_=ptb[:, :]), ["Dptb"], ["DmixT"])
                        for hf in range(2):
                            hs = slice(hf * 512, (hf + 1) * 512)
                            for kc in range(8):
                                S.op("pe", C("matmul", pmoP[:, :], lhsT=mixT[:, kc, :], rhs=Wo[:, kc, hs], start=(kc == 0), stop=(kc == 7)),
                                     ["DmixT", f"Wo{kc}"], ["DpmoP"])
                            S.op("dve", C("tensor_tensor", out=X1[:, hs], in0=pmoP[:, :], in1=xr[:, hs], op=ALU.add),
                                 ["DpmoP", "Dxr"], [X1k])
                        s4, sk4 = st4[b], f"Dst4_{b}"
                        S.op("act", C("activation", out=xn2[:], in_=X1, func=AF.Square, accum_out=s4[:, 0:1]), [X1k], ["Dxn", sk4])
                        S.op("act", C("activation", out=s4[:, 1:2], in_=s4[:, 0:1], func=AF.Ln, scale=1.0 / D, bias=cst[:, 0:1]),
                             [sk4, "cst"], [sk4])
                        S.op("act", C("activation", out=s4[:, 2:3], in_=s4[:, 1:2], func=AF.Exp, scale=-0.5), [sk4], [sk4])
                        S.op("dve", C("tensor_scalar", out=xn2[:], in0=X1, scalar1=s4[:, 2:3], scalar2=None, op0=ALU.mult),
                             [X1k, sk4], ["Dxn"])
                        for kc in range(8):
                            S.op("pe", C("transpose", out=ptr[:, kc * 128:(kc + 1) * 128], in_=xn2[:, kc * 128:(kc + 1) * 128],
                                         identity=ident), ["Dxn", "cm"], [f"Dptr{kc // 4}"])
                        for kc in range(8):
                            dst = h2T[hb][:, kc, i * 128:(i + 1) * 128]
                            src = ptr[:, kc * 128:(kc + 1) * 128]
                            if kc < 4:
                                S.op("dve", C("tensor_scalar", out=dst, in0=src, scalar1=A2[:, 0, kc:kc + 1], scalar2=B2[:, 0, kc:kc + 1],
                                              op0=ALU.mult, op1=ALU.add), [f"Dptr{kc // 4}", "A2", "B2"], [f"Dh2T{hb}_{kc}"])
                            else:
                                S.op("act", C("activation", out=dst, in_=src, func=AF.Identity, scale=A2[:, 0, kc:kc + 1],
                                              bias=B2[:, 0, kc:kc + 1]), [f"Dptr{kc // 4}", "A2", "B2"], [f"Dh2T{hb}_{kc}"])

                def ffn(g):
                    hb = g % 2
                    for hc in range(22):
                        pg_, pgk_ = pgu[hc % 2], f"Dpgu{hc % 2}"
                        for kc in range(8):
                            S.op("pe", C("matmul", pg_[:, 0:TD], lhsT=Wfi[:, kc, hc * 128:(hc + 1) * 128], rhs=h2T[hb][:, kc, :],
                                         start=(kc == 0), stop=(kc == 7)), [f"Wfi0_{hc // 2}", f"Dh2T{hb}_{kc}"], [pgk_])
                        for kc in range(8):
                            S.op("pe", C("matmul", pg_[:, TD:2 * TD], lhsT=Wfi[:, kc, FFN_H + hc * 128:FFN_H + (hc + 1) * 128],
                                         rhs=h2T[hb][:, kc, :], start=(kc == 0), stop=(kc == 7)), [f"Wfi1_{hc // 2}", f"Dh2T{hb}_{kc}"], [pgk_])
                        sgb, sgk = sg[hc % 2], f"Dsg{hc % 2}"
                        S.op("act", C("activation", out=sgb[:], in_=pg_[:, 0:TD], func=AF.Silu), [pgk_], [sgk])
                        S.op("dve", C("tensor_tensor", out=actT[:, hc, :], in0=pg_[:, TD:2 * TD], in1=sgb[:], op=ALU.mult),
                             [pgk_, sgk], [f"DactT{hc}"])
                    for i in range(GT):
                        ti = g * GT + i
                        X1, X1k = x1t[hb][i], x1k[hb][i]
                        for hf in range(2):
                            hs = slice(hf * 512, (hf + 1) * 512)
                            for hc in range(22):
                                S.op("pe", C("matmul", pmoF[:, hs], lhsT=actT[:, hc, i * 128:(i + 1) * 128], rhs=Wfo[:, hc, hs],
                                             start=(hc == 0), stop=(hc == 21)), [f"DactT{hc}", f"Wfo{hc}"], [f"DpmoF{hf}"])
                            S.op("dve", C("tensor_tensor", out=X1[:, hs], in0=pmoF[:, hs], in1=X1[:, hs], op=ALU.add),
                                 [f"DpmoF{hf}", X1k], [X1k])
                        S.dma("sp", out_d[ti * 128:(ti + 1) * 128, :], X1, [X1k], [], "Dyo")

                S.play([S.record(lambda: prep(0))])
                for g in range(NG):
                    lists = []
                    if g + 1 < NG:
                        lists.append(S.record(lambda: prep(g + 1)))
                    lists.append(S.record(lambda: ffn(g)))
                    S.play(lists)
                S.barrier()
            open_stacks.remove(phD0)
            phD0.close()
            if stop_after == "D":
                raise _Stop()
        except _Stop:
            for st_ in reversed(open_stacks):
                st_.close()
        S.finish()
    return nc


def _const_mats():
    p = np.arange(128)
    same = (p[:, None] // 64) == (p[None, :] // 64)
    triP = np.where(same & (p[:, None] <= p[None, :]), -1.0 / 16, 0.0)
    triQ = np.where(same & (p[:, None] >= p[None, :]), -1.0 / 16, 0.0)
    blk = np.where(same, 1.0 / 64, 0.0)
    i64 = np.arange(64)
    maskP = ((p[:, None] % 64) <= i64[None, :]).astype(np.float64)
    maskQ = ((p[:, None] % 64) >= i64[None, :]).astype(np.float64)
    return np.concatenate([np.eye(128), triP, triQ, blk, maskP, maskQ], axis=1).astype(np.float32)


def _rope_tables(seq, positions):
    n_freq = 16
    inv_freq = (np.float32(10000.0) ** (-np.arange(n_freq, dtype=np.float32) / np.float32(n_freq))).astype(np.float32)
    row = (positions // 64).astype(np.float32)
    col = (positions % 64).astype(np.float32)
    ang = np.stack([row[:, None] * inv_freq[None, :], col[:, None] * inv_freq[None, :]], axis=0).astype(np.float32)
    cos = np.cos(ang).astype(np.float32)
    sin = np.sin(ang).astype(np.float32)
    n = len(positions)
    cosT = np.ones((128, 256 + n), np.float32)
    sinT = np.zeros((128, 256 + n), np.float32)
    for c in range(2):
        for ax in range(2):
            for hf in range(2):
                p0 = c * 64 + ax * 32 + hf * 16
                cosT[p0:p0 + 16, 256:] = cos[ax].T
                sinT[p0:p0 + 16, 256:] = (-sin[ax].T) if hf == 0 else sin[ax].T
    return cosT, sinT


_PROG_CACHE = {}


def _prep_inputs(inputs):
    f = lambda a: np.ascontiguousarray(np.asarray(a, dtype=np.float32))
    x = f(inputs["x"]); c = f(inputs["c"]); ctx = f(inputs["ctx"]); c_ctx = f(inputs["c_ctx"])
    B, SEQ, _ = x.shape
    half = SEQ // 2
    w_mod = f(inputs["w_mod"][0]); b_mod = f(inputs["b_mod"][0]).reshape(1, -1)
    g1 = f(inputs["norm1_g"][0]); g2 = f(inputs["norm2_g"][0])
    w_in = f(inputs["w_in"][0])
    gate_up = f(inputs["gla_gate_up"][0]); gate_bias = f(inputs["gla_gate_bias"][0])
    gng = f(inputs["gla_norm_g"][0]).reshape(1, 128); dng = f(inputs["diff_norm_g"][0]).reshape(1, 128)
    gq = f(inputs["diff_q_norm_g"][0]); gk = f(inputs["diff_k_norm_g"][0])
    lq = f(inputs["diff_lambda_q"][0]).reshape(1, 128); lk = f(inputs["diff_lambda_k"][0]).reshape(1, 128)
    w_out = f(inputs["w_out"][0]); w_ffi = f(inputs["w_ffn_in"][0]); w_ffo = f(inputs["w_ffn_out"][0])
    cmat = _const_mats()
    qkg = np.stack([np.tile(gk, 2), np.tile(gq, 2)], axis=1).astype(np.float32)
    in_maps = []
    for core in range(8):
        b, hf = core // 2, core % 2
        if hf == 1:
            oth = x[b, 0:half]; own = x[b, half:SEQ]; cx = ctx[b]
            pos = np.arange(SEQ)
            zP, zQ = 0, 1
        else:
            oth = x[b, SEQ - 1:half - 1:-1]; own = x[b, half - 1::-1]; cx = ctx[b, ::-1]
            pos = SEQ - 1 - np.arange(SEQ)
            zP, zQ = 1, 0
        xin = np.ascontiguousarray(np.concatenate([cx, oth, own], axis=0))
        cosT, sinT = _rope_tables(SEQ, pos)
        vecs = np.concatenate([b_mod.reshape(48, 128), g1.reshape(8, 128), g2.reshape(8, 128),
                               c[b].reshape(8, 128), c_ctx.reshape(8, 128)], axis=0).astype(np.float32)
        w_in_c = w_in.copy()
        w_in_c[:, C_GD:C_GD + 16] = w_in[:, C_GD + 16 * zP:C_GD + 16 * zP + 16]
        w_in_c[:, C_GD + 16:C_GD + 32] = w_in[:, C_GD + 16 * zQ:C_GD + 16 * zQ + 16]
        gu = np.zeros((49, 256), np.float32)
        gu[0:16] = gate_up[zP]; gu[16] = gate_bias[zP]
        gu[32:48] = gate_up[zQ]; gu[48] = gate_bias[zQ]
        in_maps.append({
            "xin": xin, "vecs": vecs, "w_mod": w_mod, "b_mod": b_mod, "w_in": w_in_c, "gu_in": gu, "gng_in": gng, "dng_in": dng,
            "qkg_in": qkg, "lq": lq, "lk": lk, "w_out": w_out, "w_ffi": w_ffi, "w_ffo": w_ffo,
            "cosT": cosT, "sinT": sinT, "cmat": cmat,
        })
    return in_maps, B, SEQ


def kernel(**inputs):
    in_maps, B, SEQ = _prep_inputs(inputs)
    half = SEQ // 2
    nt = half // 128
    key = (nt,)
    if key not in _PROG_CACHE:
        _PROG_CACHE[key] = build_program(nt, nt)
    nc = _PROG_CACHE[key]
    res = run_bass_kernel_spmd(nc, in_maps, core_ids=list(range(8)))
    out = np.empty((B, SEQ, D), np.float32)
    for core in range(8):
        b, hf = core // 2, core % 2
        o = np.asarray(res.results[core]["out"], dtype=np.float32)
        if hf == 1:
            out[b, half:SEQ] = o
        else:
            out[b, 0:half] = o[::-1]
    return out
```
